# Optimizing a Trainium2 kernel written in Bass

```python
import jax, jax.numpy as jnp
from jax import lax
import numpy as np

D_MODEL = 2048
BATCH = 4
SEQ = 2048
DEPTH = 2
DEC_BATCH = 32
DEC_SEQ = 16
PAST_LEN = 1024

CHUNK = 64
Q_BLOCK = 128
HEAD_DIM = 128
FOX_HEADS = 8
SB_HEADS = 8
FOX_WIDTH = FOX_HEADS * HEAD_DIM
SB_WIDTH = SB_HEADS * HEAD_DIM
N_AB = 4 * FOX_WIDTH + 4 * SB_WIDTH + FOX_HEADS
MLP_EXPAND = 2
C_WIDTH = MLP_EXPAND * D_MODEL
C_GROUPS = 16
C_GROUP_DIM = C_WIDTH // C_GROUPS
C_CHUNK = 128
RMS_EPS = 1e-6
LN_EPS = 1e-5
FORGET_BIAS = 2.0
ATTN_SCALE = HEAD_DIM ** -0.5

kernel_name = 'hybrid_fox_stickbreak_gmlp_stream_step'


def rmsnorm(x, g):
    x32 = x.astype(jnp.float32)
    y = x32 * lax.rsqrt(jnp.mean(x32 * x32, axis=-1, keepdims=True) + RMS_EPS)
    return (y * g.astype(jnp.float32)).astype(x.dtype)


def layernorm(x, g, b):
    x32 = x.astype(jnp.float32)
    mu = jnp.mean(x32, axis=-1, keepdims=True)
    xc = x32 - mu
    y = xc * lax.rsqrt(jnp.mean(xc * xc, axis=-1, keepdims=True) + LN_EPS)
    return (y * g.astype(jnp.float32) + b.astype(jnp.float32)).astype(x.dtype)


def _heads(t, n):
    return t.reshape(t.shape[0], t.shape[1], n, HEAD_DIM)


def ab_project(h, w_in0, b_forget):
    p = jnp.einsum('bsd,dn->bsn', h, w_in0)
    o = 0
    parts = []
    for width in (FOX_WIDTH,) * 4 + (SB_WIDTH,) * 4:
        parts.append(p[..., o:o + width])
        o += width
    f_logit = p[..., o:o + FOX_HEADS].astype(jnp.float32) + b_forget.astype(jnp.float32)
    logf = jax.nn.log_sigmoid(f_logit)
    fq, fk, fv = (_heads(t, FOX_HEADS) for t in parts[0:3])
    sq, sk, sv = (_heads(t, SB_HEADS) for t in parts[4:7])
    return fq, fk, fv, parts[3], logf, sq, sk, sv, parts[7]


def fox_core(q, k, v, Fq, Fk, qpos, kpos):
    s = jnp.einsum('bqhd,bkhd->bhqk', q, k).astype(jnp.float32) * ATTN_SCALE
    s = s + (jnp.swapaxes(Fq, 1, 2)[..., :, None] - jnp.swapaxes(Fk, 1, 2)[..., None, :])
    causal = kpos[None, :] <= qpos[:, None]
    s = jnp.where(causal, s, -jnp.inf)
    p = jax.nn.softmax(s, axis=-1)
    return jnp.einsum('bhqk,bkhd->bqhd', p.astype(v.dtype), v)


def sb_core(q, k, v, qpos, kpos):
    z = jnp.einsum('bqhd,bkhd->bhqk', q, k).astype(jnp.float32) * ATTN_SCALE
    before = kpos[None, :] < qpos[:, None]
    log_stay = jnp.where(before, jax.nn.log_sigmoid(-z), 0.0)
    after = lax.cumsum(log_stay, axis=3, reverse=True) - log_stay
    a = jnp.where(before, jnp.exp(jax.nn.log_sigmoid(z) + after), 0.0)
    return jnp.einsum('bhqk,bkhd->bqhd', a.astype(v.dtype), v)


def fox_prompt(q, k, v, logf):
    B_, S_, H, Dh = q.shape
    nb = S_ // Q_BLOCK
    F = jnp.cumsum(logf, axis=1)
    kpos = jnp.arange(S_)
    qb = q.reshape(B_, nb, Q_BLOCK, H, Dh).swapaxes(0, 1)
    Fb = F.reshape(B_, nb, Q_BLOCK, H).swapaxes(0, 1)

    def block(args):
        i, qi, Fi = args
        return fox_core(qi, k, v, Fi, F, i * Q_BLOCK + jnp.arange(Q_BLOCK), kpos)

    o = lax.map(block, (jnp.arange(nb), qb, Fb))
    return o.swapaxes(0, 1).reshape(B_, S_, H * Dh)


def sb_prompt(q, k, v):
    B_, S_, H, Dh = q.shape
    nb = S_ // Q_BLOCK
    kpos = jnp.arange(S_)
    qb = q.reshape(B_, nb, Q_BLOCK, H, Dh).swapaxes(0, 1)

    def block(args):
        i, qi = args
        return sb_core(qi, k, v, i * Q_BLOCK + jnp.arange(Q_BLOCK), kpos)

    o = lax.map(block, (jnp.arange(nb), qb))
    return o.swapaxes(0, 1).reshape(B_, S_, H * Dh)


def c_project(h, w_in1, ln_g, ln_b):
    p = jnp.einsum('bsd,dn->bsn', h, w_in1)
    u = p[..., :C_WIDTH]
    v = layernorm(p[..., C_WIDTH:2 * C_WIDTH], ln_g, ln_b)
    z = p[..., 2 * C_WIDTH:]
    return u, v, z


def spatial_mix(v, w_sp, b_sp):
    B_, N, L, _ = v.shape
    vg = v.reshape(B_, N, L, C_GROUPS, C_GROUP_DIM)
    causal = jnp.tril(jnp.ones((L, L), dtype=bool))
    w = jnp.where(causal, w_sp[:, :L, :L], 0.0).astype(v.dtype)
    mixed = jnp.einsum('gts,bnsgc->bntgc', w, vg) + b_sp[:, :L].T[None, None, :, :, None].astype(v.dtype)
    return mixed.reshape(B_, N, L, C_WIDTH)


def setup_inputs(seed: int = 0) -> dict:
    key = jax.random.key(seed)
    ks = jax.random.split(key, 20)
    nrm = lambda k, shape, scale: jax.random.normal(k, shape, jnp.float32) * scale
    kv_shape = (DEC_BATCH, PAST_LEN, FOX_HEADS, HEAD_DIM)
    sb_shape = (DEC_BATCH, PAST_LEN, SB_HEADS, HEAD_DIM)
    return {
        'x_prompt': nrm(ks[0], (BATCH, SEQ, D_MODEL), 1.0),
        'x_sample': nrm(ks[1], (DEC_BATCH, DEC_SEQ, D_MODEL), 1.0),
        'cache_fox_k': nrm(ks[2], kv_shape, 1.0),
        'cache_fox_v': nrm(ks[3], kv_shape, 1.0),
        'cache_fox_logf': jax.nn.log_sigmoid(FORGET_BIAS + nrm(ks[4], (DEC_BATCH, PAST_LEN, FOX_HEADS), 1.0)),
        'cache_sb_k': nrm(ks[5], sb_shape, 1.0),
        'cache_sb_v': nrm(ks[6], sb_shape, 1.0),
        'norm0_g': 1.0 + nrm(ks[7], (D_MODEL,), 0.02),
        'w_in0': nrm(ks[8], (D_MODEL, N_AB), D_MODEL ** -0.5),
        'b_forget': FORGET_BIAS + nrm(ks[9], (FOX_HEADS,), 0.1),
        'w_out0': nrm(ks[10], (FOX_WIDTH + SB_WIDTH, D_MODEL), (FOX_WIDTH + SB_WIDTH) ** -0.5),
        'norm1_g': 1.0 + nrm(ks[11], (D_MODEL,), 0.02),
        'w_in1': nrm(ks[12], (D_MODEL, 3 * C_WIDTH), D_MODEL ** -0.5),
        'sgu_ln_g': 1.0 + nrm(ks[13], (C_WIDTH,), 0.02),
        'sgu_ln_b': nrm(ks[14], (C_WIDTH,), 0.02),
        'w_sp': nrm(ks[15], (C_GROUPS, C_CHUNK, C_CHUNK), C_CHUNK ** -0.5),
        'b_sp': 1.0 + nrm(ks[16], (C_GROUPS, C_CHUNK), 0.1),
        'w_out1': nrm(ks[17], (C_WIDTH, D_MODEL), C_WIDTH ** -0.5),
        'final_g': 1.0 + nrm(ks[18], (D_MODEL,), 0.02),
    }


def reference(x_prompt, x_sample, cache_fox_k, cache_fox_v, cache_fox_logf, cache_sb_k, cache_sb_v,
              norm0_g, w_in0, b_forget, w_out0, norm1_g, w_in1, sgu_ln_g, sgu_ln_b, w_sp, b_sp, w_out1, final_g):
    hp = x_prompt
    hs = x_sample
    S_ = x_prompt.shape[1]
    P = cache_fox_k.shape[1]
    T = x_sample.shape[1]
    for layer in range(DEPTH):
        if layer % 2 == 0:
            fq, fk, fv, fz, logf_p, sq, sk, sv, sz = ab_project(rmsnorm(hp, norm0_g), w_in0, b_forget)
            fo = fox_prompt(fq, fk, fv, logf_p)
            so = sb_prompt(sq, sk, sv)
            mix = jnp.concatenate([fo * jax.nn.silu(fz), so * jax.nn.silu(sz)], axis=-1)
            hp = hp + jnp.einsum('bsn,nd->bsd', mix, w_out0)
            fox_k_prompt, fox_v_prompt, fox_logf_prompt = fk, fv, logf_p
            sb_k_prompt, sb_v_prompt = sk, sv
            gq, gk, gv, gz, logf_s, tq, tk, tv, tz = ab_project(rmsnorm(hs, norm0_g), w_in0, b_forget)
            qpos = P + jnp.arange(T)
            kpos = jnp.arange(P + T)
            k_all = jnp.concatenate([cache_fox_k.astype(gk.dtype), gk], axis=1)
            v_all = jnp.concatenate([cache_fox_v.astype(gv.dtype), gv], axis=1)
            F_all = jnp.cumsum(jnp.concatenate([cache_fox_logf.astype(jnp.float32), logf_s], axis=1), axis=1)
            go = fox_core(gq, k_all, v_all, F_all[:, P:], F_all, qpos, kpos).reshape(T and hs.shape[0], T, FOX_WIDTH)
            sk_all = jnp.concatenate([cache_sb_k.astype(tk.dtype), tk], axis=1)
            sv_all = jnp.concatenate([cache_sb_v.astype(tv.dtype), tv], axis=1)
            to = sb_core(tq, sk_all, sv_all, qpos, kpos).reshape(hs.shape[0], T, SB_WIDTH)
            mix_s = jnp.concatenate([go * jax.nn.silu(gz), to * jax.nn.silu(tz)], axis=-1)
            hs = hs + jnp.einsum('bsn,nd->bsd', mix_s, w_out0)
            fox_k_sample, fox_v_sample, fox_logf_sample = gk, gv, logf_s
            sb_k_sample, sb_v_sample = tk, tv
        else:
            u, v, z = c_project(rmsnorm(hp, norm1_g), w_in1, sgu_ln_g, sgu_ln_b)
            Bp = hp.shape[0]
            mixed = spatial_mix(v.reshape(Bp, S_ // C_CHUNK, C_CHUNK, C_WIDTH), w_sp, b_sp).reshape(Bp, S_, C_WIDTH)
            hp = hp + jnp.einsum('bsn,nd->bsd', u * mixed * jax.nn.silu(z), w_out1)
            us, vs, zs = c_project(rmsnorm(hs, norm1_g), w_in1, sgu_ln_g, sgu_ln_b)
            mixed_s = spatial_mix(vs[:, None], w_sp, b_sp)[:, 0]
            hs = hs + jnp.einsum('bsn,nd->bsd', us * mixed_s * jax.nn.silu(zs), w_out1)
            sgu_v_sample = vs
    y_prompt = rmsnorm(hp, final_g)
    y_sample = rmsnorm(hs, final_g)
    return (y_prompt, y_sample,
            fox_k_prompt, fox_v_prompt, fox_logf_prompt,
            fox_k_sample, fox_v_sample, fox_logf_sample,
            sb_k_prompt, sb_v_prompt,
            sb_k_sample, sb_v_sample,
            sgu_v_sample)
```

```python
import numpy as np
from contextlib import ExitStack
import concourse.bass as bass
import concourse.mybir as mybir
from concourse.bass_utils import run_bass_kernel_spmd

F32 = mybir.dt.float32
BF16 = mybir.dt.bfloat16
AF = mybir.ActivationFunctionType
ALU = mybir.AluOpType

D = 2048
NCORES = 8
SCALE = 128 ** -0.5
OWN = {0: [0, 3, 4, 7, 8, 11, 12, 15], 1: [1, 2, 5, 6, 9, 10, 13, 14]}
OTHER = {h: [b for b in range(16) if b not in OWN[h]] for h in (0, 1)}
NTOK = 2112
NOWN = 1088
STAGE = 99
import os
PARTS = os.environ.get('PARTS', 'KOVQ')
NGROUPS = int(os.environ.get('NGROUPS', '8'))
DBG = os.environ.get('DBG', '')

COMPUTE = ("pe", "act", "dve", "pool")
ALL_ENG = ("pe", "act", "dve", "pool", "sp")


class Res:
    __slots__ = ("name", "writer", "readers", "lock")

    def __init__(self, name="", lock=None):
        self.name = name
        self.writer = None
        self.readers = []
        self.lock = lock


class Op:
    __slots__ = ("eng", "fn", "deps", "is_dma", "count", "needed", "dsem", "dval", "prewait")

    def __init__(self, eng, fn, is_dma):
        self.eng = eng
        self.fn = fn
        self.deps = []
        self.is_dma = is_dma
        self.count = None
        self.needed = False
        self.dsem = None
        self.dval = None
        self.prewait = None


class Prog:
    def __init__(self, nc):
        self.nc = nc
        self.ops = {e: [] for e in ALL_ENG}
        self.n_dma_sems = {"sp": 24, "pool": 24}
        self.dma_rr = {e: 0 for e in ALL_ENG}
        self.dma_last = {}
        self.dma_cnt = {}
        self.phase = Res("phase")

    def _record(self, op, reads, writes):
        deps = []
        for r in reads:
            if r.writer is not None:
                deps.append(r.writer)
        for w in writes:
            if w.writer is not None:
                deps.append(w.writer)
            last = {}
            for r in w.readers:
                if r.is_dma:
                    deps.append(r)
                else:
                    last[r.eng] = r
            deps.extend(last.values())
        seen = set()
        for d in deps:
            if d is op or id(d) in seen:
                continue
            seen.add(id(d))
            if op.eng == "pe" and d.eng == "pe" and not d.is_dma and not op.is_dma:
                continue
            op.deps.append(d)
            d.needed = True
        for r in reads:
            r.readers.append(op)
        for w in writes:
            w.writer = op
            w.readers = []
        self.ops[op.eng].append(op)
        return op

    def op(self, eng, fn, reads=(), writes=(), glob=False):
        reads = list(reads)
        writes = list(writes)
        for r in reads:
            if r.lock is not None and r not in writes:
                writes.append(r.lock)
        if not glob:
            reads.append(self.phase)
        return self._record(Op(eng, fn, False), reads, writes)

    def dma(self, eng, out, in_, reads=(), writes=(), glob=False):
        def fn(e, out=out, in_=in_):
            return e.dma_start(out=out, in_=in_)
        op = Op(eng, fn, True)
        n = self.n_dma_sems[eng]
        slot = self.dma_rr[eng] % n
        self.dma_rr[eng] += 1
        key = (eng, slot)
        cnt = self.dma_cnt.get(key, 0) + 1
        self.dma_cnt[key] = cnt
        op.dsem = key
        op.dval = 16 * cnt
        op.prewait = self.dma_last.get(key)
        self.dma_last[key] = op
        reads = list(reads)
        if not glob:
            reads.append(self.phase)
        return self._record(op, reads, list(writes))

    def barrier(self, scratch):
        o = Op("pool", lambda e: e.memset(scratch, 0.0), False)
        last = {}
        for r in self.phase.readers:
            if r.is_dma:
                o.deps.append(r)
            else:
                last[r.eng] = r
        if self.phase.writer is not None:
            o.deps.append(self.phase.writer)
        for r in last.values():
            o.deps.append(r)
            r.needed = True
        self.phase.writer = o
        self.phase.readers = []
        self.ops["pool"].append(o)

    def emit(self, stack):
        nc = self.nc
        eng_sem = {e: stack.enter_context(nc.semaphore("es_" + e)) for e in COMPUTE}
        dma_sem = {}
        for e, n in self.n_dma_sems.items():
            for s in range(n):
                dma_sem[(e, s)] = stack.enter_context(nc.semaphore(f"ds_{e}{s}"))
        for e in COMPUTE:
            c = 0
            for o in self.ops[e]:
                if o.needed and not o.is_dma:
                    c += 1
                    o.count = c
        block = stack.enter_context(nc.Block())
        prog = self

        prog.trace = {e: [] for e in ALL_ENG}

        def run(engname, eng):
            known = {}
            tr = prog.trace[engname]

            def wait(key, sem, val):
                if known.get(key, 0) >= val:
                    return
                eng.wait_ge(sem, val)
                tr.append(("w", key, val))
                known[key] = val

            def wait_for(d):
                if d.is_dma:
                    wait(d.dsem, dma_sem[d.dsem], d.dval)
                else:
                    if d.eng == engname and engname == "pe":
                        return
                    wait(d.eng, eng_sem[d.eng], d.count)

            def wait_all(ds):
                need = {}
                for d in ds:
                    if d.is_dma:
                        k, v = d.dsem, d.dval
                    else:
                        if d.eng == engname and engname == "pe":
                            continue
                        k, v = d.eng, d.count
                    if need.get(k, 0) < v:
                        need[k] = v
                for k, v in need.items():
                    wait(k, dma_sem[k] if isinstance(k, tuple) else eng_sem[k], v)

            for o in prog.ops[engname]:
                ds = list(o.deps)
                if o.is_dma and o.prewait is not None:
                    ds.append(o.prewait)
                wait_all(ds)
                ins = o.fn(eng)
                if o.is_dma:
                    ins.then_inc(dma_sem[o.dsem], 16)
                    tr.append(("i", o.dsem, 16))
                elif o.needed:
                    ins.then_inc(eng_sem[engname], 1)
                    tr.append(("i", engname, 1))
                else:
                    tr.append(("n", None, 0))
            for key, last in prog.dma_last.items():
                if key[0] == engname:
                    wait(key, dma_sem[key], last.dval)

        @block.tensor
        def _(e):
            run("pe", e)

        @block.scalar
        def _(e):
            run("act", e)

        @block.vector
        def _(e):
            run("dve", e)

        @block.gpsimd
        def _(e):
            run("pool", e)

        @block.sync
        def _(e):
            run("sp", e)


def build_program(stage=99):
    nc = bass.Bass("TRN2", target_bir_lowering=False)

    def din(name, shape):
        return nc.dram_tensor(name, list(shape), F32, kind="ExternalInput").ap()

    def dout(name, shape):
        return nc.dram_tensor(name, list(shape), F32, kind="ExternalOutput").ap()

    xall = din("xall", [NTOK, D])
    ck = [din("cfk", [4096, 1024]), din("csk", [4096, 1024])]
    cv = [din("cfv", [4096, 1024]), din("csv", [4096, 1024])]
    clf = din("clf", [4, 1024, 8])
    w_in0 = din("w_in0", [D, 8200])
    w_out0 = din("w_out0", [D, D])
    w_in1 = din("w_in1", [D, 12288])
    w_out1 = din("w_out1", [4096, D])
    gT_in = din("gT", [128, 48])
    bfg = din("bforget", [8])
    lngb = din("lngb", [128, 64])
    ln_g = din("ln_g", [4096])
    ln_b = din("ln_b", [4096])
    final_g = din("final_g", [D])
    w_sp = din("w_sp", [16, 128, 128])
    b_sp = din("b_sp", [16, 128])
    cst = din("cst", [128, 14 * 128])

    y_own = dout("y_own", [1024, D])
    y_s = dout("y_s", [64, D])
    okv = {}
    for nm in ("fk", "fv", "sk", "sv"):
        okv[nm] = dout("o_" + nm, [1024, 1024])
        okv[nm + "_s"] = dout("o_" + nm + "_s", [64, 1024])
    o_lf = dout("o_lf", [1024, 8])
    o_lf_s = dout("o_lf_s", [64, 8])
    o_sgu = dout("o_sgu", [64, 4096])
    hp_scr = nc.dram_tensor("hp_scr", [NOWN + 64, D], F32, kind="Internal").ap()

    st = ExitStack()
    with st:
        ARENA = 207 * 1024
        arena = st.enter_context(nc.sbuf_tensor("arena", [128, ARENA // 4], F32))
        banks = [st.enter_context(nc.psum_tensor(f"bank{i}", [128, 512], F32)) for i in range(8)]
        RB = [Res(f"bank{i}", lock=Res(f"banklock{i}")) for i in range(8)]
        p = Prog(nc)

        class Alloc:
            top = 0

        def view_at(off, shape, dt):
            esz = 4 if dt == F32 else 2
            n = int(np.prod(shape[1:]))
            assert off % 4 == 0 and off + n * esz <= ARENA, (off, shape)
            a = arena[:, off // 4: off // 4 + (n * esz + 3) // 4]
            if dt != F32:
                a = a.bitcast(dt)
            a = a[:, 0:n]
            if len(shape) == 3:
                a = a.rearrange("p (a b) -> p a b", a=shape[1])
            elif len(shape) == 4:
                a = a.rearrange("p (a b c) -> p a b c", a=shape[1], b=shape[2])
            return a

        def alloc(shape, dt, name=""):
            esz = 4 if dt == F32 else 2
            n = int(np.prod(shape[1:]))
            nbytes = (n * esz + 63) // 64 * 64
            off = Alloc.top
            Alloc.top += nbytes
            assert Alloc.top <= ARENA, (name, Alloc.top)
            return view_at(off, shape, dt)

        def bank_bf(i):
            return banks[i][:, :].bitcast(BF16)

        cst_f = alloc([128, 14, 128], F32, "cst_f")
        cst_b = alloc([128, 14, 128], BF16, "cst_b")
        (I_ID, I_TRILE, I_ONES, I_EEV, I_EOD, I_SL, I_MLE, I_MLT, I_ZERO, I_BDTRI, I_BDLE, I_BDLT, I_SAME, I_SEL) = range(14)
        R_cst = Res("cst")
        p.dma("sp", cst_f, cst.rearrange("p (a b) -> p a b", a=14), writes=[R_cst], glob=True)
        p.op("dve", lambda e: e.tensor_copy(out=cst_b, in_=cst_f), reads=[R_cst], writes=[R_cst], glob=True)
        ident_b = cst_b[:, I_ID, :]
        ones_b = cst_b[:, I_ONES, :]
        zero_b = cst_b[:, I_ZERO, :]
        SL_b = cst_b[:, I_SL, :]
        MLE_b = cst_b[:, I_MLE, :]
        MLT_b = cst_b[:, I_MLT, :]
        E_b = [cst_b[:, I_EEV, :], cst_b[:, I_EOD, :]]
        E_f = [cst_f[:, I_EEV, :], cst_f[:, I_EOD, :]]
        trile_f = cst_f[:, I_TRILE, :]
        ones_f = cst_f[:, I_ONES, :]
        SL_f = cst_f[:, I_SL, :]

        def own_first(pp, bf=True):
            t = E_b if bf else E_f
            return t[0] if pp % 2 == 0 else t[1]

        def other_first(pp, bf=True):
            t = E_b if bf else E_f
            return t[1] if pp % 2 == 0 else t[0]

        gT = alloc([128, 48], F32, "gT")
        lngbT = alloc([128, 64], F32, "lngbT")
        bfB = alloc([128, 8], F32, "bfB")
        p.dma("sp", gT, gT_in, writes=[R_cst], glob=True)
        p.dma("sp", lngbT, lngb, writes=[R_cst], glob=True)
        p.dma("sp", bfB, bfg.partition_broadcast(128), writes=[R_cst], glob=True)
        scratch = alloc([128, 16], F32, "scratch")

        NSLAB = 3
        SLAB_BYTES = 8192
        slab_off = Alloc.top
        slab_raw = [alloc([128, SLAB_BYTES // 2], BF16, f"slab{i}") for i in range(NSLAB)]
        R_slab = [[Res(f"slab{i}_{q}") for q in range(4)] for i in range(NSLAB)]
        slab_ctr = [0]
        slab_first = {}

        def load_slab(w_ap, row_chunks, col0, ncols):
            i = slab_ctr[0] % len(slab_raw)
            slab_ctr[0] += 1
            first_extra = slab_first.pop(i, [])
            assert row_chunks * ncols * 2 <= SLAB_BYTES
            v = slab_raw[i][:, 0:row_chunks * ncols].rearrange("p (c n) -> p c n", c=row_chunks)
            src = w_ap.rearrange("(c p) n -> p c n", p=128)
            step = max(1, row_chunks // 4)
            rl = []
            for qi, c0 in enumerate(range(0, row_chunks, step)):
                p.dma("pool", v[:, c0:c0 + step, :], src[:, c0:c0 + step, col0:col0 + ncols],
                      writes=[R_slab[i][qi]] + first_extra, glob=True)
                rl += [R_slab[i][qi]] * step
            return v, rl

        kvq_slabs = {}

        def issue_kvq(m, h0, parts="kvq"):
            base = m * 4096
            d_ = kvq_slabs.setdefault((m, h0), {})
            for nm_, off_ in (("k", 1024), ("v", 2048), ("q", 0)):
                if nm_ in parts:
                    d_[nm_] = load_slab(w_in0, 16, base + off_ + h0 * 128, 256)

        wlf, R_wlf = load_slab(w_in0, 16, 8192, 8)
        if stage >= 2:
            issue_kvq(0, 0, "kv")

        hT_off = Alloc.top
        hT = alloc([128, 16, NTOK], BF16, "hT")
        R_hT = [Res(f"hT{i}") for i in range(17)]
        mixT = alloc([128, 16, NOWN], BF16, "mixT")
        R_mix = [Res(f"mix{h}") for h in range(16)]
        R_mix_s = [Res(f"mixs{h}") for h in range(16)]
        KTs = alloc([128, 16, 64], BF16, "KTs")
        QTs = alloc([128, 16, 64], BF16, "QTs")
        GTs = alloc([128, 16, 64], BF16, "GTs")
        Vs = alloc([128, 16, 128], BF16, "Vs")
        R_samp = Res("samp_keep")
        logf = alloc([128, 17, 8], F32, "logf")
        Fneg = alloc([128, 16, 8], F32, "Fneg")
        PB = alloc([128, 8, 8], F32, "PB")
        R_F = Res("F")
        mark0 = Alloc.top

        def tile_rows(i):
            return 128 if i < 16 else 64

        def tile_cols(i):
            return slice(i * 128, i * 128 + tile_rows(i))

        xt = [alloc([128, D], F32, f"xt{i}") for i in range(2)]
        R_xt = [Res("xt0"), Res("xt1")]
        xn = [alloc([128, D], BF16, f"xn{i}") for i in range(2)]
        R_xn = [Res("xn0"), Res("xn1")]
        junk = alloc([128, D], BF16, "junk")
        R_junk = Res("junk")
        ssq = alloc([128, 32], F32, "ssq")
        R_ssq = Res("ssq")

        def norm_transpose(i, src_tile, R_src, rows, g_off, dstT, R_dst, col0, pb, bufs=None):
            b = i % 2
            xn, R_xn, junk, R_junk, ssq, R_ssq = bufs
            p.op("act", lambda e: e.activation(out=junk[0:rows, :], in_=src_tile[0:rows, :], func=AF.Square,
                                               accum_out=ssq[0:rows, i:i + 1]),
                 reads=[R_src], writes=[R_junk, R_ssq])
            p.op("act", lambda e: e.activation(out=ssq[0:rows, i:i + 1], in_=ssq[0:rows, i:i + 1], func=AF.Ln,
                                               scale=1.0 / D, bias=1e-6), reads=[R_ssq], writes=[R_ssq])
            p.op("act", lambda e: e.activation(out=ssq[0:rows, i:i + 1], in_=ssq[0:rows, i:i + 1], func=AF.Exp,
                                               scale=-0.5), reads=[R_ssq], writes=[R_ssq])
            p.op("dve", lambda e: e.tensor_scalar(out=xn[b][0:rows, :], in0=src_tile[0:rows, :],
                                                  scalar1=ssq[0:rows, i:i + 1], scalar2=None, op0=ALU.mult),
                 reads=[R_src, R_ssq], writes=[R_xn[b]])
            for half in range(2):
                bk = pb + half
                pv = bank_bf(bk)
                for cc in range(8):
                    c = half * 8 + cc
                    p.op("pe", lambda e, c=c, cc=cc, pv=pv: e.transpose(
                        out=pv[:, cc * 128:cc * 128 + rows], in_=xn[b][0:rows, c * 128:(c + 1) * 128],
                        identity=ident_b[0:rows, 0:rows]),
                        reads=[R_xn[b], R_cst], writes=[RB[bk]])
                pv3 = pv.rearrange("p (a b) -> p a b", a=8)
                p.op("dve", lambda e, half=half, pv3=pv3: e.tensor_tensor(
                    out=dstT[:, half * 8:(half + 1) * 8, col0:col0 + rows], in0=pv3[:, :, 0:rows],
                    in1=gT[:, g_off + half * 8:g_off + (half + 1) * 8].unsqueeze(2).to_broadcast([128, 8, rows]),
                    op=ALU.mult), reads=[RB[bk], R_cst], writes=[R_dst])

        for i in range(17):
            rows = tile_rows(i)
            b = i % 2
            p.dma("sp", xt[b][0:rows, :], xall[i * 128:i * 128 + rows, :], writes=[R_xt[b]])
            norm_transpose(i, xt[b], R_xt[b], rows, 0, hT, R_hT[i], i * 128, (i % 2) * 2,
                           bufs=(xn, R_xn, junk, R_junk, ssq, R_ssq))

        bkL = 4
        for i in range(17):
            rows = tile_rows(i)
            for c in range(16):
                p.op("pe", lambda e, i=i, c=c, rows=rows: e.matmul(
                    out=banks[bkL][0:rows, i * 8:(i + 1) * 8], lhsT=hT[:, c, i * 128:i * 128 + rows],
                    rhs=wlf[:, c, :], start=(c == 0), stop=(c == 15)),
                    reads=[R_hT[i], R_wlf[c]], writes=[RB[bkL]])
        lg = alloc([128, 17, 8], F32, "lg")
        R_lg = Res("lg")
        for (r0, r1, t0, t1) in ((0, 128, 0, 16), (0, 64, 16, 17)):
            nt = t1 - t0
            pv = banks[bkL][r0:r1, t0 * 8:t1 * 8].rearrange("p (a b) -> p a b", a=nt)
            p.op("dve", lambda e, pv=pv, r0=r0, r1=r1, t0=t0, t1=t1, nt=nt: e.tensor_tensor(
                out=lg[r0:r1, t0:t1, :], in0=pv, in1=bfB[r0:r1, :].unsqueeze(1).to_broadcast([r1 - r0, nt, 8]),
                op=ALU.add), reads=[RB[bkL], R_cst], writes=[R_lg])
            p.op("act", lambda e, r0=r0, r1=r1, t0=t0, t1=t1: e.activation(
                out=lg[r0:r1, t0:t1, :], in_=lg[r0:r1, t0:t1, :], func=AF.Exp, scale=-1.0), reads=[R_lg], writes=[R_lg])
            p.op("act", lambda e, r0=r0, r1=r1, t0=t0, t1=t1: e.activation(
                out=lg[r0:r1, t0:t1, :], in_=lg[r0:r1, t0:t1, :], func=AF.Ln, bias=1.0), reads=[R_lg], writes=[R_lg])
            p.op("dve", lambda e, r0=r0, r1=r1, t0=t0, t1=t1: e.tensor_scalar(
                out=logf[r0:r1, t0:t1, :], in0=lg[r0:r1, t0:t1, :], scalar1=-1.0, scalar2=None, op0=ALU.mult),
                reads=[R_lg], writes=[R_F])
        p.dma("sp", o_lf.rearrange("(i p) h -> p i h", p=128), logf[:, 0:8, :], reads=[R_F])
        p.dma("sp", o_lf_s, logf[0:64, 16, :], reads=[R_F])
        Spre = alloc([128, 9, 8], F32, "Spre")
        psum_pair = alloc([128, 8, 8], F32, "psum_pair")
        R_S = Res("Spre")
        p.op("dve", lambda e: e.tensor_tensor(out=psum_pair, in0=logf[:, 0:8, :], in1=logf[:, 8:16, :], op=ALU.add),
             reads=[R_F], writes=[R_S])
        p.op("dve", lambda e: e.memset(Spre[:, 0, :], 0.0), writes=[R_S])
        for pp in range(8):
            p.op("dve", lambda e, pp=pp: e.tensor_tensor(out=Spre[:, pp + 1, :], in0=Spre[:, pp, :],
                                                         in1=psum_pair[:, pp, :], op=ALU.add),
                 reads=[R_S], writes=[R_S])
        bkF = 5
        for k in range(16):
            pp = k % 8
            is_other = k >= 8
            partner = pp if is_other else 8 + pp
            Et = own_first(pp, bf=False) if is_other else other_first(pp, bf=False)
            o = banks[bkF][:, k * 8:(k + 1) * 8]
            p.op("pe", lambda e, o=o, k=k: e.matmul(out=o, lhsT=trile_f, rhs=logf[:, k, :], start=True, stop=False),
                 reads=[R_F, R_cst], writes=[RB[bkF]])
            p.op("pe", lambda e, o=o, pp=pp: e.matmul(out=o, lhsT=ones_f, rhs=Spre[:, pp, :], start=False, stop=False),
                 reads=[R_S, R_cst], writes=[RB[bkF]])
            p.op("pe", lambda e, o=o, Et=Et, partner=partner: e.matmul(out=o, lhsT=Et, rhs=logf[:, partner, :],
                                                                        start=False, stop=True),
                 reads=[R_F, R_cst], writes=[RB[bkF]])
        for pp in range(8):
            o = banks[bkF][:, 128 + pp * 8:128 + (pp + 1) * 8]
            p.op("pe", lambda e, o=o, pp=pp: e.matmul(out=o, lhsT=ones_f, rhs=Spre[:, pp, :], start=True, stop=True),
                 reads=[R_S, R_cst], writes=[RB[bkF]])
        R_F2 = Res("F2")
        p.op("dve", lambda e: e.tensor_scalar(out=Fneg, in0=banks[bkF][:, 0:128].rearrange("p (a b) -> p a b", a=16),
                                              scalar1=-1.0, scalar2=None, op0=ALU.mult),
             reads=[RB[bkF]], writes=[R_F2])
        p.op("dve", lambda e: e.tensor_copy(out=PB.rearrange("p h s -> p s h"),
                                            in_=banks[bkF][:, 128:192].rearrange("p (s h) -> p s h", s=8)),
             reads=[RB[bkF]], writes=[R_F2])

        p.barrier(scratch)
        Alloc.top = mark0

        KT = alloc([128, 2, 2048], BF16, "KT")
        Vg = alloc([128, 16, 256], BF16, "Vg")
        QT = alloc([128, 2, 1024], BF16, "QT")
        GT = alloc([128, 2, 1024], BF16, "GT")
        R_KT = [Res("KT0"), Res("KT1")]
        R_Vg = Res("Vg")
        R_QT = [Res("QT0"), Res("QT1")]
        R_GT = [Res("GT0"), Res("GT1")]
        stg = [alloc([128, 256], F32, f"stg{i}") for i in range(4)]
        R_stg = [Res(f"stg{i}") for i in range(4)]
        stg_ctr = [0]
        wA = alloc([128, 1024], F32, "wA")
        wB = alloc([128, 1024], F32, "wB")
        wC = alloc([128, 1024], F32, "wC")
        wD = alloc([128, 1024], F32, "wD")
        wE = alloc([128, 1024], F32, "wE")
        hA = alloc([128, 1024], BF16, "hA")
        hB = alloc([128, 1024], BF16, "hB")
        hC = alloc([128, 1024], BF16, "hC")
        hD = alloc([128, 1024], BF16, "hD")
        hE = alloc([128, 1024], BF16, "hE")
        R_w = {k: Res(k) for k in ("wA", "wB", "wC", "wD", "wE", "hA", "hB", "hC", "hD", "hE")}
        ev_ctr = [0]

        def evac(out, in_, reads, writes, func=None, eng="dve"):
            if "noevac" in DBG:
                return
            if "dveonly" in DBG and func is None:
                p.op("dve", lambda e: e.tensor_copy(out=out, in_=in_), reads=reads, writes=writes)
                return
            if "actonly" in DBG:
                f = func if func is not None else AF.Copy
                p.op("act", lambda e: e.activation(out=out, in_=in_, func=f), reads=reads, writes=writes)
                return
            if func is not None or eng == "act":
                f = func if func is not None else AF.Copy
                p.op("act", lambda e: e.activation(out=out, in_=in_, func=f), reads=reads, writes=writes)
            else:
                p.op("dve", lambda e: e.tensor_copy(out=out, in_=in_), reads=reads, writes=writes)
            ev_ctr[0] += 1

        pb_ctr = [0]

        def next_bank(lo=0, hi=8):
            b = lo + pb_ctr[0] % (hi - lo)
            pb_ctr[0] += 1
            return b

        def col_chunks(c0, c1):
            out = []
            c = c0
            while c < c1:
                e_ = min(c1, (c // 512 + 1) * 512)
                out.append((c, e_))
                c = e_
            return out

        out_names = {0: ("fk", "fv"), 1: ("sk", "sv")}

        def project_group(m, h0):
            base = m * 4096
            hg = m * 8 + h0
            kname, vname = out_names[m]
            d_ = kvq_slabs.pop((m, h0))
            (wk, Rwk), (wv, Rwv), (wq_, Rwq_) = d_["k"], d_["v"], d_["q"]
            for hh in (range(2) if "K" in PARTS else []):
                for (c0, c1) in ((0, 512), (512, 1024), (1024, 1536), (1536, 2048), (2048, 2112)):
                    bk = next_bank(0, 4)
                    n = c1 - c0
                    Rr = [R_hT[t] for t in range(c0 // 128, (c1 + 127) // 128)]
                    for c in range(16):
                        p.op("pe", lambda e, bk=bk, hh=hh, c=c, c0=c0, c1=c1, n=n: e.matmul(
                            out=banks[bk][:, 0:n], lhsT=wk[:, c, hh * 128:(hh + 1) * 128], rhs=hT[:, c, c0:c1],
                            start=(c == 0), stop=(c == 15)), reads=Rr + [Rwk[c]], writes=[RB[bk]])
                    if c0 < 2048:
                        evac(KT[:, hh, c0:c1], banks[bk][:, 0:n], [RB[bk]], [R_KT[hh]])
                    else:
                        evac(KTs[:, hg + hh, :], banks[bk][:, 0:64], [RB[bk]], [R_samp])
            for i in ((list(range(8)) + [16]) if "O" in PARTS else []):
                rows = tile_rows(i)
                bk = next_bank(0, 4)
                for c in range(16):
                    p.op("pe", lambda e, bk=bk, c=c, i=i, rows=rows: e.matmul(
                        out=banks[bk][0:rows, 0:256], lhsT=hT[:, c, i * 128:i * 128 + rows], rhs=wk[:, c, :],
                        start=(c == 0), stop=(c == 15)), reads=[R_hT[i], Rwk[c]], writes=[RB[bk]])
                s = stg_ctr[0] % 4
                stg_ctr[0] += 1
                evac(stg[s][0:rows, :], banks[bk][0:rows, 0:256], [RB[bk]], [R_stg[s]], eng="act")
                dst = okv[kname][i * 128:(i + 1) * 128, h0 * 128:h0 * 128 + 256] if i < 8 else \
                    okv[kname + "_s"][:, h0 * 128:h0 * 128 + 256]
                p.dma("sp", dst, stg[s][0:rows, :], reads=[R_stg[s]])
            wg_, Rwg_ = load_slab(w_in0, 16, base + 3072 + h0 * 128, 256)
            for i in (range(17) if "V" in PARTS else []):
                rows = tile_rows(i)
                bk = next_bank(0, 4)
                for c in range(16):
                    p.op("pe", lambda e, bk=bk, c=c, i=i, rows=rows: e.matmul(
                        out=banks[bk][0:rows, 0:256], lhsT=hT[:, c, i * 128:i * 128 + rows], rhs=wv[:, c, :],
                        start=(c == 0), stop=(c == 15)), reads=[R_hT[i], Rwv[c]], writes=[RB[bk]])
                if i < 16:
                    evac(Vg[:, i, :], banks[bk][:, 0:256], [RB[bk]], [R_Vg])
                else:
                    evac(Vs[0:64, hg:hg + 2, :], banks[bk][0:64, 0:256].rearrange("p (a b) -> p a b", a=2),
                         [RB[bk]], [R_samp])
                if i < 8 or i == 16:
                    s = stg_ctr[0] % 4
                    stg_ctr[0] += 1
                    evac(stg[s][0:rows, :], banks[bk][0:rows, 0:256], [RB[bk]], [R_stg[s]], eng="act")
                    dst = okv[vname][i * 128:(i + 1) * 128, h0 * 128:h0 * 128 + 256] if i < 8 else \
                        okv[vname + "_s"][:, h0 * 128:h0 * 128 + 256]
                    p.dma("sp", dst, stg[s][0:rows, :], reads=[R_stg[s]])
            for (ws, Rws, dstT, R_dst, dsts, func) in (
                    (wq_, Rwq_, QT, R_QT, QTs, None),
                    (wg_, Rwg_, GT, R_GT, GTs, AF.Silu)):
                for hh in (range(2) if "Q" in PARTS else []):
                    for (c0, c1) in ((0, 512), (512, 1024), (2048, 2112)):
                        bk = next_bank(0, 4)
                        n = c1 - c0
                        Rr = [R_hT[t] for t in range(c0 // 128, (c1 + 127) // 128)]
                        for c in range(16):
                            p.op("pe", lambda e, bk=bk, hh=hh, c=c, c0=c0, c1=c1, n=n, ws=ws: e.matmul(
                                out=banks[bk][:, 0:n], lhsT=ws[:, c, hh * 128:(hh + 1) * 128], rhs=hT[:, c, c0:c1],
                                start=(c == 0), stop=(c == 15)), reads=Rr + [Rws[c]], writes=[RB[bk]])
                        if c0 < 2048:
                            evac(dstT[:, hh, c0:c1], banks[bk][:, 0:n], [RB[bk]], [R_dst[hh]], func=func, eng="act")
                        else:
                            evac(dsts[:, hg + hh, :], banks[bk][:, 0:64], [RB[bk]], [R_samp], func=func, eng="act")

        def fox_head(hh, h):
            for q in range(2):
                p.op("pe", lambda e, q=q: e.matmul(out=banks[4 + q][:, :], lhsT=zero_b, rhs=QT[:, hh, q * 512:(q + 1) * 512],
                                                    start=True, stop=False), reads=[R_QT[hh], R_cst], writes=[RB[4 + q]])
                p.op("pe", lambda e, q=q: e.matmul(out=banks[6 + q][:, :], lhsT=zero_b, rhs=QT[:, hh, q * 512:(q + 1) * 512],
                                                    start=True, stop=False), reads=[R_QT[hh], R_cst], writes=[RB[6 + q]])
            its = [(pp, is_other) for pp in range(8) for is_other in (False, True)]

            def geom(it):
                pp, is_other = its[it]
                kb = 8 + pp if is_other else pp
                c0 = pp * 128
                return pp, is_other, kb, c0, (it % 2) * 2, col_chunks(c0, 1024)

            def qk(it):
                pp, is_other, kb, c0, sb0, chunks = geom(it)
                for (a, b_) in chunks:
                    bk = sb0 + a // 512
                    p.op("pe", lambda e, bk=bk, a=a, b_=b_, kb=kb: e.matmul(
                        out=banks[bk][:, a % 512:a % 512 + (b_ - a)], lhsT=KT[:, hh, kb * 128:(kb + 1) * 128],
                        rhs=QT[:, hh, a:b_], start=True, stop=True),
                        reads=[R_KT[hh], R_QT[hh]], writes=[RB[bk]])

            def elem(it):
                pp, is_other, kb, c0, sb0, chunks = geom(it)
                tmp = (wA, wB)[it % 2]
                Rtmp = (R_w["wA"], R_w["wB"])[it % 2]
                PT = (hA, hB)[it % 2]
                RPT = (R_w["hA"], R_w["hB"])[it % 2]
                for (a, b_) in chunks:
                    bk = sb0 + a // 512
                    ns = (b_ - a) // 128
                    s0 = a // 128
                    p.op("dve", lambda e, bk=bk, a=a, b_=b_, ns=ns, s0=s0, tmp=tmp: e.scalar_tensor_tensor(
                        out=tmp[:, a:b_].rearrange("p (s t) -> p s t", s=ns),
                        in0=banks[bk][:, a % 512:a % 512 + (b_ - a)].rearrange("p (s t) -> p s t", s=ns),
                        scalar=SCALE,
                        in1=PB[:, h, s0:s0 + ns].unsqueeze(2).to_broadcast([128, ns, 128]),
                        op0=ALU.mult, op1=ALU.add), reads=[RB[bk], R_F2], writes=[Rtmp])
                p.op("act", lambda e, c0=c0, kb=kb, tmp=tmp, PT=PT: e.activation(
                    out=PT[:, c0:1024], in_=tmp[:, c0:1024], func=AF.Exp, bias=Fneg[:, kb, h:h + 1], scale=1.0),
                    reads=[Rtmp, R_F2], writes=[RPT])
                mk = other_first(pp) if is_other else MLE_b
                p.op("pool", lambda e, c0=c0, mk=mk, PT=PT: e.tensor_tensor(
                    out=PT[:, c0:c0 + 128], in0=PT[:, c0:c0 + 128], in1=mk, op=ALU.mult),
                    reads=[RPT, R_cst], writes=[RPT])

            def pv(it):
                pp, is_other, kb, c0, sb0, chunks = geom(it)
                PT = (hA, hB)[it % 2]
                RPT = (R_w["hA"], R_w["hB"])[it % 2]
                for (a, b_) in chunks:
                    q = a // 512
                    last = is_other and ((pp == 3 and q == 0) or pp == 7)
                    p.op("pe", lambda e, q=q, a=a, b_=b_, kb=kb, last=last, PT=PT: e.matmul(
                        out=banks[4 + q][:, a % 512:a % 512 + (b_ - a)], lhsT=Vg[:, kb, hh * 128:(hh + 1) * 128],
                        rhs=PT[:, a:b_], start=False, stop=last), reads=[R_Vg, RPT], writes=[RB[4 + q]])
                    p.op("pe", lambda e, q=q, a=a, b_=b_, last=last, PT=PT: e.matmul(
                        out=banks[6 + q][:, a % 512:a % 512 + (b_ - a)], lhsT=ones_b,
                        rhs=PT[:, a:b_], start=False, stop=last), reads=[RPT, R_cst], writes=[RB[6 + q]])

            qk(0)
            for it in range(16):
                if it + 1 < 16:
                    qk(it + 1)
                elem(it)
                pv(it)
            for q in range(2):
                cs = slice(q * 512, (q + 1) * 512)
                p.op("dve", lambda e, q=q, cs=cs: e.reciprocal(out=wC[:, cs], in_=banks[6 + q][:, :]),
                     reads=[RB[6 + q]], writes=[R_w["wC"]])
                p.op("dve", lambda e, q=q, cs=cs: e.tensor_tensor(out=wC[:, cs], in0=banks[4 + q][:, :], in1=wC[:, cs],
                                                                 op=ALU.mult),
                     reads=[RB[4 + q], R_w["wC"]], writes=[R_w["wC"]])
                p.op("pool", lambda e, cs=cs: e.tensor_tensor(out=mixT[:, h, cs], in0=wC[:, cs], in1=GT[:, hh, cs],
                                                              op=ALU.mult),
                     reads=[R_w["wC"], R_GT[hh]], writes=[R_mix[h]])

        def sb_head(hh, h):
            SPS, SPSb = wE, hE
            for q in range(2):
                p.op("pe", lambda e, q=q: e.matmul(out=banks[6 + q][:, :], lhsT=zero_b, rhs=QT[:, hh, q * 512:(q + 1) * 512],
                                                    start=True, stop=False), reads=[R_QT[hh], R_cst], writes=[RB[6 + q]])
            p.op("pool", lambda e: e.memset(SPS, 0.0), writes=[R_w["wE"]])
            p.op("pool", lambda e: e.memset(SPSb, 0.0), writes=[R_w["hE"]])
            for pp in range(7, -1, -1):
                c0 = pp * 128
                chunks = col_chunks(c0, 1024)
                blocks = ((pp, 0, wA, "wA", hA, "hA", MLT_b), (8 + pp, 2, wB, "wB", hB, "hB", other_first(pp)))
                for (kb, zb, sp, spn, spm, spmn, mk) in blocks:
                    for (a, b_) in chunks:
                        bk = zb + a // 512
                        p.op("pe", lambda e, bk=bk, a=a, b_=b_, kb=kb: e.matmul(
                            out=banks[bk][:, a % 512:a % 512 + (b_ - a)], lhsT=KT[:, hh, kb * 128:(kb + 1) * 128],
                            rhs=QT[:, hh, a:b_], start=True, stop=True),
                            reads=[R_KT[hh], R_QT[hh]], writes=[RB[bk]])
                    for (a, b_) in chunks:
                        bk = zb + a // 512
                        p.op("act", lambda e, bk=bk, a=a, b_=b_: e.activation(
                            out=wC[:, a:b_], in_=banks[bk][:, a % 512:a % 512 + (b_ - a)], func=AF.Exp, scale=SCALE),
                            reads=[RB[bk]], writes=[R_w["wC"]])
                    p.op("act", lambda e, c0=c0, sp=sp: e.activation(out=sp[:, c0:1024], in_=wC[:, c0:1024], func=AF.Ln,
                                                                     bias=1.0),
                         reads=[R_w["wC"]], writes=[R_w[spn]])
                    p.op("pool", lambda e, c0=c0, sp=sp, spm=spm, mk=mk: e.tensor_tensor(
                        out=spm[:, c0:c0 + 128], in0=sp[:, c0:c0 + 128], in1=mk, op=ALU.mult),
                        reads=[R_w[spn], R_cst], writes=[R_w[spmn]])
                    if c0 + 128 < 1024:
                        p.op("pool", lambda e, c0=c0, sp=sp, spm=spm: e.tensor_copy(
                            out=spm[:, c0 + 128:1024], in_=sp[:, c0 + 128:1024]),
                            reads=[R_w[spn]], writes=[R_w[spmn]])
                for bi, (kb, zb, sp, spn, spm, spmn, mk) in enumerate(blocks):
                    ospm, ospmn = (hB, "hB") if bi == 0 else (hA, "hA")
                    Et = own_first(pp) if bi == 0 else other_first(pp)
                    for (a, b_) in chunks:
                        bk = 4 + a // 512
                        o = banks[bk][:, a % 512:a % 512 + (b_ - a)]
                        p.op("pe", lambda e, o=o, a=a, b_=b_, spm=spm: e.matmul(out=o, lhsT=SL_b, rhs=spm[:, a:b_],
                                                                                 start=True, stop=False),
                             reads=[R_w[spmn], R_cst], writes=[RB[bk]])
                        p.op("pe", lambda e, o=o, a=a, b_=b_: e.matmul(out=o, lhsT=ones_b, rhs=SPSb[:, a:b_],
                                                                       start=False, stop=False),
                             reads=[R_w["hE"], R_cst], writes=[RB[bk]])
                        p.op("pe", lambda e, o=o, a=a, b_=b_, Et=Et, ospm=ospm: e.matmul(
                            out=o, lhsT=Et, rhs=ospm[:, a:b_], start=False, stop=True),
                            reads=[R_w[ospmn], R_cst], writes=[RB[bk]])
                    u, un = (wC, "wC") if bi == 0 else (wD, "wD")
                    aT, aTn = (hC, "hC") if bi == 0 else (hD, "hD")
                    for (a, b_) in chunks:
                        bz = zb + a // 512
                        bc = 4 + a // 512
                        p.op("dve", lambda e, bz=bz, a=a, b_=b_, sp=sp, u=u: e.scalar_tensor_tensor(
                            out=u[:, a:b_], in0=banks[bz][:, a % 512:a % 512 + (b_ - a)], scalar=SCALE, in1=sp[:, a:b_],
                            op0=ALU.mult, op1=ALU.subtract), reads=[RB[bz], R_w[spn]], writes=[R_w[un]])
                        p.op("dve", lambda e, bc=bc, a=a, b_=b_, u=u: e.tensor_tensor(
                            out=u[:, a:b_], in0=u[:, a:b_], in1=banks[bc][:, a % 512:a % 512 + (b_ - a)],
                            op=ALU.subtract), reads=[RB[bc], R_w[un]], writes=[R_w[un]])
                    p.op("act", lambda e, c0=c0, u=u, aT=aT: e.activation(out=aT[:, c0:1024], in_=u[:, c0:1024],
                                                                         func=AF.Exp),
                         reads=[R_w[un]], writes=[R_w[aTn]])
                    p.op("pool", lambda e, c0=c0, aT=aT, mk=mk: e.tensor_tensor(
                        out=aT[:, c0:c0 + 128], in0=aT[:, c0:c0 + 128], in1=mk, op=ALU.mult),
                        reads=[R_w[aTn], R_cst], writes=[R_w[aTn]])
                    for (a, b_) in chunks:
                        q = a // 512
                        last = (pp == 0 and bi == 1)
                        p.op("pe", lambda e, q=q, a=a, b_=b_, kb=kb, last=last, aT=aT: e.matmul(
                            out=banks[6 + q][:, a % 512:a % 512 + (b_ - a)], lhsT=Vg[:, kb, hh * 128:(hh + 1) * 128],
                            rhs=aT[:, a:b_], start=False, stop=last), reads=[R_Vg, R_w[aTn]], writes=[RB[6 + q]])
                if pp > 0:
                    for (spm, spmn) in ((hA, "hA"), (hB, "hB")):
                        p.op("pool", lambda e, c0=c0, spm=spm: e.tensor_tensor(
                            out=SPS[:, c0:1024], in0=SPS[:, c0:1024], in1=spm[:, c0:1024], op=ALU.add),
                            reads=[R_w[spmn], R_w["wE"]], writes=[R_w["wE"]])
                    p.op("pool", lambda e, c0=c0: e.tensor_copy(out=SPSb[:, c0:1024], in_=SPS[:, c0:1024]),
                         reads=[R_w["wE"]], writes=[R_w["hE"]])
            for q in range(2):
                cs = slice(q * 512, (q + 1) * 512)
                p.op("dve", lambda e, q=q, cs=cs: e.tensor_tensor(out=mixT[:, 8 + h, cs], in0=banks[6 + q][:, :],
                                                                 in1=GT[:, hh, cs], op=ALU.mult),
                     reads=[RB[6 + q], R_GT[hh]], writes=[R_mix[8 + h]])

        R_ch = [{k: Res(f"ch{ci}_{k}") for k in ("wA", "wB", "wC", "wD", "wE", "hA", "hB", "hC", "hD", "hE")}
                for ci in range(2)]

        def sb_chain(ci, hh, h):
            zb, cb, ob = 4 * ci, 4 * ci + 2, 4 * ci + 3
            wo_ = 512 * ci
            Rc = R_ch[ci]

            def W(buf):
                return buf[:, wo_:wo_ + 512]
            spA, spB, uA, uB, SPS = W(wA), W(wB), W(wC), W(wD), W(wE)
            spmA, spmB, aA, aB, SPSb = W(hA), W(hB), W(hC), W(hD), W(hE)
            for base, pmax in ((512, 7), (0, 3)):
                p.op("pe", lambda e, base=base: e.matmul(out=banks[ob][:, :], lhsT=zero_b, rhs=QT[:, hh, base:base + 512],
                                                        start=True, stop=False),
                     reads=[R_QT[hh], R_cst], writes=[RB[ob]])
                p.op("pool", lambda e: e.memset(SPS, 0.0), writes=[Rc["wE"]])
                p.op("pool", lambda e: e.memset(SPSb, 0.0), writes=[Rc["hE"]])
                yield
                for pp in range(pmax, -1, -1):
                    lo = max(pp * 128, base) - base
                    n = 512 - lo
                    diag = pp * 128 >= base
                    g0 = base + lo
                    blocks = ((pp, 0, spA, "wA", spmA, "hA", MLT_b, uA, "wC", aA, "hC"),
                              (8 + pp, 1, spB, "wB", spmB, "hB", other_first(pp), uB, "wD", aB, "hD"))
                    for (kb, zi, sp, spn, spm, spmn, mk, u, un, aT, aTn) in blocks:
                        p.op("pe", lambda e, zi=zi, kb=kb, lo=lo, g0=g0, base=base: e.matmul(
                            out=banks[zb + zi][:, lo:512], lhsT=KT[:, hh, kb * 128:(kb + 1) * 128],
                            rhs=QT[:, hh, g0:base + 512], start=True, stop=True),
                            reads=[R_KT[hh], R_QT[hh]], writes=[RB[zb + zi]])
                    yield
                    for (kb, zi, sp, spn, spm, spmn, mk, u, un, aT, aTn) in blocks:
                        p.op("act", lambda e, zi=zi, lo=lo, u=u: e.activation(out=u[:, lo:512], in_=banks[zb + zi][:, lo:512],
                                                                             func=AF.Exp, scale=SCALE),
                             reads=[RB[zb + zi]], writes=[Rc[un]])
                        p.op("act", lambda e, lo=lo, sp=sp, u=u: e.activation(out=sp[:, lo:512], in_=u[:, lo:512],
                                                                             func=AF.Ln, bias=1.0),
                             reads=[Rc[un]], writes=[Rc[spn]])
                    yield
                    for (kb, zi, sp, spn, spm, spmn, mk, u, un, aT, aTn) in blocks:
                        if diag:
                            p.op("dve", lambda e, lo=lo, sp=sp, spm=spm, mk=mk: e.tensor_tensor(
                                out=spm[:, lo:lo + 128], in0=sp[:, lo:lo + 128], in1=mk, op=ALU.mult),
                                reads=[Rc[spn], R_cst], writes=[Rc[spmn]])
                            if lo + 128 < 512:
                                p.op("dve", lambda e, lo=lo, sp=sp, spm=spm: e.tensor_copy(
                                    out=spm[:, lo + 128:512], in_=sp[:, lo + 128:512]),
                                    reads=[Rc[spn]], writes=[Rc[spmn]])
                        else:
                            p.op("dve", lambda e, lo=lo, sp=sp, spm=spm: e.tensor_copy(out=spm[:, lo:512], in_=sp[:, lo:512]),
                                 reads=[Rc[spn]], writes=[Rc[spmn]])
                    yield
                    for bi, (kb, zi, sp, spn, spm, spmn, mk, u, un, aT, aTn) in enumerate(blocks):
                        ospm, ospmn = (spmB, "hB") if bi == 0 else (spmA, "hA")
                        Et = own_first(pp) if bi == 0 else other_first(pp)
                        o = banks[cb][:, lo:512]
                        p.op("pe", lambda e, o=o, lo=lo, spm=spm: e.matmul(out=o, lhsT=SL_b, rhs=spm[:, lo:512],
                                                                          start=True, stop=False),
                             reads=[Rc[spmn], R_cst], writes=[RB[cb]])
                        p.op("pe", lambda e, o=o, lo=lo: e.matmul(out=o, lhsT=ones_b, rhs=SPSb[:, lo:512],
                                                                  start=False, stop=False),
                             reads=[Rc["hE"], R_cst], writes=[RB[cb]])
                        p.op("pe", lambda e, o=o, lo=lo, Et=Et, ospm=ospm: e.matmul(out=o, lhsT=Et, rhs=ospm[:, lo:512],
                                                                                    start=False, stop=True),
                             reads=[Rc[ospmn], R_cst], writes=[RB[cb]])
                        yield
                        p.op("dve", lambda e, zi=zi, lo=lo, sp=sp, u=u: e.scalar_tensor_tensor(
                            out=u[:, lo:512], in0=banks[zb + zi][:, lo:512], scalar=SCALE, in1=sp[:, lo:512],
                            op0=ALU.mult, op1=ALU.subtract), reads=[RB[zb + zi], Rc[spn]], writes=[Rc[un]])
                        p.op("dve", lambda e, lo=lo, u=u: e.tensor_tensor(
                            out=u[:, lo:512], in0=u[:, lo:512], in1=banks[cb][:, lo:512], op=ALU.subtract),
                            reads=[RB[cb], Rc[un]], writes=[Rc[un]])
                        yield
                        p.op("act", lambda e, lo=lo, u=u, aT=aT: e.activation(out=aT[:, lo:512], in_=u[:, lo:512], func=AF.Exp),
                             reads=[Rc[un]], writes=[Rc[aTn]])
                        if diag:
                            p.op("pool", lambda e, lo=lo, aT=aT, mk=mk: e.tensor_tensor(
                                out=aT[:, lo:lo + 128], in0=aT[:, lo:lo + 128], in1=mk, op=ALU.mult),
                                reads=[Rc[aTn], R_cst], writes=[Rc[aTn]])
                        yield
                        last = (pp == 0 and bi == 1)
                        p.op("pe", lambda e, lo=lo, kb=kb, last=last, aT=aT: e.matmul(
                            out=banks[ob][:, lo:512], lhsT=Vg[:, kb, hh * 128:(hh + 1) * 128],
                            rhs=aT[:, lo:512], start=False, stop=last), reads=[R_Vg, Rc[aTn]], writes=[RB[ob]])
                        yield
                    if pp > 0:
                        for (spm, spmn) in ((spmA, "hA"), (spmB, "hB")):
                            p.op("pool", lambda e, lo=lo, spm=spm: e.tensor_tensor(
                                out=SPS[:, lo:512], in0=SPS[:, lo:512], in1=spm[:, lo:512], op=ALU.add),
                                reads=[Rc[spmn], Rc["wE"]], writes=[Rc["wE"]])
                        p.op("pool", lambda e, lo=lo: e.tensor_copy(out=SPSb[:, lo:512], in_=SPS[:, lo:512]),
                             reads=[Rc["wE"]], writes=[Rc["hE"]])
                        yield
                p.op("dve", lambda e, base=base: e.tensor_tensor(out=mixT[:, 8 + h, base:base + 512], in0=banks[ob][:, :],
                                                                 in1=GT[:, hh, base:base + 512], op=ALU.mult),
                     reads=[RB[ob], R_GT[hh]], writes=[R_mix[8 + h]])
                yield

        def fox_chain(ci, hh, h):
            sbk, ob, db = 4 * ci, 4 * ci + 2, 4 * ci + 3
            wo_ = 512 * ci
            Rc = R_ch[ci]

            def W(buf):
                return buf[:, wo_:wo_ + 512]
            tmps = (W(wA), W(wB))
            Rtmps = (Rc["wA"], Rc["wB"])
            PTs = (W(hA), W(hB))
            RPTs = (Rc["hA"], Rc["hB"])
            fin_ = W(wC)
            for base, pmax in ((512, 7), (0, 3)):
                for bk_ in (ob, db):
                    p.op("pe", lambda e, bk_=bk_, base=base: e.matmul(out=banks[bk_][:, :], lhsT=zero_b,
                                                                    rhs=QT[:, hh, base:base + 512], start=True, stop=False),
                         reads=[R_QT[hh], R_cst], writes=[RB[bk_]])
                its = [(pp, io) for pp in range(pmax + 1) for io in (False, True)]
                nit = len(its)

                def geom(it, base=base, its=its):
                    pp, is_other = its[it]
                    kb = 8 + pp if is_other else pp
                    lo = max(pp * 128, base) - base
                    return pp, is_other, kb, lo, pp * 128 >= base

                def qk(it, base=base, geom=geom):
                    pp, is_other, kb, lo, diag = geom(it)
                    bk = sbk + it % 2
                    p.op("pe", lambda e, bk=bk, kb=kb, lo=lo, base=base: e.matmul(
                        out=banks[bk][:, lo:512], lhsT=KT[:, hh, kb * 128:(kb + 1) * 128],
                        rhs=QT[:, hh, base + lo:base + 512], start=True, stop=True),
                        reads=[R_KT[hh], R_QT[hh]], writes=[RB[bk]])

                qk(0)
                yield
                for it in range(nit):
                    pp, is_other, kb, lo, diag = geom(it)
                    bk = sbk + it % 2
                    tmp, Rtmp, PT, RPT = tmps[it % 2], Rtmps[it % 2], PTs[it % 2], RPTs[it % 2]
                    if it + 1 < nit:
                        qk(it + 1)
                    ns = (512 - lo) // 128
                    s0 = (base + lo) // 128
                    p.op("dve", lambda e, bk=bk, lo=lo, ns=ns, s0=s0, tmp=tmp: e.scalar_tensor_tensor(
                        out=tmp[:, lo:512].rearrange("p (s t) -> p s t", s=ns),
                        in0=banks[bk][:, lo:512].rearrange("p (s t) -> p s t", s=ns), scalar=SCALE,
                        in1=PB[:, h, s0:s0 + ns].unsqueeze(2).to_broadcast([128, ns, 128]),
                        op0=ALU.mult, op1=ALU.add), reads=[RB[bk], R_F2], writes=[Rtmp])
                    yield
                    p.op("act", lambda e, lo=lo, kb=kb, tmp=tmp, PT=PT: e.activation(
                        out=PT[:, lo:512], in_=tmp[:, lo:512], func=AF.Exp, bias=Fneg[:, kb, h:h + 1], scale=1.0),
                        reads=[Rtmp, R_F2], writes=[RPT])
                    if diag:
                        mk = other_first(pp) if is_other else MLE_b
                        p.op("pool", lambda e, lo=lo, mk=mk, PT=PT: e.tensor_tensor(
                            out=PT[:, lo:lo + 128], in0=PT[:, lo:lo + 128], in1=mk, op=ALU.mult),
                            reads=[RPT, R_cst], writes=[RPT])
                    yield
                    last = (it == nit - 1)
                    p.op("pe", lambda e, lo=lo, kb=kb, last=last, PT=PT: e.matmul(
                        out=banks[ob][:, lo:512], lhsT=Vg[:, kb, hh * 128:(hh + 1) * 128], rhs=PT[:, lo:512],
                        start=False, stop=last), reads=[R_Vg, RPT], writes=[RB[ob]])
                    p.op("pe", lambda e, lo=lo, last=last, PT=PT: e.matmul(
                        out=banks[db][:, lo:512], lhsT=ones_b, rhs=PT[:, lo:512], start=False, stop=last),
                        reads=[RPT, R_cst], writes=[RB[db]])
                    yield
                p.op("dve", lambda e: e.reciprocal(out=fin_, in_=banks[db][:, :]), reads=[RB[db]], writes=[Rc["wC"]])
                p.op("dve", lambda e: e.tensor_tensor(out=fin_, in0=banks[ob][:, :], in1=fin_, op=ALU.mult),
                     reads=[RB[ob], Rc["wC"]], writes=[Rc["wC"]])
                p.op("pool", lambda e, base=base: e.tensor_tensor(out=mixT[:, h, base:base + 512], in0=fin_,
                                                                  in1=GT[:, hh, base:base + 512], op=ALU.mult),
                     reads=[Rc["wC"], R_GT[hh]], writes=[R_mix[h]])
                yield

        def run_interleaved(gens):
            gens = list(gens)
            while gens:
                for g_ in list(gens):
                    try:
                        next(g_)
                    except StopIteration:
                        gens.remove(g_)

        if stage >= 2:
            groups = [(m, h0) for m in range(2) for h0 in range(0, 8, 2)][:NGROUPS]
            issue_kvq(0, 0, "q")
            for gi, (m, h0) in enumerate(groups):
                project_group(m, h0)
                if gi + 1 < len(groups):
                    issue_kvq(*groups[gi + 1])
                if stage >= 3:
                    if m == 0:
                        if "oldfox" in DBG:
                            for hh in range(2):
                                fox_head(hh, h0 + hh)
                        else:
                            run_interleaved([fox_chain(0, 0, h0), fox_chain(1, 1, h0 + 1)])
                    elif "oldsb" in DBG:
                        pass
                    if m == 1 and h0 == 0 and ("oldsb" in DBG) != ("oldfox" in DBG):
                        p.barrier(scratch)
                    if m == 0:
                        pass
                    elif "oldsb" in DBG:
                        for hh in range(2):
                            sb_head(hh, h0 + hh)
                    else:
                        run_interleaved([sb_chain(0, 0, h0), sb_chain(1, 1, h0 + 1)])

        p.barrier(scratch)
        Alloc.top = mark0

        def sample_attention():
            offA = [hT_off]

            def allocA(shape, dt):
                esz = 4 if dt == F32 else 2
                n = int(np.prod(shape[1:]))
                off = offA[0]
                offA[0] += (n * esz + 63) // 64 * 64
                assert offA[0] <= hT_off + 16 * NTOK * 2
                return view_at(off, shape, dt)
            Kc = [allocA([128, 8, 1024], BF16) for _ in range(2)]
            Vc = [allocA([128, 8, 1024], BF16) for _ in range(2)]
            R_Kc = [[Res("Kc0a"), Res("Kc0b")], [Res("Kc1a"), Res("Kc1b")]]
            R_Vc = [[Res("Vc0a"), Res("Vc0b")], [Res("Vc1a"), Res("Vc1b")]]
            KcT = alloc([128, 8, 1024], BF16, "KcT")
            R_KcT = [Res(f"KcT{h}") for h in range(8)]
            clfT = alloc([128, 8, 32], F32, "clfT")
            csuf = alloc([128, 8, 32], F32, "csuf")
            Gsuf = alloc([128, 8, 32], F32, "Gsuf")
            Gnew = alloc([128, 8], F32, "Gnew")
            R_G = Res("G")
            sw1 = alloc([128, 1024], F32, "sw1")
            sw2 = alloc([128, 1024], F32, "sw2")
            sP = [alloc([128, 1024], BF16, f"sP{i}") for i in range(2)]
            R_sP = [Res("sP0"), Res("sP1")]
            R_sw1, R_sw2 = Res("sw1"), Res("sw2")
            Ssuf = alloc([128, 8, 128], F32, "Ssuf")
            Ssufb = alloc([128, 8, 128], BF16, "Ssufb")
            R_Ssuf = Res("Ssuf")
            nw1 = alloc([128, 512], F32, "nw1")
            nw2 = alloc([128, 512], F32, "nw2")
            Pn = alloc([128, 8, 64], BF16, "Pn")
            spmn = alloc([128, 8, 64], BF16, "spmn")
            R_n = {k: Res(k) for k in ("nw1", "nw2", "Pn", "spmn")}
            fin = alloc([128, 512], F32, "fin")
            R_fin = Res("fin")
            BDLE = cst_b[0:64, I_BDLE, 0:64]
            BDLT = cst_b[0:64, I_BDLT, 0:64]
            for b in range(4):
                p.dma("sp", clfT[:, :, b * 8:(b + 1) * 8], clf[b].rearrange("(t p) h -> p t h", p=128), writes=[R_G])
            p.op("dve", lambda e: e.memset(csuf[:, 7, :], 0.0), writes=[R_G])
            for t in range(6, -1, -1):
                p.op("dve", lambda e, t=t: e.tensor_tensor(out=csuf[:, t, :], in0=csuf[:, t + 1, :], in1=clfT[:, t + 1, :],
                                                          op=ALU.add), reads=[R_G], writes=[R_G])
            bG = 7
            for t in range(8):
                o = banks[bG][:, t * 32:(t + 1) * 32]
                p.op("pe", lambda e, o=o, t=t: e.matmul(out=o, lhsT=SL_f, rhs=clfT[:, t, :], start=True, stop=False),
                     reads=[R_G, R_cst], writes=[RB[bG]])
                p.op("pe", lambda e, o=o, t=t: e.matmul(out=o, lhsT=ones_f, rhs=csuf[:, t, :], start=False, stop=True),
                     reads=[R_G, R_cst], writes=[RB[bG]])
            p.op("pe", lambda e: e.matmul(out=banks[bG][0:64, 256:264], lhsT=cst_f[0:64, I_BDTRI, 0:64],
                                          rhs=logf[0:64, 16, :], start=True, stop=True),
                 reads=[R_F, R_cst], writes=[RB[bG]])
            R_G2 = Res("G2")
            p.op("dve", lambda e: e.tensor_copy(out=Gsuf, in_=banks[bG][:, 0:256].rearrange("p (t n) -> p t n", t=8)),
                 reads=[RB[bG]], writes=[R_G2])
            p.op("dve", lambda e: e.tensor_scalar(out=Gnew[0:64, :], in0=banks[bG][0:64, 256:264], scalar1=-1.0,
                                                  scalar2=None, op0=ALU.mult), reads=[RB[bG]], writes=[R_G2])

            if os.environ.get("DBGOUT"):
                dbg4 = nc.dram_tensor("dbg4", [128, 3072], F32, kind="ExternalOutput").ap()
                d4 = alloc([128, 3072], F32, "d4")
                R_d4 = Res("d4")
                p.op("dve", lambda e: e.tensor_copy(out=d4[:, 0:1024], in_=QTs.rearrange("p h q -> p (h q)")), reads=[R_samp], writes=[R_d4])
                p.op("dve", lambda e: e.tensor_copy(out=d4[:, 1024:2048], in_=KTs.rearrange("p h q -> p (h q)")), reads=[R_samp], writes=[R_d4])
                p.op("dve", lambda e: e.tensor_copy(out=d4[:, 2048:3072], in_=GTs.rearrange("p h q -> p (h q)")), reads=[R_samp], writes=[R_d4])
                p.dma("sp", dbg4, d4, reads=[R_d4])
                dbg2 = nc.dram_tensor("dbg2", [128, 264], F32, kind="ExternalOutput").ap()
                p.dma("sp", dbg2[:, 0:256], Gsuf.rearrange("p t n -> p (t n)"), reads=[R_G2])
                p.dma("sp", dbg2[0:64, 256:264], Gnew[0:64, :], reads=[R_G2])
            def issue_cache(j):
                m_, b_ = j // 4, j % 4
                buf_ = j % 2
                for half in range(2):
                    rows = slice(b_ * 1024 + half * 512, b_ * 1024 + (half + 1) * 512)
                    p.dma("pool", Kc[buf_][:, half * 4:(half + 1) * 4, :],
                          ck[m_][rows, :].rearrange("(t p) n -> p t n", p=128), writes=[R_Kc[buf_][half]])
                    p.dma("pool", Vc[buf_][:, half * 4:(half + 1) * 4, :],
                          cv[m_][rows, :].rearrange("(t p) n -> p t n", p=128), writes=[R_Vc[buf_][half]])

            def tr(j):
                buf_ = j % 2
                for h in range(8):
                    bk = h % 2
                    pv = bank_bf(bk)
                    for t in range(8):
                        p.op("pe", lambda e, pv=pv, t=t, h=h, buf_=buf_: e.transpose(
                            out=pv[:, t * 128:(t + 1) * 128], in_=Kc[buf_][:, t, h * 128:(h + 1) * 128],
                            identity=ident_b), reads=[R_Kc[buf_][t // 4], R_cst], writes=[RB[bk]])
                    evac(KcT[:, h, :], pv, [RB[bk]], [R_KcT[h]], eng="dve")

            issue_cache(0)
            tr(0)
            for m in range(2):
                bO, bD, bN, bC = 4, 5, 6, 7
                hb = m * 8
                for h in range(8):
                    p.op("pe", lambda e, h=h, hb=hb: e.matmul(out=banks[bN][0:64, h * 64:(h + 1) * 64], lhsT=KTs[:, hb + h, :],
                                                        rhs=QTs[:, hb + h, :], start=True, stop=True),
                         reads=[R_samp], writes=[RB[bN]])
                if m == 0:
                    p.op("dve", lambda e: e.scalar_tensor_tensor(
                        out=nw1[0:64, :].rearrange("p (h q) -> p h q", h=8),
                        in0=banks[bN][0:64, :].rearrange("p (h q) -> p h q", h=8), scalar=SCALE,
                        in1=Gnew[0:64, :].unsqueeze(2).to_broadcast([64, 8, 64]), op0=ALU.mult, op1=ALU.add),
                        reads=[RB[bN], R_G2], writes=[R_n["nw1"]])
                    p.op("act", lambda e: e.activation(out=Pn[0:64, :, :], in_=nw1[0:64, :].rearrange("p (h q) -> p h q", h=8),
                                                       func=AF.Exp), reads=[R_n["nw1"]], writes=[R_n["Pn"]])
                    p.op("pool", lambda e: e.tensor_tensor(out=Pn[0:64, :, :], in0=Pn[0:64, :, :],
                                                           in1=BDLE.unsqueeze(1).to_broadcast([64, 8, 64]), op=ALU.mult),
                         reads=[R_n["Pn"], R_cst], writes=[R_n["Pn"]])
                    p.op("pe", lambda e: e.matmul(out=banks[bN][:, :], lhsT=zero_b, rhs=cst_b[:, 0:4, :], start=True,
                                                  stop=False), reads=[R_cst], writes=[RB[bN]])
                    p.op("pe", lambda e: e.matmul(out=banks[bD][:, :], lhsT=ones_b[0:64, :],
                                                  rhs=Pn[0:64, :, :], start=True, stop=True),
                         reads=[R_n["Pn"], R_cst], writes=[RB[bD]])
                else:
                    p.op("act", lambda e: e.activation(out=nw1[0:64, :], in_=banks[bN][0:64, :], func=AF.Exp, scale=SCALE),
                         reads=[RB[bN]], writes=[R_n["nw1"]])
                    p.op("act", lambda e: e.activation(out=nw1[0:64, :], in_=nw1[0:64, :], func=AF.Ln, bias=1.0),
                         reads=[R_n["nw1"]], writes=[R_n["nw1"]])
                    p.op("pool", lambda e: e.tensor_tensor(out=spmn[0:64, :, :],
                                                           in0=nw1[0:64, :].rearrange("p (h q) -> p h q", h=8),
                                                           in1=BDLT.unsqueeze(1).to_broadcast([64, 8, 64]), op=ALU.mult),
                         reads=[R_n["nw1"], R_cst], writes=[R_n["spmn"]])
                    p.op("pe", lambda e: e.matmul(out=banks[bD][0:64, :], lhsT=SL_b[0:64, 0:64], rhs=spmn[0:64, :, :],
                                                  start=True, stop=True), reads=[R_n["spmn"], R_cst], writes=[RB[bD]])
                    p.op("dve", lambda e: e.scalar_tensor_tensor(out=nw2[0:64, :], in0=banks[bN][0:64, :], scalar=SCALE,
                                                                 in1=nw1[0:64, :], op0=ALU.mult, op1=ALU.subtract),
                         reads=[RB[bN], R_n["nw1"]], writes=[R_n["nw2"]])
                    p.op("dve", lambda e: e.tensor_tensor(out=nw2[0:64, :], in0=nw2[0:64, :], in1=banks[bD][0:64, :],
                                                          op=ALU.subtract), reads=[RB[bD], R_n["nw2"]], writes=[R_n["nw2"]])
                    p.op("act", lambda e: e.activation(out=Pn[0:64, :, :], in_=nw2[0:64, :].rearrange("p (h q) -> p h q", h=8),
                                                       func=AF.Exp), reads=[R_n["nw2"]], writes=[R_n["Pn"]])
                    p.op("pool", lambda e: e.tensor_tensor(out=Pn[0:64, :, :], in0=Pn[0:64, :, :],
                                                           in1=BDLT.unsqueeze(1).to_broadcast([64, 8, 64]), op=ALU.mult),
                         reads=[R_n["Pn"], R_cst], writes=[R_n["Pn"]])
                p.op("pe", lambda e: e.matmul(out=banks[bO][:, :], lhsT=zero_b, rhs=cst_b[:, 0:4, :], start=True, stop=False),
                     reads=[R_cst], writes=[RB[bO]])
                for h in range(8):
                    p.op("pe", lambda e, h=h, hb=hb: e.matmul(out=banks[bO][:, h * 64:(h + 1) * 64], lhsT=Vs[0:64, hb + h, :],
                                                        rhs=Pn[0:64, h, :], start=False, stop=False),
                         reads=[R_samp, R_n["Pn"]], writes=[RB[bO]])
                for b in range(4):
                    buf = (m * 4 + b) % 2
                    if m * 4 + b + 1 < 8:
                        issue_cache(m * 4 + b + 1)
                    if "notrpipe" in DBG and m * 4 + b > 0:
                        tr(m * 4 + b)
                    for t in range(8):
                        bk = 2 + t // 4
                        for h in range(8):
                            c0 = (t % 4) * 128 + h * 16
                            p.op("pe", lambda e, bk=bk, c0=c0, t=t, h=h, b=b, hb=hb: e.matmul(
                                out=banks[bk][:, c0:c0 + 16], lhsT=KcT[:, h, t * 128:(t + 1) * 128],
                                rhs=QTs[:, hb + h, b * 16:(b + 1) * 16], start=True, stop=True),
                                reads=[R_KcT[h], R_samp], writes=[RB[bk]])
                    if m * 4 + b + 1 < 8 and "notrpipe" not in DBG:
                        tr(m * 4 + b + 1)
                    Pb = sP[b % 2]
                    RPb = R_sP[b % 2]
                    Pb4 = Pb.rearrange("p (t h q) -> p t h q", t=8, h=8)
                    if m == 0:
                        for t in range(8):
                            bk = 2 + t // 4
                            cs = slice((t % 4) * 128, (t % 4 + 1) * 128)
                            p.op("dve", lambda e, t=t, b=b, bk=bk, cs=cs: e.scalar_tensor_tensor(
                                out=sw1[:, t * 128:(t + 1) * 128].rearrange("p (h q) -> p h q", h=8),
                                in0=banks[bk][:, cs].rearrange("p (h q) -> p h q", h=8), scalar=SCALE,
                                in1=Gsuf[:, t, b * 8:(b + 1) * 8].unsqueeze(2).to_broadcast([128, 8, 16]),
                                op0=ALU.mult, op1=ALU.add), reads=[RB[bk], R_G2], writes=[R_sw1])
                        p.op("act", lambda e, Pb=Pb: e.activation(out=Pb, in_=sw1, func=AF.Exp), reads=[R_sw1], writes=[RPb])
                        for t in range(8):
                            p.op("pe", lambda e, t=t, b=b, Pb=Pb: e.matmul(
                                out=banks[bN][:, b * 128:(b + 1) * 128],
                                lhsT=ones_b, rhs=Pb[:, t * 128:(t + 1) * 128], start=False,
                                stop=(b == 3 and t == 7)),
                                reads=[RPb, R_cst], writes=[RB[bN]])
                    else:
                        for hf in range(2):
                            p.op("act", lambda e, hf=hf: e.activation(out=sw1[:, hf * 512:(hf + 1) * 512], in_=banks[2 + hf][:, :],
                                                                      func=AF.Exp, scale=SCALE), reads=[RB[2 + hf]], writes=[R_sw1])
                        p.op("act", lambda e: e.activation(out=sw1, in_=sw1, func=AF.Ln, bias=1.0), reads=[R_sw1], writes=[R_sw1])
                        spb = sP[(b + 1) % 2]
                        Rspb = R_sP[(b + 1) % 2]
                        p.op("pool", lambda e, spb=spb: e.tensor_copy(out=spb, in_=sw1), reads=[R_sw1], writes=[Rspb])
                        spb3 = spb.rearrange("p (t n) -> p t n", t=8)
                        p.op("pool", lambda e: e.memset(Ssuf[:, 7, :], 0.0), writes=[R_Ssuf])
                        for t in range(6, -1, -1):
                            p.op("pool", lambda e, t=t, spb3=spb3: e.tensor_tensor(
                                out=Ssuf[:, t, :], in0=Ssuf[:, t + 1, :], in1=spb3[:, t + 1, :], op=ALU.add),
                                reads=[Rspb, R_Ssuf], writes=[R_Ssuf])
                        p.op("pool", lambda e: e.tensor_copy(out=Ssufb, in_=Ssuf), reads=[R_Ssuf], writes=[R_Ssuf])
                        for t in range(8):
                            bk = 5 + t // 4
                            o = banks[bk][:, (t % 4) * 128:(t % 4 + 1) * 128]
                            p.op("pe", lambda e, o=o, t=t, spb3=spb3: e.matmul(out=o, lhsT=SL_b, rhs=spb3[:, t, :],
                                                                               start=True, stop=False),
                                 reads=[Rspb, R_cst], writes=[RB[bk]])
                            p.op("pe", lambda e, o=o, t=t: e.matmul(out=o, lhsT=ones_b, rhs=Ssufb[:, t, :],
                                                                    start=False, stop=False),
                                 reads=[R_Ssuf, R_cst], writes=[RB[bk]])
                            p.op("pe", lambda e, o=o, b=b: e.matmul(out=o.rearrange("p (h q) -> p h q", h=8),
                                                                    lhsT=ones_b[0:64, :],
                                                                    rhs=spmn[0:64, :, b * 16:(b + 1) * 16],
                                                                    start=False, stop=True),
                                 reads=[R_n["spmn"], R_cst], writes=[RB[bk]])
                        for hf in range(2):
                            cs = slice(hf * 512, (hf + 1) * 512)
                            p.op("dve", lambda e, hf=hf, cs=cs: e.scalar_tensor_tensor(
                                out=sw2[:, cs], in0=banks[2 + hf][:, :], scalar=SCALE, in1=sw1[:, cs],
                                op0=ALU.mult, op1=ALU.subtract), reads=[RB[2 + hf], R_sw1], writes=[R_sw2])
                            p.op("dve", lambda e, hf=hf, cs=cs: e.tensor_tensor(
                                out=sw2[:, cs], in0=sw2[:, cs], in1=banks[5 + hf][:, :], op=ALU.subtract),
                                reads=[RB[5 + hf], R_sw2], writes=[R_sw2])
                        p.op("act", lambda e, Pb=Pb: e.activation(out=Pb, in_=sw2, func=AF.Exp), reads=[R_sw2], writes=[RPb])
                    for t in range(8):
                        for h in range(8):
                            last = (b == 3 and t == 7 and h == 7)
                            p.op("pe", lambda e, t=t, h=h, b=b, buf=buf, last=last, Pb4=Pb4: e.matmul(
                                out=banks[bO][:, h * 64 + b * 16:h * 64 + (b + 1) * 16],
                                lhsT=Vc[buf][:, t, h * 128:(h + 1) * 128], rhs=Pb4[:, t, h, :], start=False, stop=last),
                                reads=[R_Vc[buf][t // 4], RPb], writes=[RB[bO]])
                if m == 0:
                    p.op("dve", lambda e: e.tensor_copy(out=fin, in_=banks[bD][:, :]), reads=[RB[bD]], writes=[R_fin])
                    for b in range(4):
                        fv = fin.rearrange("p (h q) -> p h q", h=8)[:, :, b * 16:(b + 1) * 16]
                        p.op("dve", lambda e, b=b, fv=fv: e.tensor_tensor(
                            out=fv, in0=fv, in1=banks[bN][:, b * 128:(b + 1) * 128].rearrange("p (h q) -> p h q", h=8),
                            op=ALU.add), reads=[RB[bN], R_fin], writes=[R_fin])
                    p.op("dve", lambda e: e.reciprocal(out=fin, in_=fin), reads=[R_fin], writes=[R_fin])
                    p.op("dve", lambda e: e.tensor_tensor(out=fin, in0=banks[bO][:, :], in1=fin, op=ALU.mult),
                         reads=[RB[bO], R_fin], writes=[R_fin])
                else:
                    p.op("dve", lambda e: e.tensor_copy(out=fin, in_=banks[bO][:, :]), reads=[RB[bO]], writes=[R_fin])
                if os.environ.get("DBGOUT") and m == 0:
                    dbg3 = nc.dram_tensor("dbg3", [128, 2048], F32, kind="ExternalOutput").ap()
                    p.dma("sp", dbg3[:, 0:512], fin, reads=[R_fin])
                    p.dma("sp", dbg3[:, 512:1536], sw1, reads=[R_sw1])
                    p.dma("sp", dbg3[0:64, 1536:2048], nw1[0:64, :], reads=[R_n["nw1"]])
                p.op("pool", lambda e, hb=hb: e.tensor_tensor(out=mixT[:, hb:hb + 8, 1024:1088],
                                                              in0=fin.rearrange("p (h q) -> p h q", h=8),
                                                              in1=GTs[:, hb:hb + 8, :], op=ALU.mult),
                     reads=[R_fin, R_samp], writes=[R_mix_s[hb + h_] for h_ in range(8)])

        if stage >= 4:
            sample_attention()
            p.barrier(scratch)
            Alloc.top = mark0

        h1T = alloc([128, 16, NOWN], BF16, "h1T")
        R_h1T = [Res(f"h1T{i}") for i in range(9)]
        mark1 = Alloc.top
        AX = mybir.AxisListType

        pre_v = []

        def out_proj0():
            if stage >= 6 and "nopre" not in DBG:
                for sv_ in range(len(slab_raw)):
                    pre_v.append(load_slab(w_in1, 16, 4096 + sv_ * 256, 256))
            wo = view_at(hT_off, [128, 16, D], BF16)
            R_wo = [Res(f"wo{j}") for j in range(8)]
            srcw = w_out0.rearrange("(c p) n -> p c n", p=128)
            for j in range(8):
                p.dma("pool", wo[:, 2 * j:2 * j + 2, :], srcw[:, 2 * j:2 * j + 2, :], writes=[R_wo[j]])
            hpb = [alloc([128, D], F32, f"hpb{i}") for i in range(2)]
            R_hpb = [Res("hpb0"), Res("hpb1")]
            _x5 = alloc([128, D], BF16, "xn5")
            _r5 = Res("xn5")
            xn5 = [_x5, _x5]
            R_xn5 = [_r5, _r5]
            junk5 = alloc([128, D], BF16, "junk5")
            ssq5 = alloc([128, 32], F32, "ssq5")
            bufs = (xn5, R_xn5, junk5, Res("junk5"), ssq5, Res("ssq5"))
            for i in range(9):
                rows = 128 if i < 8 else 64
                r0 = i * 128 if i < 8 else 2048
                b = i % 2
                p.dma("sp", hpb[b][0:rows, :], xall[r0:r0 + rows, :], writes=[R_hpb[b]])
                Rm = R_mix if i < 8 else R_mix_s
                for q in range(4):
                    bk = 4 + q
                    for hd in range(16):
                        p.op("pe", lambda e, bk=bk, hd=hd, i=i, rows=rows, q=q: e.matmul(
                            out=banks[bk][0:rows, :], lhsT=mixT[:, hd, i * 128:i * 128 + rows],
                            rhs=wo[:, hd, q * 512:(q + 1) * 512], start=(hd == 0), stop=(hd == 15)),
                            reads=[Rm[hd], R_wo[hd // 2]], writes=[RB[bk]])
                    p.op("dve", lambda e, bk=bk, b=b, rows=rows, q=q: e.tensor_tensor(
                        out=hpb[b][0:rows, q * 512:(q + 1) * 512], in0=banks[bk][0:rows, :],
                        in1=hpb[b][0:rows, q * 512:(q + 1) * 512], op=ALU.add),
                        reads=[RB[bk], R_hpb[b]], writes=[R_hpb[b]])
                p.dma("sp", hp_scr[i * 128:i * 128 + rows, :], hpb[b][0:rows, :], reads=[R_hpb[b]])
                if os.environ.get("DBGOUT"):
                    if i == 0:
                        _NC_CACHE["dbg_hp"] = nc.dram_tensor("dbg_hp", [NOWN, D], F32, kind="ExternalOutput").ap()
                    p.dma("sp", _NC_CACHE["dbg_hp"][i * 128:i * 128 + rows, :], hpb[b][0:rows, :], reads=[R_hpb[b]])
                norm_transpose(i, hpb[b], R_hpb[b], rows, 16, h1T, R_h1T[i], i * 128, (i % 2) * 2, bufs=bufs)

        CH = 9 * 128 * 2

        def gT_view(c):
            return view_at(hT_off + c * CH, [128, NOWN], BF16)

        R_vb = [Res(f"vb{c}") for c in range(32)]

        def layer1():
            vb = view_at(hT_off, [128, 32, 9, 128], BF16)
            offB = [hT_off + 32 * CH]

            def allocB(shape, dt):
                esz = 4 if dt == F32 else 2
                n = int(np.prod(shape[1:]))
                off = offB[0]
                offB[0] += (n * esz + 63) // 64 * 64
                assert offB[0] <= mark0, (offB[0], mark0)
                return view_at(off, shape, dt)
            vsf = allocB([128, 4096], F32)
            R_vsf = Res("vsf")
            WspT = allocB([128, 16, 128], BF16)
            RSW = allocB([128, 16, 128], F32)
            bspB = allocB([128, 16, 128], F32)
            R_c1 = Res("l1consts")
            wtmp = vsf[:, 0:2048].rearrange("p (g s) -> p g s", g=16)
            wtb = vsf[:, 2048:3072].bitcast(BF16).rearrange("p (g s) -> p g s", g=16)
            mixed = alloc([128, NOWN], F32, "mixed")
            tprod = alloc([128, NOWN], F32, "tprod")
            szb = alloc([128, NOWN], BF16, "szb")
            bias2 = alloc([128, 128], F32, "bias2")
            BDs = alloc([128, 16, 64], BF16, "BDs")
            st1 = alloc([128, 9, 16], F32, "st1")
            st2 = alloc([128, 9, 16], F32, "st2")
            s1 = alloc([128, 9], F32, "s1")
            s2 = alloc([128, 9], F32, "s2")
            rstd1 = alloc([128, 9], F32, "rstd1")
            nmr1 = alloc([128, 9], F32, "nmr1")
            junk6 = alloc([128, 256], BF16, "junk6")
            R_w6 = {k: Res(k) for k in ("mixed", "tprod", "szb", "bias2", "BDs", "st", "junk6")}
            gamT = lngbT[:, 0:32]
            betT = lngbT[:, 32:64]
            p.dma("sp", wtmp, w_sp.rearrange("g t s -> t g s"), writes=[R_vsf])
            p.dma("sp", bspB.rearrange("p g t -> p (g t)"), b_sp.rearrange("g t -> (g t)").partition_broadcast(128),
                  writes=[R_c1])
            p.op("dve", lambda e: e.tensor_copy(out=wtb, in_=wtmp), reads=[R_vsf], writes=[R_vsf])
            for hf in range(2):
                pv = bank_bf(hf)
                for gg in range(8):
                    g = hf * 8 + gg
                    p.op("pe", lambda e, pv=pv, gg=gg, g=g: e.transpose(out=pv[:, gg * 128:(gg + 1) * 128], in_=wtb[:, g, :],
                                                                          identity=ident_b),
                         reads=[R_vsf, R_cst], writes=[RB[hf]])
                p.op("dve", lambda e, pv=pv, hf=hf: e.tensor_tensor(
                    out=WspT[:, hf * 8:(hf + 1) * 8, :], in0=pv.rearrange("p (g t) -> p g t", g=8),
                    in1=MLE_b.unsqueeze(1).to_broadcast([128, 8, 128]), op=ALU.mult),
                    reads=[RB[hf], R_cst], writes=[R_c1])
            for g4 in range(4):
                bk = 2 + g4 % 2
                p.op("pe", lambda e, bk=bk, g4=g4: e.matmul(out=banks[bk][:, :], lhsT=ones_b,
                                                            rhs=WspT[:, g4 * 4:(g4 + 1) * 4, :], start=True, stop=True),
                     reads=[R_c1, R_cst], writes=[RB[bk]])
                p.op("dve", lambda e, bk=bk, g4=g4: e.tensor_copy(
                    out=RSW[:, g4 * 4:(g4 + 1) * 4, :], in_=banks[bk][:, :].rearrange("p (g t) -> p g t", g=4)),
                    reads=[RB[bk]], writes=[R_c1])
            W16t = alloc([128, 16, 64], BF16, "W16t")
            R_w16 = Res("W16t")
            for bb in range(4):
                p.op("dve", lambda e, bb=bb: e.tensor_copy(out=W16t[0:16, :, 16 * bb:16 * bb + 16], in_=WspT[0:16, :, 0:16]),
                     reads=[R_c1], writes=[R_w16])
            for hf in range(2):
                p.op("pe", lambda e, hf=hf: e.matmul(out=banks[2 + hf][0:64, :], lhsT=cst_b[0:16, I_SEL, 0:64],
                                                     rhs=W16t[0:16, hf * 8:(hf + 1) * 8, :], start=True, stop=True),
                     reads=[R_w16, R_cst], writes=[RB[2 + hf]])
                p.op("dve", lambda e, hf=hf: e.tensor_tensor(
                    out=BDs[0:64, hf * 8:(hf + 1) * 8, :], in0=banks[2 + hf][0:64, :].rearrange("p (g t) -> p g t", g=8),
                    in1=cst_b[0:64, I_SAME, 0:64].unsqueeze(1).to_broadcast([64, 8, 64]), op=ALU.mult),
                    reads=[RB[2 + hf], R_cst], writes=[R_w6["BDs"]])
            p.op("dve", lambda e: e.memset(st1, 0.0), writes=[R_w6["st"]])
            p.op("dve", lambda e: e.memset(st2, 0.0), writes=[R_w6["st"]])
            if stage >= 6:
                for sv in range(16):
                    wvs, Rwvs = pre_v[sv] if sv < len(pre_v) else load_slab(w_in1, 16, 4096 + sv * 256, 256)
                    for i in range(9):
                        rows = 128 if i < 8 else 64
                        bk = next_bank(2, 8)
                        for k in range(16):
                            p.op("pe", lambda e, bk=bk, k=k, i=i, rows=rows, wvs=wvs: e.matmul(
                                out=banks[bk][0:rows, 0:256], lhsT=h1T[:, k, i * 128:i * 128 + rows], rhs=wvs[:, k, :],
                                start=(k == 0), stop=(k == 15)), reads=[R_h1T[i], Rwvs[k]], writes=[RB[bk]])
                        p.op("act", lambda e, bk=bk, i=i, rows=rows, sv=sv: e.activation(
                            out=vb[0:rows, 2 * sv:2 * sv + 2, i, :],
                            in_=banks[bk][0:rows, 0:256].rearrange("p (a b) -> p a b", a=2), func=AF.Copy,
                            accum_out=st1[0:rows, i, sv:sv + 1]),
                            reads=[RB[bk]], writes=[R_vb[2 * sv], R_vb[2 * sv + 1], R_w6["st"]])
                        p.op("act", lambda e, bk=bk, i=i, rows=rows, sv=sv: e.activation(
                            out=junk6[0:rows, :], in_=banks[bk][0:rows, 0:256], func=AF.Square,
                            accum_out=st2[0:rows, i, sv:sv + 1]),
                            reads=[RB[bk]], writes=[R_w6["junk6"], R_w6["st"]])
                        if i == 8:
                            p.op("dve", lambda e, bk=bk, sv=sv: e.tensor_copy(out=vsf[0:64, sv * 256:(sv + 1) * 256],
                                                                              in_=banks[bk][0:64, 0:256]),
                                 reads=[RB[bk]], writes=[R_vsf])
                p.op("dve", lambda e: e.reduce_sum(out=s1, in_=st1, axis=AX.X), reads=[R_w6["st"]], writes=[R_w6["st"]])
                p.op("dve", lambda e: e.reduce_sum(out=s2, in_=st2, axis=AX.X), reads=[R_w6["st"]], writes=[R_w6["st"]])
                p.op("dve", lambda e: e.tensor_scalar(out=s1, in0=s1, scalar1=1.0 / 4096, scalar2=None, op0=ALU.mult),
                     reads=[R_w6["st"]], writes=[R_w6["st"]])
                p.op("dve", lambda e: e.tensor_tensor(out=nmr1, in0=s1, in1=s1, op=ALU.mult),
                     reads=[R_w6["st"]], writes=[R_w6["st"]])
                p.op("dve", lambda e: e.scalar_tensor_tensor(out=s2, in0=s2, scalar=1.0 / 4096, in1=nmr1, op0=ALU.mult,
                                                             op1=ALU.subtract),
                     reads=[R_w6["st"]], writes=[R_w6["st"]])
                p.op("act", lambda e: e.activation(out=rstd1, in_=s2, func=AF.Ln, bias=1e-5), reads=[R_w6["st"]],
                     writes=[R_w6["st"]])
                p.op("act", lambda e: e.activation(out=rstd1, in_=rstd1, func=AF.Exp, scale=-0.5), reads=[R_w6["st"]],
                     writes=[R_w6["st"]])
                p.op("dve", lambda e: e.scalar_tensor_tensor(out=nmr1, in0=s1, scalar=-1.0, in1=rstd1, op0=ALU.mult,
                                                             op1=ALU.mult),
                     reads=[R_w6["st"]], writes=[R_w6["st"]])
                for i in range(9):
                    rows = 128 if i < 8 else 64
                    p.op("dve", lambda e, i=i, rows=rows: e.tensor_scalar(
                        out=vb[0:rows, :, i, :], in0=vb[0:rows, :, i, :], scalar1=rstd1[0:rows, i:i + 1],
                        scalar2=nmr1[0:rows, i:i + 1], op0=ALU.mult, op1=ALU.add),
                        reads=[R_w6["st"]] + R_vb, writes=R_vb)
                gb = mixed[0:64, 0:1024]
                for j in range(8):
                    cs = slice(j * 512, (j + 1) * 512)
                    p.dma("sp", gb[:, 0:512], ln_g[cs].partition_broadcast(64), writes=[R_w6["mixed"]])
                    p.dma("sp", gb[:, 512:1024], ln_b[cs].partition_broadcast(64), writes=[R_w6["mixed"]])
                    p.op("dve", lambda e, cs=cs: e.tensor_scalar(out=vsf[0:64, cs], in0=vsf[0:64, cs], scalar1=rstd1[0:64, 8:9],
                                                                 scalar2=nmr1[0:64, 8:9], op0=ALU.mult, op1=ALU.add),
                         reads=[R_vsf, R_w6["st"]], writes=[R_vsf])
                    p.op("dve", lambda e, cs=cs: e.tensor_tensor(out=vsf[0:64, cs], in0=vsf[0:64, cs], in1=gb[:, 0:512],
                                                                 op=ALU.mult), reads=[R_vsf, R_w6["mixed"]], writes=[R_vsf])
                    p.op("dve", lambda e, cs=cs: e.tensor_tensor(out=vsf[0:64, cs], in0=vsf[0:64, cs], in1=gb[:, 512:1024],
                                                                 op=ALU.add), reads=[R_vsf, R_w6["mixed"]], writes=[R_vsf])
                    p.dma("sp", o_sgu[:, cs], vsf[0:64, cs], reads=[R_vsf])
            if stage >= 7:
                n0 = len(slab_raw)
                for ex in (range(2) if "ext" in DBG else []):
                    slab_raw.append(vsf[:, ex * 2048:(ex + 1) * 2048].bitcast(BF16))
                    R_slab.append([Res(f"slabx{ex}_{q}") for q in range(4)])
                    slab_first[n0 + ex] = [R_vsf]
                us = zs = Rus = Rzs = None
                for c in range(32):
                    g = c // 2
                    if c % 2 == 0:
                        us, Rus = load_slab(w_in1, 16, (c // 2) * 256, 256)
                        zs, Rzs = load_slab(w_in1, 16, 8192 + (c // 2) * 256, 256)
                    wc = slice((c % 2) * 128, (c % 2 + 1) * 128)
                    for order_ in (("uz", "sp") if c == 0 else ("sp", "uz")):
                      if order_ == "sp":
                        for i in range(8):
                            bk = 4 + i // 4
                            p.op("pe", lambda e, bk=bk, i=i, c=c, g=g: e.matmul(
                                out=banks[bk][:, (i % 4) * 128:(i % 4 + 1) * 128], lhsT=vb[:, c, i, :], rhs=WspT[:, g, :],
                                start=True, stop=True), reads=[R_vb[c], R_c1], writes=[RB[bk]])
                        p.op("pe", lambda e, c=c, g=g: e.matmul(out=banks[7][:, 128:192], lhsT=vb[0:64, c, 8, :],
                                                                rhs=BDs[0:64, g, :], start=True, stop=True),
                             reads=[R_vb[c], R_w6["BDs"]], writes=[RB[7]])
                      else:
                        for (slab, Rs, bk0, soff) in ((us, Rus, 0, 0), (zs, Rzs, 2, 64)):
                            for (t0, t1, bk, col) in ((0, 512, bk0, 0), (512, 1024, bk0 + 1, 0), (1024, 1088, 6, soff)):
                                n = t1 - t0
                                Rr = [R_h1T[t] for t in range(t0 // 128, (t1 + 127) // 128)]
                                for k in range(16):
                                    p.op("pe", lambda e, bk=bk, col=col, n=n, k=k, wc=wc, t0=t0, t1=t1, slab=slab: e.matmul(
                                        out=banks[bk][:, col:col + n], lhsT=slab[:, k, wc], rhs=h1T[:, k, t0:t1],
                                        start=(k == 0), stop=(k == 15)), reads=Rr + [Rs[k]], writes=[RB[bk]])
                    p.op("dve", lambda e, c=c, g=g: e.scalar_tensor_tensor(
                        out=bias2, in0=RSW[:, g, :], scalar=betT[:, c:c + 1], in1=bspB[:, g, :], op0=ALU.mult, op1=ALU.add),
                        reads=[R_c1, R_cst], writes=[R_w6["bias2"]])
                    for hf in range(2):
                        p.op("dve", lambda e, hf=hf, c=c: e.scalar_tensor_tensor(
                            out=mixed[:, hf * 512:(hf + 1) * 512].rearrange("p (i t) -> p i t", i=4),
                            in0=banks[4 + hf][:, :].rearrange("p (i t) -> p i t", i=4), scalar=gamT[:, c:c + 1],
                            in1=bias2.unsqueeze(1).to_broadcast([128, 4, 128]), op0=ALU.mult, op1=ALU.add),
                            reads=[RB[4 + hf], R_w6["bias2"], R_cst], writes=[R_w6["mixed"]])
                    p.op("dve", lambda e, c=c: e.scalar_tensor_tensor(
                        out=mixed[:, 1024:1088].rearrange("p (i t) -> p i t", i=4),
                        in0=banks[7][:, 128:192].rearrange("p (i t) -> p i t", i=4), scalar=gamT[:, c:c + 1],
                        in1=bias2[:, 0:16].unsqueeze(1).to_broadcast([128, 4, 16]), op0=ALU.mult, op1=ALU.add),
                        reads=[RB[7], R_w6["bias2"], R_cst], writes=[R_w6["mixed"]])
                    for (bk, pc, oc) in ((2, slice(0, 512), slice(0, 512)), (3, slice(0, 512), slice(512, 1024)),
                                         (6, slice(64, 128), slice(1024, 1088))):
                        p.op("act", lambda e, bk=bk, pc=pc, oc=oc: e.activation(out=szb[:, oc], in_=banks[bk][:, pc],
                                                                                func=AF.Silu),
                             reads=[RB[bk]], writes=[R_w6["szb"]])
                    for (bk, pc, oc) in ((0, slice(0, 512), slice(0, 512)), (1, slice(0, 512), slice(512, 1024)),
                                         (6, slice(0, 64), slice(1024, 1088))):
                        p.op("dve", lambda e, bk=bk, pc=pc, oc=oc: e.tensor_tensor(out=tprod[:, oc], in0=banks[bk][:, pc],
                                                                                   in1=mixed[:, oc], op=ALU.mult),
                             reads=[RB[bk], R_w6["mixed"]], writes=[R_w6["tprod"]])
                    gv = gT_view(c)
                    p.op("pool", lambda e, gv=gv: e.tensor_tensor(out=gv, in0=tprod, in1=szb, op=ALU.mult),
                         reads=[R_w6["tprod"], R_w6["szb"]], writes=[R_vb[c]])

        def out_proj1():
            del slab_raw[NSLAB:]
            del R_slab[NSLAB:]
            off7 = hT_off + 32 * CH
            hpF = view_at(off7, [128, 9, D], F32)
            off7 += 9 * D * 4
            slabB = view_at(off7, [128, 32, 384], BF16)
            gfB = view_at(off7, [128, D], F32)
            off7 += 32 * 384 * 2
            ssq7 = view_at(off7, [128, 16], F32)
            off7 += 64
            junk7 = view_at(off7, [128, 512], BF16)
            off7 += 1024
            assert off7 <= ARENA, off7
            slabA = slab_raw[0]
            assert NSLAB * SLAB_BYTES >= 32 * 384 * 2
            slabA = view_at(slab_off, [128, 32, 384], BF16)
            R_hpF = [Res(f"hpF{i}") for i in range(9)]
            R_s7 = [Res("s7A"), Res("s7B")]
            slabs7 = [slabA, slabB]
            for i in range(9):
                rows = 128 if i < 8 else 64
                p.dma("sp", hpF[0:rows, i, :], hp_scr[i * 128:i * 128 + rows, :], writes=[R_hpF[i]])
            srcw = w_out1.rearrange("(c p) n -> p c n", p=128)
            for j in range(6):
                c0 = j * 384
                n = min(384, D - c0)
                sl_ = slabs7[j % 2]
                Rs = R_s7[j % 2]
                R_parts = []
                for k0 in range(0, 32, 8):
                    p.dma("pool", sl_[:, k0:k0 + 8, 0:n], srcw[:, k0:k0 + 8, c0:c0 + n], writes=[Rs] + [r for rl_ in R_slab for r in rl_])
                for i in range(9):
                    rows = 128 if i < 8 else 64
                    bk = next_bank(0, 8)
                    for c in range(32):
                        gv = gT_view(c)
                        p.op("pe", lambda e, bk=bk, c=c, i=i, rows=rows, n=n, gv=gv, sl_=sl_: e.matmul(
                            out=banks[bk][0:rows, 0:n], lhsT=gv[:, i * 128:i * 128 + rows], rhs=sl_[:, c, 0:n],
                            start=(c == 0), stop=(c == 31)), reads=[R_vb[c], Rs], writes=[RB[bk]])
                    p.op("dve", lambda e, bk=bk, i=i, rows=rows, n=n, c0=c0: e.tensor_tensor(
                        out=hpF[0:rows, i, c0:c0 + n], in0=banks[bk][0:rows, 0:n], in1=hpF[0:rows, i, c0:c0 + n],
                        op=ALU.add), reads=[RB[bk], R_hpF[i]], writes=[R_hpF[i]])
            p.op("dve", lambda e: e.memset(gfB, 0.0), reads=[R_s7[1]], writes=[R_s7[1]])
            p.dma("sp", gfB, final_g.partition_broadcast(128), reads=[R_s7[1]], writes=[R_s7[1]])
            R_q7 = Res("q7")
            for i in range(9):
                rows = 128 if i < 8 else 64
                for q in range(4):
                    p.op("act", lambda e, i=i, rows=rows, q=q: e.activation(
                        out=junk7[0:rows, :], in_=hpF[0:rows, i, q * 512:(q + 1) * 512], func=AF.Square,
                        accum_out=ssq7[0:rows, q:q + 1]), reads=[R_hpF[i]], writes=[R_q7])
                p.op("dve", lambda e, rows=rows: e.reduce_sum(out=ssq7[0:rows, 4:5], in_=ssq7[0:rows, 0:4], axis=AX.X),
                     reads=[R_q7], writes=[R_q7])
                p.op("act", lambda e, rows=rows: e.activation(out=ssq7[0:rows, 4:5], in_=ssq7[0:rows, 4:5], func=AF.Ln,
                                                              scale=1.0 / D, bias=1e-6), reads=[R_q7], writes=[R_q7])
                p.op("act", lambda e, rows=rows: e.activation(out=ssq7[0:rows, 4:5], in_=ssq7[0:rows, 4:5], func=AF.Exp,
                                                              scale=-0.5), reads=[R_q7], writes=[R_q7])
                p.op("dve", lambda e, i=i, rows=rows: e.scalar_tensor_tensor(
                    out=hpF[0:rows, i, :], in0=hpF[0:rows, i, :], scalar=ssq7[0:rows, 4:5], in1=gfB[0:rows, :],
                    op0=ALU.mult, op1=ALU.mult), reads=[R_hpF[i], R_q7, R_s7[1]], writes=[R_hpF[i]])
                dst = y_own[i * 128:(i + 1) * 128, :] if i < 8 else y_s
                p.dma("sp", dst, hpF[0:rows, i, :], reads=[R_hpF[i]])

        if stage >= 5:
            out_proj0()
            p.barrier(scratch)
            Alloc.top = mark1
            if "nol1" not in DBG:
                layer1()
                p.barrier(scratch)
            if stage >= 8:
                out_proj1()
                p.barrier(scratch)

        if os.environ.get("DBGOUT") and stage < 5:
            dbg = nc.dram_tensor("dbg", [128, 16, NOWN], F32, kind="ExternalOutput").ap()
            dst_ = [alloc([128, NOWN], F32, f"dbgst{i}") for i in range(2)]
            R_d = [Res("d0"), Res("d1")]
            for h in range(16):
                p.op("dve", lambda e, h=h: e.tensor_copy(out=dst_[h % 2], in_=mixT[:, h, :]),
                     reads=[R_mix[h], R_mix_s[h]], writes=[R_d[h % 2]])
                p.dma("sp", dbg[:, h, :], dst_[h % 2], reads=[R_d[h % 2]])

        p.emit(st)
        _NC_CACHE["trace"] = p.trace
    return nc


def _consts(half):
    c = np.zeros((14, 128, 128), np.float32)
    i = np.arange(128)
    c[0] = np.eye(128)
    c[1] = (i[:, None] <= i[None, :])
    c[2] = 1.0
    c[3] = 1.0 if half == 0 else 0.0
    c[4] = 1.0 if half == 1 else 0.0
    c[5] = (i[:, None] > i[None, :])
    c[6] = (i[:, None] <= i[None, :])
    c[7] = (i[:, None] < i[None, :])
    c[8] = 0.0
    same = (i[:, None] // 16 == i[None, :] // 16) & (i[:, None] < 64) & (i[None, :] < 64)
    c[9] = same & (i[:, None] <= i[None, :])
    c[10] = same & (i[:, None] <= i[None, :])
    c[11] = same & (i[:, None] < i[None, :])
    c[12] = same
    c[13] = (i[:, None] < 16) & (i[None, :] < 64) & (i[None, :] % 16 == i[:, None])
    return np.ascontiguousarray(c.transpose(1, 0, 2).reshape(128, 14 * 128))


_NC_CACHE = {}


def kernel(x_prompt, x_sample, cache_fox_k, cache_fox_v, cache_fox_logf, cache_sb_k, cache_sb_v,
           norm0_g, w_in0, b_forget, w_out0, norm1_g, w_in1, sgu_ln_g, sgu_ln_b, w_sp, b_sp, w_out1, final_g):
    f = lambda a: np.ascontiguousarray(np.asarray(a, dtype=np.float32))
    x_prompt, x_sample = f(x_prompt), f(x_sample)
    if "nc" not in _NC_CACHE:
        _NC_CACHE["nc"] = build_program(STAGE)
    nc = _NC_CACHE["nc"]
    gT = np.concatenate([f(g).reshape(16, 128).T for g in (norm0_g, norm1_g, final_g)], axis=1)
    lngb = np.concatenate([f(g).reshape(32, 128).T for g in (sgu_ln_g, sgu_ln_b)], axis=1)
    shared = dict(w_in0=f(w_in0), w_out0=f(w_out0), w_in1=f(w_in1), w_out1=f(w_out1),
                  gT=np.ascontiguousarray(gT), bforget=f(b_forget), lngb=np.ascontiguousarray(lngb),
                  ln_g=f(sgu_ln_g), ln_b=f(sgu_ln_b), final_g=f(final_g), w_sp=f(w_sp), b_sp=f(b_sp))
    in_maps = []
    for c in range(NCORES):
        b, half = c // 2, c % 2
        xb = x_prompt[b].reshape(16, 128, D)
        order = OWN[half] + OTHER[half]
        xs = x_sample[4 * c:4 * c + 4].reshape(64, D)
        xall = np.concatenate([xb[order].reshape(2048, D), xs], axis=0)
        m = dict(shared)
        m["xall"] = np.ascontiguousarray(xall)
        m["cfk"] = f(cache_fox_k[4 * c:4 * c + 4]).reshape(4096, 1024)
        m["cfv"] = f(cache_fox_v[4 * c:4 * c + 4]).reshape(4096, 1024)
        m["csk"] = f(cache_sb_k[4 * c:4 * c + 4]).reshape(4096, 1024)
        m["csv"] = f(cache_sb_v[4 * c:4 * c + 4]).reshape(4096, 1024)
        m["clf"] = f(cache_fox_logf[4 * c:4 * c + 4])
        m["cst"] = _consts(half)
        in_maps.append(m)
    res = run_bass_kernel_spmd(nc, in_maps, core_ids=list(range(NCORES)))
    R = res.results
    B, S = 4, 2048

    def gather_prompt(name, width):
        out = np.zeros((B, 16, 128, width), np.float32)
        for c in range(NCORES):
            b, half = c // 2, c % 2
            out[b, OWN[half]] = R[c][name].reshape(8, 128, width)
        return out.reshape(B, S, width)

    def gather_sample(name, width):
        return np.concatenate([R[c][name].reshape(4, 16, width) for c in range(NCORES)], axis=0)

    y_prompt = gather_prompt("y_own", D)
    y_sample = gather_sample("y_s", D)
    fkp = gather_prompt("o_fk", 1024).reshape(B, S, 8, 128)
    fvp = gather_prompt("o_fv", 1024).reshape(B, S, 8, 128)
    lfp = gather_prompt("o_lf", 8)
    fks = gather_sample("o_fk_s", 1024).reshape(32, 16, 8, 128)
    fvs = gather_sample("o_fv_s", 1024).reshape(32, 16, 8, 128)
    lfs = gather_sample("o_lf_s", 8)
    skp = gather_prompt("o_sk", 1024).reshape(B, S, 8, 128)
    svp = gather_prompt("o_sv", 1024).reshape(B, S, 8, 128)
    sks = gather_sample("o_sk_s", 1024).reshape(32, 16, 8, 128)
    svs = gather_sample("o_sv_s", 1024).reshape(32, 16, 8, 128)
    sgu = gather_sample("o_sgu", 4096)
    return (y_prompt, y_sample, fkp, fvp, lfp, fks, fvs, lfs, skp, svp, sks, svs, sgu)
```

```python
import numpy as np
from contextlib import ExitStack
import concourse.bass as bass
import concourse.mybir as mybir
from concourse.bass_utils import run_bass_kernel_spmd

F32 = mybir.dt.float32
BF16 = mybir.dt.bfloat16
AF = mybir.ActivationFunctionType
ALU = mybir.AluOpType

D = 2048
NCORES = 8
SCALE = 128 ** -0.5
OWN = {0: [0, 3, 4, 7, 8, 11, 12, 15], 1: [1, 2, 5, 6, 9, 10, 13, 14]}
OTHER = {h: [b for b in range(16) if b not in OWN[h]] for h in (0, 1)}
NTOK = 2112
NOWN = 1088
STAGE = 99
import os
PARTS = os.environ.get('PARTS', 'KOVQ')
NGROUPS = int(os.environ.get('NGROUPS', '8'))
DBG = os.environ.get('DBG', '')

COMPUTE = ("pe", "act", "dve", "pool")
ALL_ENG = ("pe", "act", "dve", "pool", "sp")


class Res:
    __slots__ = ("name", "writer", "readers", "lock")

    def __init__(self, name="", lock=None):
        self.name = name
        self.writer = None
        self.readers = []
        self.lock = lock


class Op:
    __slots__ = ("eng", "fn", "deps", "is_dma", "count", "needed", "dsem", "dval", "prewait")

    def __init__(self, eng, fn, is_dma):
        self.eng = eng
        self.fn = fn
        self.deps = []
        self.is_dma = is_dma
        self.count = None
        self.needed = False
        self.dsem = None
        self.dval = None
        self.prewait = None


class Prog:
    def __init__(self, nc):
        self.nc = nc
        self.ops = {e: [] for e in ALL_ENG}
        self.n_dma_sems = {"sp": 24, "pool": 24}
        self.dma_rr = {e: 0 for e in ALL_ENG}
        self.dma_last = {}
        self.dma_cnt = {}
        self.phase = Res("phase")

    def _record(self, op, reads, writes):
        deps = []
        for r in reads:
            if r.writer is not None:
                deps.append(r.writer)
        for w in writes:
            if w.writer is not None:
                deps.append(w.writer)
            last = {}
            for r in w.readers:
                if r.is_dma:
                    deps.append(r)
                else:
                    last[r.eng] = r
            deps.extend(last.values())
        seen = set()
        for d in deps:
            if d is op or id(d) in seen:
                continue
            seen.add(id(d))
            if op.eng == "pe" and d.eng == "pe" and not d.is_dma and not op.is_dma:
                continue
            op.deps.append(d)
            d.needed = True
        for r in reads:
            r.readers.append(op)
        for w in writes:
            w.writer = op
            w.readers = []
        self.ops[op.eng].append(op)
        return op

    def op(self, eng, fn, reads=(), writes=(), glob=False):
        reads = list(reads)
        writes = list(writes)
        for r in reads:
            if r.lock is not None and r not in writes:
                writes.append(r.lock)
        if not glob:
            reads.append(self.phase)
        return self._record(Op(eng, fn, False), reads, writes)

    def dma(self, eng, out, in_, reads=(), writes=(), glob=False):
        def fn(e, out=out, in_=in_):
            return e.dma_start(out=out, in_=in_)
        op = Op(eng, fn, True)
        n = self.n_dma_sems[eng]
        slot = self.dma_rr[eng] % n
        self.dma_rr[eng] += 1
        key = (eng, slot)
        cnt = self.dma_cnt.get(key, 0) + 1
        self.dma_cnt[key] = cnt
        op.dsem = key
        op.dval = 16 * cnt
        op.prewait = self.dma_last.get(key)
        self.dma_last[key] = op
        reads = list(reads)
        if not glob:
            reads.append(self.phase)
        return self._record(op, reads, list(writes))

    def barrier(self, scratch):
        o = Op("pool", lambda e: e.memset(scratch, 0.0), False)
        last = {}
        for r in self.phase.readers:
            if r.is_dma:
                o.deps.append(r)
            else:
                last[r.eng] = r
        if self.phase.writer is not None:
            o.deps.append(self.phase.writer)
        for r in last.values():
            o.deps.append(r)
            r.needed = True
        self.phase.writer = o
        self.phase.readers = []
        self.ops["pool"].append(o)

    def emit(self, stack):
        nc = self.nc
        eng_sem = {e: stack.enter_context(nc.semaphore("es_" + e)) for e in COMPUTE}
        dma_sem = {}
        for e, n in self.n_dma_sems.items():
            for s in range(n):
                dma_sem[(e, s)] = stack.enter_context(nc.semaphore(f"ds_{e}{s}"))
        for e in COMPUTE:
            c = 0
            for o in self.ops[e]:
                if o.needed and not o.is_dma:
                    c += 1
                    o.count = c
        block = stack.enter_context(nc.Block())
        prog = self

        prog.trace = {e: [] for e in ALL_ENG}

        def run(engname, eng):
            known = {}
            tr = prog.trace[engname]

            def wait(key, sem, val):
                if known.get(key, 0) >= val:
                    return
                eng.wait_ge(sem, val)
                tr.append(("w", key, val))
                known[key] = val

            def wait_for(d):
                if d.is_dma:
                    wait(d.dsem, dma_sem[d.dsem], d.dval)
                else:
                    if d.eng == engname and engname == "pe":
                        return
                    wait(d.eng, eng_sem[d.eng], d.count)

            def wait_all(ds):
                need = {}
                for d in ds:
                    if d.is_dma:
                        k, v = d.dsem, d.dval
                    else:
                        if d.eng == engname and engname == "pe":
                            continue
                        k, v = d.eng, d.count
                    if need.get(k, 0) < v:
                        need[k] = v
                for k, v in need.items():
                    wait(k, dma_sem[k] if isinstance(k, tuple) else eng_sem[k], v)

            for o in prog.ops[engname]:
                ds = list(o.deps)
                if o.is_dma and o.prewait is not None:
                    ds.append(o.prewait)
                wait_all(ds)
                ins = o.fn(eng)
                if o.is_dma:
                    ins.then_inc(dma_sem[o.dsem], 16)
                    tr.append(("i", o.dsem, 16))
                elif o.needed:
                    ins.then_inc(eng_sem[engname], 1)
                    tr.append(("i", engname, 1))
                else:
                    tr.append(("n", None, 0))
            for key, last in prog.dma_last.items():
                if key[0] == engname:
                    wait(key, dma_sem[key], last.dval)

        @block.tensor
        def _(e):
            run("pe", e)

        @block.scalar
        def _(e):
            run("act", e)

        @block.vector
        def _(e):
            run("dve", e)

        @block.gpsimd
        def _(e):
            run("pool", e)

        @block.sync
        def _(e):
            run("sp", e)


def build_program(stage=99):
    nc = bass.Bass("TRN2", target_bir_lowering=False)

    def din(name, shape):
        return nc.dram_tensor(name, list(shape), F32, kind="ExternalInput").ap()

    def dout(name, shape):
        return nc.dram_tensor(name, list(shape), F32, kind="ExternalOutput").ap()

    xall = din("xall", [NTOK, D])
    ck = [din("cfk", [4096, 1024]), din("csk", [4096, 1024])]
    cv = [din("cfv", [4096, 1024]), din("csv", [4096, 1024])]
    clf = din("clf", [4, 1024, 8])
    w_in0 = din("w_in0", [D, 8200])
    w_out0 = din("w_out0", [D, D])
    w_in1 = din("w_in1", [D, 12288])
    w_out1 = din("w_out1", [4096, D])
    gT_in = din("gT", [128, 48])
    bfg = din("bforget", [8])
    lngb = din("lngb", [128, 64])
    ln_g = din("ln_g", [4096])
    ln_b = din("ln_b", [4096])
    final_g = din("final_g", [D])
    w_sp = din("w_sp", [16, 128, 128])
    b_sp = din("b_sp", [16, 128])
    cst = din("cst", [128, 14 * 128])

    y_own = dout("y_own", [1024, D])
    y_s = dout("y_s", [64, D])
    okv = {}
    for nm in ("fk", "fv", "sk", "sv"):
        okv[nm] = dout("o_" + nm, [1024, 1024])
        okv[nm + "_s"] = dout("o_" + nm + "_s", [64, 1024])
    o_lf = dout("o_lf", [1024, 8])
    o_lf_s = dout("o_lf_s", [64, 8])
    o_sgu = dout("o_sgu", [64, 4096])
    hp_scr = nc.dram_tensor("hp_scr", [NOWN + 64, D], F32, kind="Internal").ap()

    st = ExitStack()
    with st:
        ARENA = 207 * 1024
        arena = st.enter_context(nc.sbuf_tensor("arena", [128, ARENA // 4], F32))
        banks = [st.enter_context(nc.psum_tensor(f"bank{i}", [128, 512], F32)) for i in range(8)]
        RB = [Res(f"bank{i}", lock=Res(f"banklock{i}")) for i in range(8)]
        p = Prog(nc)

        class Alloc:
            top = 0

        def view_at(off, shape, dt):
            esz = 4 if dt == F32 else 2
            n = int(np.prod(shape[1:]))
            assert off % 4 == 0 and off + n * esz <= ARENA, (off, shape)
            a = arena[:, off // 4: off // 4 + (n * esz + 3) // 4]
            if dt != F32:
                a = a.bitcast(dt)
            a = a[:, 0:n]
            if len(shape) == 3:
                a = a.rearrange("p (a b) -> p a b", a=shape[1])
            elif len(shape) == 4:
                a = a.rearrange("p (a b c) -> p a b c", a=shape[1], b=shape[2])
            return a

        def alloc(shape, dt, name=""):
            esz = 4 if dt == F32 else 2
            n = int(np.prod(shape[1:]))
            nbytes = (n * esz + 63) // 64 * 64
            off = Alloc.top
            Alloc.top += nbytes
            assert Alloc.top <= ARENA, (name, Alloc.top)
            return view_at(off, shape, dt)

        def bank_bf(i):
            return banks[i][:, :].bitcast(BF16)

        cst_f = alloc([128, 14, 128], F32, "cst_f")
        cst_b = alloc([128, 14, 128], BF16, "cst_b")
        (I_ID, I_TRILE, I_ONES, I_EEV, I_EOD, I_SL, I_MLE, I_MLT, I_ZERO, I_BDTRI, I_BDLE, I_BDLT, I_SAME, I_SEL) = range(14)
        R_cst = Res("cst")
        p.dma("sp", cst_f, cst.rearrange("p (a b) -> p a b", a=14), writes=[R_cst], glob=True)
        p.op("dve", lambda e: e.tensor_copy(out=cst_b, in_=cst_f), reads=[R_cst], writes=[R_cst], glob=True)
        ident_b = cst_b[:, I_ID, :]
        ones_b = cst_b[:, I_ONES, :]
        zero_b = cst_b[:, I_ZERO, :]
        SL_b = cst_b[:, I_SL, :]
        MLE_b = cst_b[:, I_MLE, :]
        MLT_b = cst_b[:, I_MLT, :]
        E_b = [cst_b[:, I_EEV, :], cst_b[:, I_EOD, :]]
        E_f = [cst_f[:, I_EEV, :], cst_f[:, I_EOD, :]]
        trile_f = cst_f[:, I_TRILE, :]
        ones_f = cst_f[:, I_ONES, :]
        SL_f = cst_f[:, I_SL, :]

        def own_first(pp, bf=True):
            t = E_b if bf else E_f
            return t[0] if pp % 2 == 0 else t[1]

        def other_first(pp, bf=True):
            t = E_b if bf else E_f
            return t[1] if pp % 2 == 0 else t[0]

        gT = alloc([128, 48], F32, "gT")
        lngbT = alloc([128, 64], F32, "lngbT")
        bfB = alloc([128, 8], F32, "bfB")
        p.dma("sp", gT, gT_in, writes=[R_cst], glob=True)
        p.dma("sp", lngbT, lngb, writes=[R_cst], glob=True)
        p.dma("sp", bfB, bfg.partition_broadcast(128), writes=[R_cst], glob=True)
        scratch = alloc([128, 16], F32, "scratch")

        NSLAB = 3
        SLAB_BYTES = 8192
        slab_off = Alloc.top
        slab_raw = [alloc([128, SLAB_BYTES // 2], BF16, f"slab{i}") for i in range(NSLAB)]
        R_slab = [[Res(f"slab{i}_{q}") for q in range(4)] for i in range(NSLAB)]
        slab_ctr = [0]
        slab_first = {}

        def load_slab(w_ap, row_chunks, col0, ncols):
            i = slab_ctr[0] % len(slab_raw)
            slab_ctr[0] += 1
            first_extra = slab_first.pop(i, [])
            assert row_chunks * ncols * 2 <= SLAB_BYTES
            v = slab_raw[i][:, 0:row_chunks * ncols].rearrange("p (c n) -> p c n", c=row_chunks)
            src = w_ap.rearrange("(c p) n -> p c n", p=128)
            step = max(1, row_chunks // 4)
            rl = []
            for qi, c0 in enumerate(range(0, row_chunks, step)):
                p.dma("pool", v[:, c0:c0 + step, :], src[:, c0:c0 + step, col0:col0 + ncols],
                      writes=[R_slab[i][qi]] + first_extra, glob=True)
                rl += [R_slab[i][qi]] * step
            return v, rl

        kvq_slabs = {}

        def issue_kvq(m, h0, parts="kvq"):
            base = m * 4096
            d_ = kvq_slabs.setdefault((m, h0), {})
            for nm_, off_ in (("k", 1024), ("v", 2048), ("q", 0)):
                if nm_ in parts:
                    d_[nm_] = load_slab(w_in0, 16, base + off_ + h0 * 128, 256)

        wlf, R_wlf = load_slab(w_in0, 16, 8192, 8)
        if stage >= 2:
            issue_kvq(0, 0, "kv")

        hT_off = Alloc.top
        hT = alloc([128, 16, NTOK], BF16, "hT")
        R_hT = [Res(f"hT{i}") for i in range(17)]
        mixT = alloc([128, 16, NOWN], BF16, "mixT")
        R_mix = [Res(f"mix{h}") for h in range(16)]
        R_mix_s = [Res(f"mixs{h}") for h in range(16)]
        KTs = alloc([128, 16, 64], BF16, "KTs")
        QTs = alloc([128, 16, 64], BF16, "QTs")
        GTs = alloc([128, 16, 64], BF16, "GTs")
        Vs = alloc([128, 16, 128], BF16, "Vs")
        R_samp = Res("samp_keep")
        logf = alloc([128, 17, 8], F32, "logf")
        Fneg = alloc([128, 16, 8], F32, "Fneg")
        PB = alloc([128, 8, 8], F32, "PB")
        R_F = Res("F")
        mark0 = Alloc.top

        def tile_rows(i):
            return 128 if i < 16 else 64

        def tile_cols(i):
            return slice(i * 128, i * 128 + tile_rows(i))

        xt = [alloc([128, D], F32, f"xt{i}") for i in range(2)]
        R_xt = [Res("xt0"), Res("xt1")]
        xn = [alloc([128, D], BF16, f"xn{i}") for i in range(2)]
        R_xn = [Res("xn0"), Res("xn1")]
        junk = alloc([128, D], BF16, "junk")
        R_junk = Res("junk")
        ssq = alloc([128, 32], F32, "ssq")
        R_ssq = Res("ssq")

        def norm_transpose(i, src_tile, R_src, rows, g_off, dstT, R_dst, col0, pb, bufs=None):
            b = i % 2
            xn, R_xn, junk, R_junk, ssq, R_ssq = bufs
            p.op("act", lambda e: e.activation(out=junk[0:rows, :], in_=src_tile[0:rows, :], func=AF.Square,
                                               accum_out=ssq[0:rows, i:i + 1]),
                 reads=[R_src], writes=[R_junk, R_ssq])
            p.op("act", lambda e: e.activation(out=ssq[0:rows, i:i + 1], in_=ssq[0:rows, i:i + 1], func=AF.Ln,
                                               scale=1.0 / D, bias=1e-6), reads=[R_ssq], writes=[R_ssq])
            p.op("act", lambda e: e.activation(out=ssq[0:rows, i:i + 1], in_=ssq[0:rows, i:i + 1], func=AF.Exp,
                                               scale=-0.5), reads=[R_ssq], writes=[R_ssq])
            p.op("dve", lambda e: e.tensor_scalar(out=xn[b][0:rows, :], in0=src_tile[0:rows, :],
                                                  scalar1=ssq[0:rows, i:i + 1], scalar2=None, op0=ALU.mult),
                 reads=[R_src, R_ssq], writes=[R_xn[b]])
            for half in range(2):
                bk = pb + half
                pv = bank_bf(bk)
                for cc in range(8):
                    c = half * 8 + cc
                    p.op("pe", lambda e, c=c, cc=cc, pv=pv: e.transpose(
                        out=pv[:, cc * 128:cc * 128 + rows], in_=xn[b][0:rows, c * 128:(c + 1) * 128],
                        identity=ident_b[0:rows, 0:rows]),
                        reads=[R_xn[b], R_cst], writes=[RB[bk]])
                pv3 = pv.rearrange("p (a b) -> p a b", a=8)
                p.op("dve", lambda e, half=half, pv3=pv3: e.tensor_tensor(
                    out=dstT[:, half * 8:(half + 1) * 8, col0:col0 + rows], in0=pv3[:, :, 0:rows],
                    in1=gT[:, g_off + half * 8:g_off + (half + 1) * 8].unsqueeze(2).to_broadcast([128, 8, rows]),
                    op=ALU.mult), reads=[RB[bk], R_cst], writes=[R_dst])

        for i in range(17):
            rows = tile_rows(i)
            b = i % 2
            p.dma("sp", xt[b][0:rows, :], xall[i * 128:i * 128 + rows, :], writes=[R_xt[b]])
            norm_transpose(i, xt[b], R_xt[b], rows, 0, hT, R_hT[i], i * 128, (i % 2) * 2,
                           bufs=(xn, R_xn, junk, R_junk, ssq, R_ssq))

        bkL = 4
        for i in range(17):
            rows = tile_rows(i)
            for c in range(16):
                p.op("pe", lambda e, i=i, c=c, rows=rows: e.matmul(
                    out=banks[bkL][0:rows, i * 8:(i + 1) * 8], lhsT=hT[:, c, i * 128:i * 128 + rows],
                    rhs=wlf[:, c, :], start=(c == 0), stop=(c == 15)),
                    reads=[R_hT[i], R_wlf[c]], writes=[RB[bkL]])
        lg = alloc([128, 17, 8], F32, "lg")
        R_lg = Res("lg")
        for (r0, r1, t0, t1) in ((0, 128, 0, 16), (0, 64, 16, 17)):
            nt = t1 - t0
            pv = banks[bkL][r0:r1, t0 * 8:t1 * 8].rearrange("p (a b) -> p a b", a=nt)
            p.op("dve", lambda e, pv=pv, r0=r0, r1=r1, t0=t0, t1=t1, nt=nt: e.tensor_tensor(
                out=lg[r0:r1, t0:t1, :], in0=pv, in1=bfB[r0:r1, :].unsqueeze(1).to_broadcast([r1 - r0, nt, 8]),
                op=ALU.add), reads=[RB[bkL], R_cst], writes=[R_lg])
            p.op("act", lambda e, r0=r0, r1=r1, t0=t0, t1=t1: e.activation(
                out=lg[r0:r1, t0:t1, :], in_=lg[r0:r1, t0:t1, :], func=AF.Exp, scale=-1.0), reads=[R_lg], writes=[R_lg])
            p.op("act", lambda e, r0=r0, r1=r1, t0=t0, t1=t1: e.activation(
                out=lg[r0:r1, t0:t1, :], in_=lg[r0:r1, t0:t1, :], func=AF.Ln, bias=1.0), reads=[R_lg], writes=[R_lg])
            p.op("dve", lambda e, r0=r0, r1=r1, t0=t0, t1=t1: e.tensor_scalar(
                out=logf[r0:r1, t0:t1, :], in0=lg[r0:r1, t0:t1, :], scalar1=-1.0, scalar2=None, op0=ALU.mult),
                reads=[R_lg], writes=[R_F])
        p.dma("sp", o_lf.rearrange("(i p) h -> p i h", p=128), logf[:, 0:8, :], reads=[R_F])
        p.dma("sp", o_lf_s, logf[0:64, 16, :], reads=[R_F])
        Spre = alloc([128, 9, 8], F32, "Spre")
        psum_pair = alloc([128, 8, 8], F32, "psum_pair")
        R_S = Res("Spre")
        p.op("dve", lambda e: e.tensor_tensor(out=psum_pair, in0=logf[:, 0:8, :], in1=logf[:, 8:16, :], op=ALU.add),
             reads=[R_F], writes=[R_S])
        p.op("dve", lambda e: e.memset(Spre[:, 0, :], 0.0), writes=[R_S])
        for pp in range(8):
            p.op("dve", lambda e, pp=pp: e.tensor_tensor(out=Spre[:, pp + 1, :], in0=Spre[:, pp, :],
                                                         in1=psum_pair[:, pp, :], op=ALU.add),
                 reads=[R_S], writes=[R_S])
        bkF = 5
        for k in range(16):
            pp = k % 8
            is_other = k >= 8
            partner = pp if is_other else 8 + pp
            Et = own_first(pp, bf=False) if is_other else other_first(pp, bf=False)
            o = banks[bkF][:, k * 8:(k + 1) * 8]
            p.op("pe", lambda e, o=o, k=k: e.matmul(out=o, lhsT=trile_f, rhs=logf[:, k, :], start=True, stop=False),
                 reads=[R_F, R_cst], writes=[RB[bkF]])
            p.op("pe", lambda e, o=o, pp=pp: e.matmul(out=o, lhsT=ones_f, rhs=Spre[:, pp, :], start=False, stop=False),
                 reads=[R_S, R_cst], writes=[RB[bkF]])
            p.op("pe", lambda e, o=o, Et=Et, partner=partner: e.matmul(out=o, lhsT=Et, rhs=logf[:, partner, :],
                                                                        start=False, stop=True),
                 reads=[R_F, R_cst], writes=[RB[bkF]])
        for pp in range(8):
            o = banks[bkF][:, 128 + pp * 8:128 + (pp + 1) * 8]
            p.op("pe", lambda e, o=o, pp=pp: e.matmul(out=o, lhsT=ones_f, rhs=Spre[:, pp, :], start=True, stop=True),
                 reads=[R_S, R_cst], writes=[RB[bkF]])
        R_F2 = Res("F2")
        p.op("dve", lambda e: e.tensor_scalar(out=Fneg, in0=banks[bkF][:, 0:128].rearrange("p (a b) -> p a b", a=16),
                                              scalar1=-1.0, scalar2=None, op0=ALU.mult),
             reads=[RB[bkF]], writes=[R_F2])
        p.op("dve", lambda e: e.tensor_copy(out=PB.rearrange("p h s -> p s h"),
                                            in_=banks[bkF][:, 128:192].rearrange("p (s h) -> p s h", s=8)),
             reads=[RB[bkF]], writes=[R_F2])

        p.barrier(scratch)
        Alloc.top = mark0

        KT = alloc([128, 2, 2048], BF16, "KT")
        Vg = alloc([128, 16, 256], BF16, "Vg")
        QT = alloc([128, 2, 1024], BF16, "QT")
        GT = alloc([128, 2, 1024], BF16, "GT")
        R_KT = [Res("KT0"), Res("KT1")]
        R_Vg = Res("Vg")
        R_QT = [Res("QT0"), Res("QT1")]
        R_GT = [Res("GT0"), Res("GT1")]
        stg = [alloc([128, 256], F32, f"stg{i}") for i in range(4)]
        R_stg = [Res(f"stg{i}") for i in range(4)]
        stg_ctr = [0]
        wA = alloc([128, 1024], F32, "wA")
        wB = alloc([128, 1024], F32, "wB")
        wC = alloc([128, 1024], F32, "wC")
        wD = alloc([128, 1024], F32, "wD")
        wE = alloc([128, 1024], F32, "wE")
        hA = alloc([128, 1024], BF16, "hA")
        hB = alloc([128, 1024], BF16, "hB")
        hC = alloc([128, 1024], BF16, "hC")
        hD = alloc([128, 1024], BF16, "hD")
        hE = alloc([128, 1024], BF16, "hE")
        R_w = {k: Res(k) for k in ("wA", "wB", "wC", "wD", "wE", "hA", "hB", "hC", "hD", "hE")}
        ev_ctr = [0]

        def evac(out, in_, reads, writes, func=None, eng="dve"):
            if "noevac" in DBG:
                return
            if "dveonly" in DBG and func is None:
                p.op("dve", lambda e: e.tensor_copy(out=out, in_=in_), reads=reads, writes=writes)
                return
            if "actonly" in DBG:
                f = func if func is not None else AF.Copy
                p.op("act", lambda e: e.activation(out=out, in_=in_, func=f), reads=reads, writes=writes)
                return
            if func is not None or eng == "act":
                f = func if func is not None else AF.Copy
                p.op("act", lambda e: e.activation(out=out, in_=in_, func=f), reads=reads, writes=writes)
            else:
                p.op("dve", lambda e: e.tensor_copy(out=out, in_=in_), reads=reads, writes=writes)
            ev_ctr[0] += 1

        pb_ctr = [0]

        def next_bank(lo=0, hi=8):
            b = lo + pb_ctr[0] % (hi - lo)
            pb_ctr[0] += 1
            return b

        def col_chunks(c0, c1):
            out = []
            c = c0
            while c < c1:
                e_ = min(c1, (c // 512 + 1) * 512)
                out.append((c, e_))
                c = e_
            return out

        out_names = {0: ("fk", "fv"), 1: ("sk", "sv")}

        def project_group(m, h0):
            base = m * 4096
            hg = m * 8 + h0
            kname, vname = out_names[m]
            d_ = kvq_slabs.pop((m, h0))
            (wk, Rwk), (wv, Rwv), (wq_, Rwq_) = d_["k"], d_["v"], d_["q"]
            for hh in (range(2) if "K" in PARTS else []):
                for (c0, c1) in ((0, 512), (512, 1024), (1024, 1536), (1536, 2048), (2048, 2112)):
                    bk = next_bank(0, 4)
                    n = c1 - c0
                    Rr = [R_hT[t] for t in range(c0 // 128, (c1 + 127) // 128)]
                    for c in range(16):
                        p.op("pe", lambda e, bk=bk, hh=hh, c=c, c0=c0, c1=c1, n=n: e.matmul(
                            out=banks[bk][:, 0:n], lhsT=wk[:, c, hh * 128:(hh + 1) * 128], rhs=hT[:, c, c0:c1],
                            start=(c == 0), stop=(c == 15)), reads=Rr + [Rwk[c]], writes=[RB[bk]])
                    if c0 < 2048:
                        evac(KT[:, hh, c0:c1], banks[bk][:, 0:n], [RB[bk]], [R_KT[hh]])
                    else:
                        evac(KTs[:, hg + hh, :], banks[bk][:, 0:64], [RB[bk]], [R_samp])
            for i in ((list(range(8)) + [16]) if "O" in PARTS else []):
                rows = tile_rows(i)
                bk = next_bank(0, 4)
                for c in range(16):
                    p.op("pe", lambda e, bk=bk, c=c, i=i, rows=rows: e.matmul(
                        out=banks[bk][0:rows, 0:256], lhsT=hT[:, c, i * 128:i * 128 + rows], rhs=wk[:, c, :],
                        start=(c == 0), stop=(c == 15)), reads=[R_hT[i], Rwk[c]], writes=[RB[bk]])
                s = stg_ctr[0] % 4
                stg_ctr[0] += 1
                evac(stg[s][0:rows, :], banks[bk][0:rows, 0:256], [RB[bk]], [R_stg[s]], eng="act")
                dst = okv[kname][i * 128:(i + 1) * 128, h0 * 128:h0 * 128 + 256] if i < 8 else \
                    okv[kname + "_s"][:, h0 * 128:h0 * 128 + 256]
                p.dma("sp", dst, stg[s][0:rows, :], reads=[R_stg[s]])
            wg_, Rwg_ = load_slab(w_in0, 16, base + 3072 + h0 * 128, 256)
            for i in (range(17) if "V" in PARTS else []):
                rows = tile_rows(i)
                bk = next_bank(0, 4)
                for c in range(16):
                    p.op("pe", lambda e, bk=bk, c=c, i=i, rows=rows: e.matmul(
                        out=banks[bk][0:rows, 0:256], lhsT=hT[:, c, i * 128:i * 128 + rows], rhs=wv[:, c, :],
                        start=(c == 0), stop=(c == 15)), reads=[R_hT[i], Rwv[c]], writes=[RB[bk]])
                if i < 16:
                    evac(Vg[:, i, :], banks[bk][:, 0:256], [RB[bk]], [R_Vg])
                else:
                    evac(Vs[0:64, hg:hg + 2, :], banks[bk][0:64, 0:256].rearrange("p (a b) -> p a b", a=2),
                         [RB[bk]], [R_samp])
                if i < 8 or i == 16:
                    s = stg_ctr[0] % 4
                    stg_ctr[0] += 1
                    evac(stg[s][0:rows, :], banks[bk][0:rows, 0:256], [RB[bk]], [R_stg[s]], eng="act")
                    dst = okv[vname][i * 128:(i + 1) * 128, h0 * 128:h0 * 128 + 256] if i < 8 else \
                        okv[vname + "_s"][:, h0 * 128:h0 * 128 + 256]
                    p.dma("sp", dst, stg[s][0:rows, :], reads=[R_stg[s]])
            for (ws, Rws, dstT, R_dst, dsts, func) in (
                    (wq_, Rwq_, QT, R_QT, QTs, None),
                    (wg_, Rwg_, GT, R_GT, GTs, AF.Silu)):
                for hh in (range(2) if "Q" in PARTS else []):
                    for (c0, c1) in ((0, 512), (512, 1024), (2048, 2112)):
                        bk = next_bank(0, 4)
                        n = c1 - c0
                        Rr = [R_hT[t] for t in range(c0 // 128, (c1 + 127) // 128)]
                        for c in range(16):
                            p.op("pe", lambda e, bk=bk, hh=hh, c=c, c0=c0, c1=c1, n=n, ws=ws: e.matmul(
                                out=banks[bk][:, 0:n], lhsT=ws[:, c, hh * 128:(hh + 1) * 128], rhs=hT[:, c, c0:c1],
                                start=(c == 0), stop=(c == 15)), reads=Rr + [Rws[c]], writes=[RB[bk]])
                        if c0 < 2048:
                            evac(dstT[:, hh, c0:c1], banks[bk][:, 0:n], [RB[bk]], [R_dst[hh]], func=func)
                        else:
                            evac(dsts[:, hg + hh, :], banks[bk][:, 0:64], [RB[bk]], [R_samp], func=func)

        def fox_head(hh, h):
            for q in range(2):
                p.op("pe", lambda e, q=q: e.matmul(out=banks[4 + q][:, :], lhsT=zero_b, rhs=QT[:, hh, q * 512:(q + 1) * 512],
                                                    start=True, stop=False), reads=[R_QT[hh], R_cst], writes=[RB[4 + q]])
                p.op("pe", lambda e, q=q: e.matmul(out=banks[6 + q][:, :], lhsT=zero_b, rhs=QT[:, hh, q * 512:(q + 1) * 512],
                                                    start=True, stop=False), reads=[R_QT[hh], R_cst], writes=[RB[6 + q]])
            its = [(pp, is_other) for pp in range(8) for is_other in (False, True)]

            def geom(it):
                pp, is_other = its[it]
                kb = 8 + pp if is_other else pp
                c0 = pp * 128
                return pp, is_other, kb, c0, (it % 2) * 2, col_chunks(c0, 1024)

            def qk(it):
                pp, is_other, kb, c0, sb0, chunks = geom(it)
                for (a, b_) in chunks:
                    bk = sb0 + a // 512
                    p.op("pe", lambda e, bk=bk, a=a, b_=b_, kb=kb: e.matmul(
                        out=banks[bk][:, a % 512:a % 512 + (b_ - a)], lhsT=KT[:, hh, kb * 128:(kb + 1) * 128],
                        rhs=QT[:, hh, a:b_], start=True, stop=True),
                        reads=[R_KT[hh], R_QT[hh]], writes=[RB[bk]])

            def elem(it):
                pp, is_other, kb, c0, sb0, chunks = geom(it)
                tmp = (wA, wB)[it % 2]
                Rtmp = (R_w["wA"], R_w["wB"])[it % 2]
                PT = (hA, hB)[it % 2]
                RPT = (R_w["hA"], R_w["hB"])[it % 2]
                for (a, b_) in chunks:
                    bk = sb0 + a // 512
                    ns = (b_ - a) // 128
                    s0 = a // 128
                    p.op("dve", lambda e, bk=bk, a=a, b_=b_, ns=ns, s0=s0, tmp=tmp: e.scalar_tensor_tensor(
                        out=tmp[:, a:b_].rearrange("p (s t) -> p s t", s=ns),
                        in0=banks[bk][:, a % 512:a % 512 + (b_ - a)].rearrange("p (s t) -> p s t", s=ns),
                        scalar=SCALE,
                        in1=PB[:, h, s0:s0 + ns].unsqueeze(2).to_broadcast([128, ns, 128]),
                        op0=ALU.mult, op1=ALU.add), reads=[RB[bk], R_F2], writes=[Rtmp])
                p.op("act", lambda e, c0=c0, kb=kb, tmp=tmp, PT=PT: e.activation(
                    out=PT[:, c0:1024], in_=tmp[:, c0:1024], func=AF.Exp, bias=Fneg[:, kb, h:h + 1], scale=1.0),
                    reads=[Rtmp, R_F2], writes=[RPT])
                mk = other_first(pp) if is_other else MLE_b
                p.op("pool", lambda e, c0=c0, mk=mk, PT=PT: e.tensor_tensor(
                    out=PT[:, c0:c0 + 128], in0=PT[:, c0:c0 + 128], in1=mk, op=ALU.mult),
                    reads=[RPT, R_cst], writes=[RPT])

            def pv(it):
                pp, is_other, kb, c0, sb0, chunks = geom(it)
                PT = (hA, hB)[it % 2]
                RPT = (R_w["hA"], R_w["hB"])[it % 2]
                for (a, b_) in chunks:
                    q = a // 512
                    last = is_other and ((pp == 3 and q == 0) or pp == 7)
                    p.op("pe", lambda e, q=q, a=a, b_=b_, kb=kb, last=last, PT=PT: e.matmul(
                        out=banks[4 + q][:, a % 512:a % 512 + (b_ - a)], lhsT=Vg[:, kb, hh * 128:(hh + 1) * 128],
                        rhs=PT[:, a:b_], start=False, stop=last), reads=[R_Vg, RPT], writes=[RB[4 + q]])
                    p.op("pe", lambda e, q=q, a=a, b_=b_, last=last, PT=PT: e.matmul(
                        out=banks[6 + q][:, a % 512:a % 512 + (b_ - a)], lhsT=ones_b,
                        rhs=PT[:, a:b_], start=False, stop=last), reads=[RPT, R_cst], writes=[RB[6 + q]])

            qk(0)
            for it in range(16):
                if it + 1 < 16:
                    qk(it + 1)
                elem(it)
                pv(it)
            for q in range(2):
                cs = slice(q * 512, (q + 1) * 512)
                p.op("dve", lambda e, q=q, cs=cs: e.reciprocal(out=wC[:, cs], in_=banks[6 + q][:, :]),
                     reads=[RB[6 + q]], writes=[R_w["wC"]])
                p.op("dve", lambda e, q=q, cs=cs: e.tensor_tensor(out=wC[:, cs], in0=banks[4 + q][:, :], in1=wC[:, cs],
                                                                 op=ALU.mult),
                     reads=[RB[4 + q], R_w["wC"]], writes=[R_w["wC"]])
                p.op("pool", lambda e, cs=cs: e.tensor_tensor(out=mixT[:, h, cs], in0=wC[:, cs], in1=GT[:, hh, cs],
                                                              op=ALU.mult),
                     reads=[R_w["wC"], R_GT[hh]], writes=[R_mix[h]])

        def sb_head(hh, h):
            SPS, SPSb = wE, hE
            for q in range(2):
                p.op("pe", lambda e, q=q: e.matmul(out=banks[6 + q][:, :], lhsT=zero_b, rhs=QT[:, hh, q * 512:(q + 1) * 512],
                                                    start=True, stop=False), reads=[R_QT[hh], R_cst], writes=[RB[6 + q]])
            p.op("pool", lambda e: e.memset(SPS, 0.0), writes=[R_w["wE"]])
            p.op("pool", lambda e: e.memset(SPSb, 0.0), writes=[R_w["hE"]])
            for pp in range(7, -1, -1):
                c0 = pp * 128
                chunks = col_chunks(c0, 1024)
                blocks = ((pp, 0, wA, "wA", hA, "hA", MLT_b), (8 + pp, 2, wB, "wB", hB, "hB", other_first(pp)))
                for (kb, zb, sp, spn, spm, spmn, mk) in blocks:
                    for (a, b_) in chunks:
                        bk = zb + a // 512
                        p.op("pe", lambda e, bk=bk, a=a, b_=b_, kb=kb: e.matmul(
                            out=banks[bk][:, a % 512:a % 512 + (b_ - a)], lhsT=KT[:, hh, kb * 128:(kb + 1) * 128],
                            rhs=QT[:, hh, a:b_], start=True, stop=True),
                            reads=[R_KT[hh], R_QT[hh]], writes=[RB[bk]])
                    for (a, b_) in chunks:
                        bk = zb + a // 512
                        p.op("act", lambda e, bk=bk, a=a, b_=b_: e.activation(
                            out=wC[:, a:b_], in_=banks[bk][:, a % 512:a % 512 + (b_ - a)], func=AF.Exp, scale=SCALE),
                            reads=[RB[bk]], writes=[R_w["wC"]])
                    p.op("act", lambda e, c0=c0, sp=sp: e.activation(out=sp[:, c0:1024], in_=wC[:, c0:1024], func=AF.Ln,
                                                                     bias=1.0),
                         reads=[R_w["wC"]], writes=[R_w[spn]])
                    p.op("pool", lambda e, c0=c0, sp=sp, spm=spm, mk=mk: e.tensor_tensor(
                        out=spm[:, c0:c0 + 128], in0=sp[:, c0:c0 + 128], in1=mk, op=ALU.mult),
                        reads=[R_w[spn], R_cst], writes=[R_w[spmn]])
                    if c0 + 128 < 1024:
                        p.op("pool", lambda e, c0=c0, sp=sp, spm=spm: e.tensor_copy(
                            out=spm[:, c0 + 128:1024], in_=sp[:, c0 + 128:1024]),
                            reads=[R_w[spn]], writes=[R_w[spmn]])
                for bi, (kb, zb, sp, spn, spm, spmn, mk) in enumerate(blocks):
                    ospm, ospmn = (hB, "hB") if bi == 0 else (hA, "hA")
                    Et = own_first(pp) if bi == 0 else other_first(pp)
                    for (a, b_) in chunks:
                        bk = 4 + a // 512
                        o = banks[bk][:, a % 512:a % 512 + (b_ - a)]
                        p.op("pe", lambda e, o=o, a=a, b_=b_, spm=spm: e.matmul(out=o, lhsT=SL_b, rhs=spm[:, a:b_],
                                                                                 start=True, stop=False),
                             reads=[R_w[spmn], R_cst], writes=[RB[bk]])
                        p.op("pe", lambda e, o=o, a=a, b_=b_: e.matmul(out=o, lhsT=ones_b, rhs=SPSb[:, a:b_],
                                                                       start=False, stop=False),
                             reads=[R_w["hE"], R_cst], writes=[RB[bk]])
                        p.op("pe", lambda e, o=o, a=a, b_=b_, Et=Et, ospm=ospm: e.matmul(
                            out=o, lhsT=Et, rhs=ospm[:, a:b_], start=False, stop=True),
                            reads=[R_w[ospmn], R_cst], writes=[RB[bk]])
                    u, un = (wC, "wC") if bi == 0 else (wD, "wD")
                    aT, aTn = (hC, "hC") if bi == 0 else (hD, "hD")
                    for (a, b_) in chunks:
                        bz = zb + a // 512
                        bc = 4 + a // 512
                        p.op("dve", lambda e, bz=bz, a=a, b_=b_, sp=sp, u=u: e.scalar_tensor_tensor(
                            out=u[:, a:b_], in0=banks[bz][:, a % 512:a % 512 + (b_ - a)], scalar=SCALE, in1=sp[:, a:b_],
                            op0=ALU.mult, op1=ALU.subtract), reads=[RB[bz], R_w[spn]], writes=[R_w[un]])
                        p.op("dve", lambda e, bc=bc, a=a, b_=b_, u=u: e.tensor_tensor(
                            out=u[:, a:b_], in0=u[:, a:b_], in1=banks[bc][:, a % 512:a % 512 + (b_ - a)],
                            op=ALU.subtract), reads=[RB[bc], R_w[un]], writes=[R_w[un]])
                    p.op("act", lambda e, c0=c0, u=u, aT=aT: e.activation(out=aT[:, c0:1024], in_=u[:, c0:1024],
                                                                         func=AF.Exp),
                         reads=[R_w[un]], writes=[R_w[aTn]])
                    p.op("pool", lambda e, c0=c0, aT=aT, mk=mk: e.tensor_tensor(
                        out=aT[:, c0:c0 + 128], in0=aT[:, c0:c0 + 128], in1=mk, op=ALU.mult),
                        reads=[R_w[aTn], R_cst], writes=[R_w[aTn]])
                    for (a, b_) in chunks:
                        q = a // 512
                        last = (pp == 0 and bi == 1)
                        p.op("pe", lambda e, q=q, a=a, b_=b_, kb=kb, last=last, aT=aT: e.matmul(
                            out=banks[6 + q][:, a % 512:a % 512 + (b_ - a)], lhsT=Vg[:, kb, hh * 128:(hh + 1) * 128],
                            rhs=aT[:, a:b_], start=False, stop=last), reads=[R_Vg, R_w[aTn]], writes=[RB[6 + q]])
                if pp > 0:
                    for (spm, spmn) in ((hA, "hA"), (hB, "hB")):
                        p.op("pool", lambda e, c0=c0, spm=spm: e.tensor_tensor(
                            out=SPS[:, c0:1024], in0=SPS[:, c0:1024], in1=spm[:, c0:1024], op=ALU.add),
                            reads=[R_w[spmn], R_w["wE"]], writes=[R_w["wE"]])
                    p.op("pool", lambda e, c0=c0: e.tensor_copy(out=SPSb[:, c0:1024], in_=SPS[:, c0:1024]),
                         reads=[R_w["wE"]], writes=[R_w["hE"]])
            for q in range(2):
                cs = slice(q * 512, (q + 1) * 512)
                p.op("dve", lambda e, q=q, cs=cs: e.tensor_tensor(out=mixT[:, 8 + h, cs], in0=banks[6 + q][:, :],
                                                                 in1=GT[:, hh, cs], op=ALU.mult),
                     reads=[RB[6 + q], R_GT[hh]], writes=[R_mix[8 + h]])

        R_ch = [{k: Res(f"ch{ci}_{k}") for k in ("wA", "wB", "wC", "wD", "wE", "hA", "hB", "hC", "hD", "hE")}
                for ci in range(2)]

        def sb_chain(ci, hh, h):
            zb, cb, ob = 4 * ci, 4 * ci + 2, 4 * ci + 3
            wo_ = 512 * ci
            Rc = R_ch[ci]

            def W(buf):
                return buf[:, wo_:wo_ + 512]
            spA, spB, uA, uB, SPS = W(wA), W(wB), W(wC), W(wD), W(wE)
            spmA, spmB, aA, aB, SPSb = W(hA), W(hB), W(hC), W(hD), W(hE)
            for base, pmax in ((512, 7), (0, 3)):
                p.op("pe", lambda e, base=base: e.matmul(out=banks[ob][:, :], lhsT=zero_b, rhs=QT[:, hh, base:base + 512],
                                                        start=True, stop=False),
                     reads=[R_QT[hh], R_cst], writes=[RB[ob]])
                p.op("pool", lambda e: e.memset(SPS, 0.0), writes=[Rc["wE"]])
                p.op("pool", lambda e: e.memset(SPSb, 0.0), writes=[Rc["hE"]])
                yield
                for pp in range(pmax, -1, -1):
                    lo = max(pp * 128, base) - base
                    n = 512 - lo
                    diag = pp * 128 >= base
                    g0 = base + lo
                    blocks = ((pp, 0, spA, "wA", spmA, "hA", MLT_b, uA, "wC", aA, "hC"),
                              (8 + pp, 1, spB, "wB", spmB, "hB", other_first(pp), uB, "wD", aB, "hD"))
                    for (kb, zi, sp, spn, spm, spmn, mk, u, un, aT, aTn) in blocks:
                        p.op("pe", lambda e, zi=zi, kb=kb, lo=lo, g0=g0, base=base: e.matmul(
                            out=banks[zb + zi][:, lo:512], lhsT=KT[:, hh, kb * 128:(kb + 1) * 128],
                            rhs=QT[:, hh, g0:base + 512], start=True, stop=True),
                            reads=[R_KT[hh], R_QT[hh]], writes=[RB[zb + zi]])
                    yield
                    for (kb, zi, sp, spn, spm, spmn, mk, u, un, aT, aTn) in blocks:
                        p.op("act", lambda e, zi=zi, lo=lo, u=u: e.activation(out=u[:, lo:512], in_=banks[zb + zi][:, lo:512],
                                                                             func=AF.Exp, scale=SCALE),
                             reads=[RB[zb + zi]], writes=[Rc[un]])
                        p.op("act", lambda e, lo=lo, sp=sp, u=u: e.activation(out=sp[:, lo:512], in_=u[:, lo:512],
                                                                             func=AF.Ln, bias=1.0),
                             reads=[Rc[un]], writes=[Rc[spn]])
                    yield
                    for (kb, zi, sp, spn, spm, spmn, mk, u, un, aT, aTn) in blocks:
                        if diag:
                            p.op("dve", lambda e, lo=lo, sp=sp, spm=spm, mk=mk: e.tensor_tensor(
                                out=spm[:, lo:lo + 128], in0=sp[:, lo:lo + 128], in1=mk, op=ALU.mult),
                                reads=[Rc[spn], R_cst], writes=[Rc[spmn]])
                            if lo + 128 < 512:
                                p.op("dve", lambda e, lo=lo, sp=sp, spm=spm: e.tensor_copy(
                                    out=spm[:, lo + 128:512], in_=sp[:, lo + 128:512]),
                                    reads=[Rc[spn]], writes=[Rc[spmn]])
                        else:
                            p.op("dve", lambda e, lo=lo, sp=sp, spm=spm: e.tensor_copy(out=spm[:, lo:512], in_=sp[:, lo:512]),
                                 reads=[Rc[spn]], writes=[Rc[spmn]])
                        p.op("dve", lambda e, zi=zi, lo=lo, sp=sp, u=u: e.scalar_tensor_tensor(
                            out=u[:, lo:512], in0=banks[zb + zi][:, lo:512], scalar=SCALE, in1=sp[:, lo:512],
                            op0=ALU.mult, op1=ALU.subtract), reads=[RB[zb + zi], Rc[spn]], writes=[Rc[un]])
                    yield
                    for bi, (kb, zi, sp, spn, spm, spmn, mk, u, un, aT, aTn) in enumerate(blocks):
                        ospm, ospmn = (spmB, "hB") if bi == 0 else (spmA, "hA")
                        Et = own_first(pp) if bi == 0 else other_first(pp)
                        o = banks[zb + zi][:, lo:512]
                        p.op("pe", lambda e, o=o, lo=lo, spm=spm: e.matmul(out=o, lhsT=SL_b, rhs=spm[:, lo:512],
                                                                          start=True, stop=False),
                             reads=[Rc[spmn], R_cst], writes=[RB[zb + zi]])
                        p.op("pe", lambda e, o=o, lo=lo: e.matmul(out=o, lhsT=ones_b, rhs=SPSb[:, lo:512],
                                                                  start=False, stop=False),
                             reads=[Rc["hE"], R_cst], writes=[RB[zb + zi]])
                        p.op("pe", lambda e, o=o, lo=lo, Et=Et, ospm=ospm: e.matmul(out=o, lhsT=Et, rhs=ospm[:, lo:512],
                                                                                    start=False, stop=True),
                             reads=[Rc[ospmn], R_cst], writes=[RB[zb + zi]])
                    yield
                    for (kb, zi, sp, spn, spm, spmn, mk, u, un, aT, aTn) in blocks:
                        p.op("dve", lambda e, zi=zi, lo=lo, u=u: e.tensor_tensor(
                            out=u[:, lo:512], in0=u[:, lo:512], in1=banks[zb + zi][:, lo:512], op=ALU.subtract),
                            reads=[RB[zb + zi], Rc[un]], writes=[Rc[un]])
                    yield
                    for (kb, zi, sp, spn, spm, spmn, mk, u, un, aT, aTn) in blocks:
                        p.op("act", lambda e, lo=lo, u=u, aT=aT: e.activation(out=aT[:, lo:512], in_=u[:, lo:512], func=AF.Exp),
                             reads=[Rc[un]], writes=[Rc[aTn]])
                        if diag:
                            p.op("pool", lambda e, lo=lo, aT=aT, mk=mk: e.tensor_tensor(
                                out=aT[:, lo:lo + 128], in0=aT[:, lo:lo + 128], in1=mk, op=ALU.mult),
                                reads=[Rc[aTn], R_cst], writes=[Rc[aTn]])
                    yield
                    for bi, (kb, zi, sp, spn, spm, spmn, mk, u, un, aT, aTn) in enumerate(blocks):
                        last = (pp == 0 and bi == 1)
                        p.op("pe", lambda e, lo=lo, kb=kb, last=last, aT=aT: e.matmul(
                            out=banks[ob][:, lo:512], lhsT=Vg[:, kb, hh * 128:(hh + 1) * 128],
                            rhs=aT[:, lo:512], start=False, stop=last), reads=[R_Vg, Rc[aTn]], writes=[RB[ob]])
                    yield
                    if pp > 0:
                        for (spm, spmn) in ((spmA, "hA"), (spmB, "hB")):
                            p.op("pool", lambda e, lo=lo, spm=spm: e.tensor_tensor(
                                out=SPS[:, lo:512], in0=SPS[:, lo:512], in1=spm[:, lo:512], op=ALU.add),
                                reads=[Rc[spmn], Rc["wE"]], writes=[Rc["wE"]])
                        p.op("pool", lambda e, lo=lo: e.tensor_copy(out=SPSb[:, lo:512], in_=SPS[:, lo:512]),
                             reads=[Rc["wE"]], writes=[Rc["hE"]])
                        yield
                p.op("dve", lambda e, base=base: e.tensor_tensor(out=mixT[:, 8 + h, base:base + 512], in0=banks[ob][:, :],
                                                                 in1=GT[:, hh, base:base + 512], op=ALU.mult),
                     reads=[RB[ob], R_GT[hh]], writes=[R_mix[8 + h]])
                yield

        def fox_chain(ci, hh, h):
            sbk, ob, db = 4 * ci, 4 * ci + 2, 4 * ci + 3
            wo_ = 512 * ci
            Rc = R_ch[ci]

            def W(buf):
                return buf[:, wo_:wo_ + 512]
            tmps = (W(wA), W(wB))
            Rtmps = (Rc["wA"], Rc["wB"])
            PTs = (W(hA), W(hB))
            RPTs = (Rc["hA"], Rc["hB"])
            fin_ = W(wC)
            for base, pmax in ((512, 7), (0, 3)):
                for bk_ in (ob, db):
                    p.op("pe", lambda e, bk_=bk_, base=base: e.matmul(out=banks[bk_][:, :], lhsT=zero_b,
                                                                    rhs=QT[:, hh, base:base + 512], start=True, stop=False),
                         reads=[R_QT[hh], R_cst], writes=[RB[bk_]])
                its = [(pp, io) for pp in range(pmax + 1) for io in (False, True)]
                nit = len(its)

                def geom(it, base=base, its=its):
                    pp, is_other = its[it]
                    kb = 8 + pp if is_other else pp
                    lo = max(pp * 128, base) - base
                    return pp, is_other, kb, lo, pp * 128 >= base

                def qk(it, base=base, geom=geom):
                    pp, is_other, kb, lo, diag = geom(it)
                    bk = sbk + it % 2
                    p.op("pe", lambda e, bk=bk, kb=kb, lo=lo, base=base: e.matmul(
                        out=banks[bk][:, lo:512], lhsT=KT[:, hh, kb * 128:(kb + 1) * 128],
                        rhs=QT[:, hh, base + lo:base + 512], start=True, stop=True),
                        reads=[R_KT[hh], R_QT[hh]], writes=[RB[bk]])

                qk(0)
                yield
                for it in range(nit):
                    pp, is_other, kb, lo, diag = geom(it)
                    bk = sbk + it % 2
                    tmp, Rtmp, PT, RPT = tmps[it % 2], Rtmps[it % 2], PTs[it % 2], RPTs[it % 2]
                    if it + 1 < nit:
                        qk(it + 1)
                    ns = (512 - lo) // 128
                    s0 = (base + lo) // 128
                    p.op("dve", lambda e, bk=bk, lo=lo, ns=ns, s0=s0, tmp=tmp: e.scalar_tensor_tensor(
                        out=tmp[:, lo:512].rearrange("p (s t) -> p s t", s=ns),
                        in0=banks[bk][:, lo:512].rearrange("p (s t) -> p s t", s=ns), scalar=SCALE,
                        in1=PB[:, h, s0:s0 + ns].unsqueeze(2).to_broadcast([128, ns, 128]),
                        op0=ALU.mult, op1=ALU.add), reads=[RB[bk], R_F2], writes=[Rtmp])
                    yield
                    p.op("act", lambda e, lo=lo, kb=kb, tmp=tmp, PT=PT: e.activation(
                        out=PT[:, lo:512], in_=tmp[:, lo:512], func=AF.Exp, bias=Fneg[:, kb, h:h + 1], scale=1.0),
                        reads=[Rtmp, R_F2], writes=[RPT])
                    if diag:
                        mk = other_first(pp) if is_other else MLE_b
                        p.op("pool", lambda e, lo=lo, mk=mk, PT=PT: e.tensor_tensor(
                            out=PT[:, lo:lo + 128], in0=PT[:, lo:lo + 128], in1=mk, op=ALU.mult),
                            reads=[RPT, R_cst], writes=[RPT])
                    yield
                    last = (it == nit - 1)
                    p.op("pe", lambda e, lo=lo, kb=kb, last=last, PT=PT: e.matmul(
                        out=banks[ob][:, lo:512], lhsT=Vg[:, kb, hh * 128:(hh + 1) * 128], rhs=PT[:, lo:512],
                        start=False, stop=last), reads=[R_Vg, RPT], writes=[RB[ob]])
                    p.op("pe", lambda e, lo=lo, last=last, PT=PT: e.matmul(
                        out=banks[db][:, lo:512], lhsT=ones_b, rhs=PT[:, lo:512], start=False, stop=last),
                        reads=[RPT, R_cst], writes=[RB[db]])
                    yield
                p.op("dve", lambda e: e.reciprocal(out=fin_, in_=banks[db][:, :]), reads=[RB[db]], writes=[Rc["wC"]])
                p.op("dve", lambda e: e.tensor_tensor(out=fin_, in0=banks[ob][:, :], in1=fin_, op=ALU.mult),
                     reads=[RB[ob], Rc["wC"]], writes=[Rc["wC"]])
                p.op("pool", lambda e, base=base: e.tensor_tensor(out=mixT[:, h, base:base + 512], in0=fin_,
                                                                  in1=GT[:, hh, base:base + 512], op=ALU.mult),
                     reads=[Rc["wC"], R_GT[hh]], writes=[R_mix[h]])
                yield

        def run_interleaved(gens):
            gens = list(gens)
            while gens:
                for g_ in list(gens):
                    try:
                        next(g_)
                    except StopIteration:
                        gens.remove(g_)

        if stage >= 2:
            groups = [(m, h0) for m in range(2) for h0 in range(0, 8, 2)][:NGROUPS]
            issue_kvq(0, 0, "q")
            for gi, (m, h0) in enumerate(groups):
                project_group(m, h0)
                if gi + 1 < len(groups):
                    issue_kvq(*groups[gi + 1])
                if stage >= 3:
                    if m == 0:
                        if "oldfox" in DBG:
                            for hh in range(2):
                                fox_head(hh, h0 + hh)
                        else:
                            run_interleaved([fox_chain(0, 0, h0), fox_chain(1, 1, h0 + 1)])
                    elif "oldsb" in DBG:
                        pass
                    if m == 1 and h0 == 0 and ("oldsb" in DBG) != ("oldfox" in DBG):
                        p.barrier(scratch)
                    if m == 0:
                        pass
                    elif "oldsb" in DBG:
                        for hh in range(2):
                            sb_head(hh, h0 + hh)
                    else:
                        run_interleaved([sb_chain(0, 0, h0), sb_chain(1, 1, h0 + 1)])

        p.barrier(scratch)
        Alloc.top = mark0

        def sample_attention():
            offA = [hT_off]

            def allocA(shape, dt):
                esz = 4 if dt == F32 else 2
                n = int(np.prod(shape[1:]))
                off = offA[0]
                offA[0] += (n * esz + 63) // 64 * 64
                assert offA[0] <= hT_off + 16 * NTOK * 2
                return view_at(off, shape, dt)
            Kc = [allocA([128, 8, 1024], BF16) for _ in range(2)]
            Vc = [allocA([128, 8, 1024], BF16) for _ in range(2)]
            R_Kc = [[Res("Kc0a"), Res("Kc0b")], [Res("Kc1a"), Res("Kc1b")]]
            R_Vc = [[Res("Vc0a"), Res("Vc0b")], [Res("Vc1a"), Res("Vc1b")]]
            KcT = alloc([128, 8, 1024], BF16, "KcT")
            R_KcT = [Res(f"KcT{h}") for h in range(8)]
            clfT = alloc([128, 8, 32], F32, "clfT")
            csuf = alloc([128, 8, 32], F32, "csuf")
            Gsuf = alloc([128, 8, 32], F32, "Gsuf")
            Gnew = alloc([128, 8], F32, "Gnew")
            R_G = Res("G")
            sw1 = alloc([128, 1024], F32, "sw1")
            sw2 = alloc([128, 1024], F32, "sw2")
            sP = [alloc([128, 1024], BF16, f"sP{i}") for i in range(2)]
            R_sP = [Res("sP0"), Res("sP1")]
            R_sw1, R_sw2 = Res("sw1"), Res("sw2")
            Ssuf = alloc([128, 8, 128], F32, "Ssuf")
            Ssufb = alloc([128, 8, 128], BF16, "Ssufb")
            R_Ssuf = Res("Ssuf")
            nw1 = alloc([128, 512], F32, "nw1")
            nw2 = alloc([128, 512], F32, "nw2")
            Pn = alloc([128, 8, 64], BF16, "Pn")
            spmn = alloc([128, 8, 64], BF16, "spmn")
            R_n = {k: Res(k) for k in ("nw1", "nw2", "Pn", "spmn")}
            fin = alloc([128, 512], F32, "fin")
            R_fin = Res("fin")
            BDLE = cst_b[0:64, I_BDLE, 0:64]
            BDLT = cst_b[0:64, I_BDLT, 0:64]
            for b in range(4):
                p.dma("sp", clfT[:, :, b * 8:(b + 1) * 8], clf[b].rearrange("(t p) h -> p t h", p=128), writes=[R_G])
            p.op("dve", lambda e: e.memset(csuf[:, 7, :], 0.0), writes=[R_G])
            for t in range(6, -1, -1):
                p.op("dve", lambda e, t=t: e.tensor_tensor(out=csuf[:, t, :], in0=csuf[:, t + 1, :], in1=clfT[:, t + 1, :],
                                                          op=ALU.add), reads=[R_G], writes=[R_G])
            bG = 7
            for t in range(8):
                o = banks[bG][:, t * 32:(t + 1) * 32]
                p.op("pe", lambda e, o=o, t=t: e.matmul(out=o, lhsT=SL_f, rhs=clfT[:, t, :], start=True, stop=False),
                     reads=[R_G, R_cst], writes=[RB[bG]])
                p.op("pe", lambda e, o=o, t=t: e.matmul(out=o, lhsT=ones_f, rhs=csuf[:, t, :], start=False, stop=True),
                     reads=[R_G, R_cst], writes=[RB[bG]])
            p.op("pe", lambda e: e.matmul(out=banks[bG][0:64, 256:264], lhsT=cst_f[0:64, I_BDTRI, 0:64],
                                          rhs=logf[0:64, 16, :], start=True, stop=True),
                 reads=[R_F, R_cst], writes=[RB[bG]])
            R_G2 = Res("G2")
            p.op("dve", lambda e: e.tensor_copy(out=Gsuf, in_=banks[bG][:, 0:256].rearrange("p (t n) -> p t n", t=8)),
                 reads=[RB[bG]], writes=[R_G2])
            p.op("dve", lambda e: e.tensor_scalar(out=Gnew[0:64, :], in0=banks[bG][0:64, 256:264], scalar1=-1.0,
                                                  scalar2=None, op0=ALU.mult), reads=[RB[bG]], writes=[R_G2])

            if os.environ.get("DBGOUT"):
                dbg4 = nc.dram_tensor("dbg4", [128, 3072], F32, kind="ExternalOutput").ap()
                d4 = alloc([128, 3072], F32, "d4")
                R_d4 = Res("d4")
                p.op("dve", lambda e: e.tensor_copy(out=d4[:, 0:1024], in_=QTs.rearrange("p h q -> p (h q)")), reads=[R_samp], writes=[R_d4])
                p.op("dve", lambda e: e.tensor_copy(out=d4[:, 1024:2048], in_=KTs.rearrange("p h q -> p (h q)")), reads=[R_samp], writes=[R_d4])
                p.op("dve", lambda e: e.tensor_copy(out=d4[:, 2048:3072], in_=GTs.rearrange("p h q -> p (h q)")), reads=[R_samp], writes=[R_d4])
                p.dma("sp", dbg4, d4, reads=[R_d4])
                dbg2 = nc.dram_tensor("dbg2", [128, 264], F32, kind="ExternalOutput").ap()
                p.dma("sp", dbg2[:, 0:256], Gsuf.rearrange("p t n -> p (t n)"), reads=[R_G2])
                p.dma("sp", dbg2[0:64, 256:264], Gnew[0:64, :], reads=[R_G2])
            def issue_cache(j):
                m_, b_ = j // 4, j % 4
                buf_ = j % 2
                for half in range(2):
                    rows = slice(b_ * 1024 + half * 512, b_ * 1024 + (half + 1) * 512)
                    p.dma("pool", Kc[buf_][:, half * 4:(half + 1) * 4, :],
                          ck[m_][rows, :].rearrange("(t p) n -> p t n", p=128), writes=[R_Kc[buf_][half]])
                    p.dma("pool", Vc[buf_][:, half * 4:(half + 1) * 4, :],
                          cv[m_][rows, :].rearrange("(t p) n -> p t n", p=128), writes=[R_Vc[buf_][half]])

            def tr(j):
                buf_ = j % 2
                for h in range(8):
                    bk = h % 2
                    pv = bank_bf(bk)
                    for t in range(8):
                        p.op("pe", lambda e, pv=pv, t=t, h=h, buf_=buf_: e.transpose(
                            out=pv[:, t * 128:(t + 1) * 128], in_=Kc[buf_][:, t, h * 128:(h + 1) * 128],
                            identity=ident_b), reads=[R_Kc[buf_][t // 4], R_cst], writes=[RB[bk]])
                    evac(KcT[:, h, :], pv, [RB[bk]], [R_KcT[h]], eng=("dve" if h % 2 == 0 else "act"))

            issue_cache(0)
            tr(0)
            for m in range(2):
                bO, bD, bN, bC = 4, 5, 6, 7
                hb = m * 8
                for h in range(8):
                    p.op("pe", lambda e, h=h, hb=hb: e.matmul(out=banks[bN][0:64, h * 64:(h + 1) * 64], lhsT=KTs[:, hb + h, :],
                                                        rhs=QTs[:, hb + h, :], start=True, stop=True),
                         reads=[R_samp], writes=[RB[bN]])
                if m == 0:
                    p.op("dve", lambda e: e.scalar_tensor_tensor(
                        out=nw1[0:64, :].rearrange("p (h q) -> p h q", h=8),
                        in0=banks[bN][0:64, :].rearrange("p (h q) -> p h q", h=8), scalar=SCALE,
                        in1=Gnew[0:64, :].unsqueeze(2).to_broadcast([64, 8, 64]), op0=ALU.mult, op1=ALU.add),
                        reads=[RB[bN], R_G2], writes=[R_n["nw1"]])
                    p.op("act", lambda e: e.activation(out=Pn[0:64, :, :], in_=nw1[0:64, :].rearrange("p (h q) -> p h q", h=8),
                                                       func=AF.Exp), reads=[R_n["nw1"]], writes=[R_n["Pn"]])
                    p.op("pool", lambda e: e.tensor_tensor(out=Pn[0:64, :, :], in0=Pn[0:64, :, :],
                                                           in1=BDLE.unsqueeze(1).to_broadcast([64, 8, 64]), op=ALU.mult),
                         reads=[R_n["Pn"], R_cst], writes=[R_n["Pn"]])
                    p.op("pe", lambda e: e.matmul(out=banks[bN][:, :], lhsT=zero_b, rhs=cst_b[:, 0:4, :], start=True,
                                                  stop=False), reads=[R_cst], writes=[RB[bN]])
                    p.op("pe", lambda e: e.matmul(out=banks[bD][:, :], lhsT=ones_b[0:64, :],
                                                  rhs=Pn[0:64, :, :], start=True, stop=True),
                         reads=[R_n["Pn"], R_cst], writes=[RB[bD]])
                else:
                    p.op("act", lambda e: e.activation(out=nw1[0:64, :], in_=banks[bN][0:64, :], func=AF.Exp, scale=SCALE),
                         reads=[RB[bN]], writes=[R_n["nw1"]])
                    p.op("act", lambda e: e.activation(out=nw1[0:64, :], in_=nw1[0:64, :], func=AF.Ln, bias=1.0),
                         reads=[R_n["nw1"]], writes=[R_n["nw1"]])
                    p.op("pool", lambda e: e.tensor_tensor(out=spmn[0:64, :, :],
                                                           in0=nw1[0:64, :].rearrange("p (h q) -> p h q", h=8),
                                                           in1=BDLT.unsqueeze(1).to_broadcast([64, 8, 64]), op=ALU.mult),
                         reads=[R_n["nw1"], R_cst], writes=[R_n["spmn"]])
                    p.op("pe", lambda e: e.matmul(out=banks[bD][0:64, :], lhsT=SL_b[0:64, 0:64], rhs=spmn[0:64, :, :],
                                                  start=True, stop=True), reads=[R_n["spmn"], R_cst], writes=[RB[bD]])
                    p.op("dve", lambda e: e.scalar_tensor_tensor(out=nw2[0:64, :], in0=banks[bN][0:64, :], scalar=SCALE,
                                                                 in1=nw1[0:64, :], op0=ALU.mult, op1=ALU.subtract),
                         reads=[RB[bN], R_n["nw1"]], writes=[R_n["nw2"]])
                    p.op("dve", lambda e: e.tensor_tensor(out=nw2[0:64, :], in0=nw2[0:64, :], in1=banks[bD][0:64, :],
                                                          op=ALU.subtract), reads=[RB[bD], R_n["nw2"]], writes=[R_n["nw2"]])
                    p.op("act", lambda e: e.activation(out=Pn[0:64, :, :], in_=nw2[0:64, :].rearrange("p (h q) -> p h q", h=8),
                                                       func=AF.Exp), reads=[R_n["nw2"]], writes=[R_n["Pn"]])
                    p.op("pool", lambda e: e.tensor_tensor(out=Pn[0:64, :, :], in0=Pn[0:64, :, :],
                                                           in1=BDLT.unsqueeze(1).to_broadcast([64, 8, 64]), op=ALU.mult),
                         reads=[R_n["Pn"], R_cst], writes=[R_n["Pn"]])
                p.op("pe", lambda e: e.matmul(out=banks[bO][:, :], lhsT=zero_b, rhs=cst_b[:, 0:4, :], start=True, stop=False),
                     reads=[R_cst], writes=[RB[bO]])
                for h in range(8):
                    p.op("pe", lambda e, h=h, hb=hb: e.matmul(out=banks[bO][:, h * 64:(h + 1) * 64], lhsT=Vs[0:64, hb + h, :],
                                                        rhs=Pn[0:64, h, :], start=False, stop=False),
                         reads=[R_samp, R_n["Pn"]], writes=[RB[bO]])
                for b in range(4):
                    buf = (m * 4 + b) % 2
                    if m * 4 + b + 1 < 8:
                        issue_cache(m * 4 + b + 1)
                    if "trpipe" not in DBG and m * 4 + b > 0:
                        tr(m * 4 + b)
                    for t in range(8):
                        bk = 2 + t // 4
                        for h in range(8):
                            c0 = (t % 4) * 128 + h * 16
                            p.op("pe", lambda e, bk=bk, c0=c0, t=t, h=h, b=b, hb=hb: e.matmul(
                                out=banks[bk][:, c0:c0 + 16], lhsT=KcT[:, h, t * 128:(t + 1) * 128],
                                rhs=QTs[:, hb + h, b * 16:(b + 1) * 16], start=True, stop=True),
                                reads=[R_KcT[h], R_samp], writes=[RB[bk]])
                    if m * 4 + b + 1 < 8 and "trpipe" in DBG:
                        tr(m * 4 + b + 1)
                    Pb = sP[b % 2]
                    RPb = R_sP[b % 2]
                    Pb4 = Pb.rearrange("p (t h q) -> p t h q", t=8, h=8)
                    if m == 0:
                        for t in range(8):
                            bk = 2 + t // 4
                            cs = slice((t % 4) * 128, (t % 4 + 1) * 128)
                            p.op("dve", lambda e, t=t, b=b, bk=bk, cs=cs: e.scalar_tensor_tensor(
                                out=sw1[:, t * 128:(t + 1) * 128].rearrange("p (h q) -> p h q", h=8),
                                in0=banks[bk][:, cs].rearrange("p (h q) -> p h q", h=8), scalar=SCALE,
                                in1=Gsuf[:, t, b * 8:(b + 1) * 8].unsqueeze(2).to_broadcast([128, 8, 16]),
                                op0=ALU.mult, op1=ALU.add), reads=[RB[bk], R_G2], writes=[R_sw1])
                        p.op("act", lambda e, Pb=Pb: e.activation(out=Pb, in_=sw1, func=AF.Exp), reads=[R_sw1], writes=[RPb])
                        for t in range(8):
                            p.op("pe", lambda e, t=t, b=b, Pb=Pb: e.matmul(
                                out=banks[bN][:, b * 128:(b + 1) * 128],
                                lhsT=ones_b, rhs=Pb[:, t * 128:(t + 1) * 128], start=False,
                                stop=(b == 3 and t == 7)),
                                reads=[RPb, R_cst], writes=[RB[bN]])
                    else:
                        for hf in range(2):
                            p.op("act", lambda e, hf=hf: e.activation(out=sw1[:, hf * 512:(hf + 1) * 512], in_=banks[2 + hf][:, :],
                                                                      func=AF.Exp, scale=SCALE), reads=[RB[2 + hf]], writes=[R_sw1])
                        p.op("act", lambda e: e.activation(out=sw1, in_=sw1, func=AF.Ln, bias=1.0), reads=[R_sw1], writes=[R_sw1])
                        spb = sP[(b + 1) % 2]
                        Rspb = R_sP[(b + 1) % 2]
                        p.op("pool", lambda e, spb=spb: e.tensor_copy(out=spb, in_=sw1), reads=[R_sw1], writes=[Rspb])
                        spb3 = spb.rearrange("p (t n) -> p t n", t=8)
                        p.op("pool", lambda e: e.memset(Ssuf[:, 7, :], 0.0), writes=[R_Ssuf])
                        for t in range(6, -1, -1):
                            p.op("pool", lambda e, t=t, spb3=spb3: e.tensor_tensor(
                                out=Ssuf[:, t, :], in0=Ssuf[:, t + 1, :], in1=spb3[:, t + 1, :], op=ALU.add),
                                reads=[Rspb, R_Ssuf], writes=[R_Ssuf])
                        p.op("pool", lambda e: e.tensor_copy(out=Ssufb, in_=Ssuf), reads=[R_Ssuf], writes=[R_Ssuf])
                        for t in range(8):
                            bk = 5 + t // 4
                            o = banks[bk][:, (t % 4) * 128:(t % 4 + 1) * 128]
                            p.op("pe", lambda e, o=o, t=t, spb3=spb3: e.matmul(out=o, lhsT=SL_b, rhs=spb3[:, t, :],
                                                                               start=True, stop=False),
                                 reads=[Rspb, R_cst], writes=[RB[bk]])
                            p.op("pe", lambda e, o=o, t=t: e.matmul(out=o, lhsT=ones_b, rhs=Ssufb[:, t, :],
                                                                    start=False, stop=False),
                                 reads=[R_Ssuf, R_cst], writes=[RB[bk]])
                            p.op("pe", lambda e, o=o, b=b: e.matmul(out=o.rearrange("p (h q) -> p h q", h=8),
                                                                    lhsT=ones_b[0:64, :],
                                                                    rhs=spmn[0:64, :, b * 16:(b + 1) * 16],
                                                                    start=False, stop=True),
                                 reads=[R_n["spmn"], R_cst], writes=[RB[bk]])
                        for hf in range(2):
                            cs = slice(hf * 512, (hf + 1) * 512)
                            p.op("dve", lambda e, hf=hf, cs=cs: e.scalar_tensor_tensor(
                                out=sw2[:, cs], in0=banks[2 + hf][:, :], scalar=SCALE, in1=sw1[:, cs],
                                op0=ALU.mult, op1=ALU.subtract), reads=[RB[2 + hf], R_sw1], writes=[R_sw2])
                            p.op("dve", lambda e, hf=hf, cs=cs: e.tensor_tensor(
                                out=sw2[:, cs], in0=sw2[:, cs], in1=banks[5 + hf][:, :], op=ALU.subtract),
                                reads=[RB[5 + hf], R_sw2], writes=[R_sw2])
                        p.op("act", lambda e, Pb=Pb: e.activation(out=Pb, in_=sw2, func=AF.Exp), reads=[R_sw2], writes=[RPb])
                    for t in range(8):
                        for h in range(8):
                            last = (b == 3 and t == 7 and h == 7)
                            p.op("pe", lambda e, t=t, h=h, b=b, buf=buf, last=last, Pb4=Pb4: e.matmul(
                                out=banks[bO][:, h * 64 + b * 16:h * 64 + (b + 1) * 16],
                                lhsT=Vc[buf][:, t, h * 128:(h + 1) * 128], rhs=Pb4[:, t, h, :], start=False, stop=last),
                                reads=[R_Vc[buf][t // 4], RPb], writes=[RB[bO]])
                if m == 0:
                    p.op("dve", lambda e: e.tensor_copy(out=fin, in_=banks[bD][:, :]), reads=[RB[bD]], writes=[R_fin])
                    for b in range(4):
                        fv = fin.rearrange("p (h q) -> p h q", h=8)[:, :, b * 16:(b + 1) * 16]
                        p.op("dve", lambda e, b=b, fv=fv: e.tensor_tensor(
                            out=fv, in0=fv, in1=banks[bN][:, b * 128:(b + 1) * 128].rearrange("p (h q) -> p h q", h=8),
                            op=ALU.add), reads=[RB[bN], R_fin], writes=[R_fin])
                    p.op("dve", lambda e: e.reciprocal(out=fin, in_=fin), reads=[R_fin], writes=[R_fin])
                    p.op("dve", lambda e: e.tensor_tensor(out=fin, in0=banks[bO][:, :], in1=fin, op=ALU.mult),
                         reads=[RB[bO], R_fin], writes=[R_fin])
                else:
                    p.op("dve", lambda e: e.tensor_copy(out=fin, in_=banks[bO][:, :]), reads=[RB[bO]], writes=[R_fin])
                if os.environ.get("DBGOUT") and m == 0:
                    dbg3 = nc.dram_tensor("dbg3", [128, 2048], F32, kind="ExternalOutput").ap()
                    p.dma("sp", dbg3[:, 0:512], fin, reads=[R_fin])
                    p.dma("sp", dbg3[:, 512:1536], sw1, reads=[R_sw1])
                    p.dma("sp", dbg3[0:64, 1536:2048], nw1[0:64, :], reads=[R_n["nw1"]])
                p.op("pool", lambda e, hb=hb: e.tensor_tensor(out=mixT[:, hb:hb + 8, 1024:1088],
                                                              in0=fin.rearrange("p (h q) -> p h q", h=8),
                                                              in1=GTs[:, hb:hb + 8, :], op=ALU.mult),
                     reads=[R_fin, R_samp], writes=[R_mix_s[hb + h_] for h_ in range(8)])

        if stage >= 4:
            sample_attention()
            p.barrier(scratch)
            Alloc.top = mark0

        h1T = alloc([128, 16, NOWN], BF16, "h1T")
        R_h1T = [Res(f"h1T{i}") for i in range(9)]
        mark1 = Alloc.top
        AX = mybir.AxisListType

        pre_v = []

        def out_proj0():
            if stage >= 6 and "nopre" not in DBG:
                for sv_ in range(len(slab_raw)):
                    pre_v.append(load_slab(w_in1, 16, 4096 + sv_ * 256, 256))
            wo = view_at(hT_off, [128, 16, D], BF16)
            R_wo = [Res(f"wo{j}") for j in range(8)]
            srcw = w_out0.rearrange("(c p) n -> p c n", p=128)
            for j in range(8):
                p.dma("pool", wo[:, 2 * j:2 * j + 2, :], srcw[:, 2 * j:2 * j + 2, :], writes=[R_wo[j]])
            hpb = [alloc([128, D], F32, f"hpb{i}") for i in range(2)]
            R_hpb = [Res("hpb0"), Res("hpb1")]
            _x5 = alloc([128, D], BF16, "xn5")
            _r5 = Res("xn5")
            xn5 = [_x5, _x5]
            R_xn5 = [_r5, _r5]
            junk5 = alloc([128, D], BF16, "junk5")
            ssq5 = alloc([128, 32], F32, "ssq5")
            bufs = (xn5, R_xn5, junk5, Res("junk5"), ssq5, Res("ssq5"))
            for i in range(9):
                rows = 128 if i < 8 else 64
                r0 = i * 128 if i < 8 else 2048
                b = i % 2
                p.dma("sp", hpb[b][0:rows, :], xall[r0:r0 + rows, :], writes=[R_hpb[b]])
                Rm = R_mix if i < 8 else R_mix_s
                for q in range(4):
                    bk = 4 + q
                    for hd in range(16):
                        p.op("pe", lambda e, bk=bk, hd=hd, i=i, rows=rows, q=q: e.matmul(
                            out=banks[bk][0:rows, :], lhsT=mixT[:, hd, i * 128:i * 128 + rows],
                            rhs=wo[:, hd, q * 512:(q + 1) * 512], start=(hd == 0), stop=(hd == 15)),
                            reads=[Rm[hd], R_wo[hd // 2]], writes=[RB[bk]])
                    p.op("dve", lambda e, bk=bk, b=b, rows=rows, q=q: e.tensor_tensor(
                        out=hpb[b][0:rows, q * 512:(q + 1) * 512], in0=banks[bk][0:rows, :],
                        in1=hpb[b][0:rows, q * 512:(q + 1) * 512], op=ALU.add),
                        reads=[RB[bk], R_hpb[b]], writes=[R_hpb[b]])
                p.dma("sp", hp_scr[i * 128:i * 128 + rows, :], hpb[b][0:rows, :], reads=[R_hpb[b]])
                if os.environ.get("DBGOUT"):
                    if i == 0:
                        _NC_CACHE["dbg_hp"] = nc.dram_tensor("dbg_hp", [NOWN, D], F32, kind="ExternalOutput").ap()
                    p.dma("sp", _NC_CACHE["dbg_hp"][i * 128:i * 128 + rows, :], hpb[b][0:rows, :], reads=[R_hpb[b]])
                norm_transpose(i, hpb[b], R_hpb[b], rows, 16, h1T, R_h1T[i], i * 128, (i % 2) * 2, bufs=bufs)

        CH = 9 * 128 * 2

        def gT_view(c):
            return view_at(hT_off + c * CH, [128, NOWN], BF16)

        R_vb = [Res(f"vb{c}") for c in range(32)]

        def layer1():
            vb = view_at(hT_off, [128, 32, 9, 128], BF16)
            offB = [hT_off + 32 * CH]

            def allocB(shape, dt):
                esz = 4 if dt == F32 else 2
                n = int(np.prod(shape[1:]))
                off = offB[0]
                offB[0] += (n * esz + 63) // 64 * 64
                assert offB[0] <= mark0, (offB[0], mark0)
                return view_at(off, shape, dt)
            vsf = allocB([128, 4096], F32)
            R_vsf = Res("vsf")
            WspT = allocB([128, 16, 128], BF16)
            RSW = allocB([128, 16, 128], F32)
            bspB = allocB([128, 16, 128], F32)
            R_c1 = Res("l1consts")
            wtmp = vsf[:, 0:2048].rearrange("p (g s) -> p g s", g=16)
            wtb = vsf[:, 2048:3072].bitcast(BF16).rearrange("p (g s) -> p g s", g=16)
            mixed = alloc([128, NOWN], F32, "mixed")
            tprod = alloc([128, NOWN], F32, "tprod")
            szb = alloc([128, NOWN], BF16, "szb")
            bias2 = alloc([128, 128], F32, "bias2")
            BDs = alloc([128, 16, 64], BF16, "BDs")
            st1 = alloc([128, 9, 16], F32, "st1")
            st2 = alloc([128, 9, 16], F32, "st2")
            s1 = alloc([128, 9], F32, "s1")
            s2 = alloc([128, 9], F32, "s2")
            rstd1 = alloc([128, 9], F32, "rstd1")
            nmr1 = alloc([128, 9], F32, "nmr1")
            junk6 = alloc([128, 256], BF16, "junk6")
            R_w6 = {k: Res(k) for k in ("mixed", "tprod", "szb", "bias2", "BDs", "st", "junk6")}
            gamT = lngbT[:, 0:32]
            betT = lngbT[:, 32:64]
            p.dma("sp", wtmp, w_sp.rearrange("g t s -> t g s"), writes=[R_vsf])
            p.dma("sp", bspB.rearrange("p g t -> p (g t)"), b_sp.rearrange("g t -> (g t)").partition_broadcast(128),
                  writes=[R_c1])
            p.op("dve", lambda e: e.tensor_copy(out=wtb, in_=wtmp), reads=[R_vsf], writes=[R_vsf])
            for hf in range(2):
                pv = bank_bf(hf)
                for gg in range(8):
                    g = hf * 8 + gg
                    p.op("pe", lambda e, pv=pv, gg=gg, g=g: e.transpose(out=pv[:, gg * 128:(gg + 1) * 128], in_=wtb[:, g, :],
                                                                          identity=ident_b),
                         reads=[R_vsf, R_cst], writes=[RB[hf]])
                p.op("dve", lambda e, pv=pv, hf=hf: e.tensor_tensor(
                    out=WspT[:, hf * 8:(hf + 1) * 8, :], in0=pv.rearrange("p (g t) -> p g t", g=8),
                    in1=MLE_b.unsqueeze(1).to_broadcast([128, 8, 128]), op=ALU.mult),
                    reads=[RB[hf], R_cst], writes=[R_c1])
            for g4 in range(4):
                bk = 2 + g4 % 2
                p.op("pe", lambda e, bk=bk, g4=g4: e.matmul(out=banks[bk][:, :], lhsT=ones_b,
                                                            rhs=WspT[:, g4 * 4:(g4 + 1) * 4, :], start=True, stop=True),
                     reads=[R_c1, R_cst], writes=[RB[bk]])
                p.op("dve", lambda e, bk=bk, g4=g4: e.tensor_copy(
                    out=RSW[:, g4 * 4:(g4 + 1) * 4, :], in_=banks[bk][:, :].rearrange("p (g t) -> p g t", g=4)),
                    reads=[RB[bk]], writes=[R_c1])
            W16t = alloc([128, 16, 64], BF16, "W16t")
            R_w16 = Res("W16t")
            for bb in range(4):
                p.op("dve", lambda e, bb=bb: e.tensor_copy(out=W16t[0:16, :, 16 * bb:16 * bb + 16], in_=WspT[0:16, :, 0:16]),
                     reads=[R_c1], writes=[R_w16])
            for hf in range(2):
                p.op("pe", lambda e, hf=hf: e.matmul(out=banks[2 + hf][0:64, :], lhsT=cst_b[0:16, I_SEL, 0:64],
                                                     rhs=W16t[0:16, hf * 8:(hf + 1) * 8, :], start=True, stop=True),
                     reads=[R_w16, R_cst], writes=[RB[2 + hf]])
                p.op("dve", lambda e, hf=hf: e.tensor_tensor(
                    out=BDs[0:64, hf * 8:(hf + 1) * 8, :], in0=banks[2 + hf][0:64, :].rearrange("p (g t) -> p g t", g=8),
                    in1=cst_b[0:64, I_SAME, 0:64].unsqueeze(1).to_broadcast([64, 8, 64]), op=ALU.mult),
                    reads=[RB[2 + hf], R_cst], writes=[R_w6["BDs"]])
            p.op("dve", lambda e: e.memset(st1, 0.0), writes=[R_w6["st"]])
            p.op("dve", lambda e: e.memset(st2, 0.0), writes=[R_w6["st"]])
            if stage >= 6:
                for sv in range(16):
                    wvs, Rwvs = pre_v[sv] if sv < len(pre_v) else load_slab(w_in1, 16, 4096 + sv * 256, 256)
                    for i in range(9):
                        rows = 128 if i < 8 else 64
                        bk = next_bank(2, 8)
                        for k in range(16):
                            p.op("pe", lambda e, bk=bk, k=k, i=i, rows=rows, wvs=wvs: e.matmul(
                                out=banks[bk][0:rows, 0:256], lhsT=h1T[:, k, i * 128:i * 128 + rows], rhs=wvs[:, k, :],
                                start=(k == 0), stop=(k == 15)), reads=[R_h1T[i], Rwvs[k]], writes=[RB[bk]])
                        p.op("act", lambda e, bk=bk, i=i, rows=rows, sv=sv: e.activation(
                            out=vb[0:rows, 2 * sv:2 * sv + 2, i, :],
                            in_=banks[bk][0:rows, 0:256].rearrange("p (a b) -> p a b", a=2), func=AF.Copy,
                            accum_out=st1[0:rows, i, sv:sv + 1]),
                            reads=[RB[bk]], writes=[R_vb[2 * sv], R_vb[2 * sv + 1], R_w6["st"]])
                        p.op("act", lambda e, bk=bk, i=i, rows=rows, sv=sv: e.activation(
                            out=junk6[0:rows, :], in_=banks[bk][0:rows, 0:256], func=AF.Square,
                            accum_out=st2[0:rows, i, sv:sv + 1]),
                            reads=[RB[bk]], writes=[R_w6["junk6"], R_w6["st"]])
                        if i == 8:
                            p.op("dve", lambda e, bk=bk, sv=sv: e.tensor_copy(out=vsf[0:64, sv * 256:(sv + 1) * 256],
                                                                              in_=banks[bk][0:64, 0:256]),
                                 reads=[RB[bk]], writes=[R_vsf])
                p.op("dve", lambda e: e.reduce_sum(out=s1, in_=st1, axis=AX.X), reads=[R_w6["st"]], writes=[R_w6["st"]])
                p.op("dve", lambda e: e.reduce_sum(out=s2, in_=st2, axis=AX.X), reads=[R_w6["st"]], writes=[R_w6["st"]])
                p.op("dve", lambda e: e.tensor_scalar(out=s1, in0=s1, scalar1=1.0 / 4096, scalar2=None, op0=ALU.mult),
                     reads=[R_w6["st"]], writes=[R_w6["st"]])
                p.op("dve", lambda e: e.tensor_tensor(out=nmr1, in0=s1, in1=s1, op=ALU.mult),
                     reads=[R_w6["st"]], writes=[R_w6["st"]])
                p.op("dve", lambda e: e.scalar_tensor_tensor(out=s2, in0=s2, scalar=1.0 / 4096, in1=nmr1, op0=ALU.mult,
                                                             op1=ALU.subtract),
                     reads=[R_w6["st"]], writes=[R_w6["st"]])
                p.op("act", lambda e: e.activation(out=rstd1, in_=s2, func=AF.Ln, bias=1e-5), reads=[R_w6["st"]],
                     writes=[R_w6["st"]])
                p.op("act", lambda e: e.activation(out=rstd1, in_=rstd1, func=AF.Exp, scale=-0.5), reads=[R_w6["st"]],
                     writes=[R_w6["st"]])
                p.op("dve", lambda e: e.scalar_tensor_tensor(out=nmr1, in0=s1, scalar=-1.0, in1=rstd1, op0=ALU.mult,
                                                             op1=ALU.mult),
                     reads=[R_w6["st"]], writes=[R_w6["st"]])
                for i in range(9):
                    rows = 128 if i < 8 else 64
                    p.op("dve", lambda e, i=i, rows=rows: e.tensor_scalar(
                        out=vb[0:rows, :, i, :], in0=vb[0:rows, :, i, :], scalar1=rstd1[0:rows, i:i + 1],
                        scalar2=nmr1[0:rows, i:i + 1], op0=ALU.mult, op1=ALU.add),
                        reads=[R_w6["st"]] + R_vb, writes=R_vb)
                gb = mixed[0:64, 0:1024]
                for j in range(8):
                    cs = slice(j * 512, (j + 1) * 512)
                    p.dma("sp", gb[:, 0:512], ln_g[cs].partition_broadcast(64), writes=[R_w6["mixed"]])
                    p.dma("sp", gb[:, 512:1024], ln_b[cs].partition_broadcast(64), writes=[R_w6["mixed"]])
                    p.op("dve", lambda e, cs=cs: e.tensor_scalar(out=vsf[0:64, cs], in0=vsf[0:64, cs], scalar1=rstd1[0:64, 8:9],
                                                                 scalar2=nmr1[0:64, 8:9], op0=ALU.mult, op1=ALU.add),
                         reads=[R_vsf, R_w6["st"]], writes=[R_vsf])
                    p.op("dve", lambda e, cs=cs: e.tensor_tensor(out=vsf[0:64, cs], in0=vsf[0:64, cs], in1=gb[:, 0:512],
                                                                 op=ALU.mult), reads=[R_vsf, R_w6["mixed"]], writes=[R_vsf])
                    p.op("dve", lambda e, cs=cs: e.tensor_tensor(out=vsf[0:64, cs], in0=vsf[0:64, cs], in1=gb[:, 512:1024],
                                                                 op=ALU.add), reads=[R_vsf, R_w6["mixed"]], writes=[R_vsf])
                    p.dma("sp", o_sgu[:, cs], vsf[0:64, cs], reads=[R_vsf])
            if stage >= 7:
                n0 = len(slab_raw)
                for ex in (range(2) if "ext" in DBG else []):
                    slab_raw.append(vsf[:, ex * 2048:(ex + 1) * 2048].bitcast(BF16))
                    R_slab.append([Res(f"slabx{ex}_{q}") for q in range(4)])
                    slab_first[n0 + ex] = [R_vsf]
                us = zs = Rus = Rzs = None
                for c in range(32):
                    g = c // 2
                    if c % 2 == 0:
                        us, Rus = load_slab(w_in1, 16, (c // 2) * 256, 256)
                        zs, Rzs = load_slab(w_in1, 16, 8192 + (c // 2) * 256, 256)
                    wc = slice((c % 2) * 128, (c % 2 + 1) * 128)
                    for i in range(8):
                        bk = 4 + i // 4
                        p.op("pe", lambda e, bk=bk, i=i, c=c, g=g: e.matmul(
                            out=banks[bk][:, (i % 4) * 128:(i % 4 + 1) * 128], lhsT=vb[:, c, i, :], rhs=WspT[:, g, :],
                            start=True, stop=True), reads=[R_vb[c], R_c1], writes=[RB[bk]])
                    p.op("pe", lambda e, c=c, g=g: e.matmul(out=banks[7][:, 128:192], lhsT=vb[0:64, c, 8, :],
                                                            rhs=BDs[0:64, g, :], start=True, stop=True),
                         reads=[R_vb[c], R_w6["BDs"]], writes=[RB[7]])
                    for (slab, Rs, bk0, soff) in ((us, Rus, 0, 0), (zs, Rzs, 2, 64)):
                        for (t0, t1, bk, col) in ((0, 512, bk0, 0), (512, 1024, bk0 + 1, 0), (1024, 1088, 6, soff)):
                            n = t1 - t0
                            Rr = [R_h1T[t] for t in range(t0 // 128, (t1 + 127) // 128)]
                            for k in range(16):
                                p.op("pe", lambda e, bk=bk, col=col, n=n, k=k, wc=wc, t0=t0, t1=t1, slab=slab: e.matmul(
                                    out=banks[bk][:, col:col + n], lhsT=slab[:, k, wc], rhs=h1T[:, k, t0:t1],
                                    start=(k == 0), stop=(k == 15)), reads=Rr + [Rs[k]], writes=[RB[bk]])
                    p.op("dve", lambda e, c=c, g=g: e.scalar_tensor_tensor(
                        out=bias2, in0=RSW[:, g, :], scalar=betT[:, c:c + 1], in1=bspB[:, g, :], op0=ALU.mult, op1=ALU.add),
                        reads=[R_c1, R_cst], writes=[R_w6["bias2"]])
                    for hf in range(2):
                        p.op("dve", lambda e, hf=hf, c=c: e.scalar_tensor_tensor(
                            out=mixed[:, hf * 512:(hf + 1) * 512].rearrange("p (i t) -> p i t", i=4),
                            in0=banks[4 + hf][:, :].rearrange("p (i t) -> p i t", i=4), scalar=gamT[:, c:c + 1],
                            in1=bias2.unsqueeze(1).to_broadcast([128, 4, 128]), op0=ALU.mult, op1=ALU.add),
                            reads=[RB[4 + hf], R_w6["bias2"], R_cst], writes=[R_w6["mixed"]])
                    p.op("dve", lambda e, c=c: e.scalar_tensor_tensor(
                        out=mixed[:, 1024:1088].rearrange("p (i t) -> p i t", i=4),
                        in0=banks[7][:, 128:192].rearrange("p (i t) -> p i t", i=4), scalar=gamT[:, c:c + 1],
                        in1=bias2[:, 0:16].unsqueeze(1).to_broadcast([128, 4, 16]), op0=ALU.mult, op1=ALU.add),
                        reads=[RB[7], R_w6["bias2"], R_cst], writes=[R_w6["mixed"]])
                    for (bk, pc, oc) in ((2, slice(0, 512), slice(0, 512)), (3, slice(0, 512), slice(512, 1024)),
                                         (6, slice(64, 128), slice(1024, 1088))):
                        p.op("act", lambda e, bk=bk, pc=pc, oc=oc: e.activation(out=szb[:, oc], in_=banks[bk][:, pc],
                                                                                func=AF.Silu),
                             reads=[RB[bk]], writes=[R_w6["szb"]])
                    for (bk, pc, oc) in ((0, slice(0, 512), slice(0, 512)), (1, slice(0, 512), slice(512, 1024)),
                                         (6, slice(0, 64), slice(1024, 1088))):
                        p.op("dve", lambda e, bk=bk, pc=pc, oc=oc: e.tensor_tensor(out=tprod[:, oc], in0=banks[bk][:, pc],
                                                                                   in1=mixed[:, oc], op=ALU.mult),
                             reads=[RB[bk], R_w6["mixed"]], writes=[R_w6["tprod"]])
                    gv = gT_view(c)
                    p.op("pool", lambda e, gv=gv: e.tensor_tensor(out=gv, in0=tprod, in1=szb, op=ALU.mult),
                         reads=[R_w6["tprod"], R_w6["szb"]], writes=[R_vb[c]])

        def out_proj1():
            del slab_raw[NSLAB:]
            del R_slab[NSLAB:]
            off7 = hT_off + 32 * CH
            hpF = view_at(off7, [128, 9, D], F32)
            off7 += 9 * D * 4
            slabB = view_at(off7, [128, 32, 384], BF16)
            gfB = view_at(off7, [128, D], F32)
            off7 += 32 * 384 * 2
            ssq7 = view_at(off7, [128, 16], F32)
            off7 += 64
            junk7 = view_at(off7, [128, 512], BF16)
            off7 += 1024
            assert off7 <= ARENA, off7
            slabA = slab_raw[0]
            assert NSLAB * SLAB_BYTES >= 32 * 384 * 2
            slabA = view_at(slab_off, [128, 32, 384], BF16)
            R_hpF = [Res(f"hpF{i}") for i in range(9)]
            R_s7 = [Res("s7A"), Res("s7B")]
            slabs7 = [slabA, slabB]
            for i in range(9):
                rows = 128 if i < 8 else 64
                p.dma("sp", hpF[0:rows, i, :], hp_scr[i * 128:i * 128 + rows, :], writes=[R_hpF[i]])
            srcw = w_out1.rearrange("(c p) n -> p c n", p=128)
            for j in range(6):
                c0 = j * 384
                n = min(384, D - c0)
                sl_ = slabs7[j % 2]
                Rs = R_s7[j % 2]
                R_parts = []
                for k0 in range(0, 32, 8):
                    p.dma("pool", sl_[:, k0:k0 + 8, 0:n], srcw[:, k0:k0 + 8, c0:c0 + n], writes=[Rs] + [r for rl_ in R_slab for r in rl_])
                for i in range(9):
                    rows = 128 if i < 8 else 64
                    bk = next_bank(0, 8)
                    for c in range(32):
                        gv = gT_view(c)
                        p.op("pe", lambda e, bk=bk, c=c, i=i, rows=rows, n=n, gv=gv, sl_=sl_: e.matmul(
                            out=banks[bk][0:rows, 0:n], lhsT=gv[:, i * 128:i * 128 + rows], rhs=sl_[:, c, 0:n],
                            start=(c == 0), stop=(c == 31)), reads=[R_vb[c], Rs], writes=[RB[bk]])
                    p.op("dve", lambda e, bk=bk, i=i, rows=rows, n=n, c0=c0: e.tensor_tensor(
                        out=hpF[0:rows, i, c0:c0 + n], in0=banks[bk][0:rows, 0:n], in1=hpF[0:rows, i, c0:c0 + n],
                        op=ALU.add), reads=[RB[bk], R_hpF[i]], writes=[R_hpF[i]])
            p.op("dve", lambda e: e.memset(gfB, 0.0), reads=[R_s7[1]], writes=[R_s7[1]])
            p.dma("sp", gfB, final_g.partition_broadcast(128), reads=[R_s7[1]], writes=[R_s7[1]])
            R_q7 = Res("q7")
            for i in range(9):
                rows = 128 if i < 8 else 64
                for q in range(4):
                    p.op("act", lambda e, i=i, rows=rows, q=q: e.activation(
                        out=junk7[0:rows, :], in_=hpF[0:rows, i, q * 512:(q + 1) * 512], func=AF.Square,
                        accum_out=ssq7[0:rows, q:q + 1]), reads=[R_hpF[i]], writes=[R_q7])
                p.op("dve", lambda e, rows=rows: e.reduce_sum(out=ssq7[0:rows, 4:5], in_=ssq7[0:rows, 0:4], axis=AX.X),
                     reads=[R_q7], writes=[R_q7])
                p.op("act", lambda e, rows=rows: e.activation(out=ssq7[0:rows, 4:5], in_=ssq7[0:rows, 4:5], func=AF.Ln,
                                                              scale=1.0 / D, bias=1e-6), reads=[R_q7], writes=[R_q7])
                p.op("act", lambda e, rows=rows: e.activation(out=ssq7[0:rows, 4:5], in_=ssq7[0:rows, 4:5], func=AF.Exp,
                                                              scale=-0.5), reads=[R_q7], writes=[R_q7])
                p.op("dve", lambda e, i=i, rows=rows: e.scalar_tensor_tensor(
                    out=hpF[0:rows, i, :], in0=hpF[0:rows, i, :], scalar=ssq7[0:rows, 4:5], in1=gfB[0:rows, :],
                    op0=ALU.mult, op1=ALU.mult), reads=[R_hpF[i], R_q7, R_s7[1]], writes=[R_hpF[i]])
                dst = y_own[i * 128:(i + 1) * 128, :] if i < 8 else y_s
                p.dma("sp", dst, hpF[0:rows, i, :], reads=[R_hpF[i]])

        if stage >= 5:
            out_proj0()
            p.barrier(scratch)
            Alloc.top = mark1
            if "nol1" not in DBG:
                layer1()
                p.barrier(scratch)
            if stage >= 8:
                out_proj1()
                p.barrier(scratch)

        if os.environ.get("DBGOUT") and stage < 5:
            dbg = nc.dram_tensor("dbg", [128, 16, NOWN], F32, kind="ExternalOutput").ap()
            dst_ = [alloc([128, NOWN], F32, f"dbgst{i}") for i in range(2)]
            R_d = [Res("d0"), Res("d1")]
            for h in range(16):
                p.op("dve", lambda e, h=h: e.tensor_copy(out=dst_[h % 2], in_=mixT[:, h, :]),
                     reads=[R_mix[h], R_mix_s[h]], writes=[R_d[h % 2]])
                p.dma("sp", dbg[:, h, :], dst_[h % 2], reads=[R_d[h % 2]])

        p.emit(st)
        _NC_CACHE["trace"] = p.trace
    return nc


def _consts(half):
    c = np.zeros((14, 128, 128), np.float32)
    i = np.arange(128)
    c[0] = np.eye(128)
    c[1] = (i[:, None] <= i[None, :])
    c[2] = 1.0
    c[3] = 1.0 if half == 0 else 0.0
    c[4] = 1.0 if half == 1 else 0.0
    c[5] = (i[:, None] > i[None, :])
    c[6] = (i[:, None] <= i[None, :])
    c[7] = (i[:, None] < i[None, :])
    c[8] = 0.0
    same = (i[:, None] // 16 == i[None, :] // 16) & (i[:, None] < 64) & (i[None, :] < 64)
    c[9] = same & (i[:, None] <= i[None, :])
    c[10] = same & (i[:, None] <= i[None, :])
    c[11] = same & (i[:, None] < i[None, :])
    c[12] = same
    c[13] = (i[:, None] < 16) & (i[None, :] < 64) & (i[None, :] % 16 == i[:, None])
    return np.ascontiguousarray(c.transpose(1, 0, 2).reshape(128, 14 * 128))


_NC_CACHE = {}


def kernel(x_prompt, x_sample, cache_fox_k, cache_fox_v, cache_fox_logf, cache_sb_k, cache_sb_v,
           norm0_g, w_in0, b_forget, w_out0, norm1_g, w_in1, sgu_ln_g, sgu_ln_b, w_sp, b_sp, w_out1, final_g):
    f = lambda a: np.ascontiguousarray(np.asarray(a, dtype=np.float32))
    x_prompt, x_sample = f(x_prompt), f(x_sample)
    if "nc" not in _NC_CACHE:
        _NC_CACHE["nc"] = build_program(STAGE)
    nc = _NC_CACHE["nc"]
    gT = np.concatenate([f(g).reshape(16, 128).T for g in (norm0_g, norm1_g, final_g)], axis=1)
    lngb = np.concatenate([f(g).reshape(32, 128).T for g in (sgu_ln_g, sgu_ln_b)], axis=1)
    shared = dict(w_in0=f(w_in0), w_out0=f(w_out0), w_in1=f(w_in1), w_out1=f(w_out1),
                  gT=np.ascontiguousarray(gT), bforget=f(b_forget), lngb=np.ascontiguousarray(lngb),
                  ln_g=f(sgu_ln_g), ln_b=f(sgu_ln_b), final_g=f(final_g), w_sp=f(w_sp), b_sp=f(b_sp))
    in_maps = []
    for c in range(NCORES):
        b, half = c // 2, c % 2
        xb = x_prompt[b].reshape(16, 128, D)
        order = OWN[half] + OTHER[half]
        xs = x_sample[4 * c:4 * c + 4].reshape(64, D)
        xall = np.concatenate([xb[order].reshape(2048, D), xs], axis=0)
        m = dict(shared)
        m["xall"] = np.ascontiguousarray(xall)
        m["cfk"] = f(cache_fox_k[4 * c:4 * c + 4]).reshape(4096, 1024)
        m["cfv"] = f(cache_fox_v[4 * c:4 * c + 4]).reshape(4096, 1024)
        m["csk"] = f(cache_sb_k[4 * c:4 * c + 4]).reshape(4096, 1024)
        m["csv"] = f(cache_sb_v[4 * c:4 * c + 4]).reshape(4096, 1024)
        m["clf"] = f(cache_fox_logf[4 * c:4 * c + 4])
        m["cst"] = _consts(half)
        in_maps.append(m)
    res = run_bass_kernel_spmd(nc, in_maps, core_ids=list(range(NCORES)))
    R = res.results
    B, S = 4, 2048

    def gather_prompt(name, width):
        out = np.zeros((B, 16, 128, width), np.float32)
        for c in range(NCORES):
            b, half = c // 2, c % 2
            out[b, OWN[half]] = R[c][name].reshape(8, 128, width)
        return out.reshape(B, S, width)

    def gather_sample(name, width):
        return np.concatenate([R[c][name].reshape(4, 16, width) for c in range(NCORES)], axis=0)

    y_prompt = gather_prompt("y_own", D)
    y_sample = gather_sample("y_s", D)
    fkp = gather_prompt("o_fk", 1024).reshape(B, S, 8, 128)
    fvp = gather_prompt("o_fv", 1024).reshape(B, S, 8, 128)
    lfp = gather_prompt("o_lf", 8)
    fks = gather_sample("o_fk_s", 1024).reshape(32, 16, 8, 128)
    fvs = gather_sample("o_fv_s", 1024).reshape(32, 16, 8, 128)
    lfs = gather_sample("o_lf_s", 8)
    skp = gather_prompt("o_sk", 1024).reshape(B, S, 8, 128)
    svp = gather_prompt("o_sv", 1024).reshape(B, S, 8, 128)
    sks = gather_sample("o_sk_s", 1024).reshape(32, 16, 8, 128)
    svs = gather_sample("o_sv_s", 1024).reshape(32, 16, 8, 128)
    sgu = gather_sample("o_sgu", 4096)
    return (y_prompt, y_sample, fkp, fvp, lfp, fks, fvs, lfs, skp, svp, sks, svs, sgu)
```

```python
import numpy as np
from contextlib import ExitStack
import concourse.bass as bass
import concourse.mybir as mybir
from concourse.bass_utils import run_bass_kernel_spmd

F32 = mybir.dt.float32
BF16 = mybir.dt.bfloat16
AF = mybir.ActivationFunctionType
ALU = mybir.AluOpType

D = 2048
NCORES = 8
SCALE = 128 ** -0.5
OWN = {0: [0, 3, 4, 7, 8, 11, 12, 15], 1: [1, 2, 5, 6, 9, 10, 13, 14]}
OTHER = {h: [b for b in range(16) if b not in OWN[h]] for h in (0, 1)}
NTOK = 2112
NOWN = 1088
STAGE = 99
import os
PARTS = os.environ.get('PARTS', 'KOVQ')
NGROUPS = int(os.environ.get('NGROUPS', '8'))
DBG = os.environ.get('DBG', '')

COMPUTE = ("pe", "act", "dve", "pool")
ALL_ENG = ("pe", "act", "dve", "pool", "sp")


class Res:
    __slots__ = ("name", "writer", "readers", "lock")

    def __init__(self, name="", lock=None):
        self.name = name
        self.writer = None
        self.readers = []
        self.lock = lock


class Op:
    __slots__ = ("eng", "fn", "deps", "is_dma", "count", "needed", "dsem", "dval", "prewait")

    def __init__(self, eng, fn, is_dma):
        self.eng = eng
        self.fn = fn
        self.deps = []
        self.is_dma = is_dma
        self.count = None
        self.needed = False
        self.dsem = None
        self.dval = None
        self.prewait = None


class Prog:
    def __init__(self, nc):
        self.nc = nc
        self.ops = {e: [] for e in ALL_ENG}
        self.n_dma_sems = {"sp": 24, "pool": 24}
        self.dma_rr = {e: 0 for e in ALL_ENG}
        self.dma_last = {}
        self.dma_cnt = {}
        self.phase = Res("phase")

    def _record(self, op, reads, writes):
        deps = []
        for r in reads:
            if r.writer is not None:
                deps.append(r.writer)
        for w in writes:
            if w.writer is not None:
                deps.append(w.writer)
            last = {}
            for r in w.readers:
                if r.is_dma:
                    deps.append(r)
                else:
                    last[r.eng] = r
            deps.extend(last.values())
        seen = set()
        for d in deps:
            if d is op or id(d) in seen:
                continue
            seen.add(id(d))
            if op.eng == "pe" and d.eng == "pe" and not d.is_dma and not op.is_dma:
                continue
            op.deps.append(d)
            d.needed = True
        for r in reads:
            r.readers.append(op)
        for w in writes:
            w.writer = op
            w.readers = []
        self.ops[op.eng].append(op)
        return op

    def op(self, eng, fn, reads=(), writes=(), glob=False):
        reads = list(reads)
        writes = list(writes)
        for r in reads:
            if r.lock is not None and r not in writes:
                writes.append(r.lock)
        if not glob:
            reads.append(self.phase)
        return self._record(Op(eng, fn, False), reads, writes)

    def dma(self, eng, out, in_, reads=(), writes=(), glob=False):
        def fn(e, out=out, in_=in_):
            return e.dma_start(out=out, in_=in_)
        op = Op(eng, fn, True)
        n = self.n_dma_sems[eng]
        slot = self.dma_rr[eng] % n
        self.dma_rr[eng] += 1
        key = (eng, slot)
        cnt = self.dma_cnt.get(key, 0) + 1
        self.dma_cnt[key] = cnt
        op.dsem = key
        op.dval = 16 * cnt
        op.prewait = self.dma_last.get(key)
        self.dma_last[key] = op
        reads = list(reads)
        if not glob:
            reads.append(self.phase)
        return self._record(op, reads, list(writes))

    def barrier(self, scratch):
        o = Op("pool", lambda e: e.memset(scratch, 0.0), False)
        last = {}
        for r in self.phase.readers:
            if r.is_dma:
                o.deps.append(r)
            else:
                last[r.eng] = r
        if self.phase.writer is not None:
            o.deps.append(self.phase.writer)
        for r in last.values():
            o.deps.append(r)
            r.needed = True
        self.phase.writer = o
        self.phase.readers = []
        self.ops["pool"].append(o)

    def emit(self, stack):
        nc = self.nc
        eng_sem = {e: stack.enter_context(nc.semaphore("es_" + e)) for e in COMPUTE}
        dma_sem = {}
        for e, n in self.n_dma_sems.items():
            for s in range(n):
                dma_sem[(e, s)] = stack.enter_context(nc.semaphore(f"ds_{e}{s}"))
        for e in COMPUTE:
            c = 0
            for o in self.ops[e]:
                if o.needed and not o.is_dma:
                    c += 1
                    o.count = c
        block = stack.enter_context(nc.Block())
        prog = self

        prog.trace = {e: [] for e in ALL_ENG}

        def run(engname, eng):
            known = {}
            tr = prog.trace[engname]

            def wait(key, sem, val):
                if known.get(key, 0) >= val:
                    return
                eng.wait_ge(sem, val)
                tr.append(("w", key, val))
                known[key] = val

            def wait_for(d):
                if d.is_dma:
                    wait(d.dsem, dma_sem[d.dsem], d.dval)
                else:
                    if d.eng == engname and engname == "pe":
                        return
                    wait(d.eng, eng_sem[d.eng], d.count)

            def wait_all(ds):
                need = {}
                for d in ds:
                    if d.is_dma:
                        k, v = d.dsem, d.dval
                    else:
                        if d.eng == engname and engname == "pe":
                            continue
                        k, v = d.eng, d.count
                    if need.get(k, 0) < v:
                        need[k] = v
                for k, v in need.items():
                    wait(k, dma_sem[k] if isinstance(k, tuple) else eng_sem[k], v)

            for o in prog.ops[engname]:
                ds = list(o.deps)
                if o.is_dma and o.prewait is not None:
                    ds.append(o.prewait)
                wait_all(ds)
                ins = o.fn(eng)
                if o.is_dma:
                    ins.then_inc(dma_sem[o.dsem], 16)
                    tr.append(("i", o.dsem, 16))
                elif o.needed:
                    ins.then_inc(eng_sem[engname], 1)
                    tr.append(("i", engname, 1))
                else:
                    tr.append(("n", None, 0))
            for key, last in prog.dma_last.items():
                if key[0] == engname:
                    wait(key, dma_sem[key], last.dval)

        @block.tensor
        def _(e):
            run("pe", e)

        @block.scalar
        def _(e):
            run("act", e)

        @block.vector
        def _(e):
            run("dve", e)

        @block.gpsimd
        def _(e):
            run("pool", e)

        @block.sync
        def _(e):
            run("sp", e)


def build_program(stage=99):
    nc = bass.Bass("TRN2", target_bir_lowering=False)

    def din(name, shape):
        return nc.dram_tensor(name, list(shape), F32, kind="ExternalInput").ap()

    def dout(name, shape):
        return nc.dram_tensor(name, list(shape), F32, kind="ExternalOutput").ap()

    xall = din("xall", [NTOK, D])
    ck = [din("cfk", [4096, 1024]), din("csk", [4096, 1024])]
    cv = [din("cfv", [4096, 1024]), din("csv", [4096, 1024])]
    clf = din("clf", [4, 1024, 8])
    w_in0 = din("w_in0", [D, 8200])
    w_out0 = din("w_out0", [D, D])
    w_in1 = din("w_in1", [D, 12288])
    w_out1 = din("w_out1", [4096, D])
    gT_in = din("gT", [128, 48])
    bfg = din("bforget", [8])
    lngb = din("lngb", [128, 64])
    ln_g = din("ln_g", [4096])
    ln_b = din("ln_b", [4096])
    final_g = din("final_g", [D])
    w_sp = din("w_sp", [16, 128, 128])
    b_sp = din("b_sp", [16, 128])
    cst = din("cst", [128, 14 * 128])

    y_own = dout("y_own", [1024, D])
    y_s = dout("y_s", [64, D])
    okv = {}
    for nm in ("fk", "fv", "sk", "sv"):
        okv[nm] = dout("o_" + nm, [1024, 1024])
        okv[nm + "_s"] = dout("o_" + nm + "_s", [64, 1024])
    o_lf = dout("o_lf", [1024, 8])
    o_lf_s = dout("o_lf_s", [64, 8])
    o_sgu = dout("o_sgu", [64, 4096])
    hp_scr = nc.dram_tensor("hp_scr", [NOWN + 64, D], F32, kind="Internal").ap()

    st = ExitStack()
    with st:
        ARENA = 207 * 1024
        arena = st.enter_context(nc.sbuf_tensor("arena", [128, ARENA // 4], F32))
        banks = [st.enter_context(nc.psum_tensor(f"bank{i}", [128, 512], F32)) for i in range(8)]
        RB = [Res(f"bank{i}", lock=Res(f"banklock{i}")) for i in range(8)]
        p = Prog(nc)

        class Alloc:
            top = 0

        def view_at(off, shape, dt):
            esz = 4 if dt == F32 else 2
            n = int(np.prod(shape[1:]))
            assert off % 4 == 0 and off + n * esz <= ARENA, (off, shape)
            a = arena[:, off // 4: off // 4 + (n * esz + 3) // 4]
            if dt != F32:
                a = a.bitcast(dt)
            a = a[:, 0:n]
            if len(shape) == 3:
                a = a.rearrange("p (a b) -> p a b", a=shape[1])
            elif len(shape) == 4:
                a = a.rearrange("p (a b c) -> p a b c", a=shape[1], b=shape[2])
            return a

        def alloc(shape, dt, name=""):
            esz = 4 if dt == F32 else 2
            n = int(np.prod(shape[1:]))
            nbytes = (n * esz + 63) // 64 * 64
            off = Alloc.top
            Alloc.top += nbytes
            assert Alloc.top <= ARENA, (name, Alloc.top)
            return view_at(off, shape, dt)

        def bank_bf(i):
            return banks[i][:, :].bitcast(BF16)

        cst_f = alloc([128, 14, 128], F32, "cst_f")
        cst_b = alloc([128, 14, 128], BF16, "cst_b")
        (I_ID, I_TRILE, I_ONES, I_EEV, I_EOD, I_SL, I_MLE, I_MLT, I_ZERO, I_BDTRI, I_BDLE, I_BDLT, I_SAME, I_SEL) = range(14)
        R_cst = Res("cst")
        p.dma("sp", cst_f, cst.rearrange("p (a b) -> p a b", a=14), writes=[R_cst], glob=True)
        p.op("dve", lambda e: e.tensor_copy(out=cst_b, in_=cst_f), reads=[R_cst], writes=[R_cst], glob=True)
        ident_b = cst_b[:, I_ID, :]
        ones_b = cst_b[:, I_ONES, :]
        zero_b = cst_b[:, I_ZERO, :]
        SL_b = cst_b[:, I_SL, :]
        MLE_b = cst_b[:, I_MLE, :]
        MLT_b = cst_b[:, I_MLT, :]
        E_b = [cst_b[:, I_EEV, :], cst_b[:, I_EOD, :]]
        E_f = [cst_f[:, I_EEV, :], cst_f[:, I_EOD, :]]
        trile_f = cst_f[:, I_TRILE, :]
        ones_f = cst_f[:, I_ONES, :]
        SL_f = cst_f[:, I_SL, :]

        def own_first(pp, bf=True):
            t = E_b if bf else E_f
            return t[0] if pp % 2 == 0 else t[1]

        def other_first(pp, bf=True):
            t = E_b if bf else E_f
            return t[1] if pp % 2 == 0 else t[0]

        gT = alloc([128, 48], F32, "gT")
        lngbT = alloc([128, 64], F32, "lngbT")
        bfB = alloc([128, 8], F32, "bfB")
        p.dma("sp", gT, gT_in, writes=[R_cst], glob=True)
        p.dma("sp", lngbT, lngb, writes=[R_cst], glob=True)
        p.dma("sp", bfB, bfg.partition_broadcast(128), writes=[R_cst], glob=True)
        scratch = alloc([128, 16], F32, "scratch")

        NSLAB = 3
        SLAB_BYTES = 8192
        slab_off = Alloc.top
        slab_raw = [alloc([128, SLAB_BYTES // 2], BF16, f"slab{i}") for i in range(NSLAB)]
        R_slab = [[Res(f"slab{i}_{q}") for q in range(4)] for i in range(NSLAB)]
        slab_ctr = [0]
        slab_first = {}

        def load_slab(w_ap, row_chunks, col0, ncols):
            i = slab_ctr[0] % len(slab_raw)
            slab_ctr[0] += 1
            first_extra = slab_first.pop(i, [])
            assert row_chunks * ncols * 2 <= SLAB_BYTES
            v = slab_raw[i][:, 0:row_chunks * ncols].rearrange("p (c n) -> p c n", c=row_chunks)
            src = w_ap.rearrange("(c p) n -> p c n", p=128)
            step = max(1, row_chunks // 4)
            rl = []
            for qi, c0 in enumerate(range(0, row_chunks, step)):
                p.dma("pool", v[:, c0:c0 + step, :], src[:, c0:c0 + step, col0:col0 + ncols],
                      writes=[R_slab[i][qi]] + first_extra, glob=True)
                rl += [R_slab[i][qi]] * step
            return v, rl

        kvq_slabs = {}

        def issue_kvq(m, h0, parts="kvq"):
            base = m * 4096
            d_ = kvq_slabs.setdefault((m, h0), {})
            for nm_, off_ in (("k", 1024), ("v", 2048), ("q", 0)):
                if nm_ in parts:
                    d_[nm_] = load_slab(w_in0, 16, base + off_ + h0 * 128, 256)

        wlf, R_wlf = load_slab(w_in0, 16, 8192, 8)
        if stage >= 2:
            issue_kvq(0, 0, "kv")

        hT_off = Alloc.top
        hT = alloc([128, 16, NTOK], BF16, "hT")
        R_hT = [Res(f"hT{i}") for i in range(17)]
        mixT = alloc([128, 16, NOWN], BF16, "mixT")
        R_mix = [Res(f"mix{h}") for h in range(16)]
        R_mix_s = [Res(f"mixs{h}") for h in range(16)]
        KTs = alloc([128, 16, 64], BF16, "KTs")
        QTs = alloc([128, 16, 64], BF16, "QTs")
        GTs = alloc([128, 16, 64], BF16, "GTs")
        Vs = alloc([128, 16, 128], BF16, "Vs")
        R_samp = Res("samp_keep")
        logf = alloc([128, 17, 8], F32, "logf")
        Fneg = alloc([128, 16, 8], F32, "Fneg")
        PB = alloc([128, 8, 8], F32, "PB")
        R_F = Res("F")
        mark0 = Alloc.top

        def tile_rows(i):
            return 128 if i < 16 else 64

        def tile_cols(i):
            return slice(i * 128, i * 128 + tile_rows(i))

        xt = [alloc([128, D], F32, f"xt{i}") for i in range(2)]
        R_xt = [Res("xt0"), Res("xt1")]
        xn = [alloc([128, D], BF16, f"xn{i}") for i in range(2)]
        R_xn = [Res("xn0"), Res("xn1")]
        junk = alloc([128, D], BF16, "junk")
        R_junk = Res("junk")
        ssq = alloc([128, 32], F32, "ssq")
        R_ssq = Res("ssq")

        def norm_transpose(i, src_tile, R_src, rows, g_off, dstT, R_dst, col0, pb, bufs=None):
            b = i % 2
            xn, R_xn, junk, R_junk, ssq, R_ssq = bufs
            p.op("act", lambda e: e.activation(out=junk[0:rows, :], in_=src_tile[0:rows, :], func=AF.Square,
                                               accum_out=ssq[0:rows, i:i + 1]),
                 reads=[R_src], writes=[R_junk, R_ssq])
            p.op("act", lambda e: e.activation(out=ssq[0:rows, i:i + 1], in_=ssq[0:rows, i:i + 1], func=AF.Ln,
                                               scale=1.0 / D, bias=1e-6), reads=[R_ssq], writes=[R_ssq])
            p.op("act", lambda e: e.activation(out=ssq[0:rows, i:i + 1], in_=ssq[0:rows, i:i + 1], func=AF.Exp,
                                               scale=-0.5), reads=[R_ssq], writes=[R_ssq])
            p.op("dve", lambda e: e.tensor_scalar(out=xn[b][0:rows, :], in0=src_tile[0:rows, :],
                                                  scalar1=ssq[0:rows, i:i + 1], scalar2=None, op0=ALU.mult),
                 reads=[R_src, R_ssq], writes=[R_xn[b]])
            for half in range(2):
                bk = pb + half
                pv = bank_bf(bk)
                for cc in range(8):
                    c = half * 8 + cc
                    p.op("pe", lambda e, c=c, cc=cc, pv=pv: e.transpose(
                        out=pv[:, cc * 128:cc * 128 + rows], in_=xn[b][0:rows, c * 128:(c + 1) * 128],
                        identity=ident_b[0:rows, 0:rows]),
                        reads=[R_xn[b], R_cst], writes=[RB[bk]])
                pv3 = pv.rearrange("p (a b) -> p a b", a=8)
                p.op("dve", lambda e, half=half, pv3=pv3: e.tensor_tensor(
                    out=dstT[:, half * 8:(half + 1) * 8, col0:col0 + rows], in0=pv3[:, :, 0:rows],
                    in1=gT[:, g_off + half * 8:g_off + (half + 1) * 8].unsqueeze(2).to_broadcast([128, 8, rows]),
                    op=ALU.mult), reads=[RB[bk], R_cst], writes=[R_dst])

        for i in range(17):
            rows = tile_rows(i)
            b = i % 2
            p.dma("sp", xt[b][0:rows, :], xall[i * 128:i * 128 + rows, :], writes=[R_xt[b]])
            norm_transpose(i, xt[b], R_xt[b], rows, 0, hT, R_hT[i], i * 128, (i % 2) * 2,
                           bufs=(xn, R_xn, junk, R_junk, ssq, R_ssq))

        bkL = 4
        for i in range(17):
            rows = tile_rows(i)
            for c in range(16):
                p.op("pe", lambda e, i=i, c=c, rows=rows: e.matmul(
                    out=banks[bkL][0:rows, i * 8:(i + 1) * 8], lhsT=hT[:, c, i * 128:i * 128 + rows],
                    rhs=wlf[:, c, :], start=(c == 0), stop=(c == 15)),
                    reads=[R_hT[i], R_wlf[c]], writes=[RB[bkL]])
        lg = alloc([128, 17, 8], F32, "lg")
        R_lg = Res("lg")
        for (r0, r1, t0, t1) in ((0, 128, 0, 16), (0, 64, 16, 17)):
            nt = t1 - t0
            pv = banks[bkL][r0:r1, t0 * 8:t1 * 8].rearrange("p (a b) -> p a b", a=nt)
            p.op("dve", lambda e, pv=pv, r0=r0, r1=r1, t0=t0, t1=t1, nt=nt: e.tensor_tensor(
                out=lg[r0:r1, t0:t1, :], in0=pv, in1=bfB[r0:r1, :].unsqueeze(1).to_broadcast([r1 - r0, nt, 8]),
                op=ALU.add), reads=[RB[bkL], R_cst], writes=[R_lg])
            p.op("act", lambda e, r0=r0, r1=r1, t0=t0, t1=t1: e.activation(
                out=lg[r0:r1, t0:t1, :], in_=lg[r0:r1, t0:t1, :], func=AF.Exp, scale=-1.0), reads=[R_lg], writes=[R_lg])
            p.op("act", lambda e, r0=r0, r1=r1, t0=t0, t1=t1: e.activation(
                out=lg[r0:r1, t0:t1, :], in_=lg[r0:r1, t0:t1, :], func=AF.Ln, bias=1.0), reads=[R_lg], writes=[R_lg])
            p.op("dve", lambda e, r0=r0, r1=r1, t0=t0, t1=t1: e.tensor_scalar(
                out=logf[r0:r1, t0:t1, :], in0=lg[r0:r1, t0:t1, :], scalar1=-1.0, scalar2=None, op0=ALU.mult),
                reads=[R_lg], writes=[R_F])
        p.dma("sp", o_lf.rearrange("(i p) h -> p i h", p=128), logf[:, 0:8, :], reads=[R_F])
        p.dma("sp", o_lf_s, logf[0:64, 16, :], reads=[R_F])
        Spre = alloc([128, 9, 8], F32, "Spre")
        psum_pair = alloc([128, 8, 8], F32, "psum_pair")
        R_S = Res("Spre")
        p.op("dve", lambda e: e.tensor_tensor(out=psum_pair, in0=logf[:, 0:8, :], in1=logf[:, 8:16, :], op=ALU.add),
             reads=[R_F], writes=[R_S])
        p.op("dve", lambda e: e.memset(Spre[:, 0, :], 0.0), writes=[R_S])
        for pp in range(8):
            p.op("dve", lambda e, pp=pp: e.tensor_tensor(out=Spre[:, pp + 1, :], in0=Spre[:, pp, :],
                                                         in1=psum_pair[:, pp, :], op=ALU.add),
                 reads=[R_S], writes=[R_S])
        bkF = 5
        for k in range(16):
            pp = k % 8
            is_other = k >= 8
            partner = pp if is_other else 8 + pp
            Et = own_first(pp, bf=False) if is_other else other_first(pp, bf=False)
            o = banks[bkF][:, k * 8:(k + 1) * 8]
            p.op("pe", lambda e, o=o, k=k: e.matmul(out=o, lhsT=trile_f, rhs=logf[:, k, :], start=True, stop=False),
                 reads=[R_F, R_cst], writes=[RB[bkF]])
            p.op("pe", lambda e, o=o, pp=pp: e.matmul(out=o, lhsT=ones_f, rhs=Spre[:, pp, :], start=False, stop=False),
                 reads=[R_S, R_cst], writes=[RB[bkF]])
            p.op("pe", lambda e, o=o, Et=Et, partner=partner: e.matmul(out=o, lhsT=Et, rhs=logf[:, partner, :],
                                                                        start=False, stop=True),
                 reads=[R_F, R_cst], writes=[RB[bkF]])
        for pp in range(8):
            o = banks[bkF][:, 128 + pp * 8:128 + (pp + 1) * 8]
            p.op("pe", lambda e, o=o, pp=pp: e.matmul(out=o, lhsT=ones_f, rhs=Spre[:, pp, :], start=True, stop=True),
                 reads=[R_S, R_cst], writes=[RB[bkF]])
        R_F2 = Res("F2")
        p.op("dve", lambda e: e.tensor_scalar(out=Fneg, in0=banks[bkF][:, 0:128].rearrange("p (a b) -> p a b", a=16),
                                              scalar1=-1.0, scalar2=None, op0=ALU.mult),
             reads=[RB[bkF]], writes=[R_F2])
        p.op("dve", lambda e: e.tensor_copy(out=PB.rearrange("p h s -> p s h"),
                                            in_=banks[bkF][:, 128:192].rearrange("p (s h) -> p s h", s=8)),
             reads=[RB[bkF]], writes=[R_F2])

        p.barrier(scratch)
        Alloc.top = mark0

        KT = alloc([128, 2, 2048], BF16, "KT")
        Vg = alloc([128, 16, 256], BF16, "Vg")
        QT = alloc([128, 2, 1024], BF16, "QT")
        GT = alloc([128, 2, 1024], BF16, "GT")
        R_KT = [Res("KT0"), Res("KT1")]
        R_Vg = Res("Vg")
        R_QT = [Res("QT0"), Res("QT1")]
        R_GT = [Res("GT0"), Res("GT1")]
        stg = [alloc([128, 256], F32, f"stg{i}") for i in range(4)]
        R_stg = [Res(f"stg{i}") for i in range(4)]
        stg_ctr = [0]
        wA = alloc([128, 1024], F32, "wA")
        wB = alloc([128, 1024], F32, "wB")
        wC = alloc([128, 1024], F32, "wC")
        wD = alloc([128, 1024], F32, "wD")
        wE = alloc([128, 1024], F32, "wE")
        hA = alloc([128, 1024], BF16, "hA")
        hB = alloc([128, 1024], BF16, "hB")
        hC = alloc([128, 1024], BF16, "hC")
        hD = alloc([128, 1024], BF16, "hD")
        hE = alloc([128, 1024], BF16, "hE")
        R_w = {k: Res(k) for k in ("wA", "wB", "wC", "wD", "wE", "hA", "hB", "hC", "hD", "hE")}
        ev_ctr = [0]

        def evac(out, in_, reads, writes, func=None, eng="dve"):
            if "noevac" in DBG:
                return
            if "dveonly" in DBG and func is None:
                p.op("dve", lambda e: e.tensor_copy(out=out, in_=in_), reads=reads, writes=writes)
                return
            if "actonly" in DBG:
                f = func if func is not None else AF.Copy
                p.op("act", lambda e: e.activation(out=out, in_=in_, func=f), reads=reads, writes=writes)
                return
            if func is not None or eng == "act":
                f = func if func is not None else AF.Copy
                p.op("act", lambda e: e.activation(out=out, in_=in_, func=f), reads=reads, writes=writes)
            else:
                p.op("dve", lambda e: e.tensor_copy(out=out, in_=in_), reads=reads, writes=writes)
            ev_ctr[0] += 1

        pb_ctr = [0]

        def next_bank(lo=0, hi=8):
            b = lo + pb_ctr[0] % (hi - lo)
            pb_ctr[0] += 1
            return b

        def col_chunks(c0, c1):
            out = []
            c = c0
            while c < c1:
                e_ = min(c1, (c // 512 + 1) * 512)
                out.append((c, e_))
                c = e_
            return out

        out_names = {0: ("fk", "fv"), 1: ("sk", "sv")}

        def project_group(m, h0):
            base = m * 4096
            hg = m * 8 + h0
            kname, vname = out_names[m]
            d_ = kvq_slabs.pop((m, h0))
            (wk, Rwk), (wv, Rwv), (wq_, Rwq_) = d_["k"], d_["v"], d_["q"]
            for hh in (range(2) if "K" in PARTS else []):
                for (c0, c1) in ((0, 512), (512, 1024), (1024, 1536), (1536, 2048), (2048, 2112)):
                    bk = next_bank(0, 4)
                    n = c1 - c0
                    Rr = [R_hT[t] for t in range(c0 // 128, (c1 + 127) // 128)]
                    for c in range(16):
                        p.op("pe", lambda e, bk=bk, hh=hh, c=c, c0=c0, c1=c1, n=n: e.matmul(
                            out=banks[bk][:, 0:n], lhsT=wk[:, c, hh * 128:(hh + 1) * 128], rhs=hT[:, c, c0:c1],
                            start=(c == 0), stop=(c == 15)), reads=Rr + [Rwk[c]], writes=[RB[bk]])
                    if c0 < 2048:
                        evac(KT[:, hh, c0:c1], banks[bk][:, 0:n], [RB[bk]], [R_KT[hh]])
                    else:
                        evac(KTs[:, hg + hh, :], banks[bk][:, 0:64], [RB[bk]], [R_samp])
            for i in ((list(range(8)) + [16]) if "O" in PARTS else []):
                rows = tile_rows(i)
                bk = next_bank(0, 4)
                for c in range(16):
                    p.op("pe", lambda e, bk=bk, c=c, i=i, rows=rows: e.matmul(
                        out=banks[bk][0:rows, 0:256], lhsT=hT[:, c, i * 128:i * 128 + rows], rhs=wk[:, c, :],
                        start=(c == 0), stop=(c == 15)), reads=[R_hT[i], Rwk[c]], writes=[RB[bk]])
                s = stg_ctr[0] % 4
                stg_ctr[0] += 1
                evac(stg[s][0:rows, :], banks[bk][0:rows, 0:256], [RB[bk]], [R_stg[s]], eng="act")
                dst = okv[kname][i * 128:(i + 1) * 128, h0 * 128:h0 * 128 + 256] if i < 8 else \
                    okv[kname + "_s"][:, h0 * 128:h0 * 128 + 256]
                p.dma("sp", dst, stg[s][0:rows, :], reads=[R_stg[s]])
            wg_, Rwg_ = load_slab(w_in0, 16, base + 3072 + h0 * 128, 256)
            for i in (range(17) if "V" in PARTS else []):
                rows = tile_rows(i)
                bk = next_bank(0, 4)
                for c in range(16):
                    p.op("pe", lambda e, bk=bk, c=c, i=i, rows=rows: e.matmul(
                        out=banks[bk][0:rows, 0:256], lhsT=hT[:, c, i * 128:i * 128 + rows], rhs=wv[:, c, :],
                        start=(c == 0), stop=(c == 15)), reads=[R_hT[i], Rwv[c]], writes=[RB[bk]])
                if i < 16:
                    evac(Vg[:, i, :], banks[bk][:, 0:256], [RB[bk]], [R_Vg])
                else:
                    evac(Vs[0:64, hg:hg + 2, :], banks[bk][0:64, 0:256].rearrange("p (a b) -> p a b", a=2),
                         [RB[bk]], [R_samp])
                if i < 8 or i == 16:
                    s = stg_ctr[0] % 4
                    stg_ctr[0] += 1
                    evac(stg[s][0:rows, :], banks[bk][0:rows, 0:256], [RB[bk]], [R_stg[s]], eng="act")
                    dst = okv[vname][i * 128:(i + 1) * 128, h0 * 128:h0 * 128 + 256] if i < 8 else \
                        okv[vname + "_s"][:, h0 * 128:h0 * 128 + 256]
                    p.dma("sp", dst, stg[s][0:rows, :], reads=[R_stg[s]])
            for (ws, Rws, dstT, R_dst, dsts, func) in (
                    (wq_, Rwq_, QT, R_QT, QTs, None),
                    (wg_, Rwg_, GT, R_GT, GTs, AF.Silu)):
                for hh in (range(2) if "Q" in PARTS else []):
                    for (c0, c1) in ((0, 512), (512, 1024), (2048, 2112)):
                        bk = next_bank(0, 4)
                        n = c1 - c0
                        Rr = [R_hT[t] for t in range(c0 // 128, (c1 + 127) // 128)]
                        for c in range(16):
                            p.op("pe", lambda e, bk=bk, hh=hh, c=c, c0=c0, c1=c1, n=n, ws=ws: e.matmul(
                                out=banks[bk][:, 0:n], lhsT=ws[:, c, hh * 128:(hh + 1) * 128], rhs=hT[:, c, c0:c1],
                                start=(c == 0), stop=(c == 15)), reads=Rr + [Rws[c]], writes=[RB[bk]])
                        if c0 < 2048:
                            evac(dstT[:, hh, c0:c1], banks[bk][:, 0:n], [RB[bk]], [R_dst[hh]], func=func)
                        else:
                            evac(dsts[:, hg + hh, :], banks[bk][:, 0:64], [RB[bk]], [R_samp], func=func)

        def fox_head(hh, h):
            for q in range(2):
                p.op("pe", lambda e, q=q: e.matmul(out=banks[4 + q][:, :], lhsT=zero_b, rhs=QT[:, hh, q * 512:(q + 1) * 512],
                                                    start=True, stop=False), reads=[R_QT[hh], R_cst], writes=[RB[4 + q]])
                p.op("pe", lambda e, q=q: e.matmul(out=banks[6 + q][:, :], lhsT=zero_b, rhs=QT[:, hh, q * 512:(q + 1) * 512],
                                                    start=True, stop=False), reads=[R_QT[hh], R_cst], writes=[RB[6 + q]])
            its = [(pp, is_other) for pp in range(8) for is_other in (False, True)]

            def geom(it):
                pp, is_other = its[it]
                kb = 8 + pp if is_other else pp
                c0 = pp * 128
                return pp, is_other, kb, c0, (it % 2) * 2, col_chunks(c0, 1024)

            def qk(it):
                pp, is_other, kb, c0, sb0, chunks = geom(it)
                for (a, b_) in chunks:
                    bk = sb0 + a // 512
                    p.op("pe", lambda e, bk=bk, a=a, b_=b_, kb=kb: e.matmul(
                        out=banks[bk][:, a % 512:a % 512 + (b_ - a)], lhsT=KT[:, hh, kb * 128:(kb + 1) * 128],
                        rhs=QT[:, hh, a:b_], start=True, stop=True),
                        reads=[R_KT[hh], R_QT[hh]], writes=[RB[bk]])

            def elem(it):
                pp, is_other, kb, c0, sb0, chunks = geom(it)
                tmp = (wA, wB)[it % 2]
                Rtmp = (R_w["wA"], R_w["wB"])[it % 2]
                PT = (hA, hB)[it % 2]
                RPT = (R_w["hA"], R_w["hB"])[it % 2]
                for (a, b_) in chunks:
                    bk = sb0 + a // 512
                    ns = (b_ - a) // 128
                    s0 = a // 128
                    p.op("dve", lambda e, bk=bk, a=a, b_=b_, ns=ns, s0=s0, tmp=tmp: e.scalar_tensor_tensor(
                        out=tmp[:, a:b_].rearrange("p (s t) -> p s t", s=ns),
                        in0=banks[bk][:, a % 512:a % 512 + (b_ - a)].rearrange("p (s t) -> p s t", s=ns),
                        scalar=SCALE,
                        in1=PB[:, h, s0:s0 + ns].unsqueeze(2).to_broadcast([128, ns, 128]),
                        op0=ALU.mult, op1=ALU.add), reads=[RB[bk], R_F2], writes=[Rtmp])
                p.op("act", lambda e, c0=c0, kb=kb, tmp=tmp, PT=PT: e.activation(
                    out=PT[:, c0:1024], in_=tmp[:, c0:1024], func=AF.Exp, bias=Fneg[:, kb, h:h + 1], scale=1.0),
                    reads=[Rtmp, R_F2], writes=[RPT])
                mk = other_first(pp) if is_other else MLE_b
                p.op("pool", lambda e, c0=c0, mk=mk, PT=PT: e.tensor_tensor(
                    out=PT[:, c0:c0 + 128], in0=PT[:, c0:c0 + 128], in1=mk, op=ALU.mult),
                    reads=[RPT, R_cst], writes=[RPT])

            def pv(it):
                pp, is_other, kb, c0, sb0, chunks = geom(it)
                PT = (hA, hB)[it % 2]
                RPT = (R_w["hA"], R_w["hB"])[it % 2]
                for (a, b_) in chunks:
                    q = a // 512
                    last = is_other and ((pp == 3 and q == 0) or pp == 7)
                    p.op("pe", lambda e, q=q, a=a, b_=b_, kb=kb, last=last, PT=PT: e.matmul(
                        out=banks[4 + q][:, a % 512:a % 512 + (b_ - a)], lhsT=Vg[:, kb, hh * 128:(hh + 1) * 128],
                        rhs=PT[:, a:b_], start=False, stop=last), reads=[R_Vg, RPT], writes=[RB[4 + q]])
                    p.op("pe", lambda e, q=q, a=a, b_=b_, last=last, PT=PT: e.matmul(
                        out=banks[6 + q][:, a % 512:a % 512 + (b_ - a)], lhsT=ones_b,
                        rhs=PT[:, a:b_], start=False, stop=last), reads=[RPT, R_cst], writes=[RB[6 + q]])

            qk(0)
            for it in range(16):
                if it + 1 < 16:
                    qk(it + 1)
                elem(it)
                pv(it)
            for q in range(2):
                cs = slice(q * 512, (q + 1) * 512)
                p.op("dve", lambda e, q=q, cs=cs: e.reciprocal(out=wC[:, cs], in_=banks[6 + q][:, :]),
                     reads=[RB[6 + q]], writes=[R_w["wC"]])
                p.op("dve", lambda e, q=q, cs=cs: e.tensor_tensor(out=wC[:, cs], in0=banks[4 + q][:, :], in1=wC[:, cs],
                                                                 op=ALU.mult),
                     reads=[RB[4 + q], R_w["wC"]], writes=[R_w["wC"]])
                p.op("pool", lambda e, cs=cs: e.tensor_tensor(out=mixT[:, h, cs], in0=wC[:, cs], in1=GT[:, hh, cs],
                                                              op=ALU.mult),
                     reads=[R_w["wC"], R_GT[hh]], writes=[R_mix[h]])

        def sb_head(hh, h):
            SPS, SPSb = wE, hE
            for q in range(2):
                p.op("pe", lambda e, q=q: e.matmul(out=banks[6 + q][:, :], lhsT=zero_b, rhs=QT[:, hh, q * 512:(q + 1) * 512],
                                                    start=True, stop=False), reads=[R_QT[hh], R_cst], writes=[RB[6 + q]])
            p.op("pool", lambda e: e.memset(SPS, 0.0), writes=[R_w["wE"]])
            p.op("pool", lambda e: e.memset(SPSb, 0.0), writes=[R_w["hE"]])
            for pp in range(7, -1, -1):
                c0 = pp * 128
                chunks = col_chunks(c0, 1024)
                blocks = ((pp, 0, wA, "wA", hA, "hA", MLT_b), (8 + pp, 2, wB, "wB", hB, "hB", other_first(pp)))
                for (kb, zb, sp, spn, spm, spmn, mk) in blocks:
                    for (a, b_) in chunks:
                        bk = zb + a // 512
                        p.op("pe", lambda e, bk=bk, a=a, b_=b_, kb=kb: e.matmul(
                            out=banks[bk][:, a % 512:a % 512 + (b_ - a)], lhsT=KT[:, hh, kb * 128:(kb + 1) * 128],
                            rhs=QT[:, hh, a:b_], start=True, stop=True),
                            reads=[R_KT[hh], R_QT[hh]], writes=[RB[bk]])
                    for (a, b_) in chunks:
                        bk = zb + a // 512
                        p.op("act", lambda e, bk=bk, a=a, b_=b_: e.activation(
                            out=wC[:, a:b_], in_=banks[bk][:, a % 512:a % 512 + (b_ - a)], func=AF.Exp, scale=SCALE),
                            reads=[RB[bk]], writes=[R_w["wC"]])
                    p.op("act", lambda e, c0=c0, sp=sp: e.activation(out=sp[:, c0:1024], in_=wC[:, c0:1024], func=AF.Ln,
                                                                     bias=1.0),
                         reads=[R_w["wC"]], writes=[R_w[spn]])
                    p.op("pool", lambda e, c0=c0, sp=sp, spm=spm, mk=mk: e.tensor_tensor(
                        out=spm[:, c0:c0 + 128], in0=sp[:, c0:c0 + 128], in1=mk, op=ALU.mult),
                        reads=[R_w[spn], R_cst], writes=[R_w[spmn]])
                    if c0 + 128 < 1024:
                        p.op("pool", lambda e, c0=c0, sp=sp, spm=spm: e.tensor_copy(
                            out=spm[:, c0 + 128:1024], in_=sp[:, c0 + 128:1024]),
                            reads=[R_w[spn]], writes=[R_w[spmn]])
                for bi, (kb, zb, sp, spn, spm, spmn, mk) in enumerate(blocks):
                    ospm, ospmn = (hB, "hB") if bi == 0 else (hA, "hA")
                    Et = own_first(pp) if bi == 0 else other_first(pp)
                    for (a, b_) in chunks:
                        bk = 4 + a // 512
                        o = banks[bk][:, a % 512:a % 512 + (b_ - a)]
                        p.op("pe", lambda e, o=o, a=a, b_=b_, spm=spm: e.matmul(out=o, lhsT=SL_b, rhs=spm[:, a:b_],
                                                                                 start=True, stop=False),
                             reads=[R_w[spmn], R_cst], writes=[RB[bk]])
                        p.op("pe", lambda e, o=o, a=a, b_=b_: e.matmul(out=o, lhsT=ones_b, rhs=SPSb[:, a:b_],
                                                                       start=False, stop=False),
                             reads=[R_w["hE"], R_cst], writes=[RB[bk]])
                        p.op("pe", lambda e, o=o, a=a, b_=b_, Et=Et, ospm=ospm: e.matmul(
                            out=o, lhsT=Et, rhs=ospm[:, a:b_], start=False, stop=True),
                            reads=[R_w[ospmn], R_cst], writes=[RB[bk]])
                    u, un = (wC, "wC") if bi == 0 else (wD, "wD")
                    aT, aTn = (hC, "hC") if bi == 0 else (hD, "hD")
                    for (a, b_) in chunks:
                        bz = zb + a // 512
                        bc = 4 + a // 512
                        p.op("dve", lambda e, bz=bz, a=a, b_=b_, sp=sp, u=u: e.scalar_tensor_tensor(
                            out=u[:, a:b_], in0=banks[bz][:, a % 512:a % 512 + (b_ - a)], scalar=SCALE, in1=sp[:, a:b_],
                            op0=ALU.mult, op1=ALU.subtract), reads=[RB[bz], R_w[spn]], writes=[R_w[un]])
                        p.op("dve", lambda e, bc=bc, a=a, b_=b_, u=u: e.tensor_tensor(
                            out=u[:, a:b_], in0=u[:, a:b_], in1=banks[bc][:, a % 512:a % 512 + (b_ - a)],
                            op=ALU.subtract), reads=[RB[bc], R_w[un]], writes=[R_w[un]])
                    p.op("act", lambda e, c0=c0, u=u, aT=aT: e.activation(out=aT[:, c0:1024], in_=u[:, c0:1024],
                                                                         func=AF.Exp),
                         reads=[R_w[un]], writes=[R_w[aTn]])
                    p.op("pool", lambda e, c0=c0, aT=aT, mk=mk: e.tensor_tensor(
                        out=aT[:, c0:c0 + 128], in0=aT[:, c0:c0 + 128], in1=mk, op=ALU.mult),
                        reads=[R_w[aTn], R_cst], writes=[R_w[aTn]])
                    for (a, b_) in chunks:
                        q = a // 512
                        last = (pp == 0 and bi == 1)
                        p.op("pe", lambda e, q=q, a=a, b_=b_, kb=kb, last=last, aT=aT: e.matmul(
                            out=banks[6 + q][:, a % 512:a % 512 + (b_ - a)], lhsT=Vg[:, kb, hh * 128:(hh + 1) * 128],
                            rhs=aT[:, a:b_], start=False, stop=last), reads=[R_Vg, R_w[aTn]], writes=[RB[6 + q]])
                if pp > 0:
                    for (spm, spmn) in ((hA, "hA"), (hB, "hB")):
                        p.op("pool", lambda e, c0=c0, spm=spm: e.tensor_tensor(
                            out=SPS[:, c0:1024], in0=SPS[:, c0:1024], in1=spm[:, c0:1024], op=ALU.add),
                            reads=[R_w[spmn], R_w["wE"]], writes=[R_w["wE"]])
                    p.op("pool", lambda e, c0=c0: e.tensor_copy(out=SPSb[:, c0:1024], in_=SPS[:, c0:1024]),
                         reads=[R_w["wE"]], writes=[R_w["hE"]])
            for q in range(2):
                cs = slice(q * 512, (q + 1) * 512)
                p.op("dve", lambda e, q=q, cs=cs: e.tensor_tensor(out=mixT[:, 8 + h, cs], in0=banks[6 + q][:, :],
                                                                 in1=GT[:, hh, cs], op=ALU.mult),
                     reads=[RB[6 + q], R_GT[hh]], writes=[R_mix[8 + h]])

        R_ch = [{k: Res(f"ch{ci}_{k}") for k in ("wA", "wB", "wC", "wD", "wE", "hA", "hB", "hC", "hD", "hE")}
                for ci in range(2)]

        def sb_chain(ci, hh, h):
            zb, cb, ob = 4 * ci, 4 * ci + 2, 4 * ci + 3
            wo_ = 512 * ci
            Rc = R_ch[ci]

            def W(buf):
                return buf[:, wo_:wo_ + 512]
            spA, spB, uA, uB, SPS = W(wA), W(wB), W(wC), W(wD), W(wE)
            spmA, spmB, aA, aB, SPSb = W(hA), W(hB), W(hC), W(hD), W(hE)
            for base, pmax in ((512, 7), (0, 3)):
                p.op("pe", lambda e, base=base: e.matmul(out=banks[ob][:, :], lhsT=zero_b, rhs=QT[:, hh, base:base + 512],
                                                        start=True, stop=False),
                     reads=[R_QT[hh], R_cst], writes=[RB[ob]])
                p.op("pool", lambda e: e.memset(SPS, 0.0), writes=[Rc["wE"]])
                p.op("pool", lambda e: e.memset(SPSb, 0.0), writes=[Rc["hE"]])
                yield
                for pp in range(pmax, -1, -1):
                    lo = max(pp * 128, base) - base
                    n = 512 - lo
                    diag = pp * 128 >= base
                    g0 = base + lo
                    blocks = ((pp, 0, spA, "wA", spmA, "hA", MLT_b, uA, "wC", aA, "hC"),
                              (8 + pp, 1, spB, "wB", spmB, "hB", other_first(pp), uB, "wD", aB, "hD"))
                    for (kb, zi, sp, spn, spm, spmn, mk, u, un, aT, aTn) in blocks:
                        p.op("pe", lambda e, zi=zi, kb=kb, lo=lo, g0=g0, base=base: e.matmul(
                            out=banks[zb + zi][:, lo:512], lhsT=KT[:, hh, kb * 128:(kb + 1) * 128],
                            rhs=QT[:, hh, g0:base + 512], start=True, stop=True),
                            reads=[R_KT[hh], R_QT[hh]], writes=[RB[zb + zi]])
                    yield
                    for (kb, zi, sp, spn, spm, spmn, mk, u, un, aT, aTn) in blocks:
                        p.op("act", lambda e, zi=zi, lo=lo, u=u: e.activation(out=u[:, lo:512], in_=banks[zb + zi][:, lo:512],
                                                                             func=AF.Exp, scale=SCALE),
                             reads=[RB[zb + zi]], writes=[Rc[un]])
                        p.op("act", lambda e, lo=lo, sp=sp, u=u: e.activation(out=sp[:, lo:512], in_=u[:, lo:512],
                                                                             func=AF.Ln, bias=1.0),
                             reads=[Rc[un]], writes=[Rc[spn]])
                    yield
                    for (kb, zi, sp, spn, spm, spmn, mk, u, un, aT, aTn) in blocks:
                        if diag:
                            p.op("dve", lambda e, lo=lo, sp=sp, spm=spm, mk=mk: e.tensor_tensor(
                                out=spm[:, lo:lo + 128], in0=sp[:, lo:lo + 128], in1=mk, op=ALU.mult),
                                reads=[Rc[spn], R_cst], writes=[Rc[spmn]])
                            if lo + 128 < 512:
                                p.op("dve", lambda e, lo=lo, sp=sp, spm=spm: e.tensor_copy(
                                    out=spm[:, lo + 128:512], in_=sp[:, lo + 128:512]),
                                    reads=[Rc[spn]], writes=[Rc[spmn]])
                        else:
                            p.op("dve", lambda e, lo=lo, sp=sp, spm=spm: e.tensor_copy(out=spm[:, lo:512], in_=sp[:, lo:512]),
                                 reads=[Rc[spn]], writes=[Rc[spmn]])
                        p.op("dve", lambda e, zi=zi, lo=lo, sp=sp, u=u: e.scalar_tensor_tensor(
                            out=u[:, lo:512], in0=banks[zb + zi][:, lo:512], scalar=SCALE, in1=sp[:, lo:512],
                            op0=ALU.mult, op1=ALU.subtract), reads=[RB[zb + zi], Rc[spn]], writes=[Rc[un]])
                    yield
                    for bi, (kb, zi, sp, spn, spm, spmn, mk, u, un, aT, aTn) in enumerate(blocks):
                        ospm, ospmn = (spmB, "hB") if bi == 0 else (spmA, "hA")
                        Et = own_first(pp) if bi == 0 else other_first(pp)
                        o = banks[zb + zi][:, lo:512]
                        p.op("pe", lambda e, o=o, lo=lo, spm=spm: e.matmul(out=o, lhsT=SL_b, rhs=spm[:, lo:512],
                                                                          start=True, stop=False),
                             reads=[Rc[spmn], R_cst], writes=[RB[zb + zi]])
                        p.op("pe", lambda e, o=o, lo=lo: e.matmul(out=o, lhsT=ones_b, rhs=SPSb[:, lo:512],
                                                                  start=False, stop=False),
                             reads=[Rc["hE"], R_cst], writes=[RB[zb + zi]])
                        p.op("pe", lambda e, o=o, lo=lo, Et=Et, ospm=ospm: e.matmul(out=o, lhsT=Et, rhs=ospm[:, lo:512],
                                                                                    start=False, stop=True),
                             reads=[Rc[ospmn], R_cst], writes=[RB[zb + zi]])
                    yield
                    for (kb, zi, sp, spn, spm, spmn, mk, u, un, aT, aTn) in blocks:
                        p.op("dve", lambda e, zi=zi, lo=lo, u=u: e.tensor_tensor(
                            out=u[:, lo:512], in0=u[:, lo:512], in1=banks[zb + zi][:, lo:512], op=ALU.subtract),
                            reads=[RB[zb + zi], Rc[un]], writes=[Rc[un]])
                    yield
                    for (kb, zi, sp, spn, spm, spmn, mk, u, un, aT, aTn) in blocks:
                        p.op("act", lambda e, lo=lo, u=u, aT=aT: e.activation(out=aT[:, lo:512], in_=u[:, lo:512], func=AF.Exp),
                             reads=[Rc[un]], writes=[Rc[aTn]])
                        if diag:
                            p.op("pool", lambda e, lo=lo, aT=aT, mk=mk: e.tensor_tensor(
                                out=aT[:, lo:lo + 128], in0=aT[:, lo:lo + 128], in1=mk, op=ALU.mult),
                                reads=[Rc[aTn], R_cst], writes=[Rc[aTn]])
                    yield
                    for bi, (kb, zi, sp, spn, spm, spmn, mk, u, un, aT, aTn) in enumerate(blocks):
                        last = (pp == 0 and bi == 1)
                        p.op("pe", lambda e, lo=lo, kb=kb, last=last, aT=aT: e.matmul(
                            out=banks[ob][:, lo:512], lhsT=Vg[:, kb, hh * 128:(hh + 1) * 128],
                            rhs=aT[:, lo:512], start=False, stop=last), reads=[R_Vg, Rc[aTn]], writes=[RB[ob]])
                    yield
                    if pp > 0:
                        for (spm, spmn) in ((spmA, "hA"), (spmB, "hB")):
                            p.op("pool", lambda e, lo=lo, spm=spm: e.tensor_tensor(
                                out=SPS[:, lo:512], in0=SPS[:, lo:512], in1=spm[:, lo:512], op=ALU.add),
                                reads=[Rc[spmn], Rc["wE"]], writes=[Rc["wE"]])
                        p.op("pool", lambda e, lo=lo: e.tensor_copy(out=SPSb[:, lo:512], in_=SPS[:, lo:512]),
                             reads=[Rc["wE"]], writes=[Rc["hE"]])
                        yield
                p.op("dve", lambda e, base=base: e.tensor_tensor(out=mixT[:, 8 + h, base:base + 512], in0=banks[ob][:, :],
                                                                 in1=GT[:, hh, base:base + 512], op=ALU.mult),
                     reads=[RB[ob], R_GT[hh]], writes=[R_mix[8 + h]])
                yield

        def fox_chain(ci, hh, h):
            sbk, ob, db = 4 * ci, 4 * ci + 2, 4 * ci + 3
            wo_ = 512 * ci
            Rc = R_ch[ci]

            def W(buf):
                return buf[:, wo_:wo_ + 512]
            tmps = (W(wA), W(wB))
            Rtmps = (Rc["wA"], Rc["wB"])
            PTs = (W(hA), W(hB))
            RPTs = (Rc["hA"], Rc["hB"])
            fin_ = W(wC)
            for base, pmax in ((512, 7), (0, 3)):
                for bk_ in (ob, db):
                    p.op("pe", lambda e, bk_=bk_, base=base: e.matmul(out=banks[bk_][:, :], lhsT=zero_b,
                                                                    rhs=QT[:, hh, base:base + 512], start=True, stop=False),
                         reads=[R_QT[hh], R_cst], writes=[RB[bk_]])
                its = [(pp, io) for pp in range(pmax + 1) for io in (False, True)]
                nit = len(its)

                def geom(it, base=base, its=its):
                    pp, is_other = its[it]
                    kb = 8 + pp if is_other else pp
                    lo = max(pp * 128, base) - base
                    return pp, is_other, kb, lo, pp * 128 >= base

                def qk(it, base=base, geom=geom):
                    pp, is_other, kb, lo, diag = geom(it)
                    bk = sbk + it % 2
                    p.op("pe", lambda e, bk=bk, kb=kb, lo=lo, base=base: e.matmul(
                        out=banks[bk][:, lo:512], lhsT=KT[:, hh, kb * 128:(kb + 1) * 128],
                        rhs=QT[:, hh, base + lo:base + 512], start=True, stop=True),
                        reads=[R_KT[hh], R_QT[hh]], writes=[RB[bk]])

                qk(0)
                yield
                for it in range(nit):
                    pp, is_other, kb, lo, diag = geom(it)
                    bk = sbk + it % 2
                    tmp, Rtmp, PT, RPT = tmps[it % 2], Rtmps[it % 2], PTs[it % 2], RPTs[it % 2]
                    if it + 1 < nit:
                        qk(it + 1)
                    ns = (512 - lo) // 128
                    s0 = (base + lo) // 128
                    p.op("dve", lambda e, bk=bk, lo=lo, ns=ns, s0=s0, tmp=tmp: e.scalar_tensor_tensor(
                        out=tmp[:, lo:512].rearrange("p (s t) -> p s t", s=ns),
                        in0=banks[bk][:, lo:512].rearrange("p (s t) -> p s t", s=ns), scalar=SCALE,
                        in1=PB[:, h, s0:s0 + ns].unsqueeze(2).to_broadcast([128, ns, 128]),
                        op0=ALU.mult, op1=ALU.add), reads=[RB[bk], R_F2], writes=[Rtmp])
                    yield
                    p.op("act", lambda e, lo=lo, kb=kb, tmp=tmp, PT=PT: e.activation(
                        out=PT[:, lo:512], in_=tmp[:, lo:512], func=AF.Exp, bias=Fneg[:, kb, h:h + 1], scale=1.0),
                        reads=[Rtmp, R_F2], writes=[RPT])
                    if diag:
                        mk = other_first(pp) if is_other else MLE_b
                        p.op("pool", lambda e, lo=lo, mk=mk, PT=PT: e.tensor_tensor(
                            out=PT[:, lo:lo + 128], in0=PT[:, lo:lo + 128], in1=mk, op=ALU.mult),
                            reads=[RPT, R_cst], writes=[RPT])
                    yield
                    last = (it == nit - 1)
                    p.op("pe", lambda e, lo=lo, kb=kb, last=last, PT=PT: e.matmul(
                        out=banks[ob][:, lo:512], lhsT=Vg[:, kb, hh * 128:(hh + 1) * 128], rhs=PT[:, lo:512],
                        start=False, stop=last), reads=[R_Vg, RPT], writes=[RB[ob]])
                    p.op("pe", lambda e, lo=lo, last=last, PT=PT: e.matmul(
                        out=banks[db][:, lo:512], lhsT=ones_b, rhs=PT[:, lo:512], start=False, stop=last),
                        reads=[RPT, R_cst], writes=[RB[db]])
                    yield
                p.op("dve", lambda e: e.reciprocal(out=fin_, in_=banks[db][:, :]), reads=[RB[db]], writes=[Rc["wC"]])
                p.op("dve", lambda e: e.tensor_tensor(out=fin_, in0=banks[ob][:, :], in1=fin_, op=ALU.mult),
                     reads=[RB[ob], Rc["wC"]], writes=[Rc["wC"]])
                p.op("pool", lambda e, base=base: e.tensor_tensor(out=mixT[:, h, base:base + 512], in0=fin_,
                                                                  in1=GT[:, hh, base:base + 512], op=ALU.mult),
                     reads=[Rc["wC"], R_GT[hh]], writes=[R_mix[h]])
                yield

        def run_interleaved(gens):
            gens = list(gens)
            while gens:
                for g_ in list(gens):
                    try:
                        next(g_)
                    except StopIteration:
                        gens.remove(g_)

        if stage >= 2:
            groups = [(m, h0) for m in range(2) for h0 in range(0, 8, 2)][:NGROUPS]
            issue_kvq(0, 0, "q")
            for gi, (m, h0) in enumerate(groups):
                project_group(m, h0)
                if gi + 1 < len(groups):
                    issue_kvq(*groups[gi + 1])
                if stage >= 3:
                    if m == 0:
                        if "oldfox" in DBG:
                            for hh in range(2):
                                fox_head(hh, h0 + hh)
                        else:
                            run_interleaved([fox_chain(0, 0, h0), fox_chain(1, 1, h0 + 1)])
                    elif "oldsb" in DBG:
                        pass
                    if m == 1 and h0 == 0 and ("oldsb" in DBG) != ("oldfox" in DBG):
                        p.barrier(scratch)
                    if m == 0:
                        pass
                    elif "oldsb" in DBG:
                        for hh in range(2):
                            sb_head(hh, h0 + hh)
                    else:
                        run_interleaved([sb_chain(0, 0, h0), sb_chain(1, 1, h0 + 1)])

        p.barrier(scratch)
        Alloc.top = mark0

        def sample_attention():
            offA = [hT_off]

            def allocA(shape, dt):
                esz = 4 if dt == F32 else 2
                n = int(np.prod(shape[1:]))
                off = offA[0]
                offA[0] += (n * esz + 63) // 64 * 64
                assert offA[0] <= hT_off + 16 * NTOK * 2
                return view_at(off, shape, dt)
            Kc = [allocA([128, 8, 1024], BF16) for _ in range(2)]
            Vc = [allocA([128, 8, 1024], BF16) for _ in range(2)]
            R_Kc = [[Res("Kc0a"), Res("Kc0b")], [Res("Kc1a"), Res("Kc1b")]]
            R_Vc = [[Res("Vc0a"), Res("Vc0b")], [Res("Vc1a"), Res("Vc1b")]]
            KcT = alloc([128, 8, 1024], BF16, "KcT")
            R_KcT = [Res(f"KcT{h}") for h in range(8)]
            clfT = alloc([128, 8, 32], F32, "clfT")
            csuf = alloc([128, 8, 32], F32, "csuf")
            Gsuf = alloc([128, 8, 32], F32, "Gsuf")
            Gnew = alloc([128, 8], F32, "Gnew")
            R_G = Res("G")
            sw1 = alloc([128, 1024], F32, "sw1")
            sw2 = alloc([128, 1024], F32, "sw2")
            sP = [alloc([128, 1024], BF16, f"sP{i}") for i in range(2)]
            R_sP = [Res("sP0"), Res("sP1")]
            R_sw1, R_sw2 = Res("sw1"), Res("sw2")
            Ssuf = alloc([128, 8, 128], F32, "Ssuf")
            Ssufb = alloc([128, 8, 128], BF16, "Ssufb")
            R_Ssuf = Res("Ssuf")
            nw1 = alloc([128, 512], F32, "nw1")
            nw2 = alloc([128, 512], F32, "nw2")
            Pn = alloc([128, 8, 64], BF16, "Pn")
            spmn = alloc([128, 8, 64], BF16, "spmn")
            R_n = {k: Res(k) for k in ("nw1", "nw2", "Pn", "spmn")}
            fin = alloc([128, 512], F32, "fin")
            R_fin = Res("fin")
            BDLE = cst_b[0:64, I_BDLE, 0:64]
            BDLT = cst_b[0:64, I_BDLT, 0:64]
            for b in range(4):
                p.dma("sp", clfT[:, :, b * 8:(b + 1) * 8], clf[b].rearrange("(t p) h -> p t h", p=128), writes=[R_G])
            p.op("dve", lambda e: e.memset(csuf[:, 7, :], 0.0), writes=[R_G])
            for t in range(6, -1, -1):
                p.op("dve", lambda e, t=t: e.tensor_tensor(out=csuf[:, t, :], in0=csuf[:, t + 1, :], in1=clfT[:, t + 1, :],
                                                          op=ALU.add), reads=[R_G], writes=[R_G])
            bG = 7
            for t in range(8):
                o = banks[bG][:, t * 32:(t + 1) * 32]
                p.op("pe", lambda e, o=o, t=t: e.matmul(out=o, lhsT=SL_f, rhs=clfT[:, t, :], start=True, stop=False),
                     reads=[R_G, R_cst], writes=[RB[bG]])
                p.op("pe", lambda e, o=o, t=t: e.matmul(out=o, lhsT=ones_f, rhs=csuf[:, t, :], start=False, stop=True),
                     reads=[R_G, R_cst], writes=[RB[bG]])
            p.op("pe", lambda e: e.matmul(out=banks[bG][0:64, 256:264], lhsT=cst_f[0:64, I_BDTRI, 0:64],
                                          rhs=logf[0:64, 16, :], start=True, stop=True),
                 reads=[R_F, R_cst], writes=[RB[bG]])
            R_G2 = Res("G2")
            p.op("dve", lambda e: e.tensor_copy(out=Gsuf, in_=banks[bG][:, 0:256].rearrange("p (t n) -> p t n", t=8)),
                 reads=[RB[bG]], writes=[R_G2])
            p.op("dve", lambda e: e.tensor_scalar(out=Gnew[0:64, :], in0=banks[bG][0:64, 256:264], scalar1=-1.0,
                                                  scalar2=None, op0=ALU.mult), reads=[RB[bG]], writes=[R_G2])

            if os.environ.get("DBGOUT"):
                dbg4 = nc.dram_tensor("dbg4", [128, 3072], F32, kind="ExternalOutput").ap()
                d4 = alloc([128, 3072], F32, "d4")
                R_d4 = Res("d4")
                p.op("dve", lambda e: e.tensor_copy(out=d4[:, 0:1024], in_=QTs.rearrange("p h q -> p (h q)")), reads=[R_samp], writes=[R_d4])
                p.op("dve", lambda e: e.tensor_copy(out=d4[:, 1024:2048], in_=KTs.rearrange("p h q -> p (h q)")), reads=[R_samp], writes=[R_d4])
                p.op("dve", lambda e: e.tensor_copy(out=d4[:, 2048:3072], in_=GTs.rearrange("p h q -> p (h q)")), reads=[R_samp], writes=[R_d4])
                p.dma("sp", dbg4, d4, reads=[R_d4])
                dbg2 = nc.dram_tensor("dbg2", [128, 264], F32, kind="ExternalOutput").ap()
                p.dma("sp", dbg2[:, 0:256], Gsuf.rearrange("p t n -> p (t n)"), reads=[R_G2])
                p.dma("sp", dbg2[0:64, 256:264], Gnew[0:64, :], reads=[R_G2])
            def issue_cache(j):
                m_, b_ = j // 4, j % 4
                buf_ = j % 2
                for half in range(2):
                    rows = slice(b_ * 1024 + half * 512, b_ * 1024 + (half + 1) * 512)
                    p.dma("pool", Kc[buf_][:, half * 4:(half + 1) * 4, :],
                          ck[m_][rows, :].rearrange("(t p) n -> p t n", p=128), writes=[R_Kc[buf_][half]])
                    p.dma("pool", Vc[buf_][:, half * 4:(half + 1) * 4, :],
                          cv[m_][rows, :].rearrange("(t p) n -> p t n", p=128), writes=[R_Vc[buf_][half]])

            def tr(j):
                buf_ = j % 2
                for h in range(8):
                    bk = h % 2
                    pv = bank_bf(bk)
                    for t in range(8):
                        p.op("pe", lambda e, pv=pv, t=t, h=h, buf_=buf_: e.transpose(
                            out=pv[:, t * 128:(t + 1) * 128], in_=Kc[buf_][:, t, h * 128:(h + 1) * 128],
                            identity=ident_b), reads=[R_Kc[buf_][t // 4], R_cst], writes=[RB[bk]])
                    evac(KcT[:, h, :], pv, [RB[bk]], [R_KcT[h]], eng=("dve" if h % 2 == 0 else "act"))

            issue_cache(0)
            tr(0)
            for m in range(2):
                bO, bD, bN, bC = 4, 5, 6, 7
                hb = m * 8
                for h in range(8):
                    p.op("pe", lambda e, h=h, hb=hb: e.matmul(out=banks[bN][0:64, h * 64:(h + 1) * 64], lhsT=KTs[:, hb + h, :],
                                                        rhs=QTs[:, hb + h, :], start=True, stop=True),
                         reads=[R_samp], writes=[RB[bN]])
                if m == 0:
                    p.op("dve", lambda e: e.scalar_tensor_tensor(
                        out=nw1[0:64, :].rearrange("p (h q) -> p h q", h=8),
                        in0=banks[bN][0:64, :].rearrange("p (h q) -> p h q", h=8), scalar=SCALE,
                        in1=Gnew[0:64, :].unsqueeze(2).to_broadcast([64, 8, 64]), op0=ALU.mult, op1=ALU.add),
                        reads=[RB[bN], R_G2], writes=[R_n["nw1"]])
                    p.op("act", lambda e: e.activation(out=Pn[0:64, :, :], in_=nw1[0:64, :].rearrange("p (h q) -> p h q", h=8),
                                                       func=AF.Exp), reads=[R_n["nw1"]], writes=[R_n["Pn"]])
                    p.op("pool", lambda e: e.tensor_tensor(out=Pn[0:64, :, :], in0=Pn[0:64, :, :],
                                                           in1=BDLE.unsqueeze(1).to_broadcast([64, 8, 64]), op=ALU.mult),
                         reads=[R_n["Pn"], R_cst], writes=[R_n["Pn"]])
                    p.op("pe", lambda e: e.matmul(out=banks[bN][:, :], lhsT=zero_b, rhs=cst_b[:, 0:4, :], start=True,
                                                  stop=False), reads=[R_cst], writes=[RB[bN]])
                    p.op("pe", lambda e: e.matmul(out=banks[bD][:, :], lhsT=ones_b[0:64, :],
                                                  rhs=Pn[0:64, :, :], start=True, stop=True),
                         reads=[R_n["Pn"], R_cst], writes=[RB[bD]])
                else:
                    p.op("act", lambda e: e.activation(out=nw1[0:64, :], in_=banks[bN][0:64, :], func=AF.Exp, scale=SCALE),
                         reads=[RB[bN]], writes=[R_n["nw1"]])
                    p.op("act", lambda e: e.activation(out=nw1[0:64, :], in_=nw1[0:64, :], func=AF.Ln, bias=1.0),
                         reads=[R_n["nw1"]], writes=[R_n["nw1"]])
                    p.op("pool", lambda e: e.tensor_tensor(out=spmn[0:64, :, :],
                                                           in0=nw1[0:64, :].rearrange("p (h q) -> p h q", h=8),
                                                           in1=BDLT.unsqueeze(1).to_broadcast([64, 8, 64]), op=ALU.mult),
                         reads=[R_n["nw1"], R_cst], writes=[R_n["spmn"]])
                    p.op("pe", lambda e: e.matmul(out=banks[bD][0:64, :], lhsT=SL_b[0:64, 0:64], rhs=spmn[0:64, :, :],
                                                  start=True, stop=True), reads=[R_n["spmn"], R_cst], writes=[RB[bD]])
                    p.op("dve", lambda e: e.scalar_tensor_tensor(out=nw2[0:64, :], in0=banks[bN][0:64, :], scalar=SCALE,
                                                                 in1=nw1[0:64, :], op0=ALU.mult, op1=ALU.subtract),
                         reads=[RB[bN], R_n["nw1"]], writes=[R_n["nw2"]])
                    p.op("dve", lambda e: e.tensor_tensor(out=nw2[0:64, :], in0=nw2[0:64, :], in1=banks[bD][0:64, :],
                                                          op=ALU.subtract), reads=[RB[bD], R_n["nw2"]], writes=[R_n["nw2"]])
                    p.op("act", lambda e: e.activation(out=Pn[0:64, :, :], in_=nw2[0:64, :].rearrange("p (h q) -> p h q", h=8),
                                                       func=AF.Exp), reads=[R_n["nw2"]], writes=[R_n["Pn"]])
                    p.op("pool", lambda e: e.tensor_tensor(out=Pn[0:64, :, :], in0=Pn[0:64, :, :],
                                                           in1=BDLT.unsqueeze(1).to_broadcast([64, 8, 64]), op=ALU.mult),
                         reads=[R_n["Pn"], R_cst], writes=[R_n["Pn"]])
                p.op("pe", lambda e: e.matmul(out=banks[bO][:, :], lhsT=zero_b, rhs=cst_b[:, 0:4, :], start=True, stop=False),
                     reads=[R_cst], writes=[RB[bO]])
                for h in range(8):
                    p.op("pe", lambda e, h=h, hb=hb: e.matmul(out=banks[bO][:, h * 64:(h + 1) * 64], lhsT=Vs[0:64, hb + h, :],
                                                        rhs=Pn[0:64, h, :], start=False, stop=False),
                         reads=[R_samp, R_n["Pn"]], writes=[RB[bO]])
                for b in range(4):
                    buf = (m * 4 + b) % 2
                    if m * 4 + b + 1 < 8:
                        issue_cache(m * 4 + b + 1)
                    if "trpipe" not in DBG and m * 4 + b > 0:
                        tr(m * 4 + b)
                    for t in range(8):
                        bk = 2 + t // 4
                        for h in range(8):
                            c0 = (t % 4) * 128 + h * 16
                            p.op("pe", lambda e, bk=bk, c0=c0, t=t, h=h, b=b, hb=hb: e.matmul(
                                out=banks[bk][:, c0:c0 + 16], lhsT=KcT[:, h, t * 128:(t + 1) * 128],
                                rhs=QTs[:, hb + h, b * 16:(b + 1) * 16], start=True, stop=True),
                                reads=[R_KcT[h], R_samp], writes=[RB[bk]])
                    if m * 4 + b + 1 < 8 and "trpipe" in DBG:
                        tr(m * 4 + b + 1)
                    Pb = sP[b % 2]
                    RPb = R_sP[b % 2]
                    Pb4 = Pb.rearrange("p (t h q) -> p t h q", t=8, h=8)
                    if m == 0:
                        for t in range(8):
                            bk = 2 + t // 4
                            cs = slice((t % 4) * 128, (t % 4 + 1) * 128)
                            p.op("dve", lambda e, t=t, b=b, bk=bk, cs=cs: e.scalar_tensor_tensor(
                                out=sw1[:, t * 128:(t + 1) * 128].rearrange("p (h q) -> p h q", h=8),
                                in0=banks[bk][:, cs].rearrange("p (h q) -> p h q", h=8), scalar=SCALE,
                                in1=Gsuf[:, t, b * 8:(b + 1) * 8].unsqueeze(2).to_broadcast([128, 8, 16]),
                                op0=ALU.mult, op1=ALU.add), reads=[RB[bk], R_G2], writes=[R_sw1])
                        p.op("act", lambda e, Pb=Pb: e.activation(out=Pb, in_=sw1, func=AF.Exp), reads=[R_sw1], writes=[RPb])
                        for t in range(8):
                            p.op("pe", lambda e, t=t, b=b, Pb=Pb: e.matmul(
                                out=banks[bN][:, b * 128:(b + 1) * 128],
                                lhsT=ones_b, rhs=Pb[:, t * 128:(t + 1) * 128], start=False,
                                stop=(b == 3 and t == 7)),
                                reads=[RPb, R_cst], writes=[RB[bN]])
                    else:
                        for hf in range(2):
                            p.op("act", lambda e, hf=hf: e.activation(out=sw1[:, hf * 512:(hf + 1) * 512], in_=banks[2 + hf][:, :],
                                                                      func=AF.Exp, scale=SCALE), reads=[RB[2 + hf]], writes=[R_sw1])
                        p.op("act", lambda e: e.activation(out=sw1, in_=sw1, func=AF.Ln, bias=1.0), reads=[R_sw1], writes=[R_sw1])
                        spb = sP[(b + 1) % 2]
                        Rspb = R_sP[(b + 1) % 2]
                        p.op("pool", lambda e, spb=spb: e.tensor_copy(out=spb, in_=sw1), reads=[R_sw1], writes=[Rspb])
                        spb3 = spb.rearrange("p (t n) -> p t n", t=8)
                        p.op("pool", lambda e: e.memset(Ssuf[:, 7, :], 0.0), writes=[R_Ssuf])
                        for t in range(6, -1, -1):
                            p.op("pool", lambda e, t=t, spb3=spb3: e.tensor_tensor(
                                out=Ssuf[:, t, :], in0=Ssuf[:, t + 1, :], in1=spb3[:, t + 1, :], op=ALU.add),
                                reads=[Rspb, R_Ssuf], writes=[R_Ssuf])
                        p.op("pool", lambda e: e.tensor_copy(out=Ssufb, in_=Ssuf), reads=[R_Ssuf], writes=[R_Ssuf])
                        for t in range(8):
                            bk = 5 + t // 4
                            o = banks[bk][:, (t % 4) * 128:(t % 4 + 1) * 128]
                            p.op("pe", lambda e, o=o, t=t, spb3=spb3: e.matmul(out=o, lhsT=SL_b, rhs=spb3[:, t, :],
                                                                               start=True, stop=False),
                                 reads=[Rspb, R_cst], writes=[RB[bk]])
                            p.op("pe", lambda e, o=o, t=t: e.matmul(out=o, lhsT=ones_b, rhs=Ssufb[:, t, :],
                                                                    start=False, stop=False),
                                 reads=[R_Ssuf, R_cst], writes=[RB[bk]])
                            p.op("pe", lambda e, o=o, b=b: e.matmul(out=o.rearrange("p (h q) -> p h q", h=8),
                                                                    lhsT=ones_b[0:64, :],
                                                                    rhs=spmn[0:64, :, b * 16:(b + 1) * 16],
                                                                    start=False, stop=True),
                                 reads=[R_n["spmn"], R_cst], writes=[RB[bk]])
                        for hf in range(2):
                            cs = slice(hf * 512, (hf + 1) * 512)
                            p.op("dve", lambda e, hf=hf, cs=cs: e.scalar_tensor_tensor(
                                out=sw2[:, cs], in0=banks[2 + hf][:, :], scalar=SCALE, in1=sw1[:, cs],
                                op0=ALU.mult, op1=ALU.subtract), reads=[RB[2 + hf], R_sw1], writes=[R_sw2])
                            p.op("dve", lambda e, hf=hf, cs=cs: e.tensor_tensor(
                                out=sw2[:, cs], in0=sw2[:, cs], in1=banks[5 + hf][:, :], op=ALU.subtract),
                                reads=[RB[5 + hf], R_sw2], writes=[R_sw2])
                        p.op("act", lambda e, Pb=Pb: e.activation(out=Pb, in_=sw2, func=AF.Exp), reads=[R_sw2], writes=[RPb])
                    for t in range(8):
                        for h in range(8):
                            last = (b == 3 and t == 7 and h == 7)
                            p.op("pe", lambda e, t=t, h=h, b=b, buf=buf, last=last, Pb4=Pb4: e.matmul(
                                out=banks[bO][:, h * 64 + b * 16:h * 64 + (b + 1) * 16],
                                lhsT=Vc[buf][:, t, h * 128:(h + 1) * 128], rhs=Pb4[:, t, h, :], start=False, stop=last),
                                reads=[R_Vc[buf][t // 4], RPb], writes=[RB[bO]])
                if m == 0:
                    p.op("dve", lambda e: e.tensor_copy(out=fin, in_=banks[bD][:, :]), reads=[RB[bD]], writes=[R_fin])
                    for b in range(4):
                        fv = fin.rearrange("p (h q) -> p h q", h=8)[:, :, b * 16:(b + 1) * 16]
                        p.op("dve", lambda e, b=b, fv=fv: e.tensor_tensor(
                            out=fv, in0=fv, in1=banks[bN][:, b * 128:(b + 1) * 128].rearrange("p (h q) -> p h q", h=8),
                            op=ALU.add), reads=[RB[bN], R_fin], writes=[R_fin])
                    p.op("dve", lambda e: e.reciprocal(out=fin, in_=fin), reads=[R_fin], writes=[R_fin])
                    p.op("dve", lambda e: e.tensor_tensor(out=fin, in0=banks[bO][:, :], in1=fin, op=ALU.mult),
                         reads=[RB[bO], R_fin], writes=[R_fin])
                else:
                    p.op("dve", lambda e: e.tensor_copy(out=fin, in_=banks[bO][:, :]), reads=[RB[bO]], writes=[R_fin])
                if os.environ.get("DBGOUT") and m == 0:
                    dbg3 = nc.dram_tensor("dbg3", [128, 2048], F32, kind="ExternalOutput").ap()
                    p.dma("sp", dbg3[:, 0:512], fin, reads=[R_fin])
                    p.dma("sp", dbg3[:, 512:1536], sw1, reads=[R_sw1])
                    p.dma("sp", dbg3[0:64, 1536:2048], nw1[0:64, :], reads=[R_n["nw1"]])
                p.op("pool", lambda e, hb=hb: e.tensor_tensor(out=mixT[:, hb:hb + 8, 1024:1088],
                                                              in0=fin.rearrange("p (h q) -> p h q", h=8),
                                                              in1=GTs[:, hb:hb + 8, :], op=ALU.mult),
                     reads=[R_fin, R_samp], writes=[R_mix_s[hb + h_] for h_ in range(8)])

        if stage >= 4:
            sample_attention()
            p.barrier(scratch)
            Alloc.top = mark0

        h1T = alloc([128, 16, NOWN], BF16, "h1T")
        R_h1T = [Res(f"h1T{i}") for i in range(9)]
        mark1 = Alloc.top
        AX = mybir.AxisListType

        pre_v = []

        def out_proj0():
            if stage >= 6 and "nopre" not in DBG:
                for sv_ in range(len(slab_raw)):
                    pre_v.append(load_slab(w_in1, 16, 4096 + sv_ * 256, 256))
            wo = view_at(hT_off, [128, 16, D], BF16)
            R_wo = [Res(f"wo{j}") for j in range(8)]
            srcw = w_out0.rearrange("(c p) n -> p c n", p=128)
            for j in range(8):
                p.dma("pool", wo[:, 2 * j:2 * j + 2, :], srcw[:, 2 * j:2 * j + 2, :], writes=[R_wo[j]])
            hpb = [alloc([128, D], F32, f"hpb{i}") for i in range(2)]
            R_hpb = [Res("hpb0"), Res("hpb1")]
            _x5 = alloc([128, D], BF16, "xn5")
            _r5 = Res("xn5")
            xn5 = [_x5, _x5]
            R_xn5 = [_r5, _r5]
            junk5 = alloc([128, D], BF16, "junk5")
            ssq5 = alloc([128, 32], F32, "ssq5")
            bufs = (xn5, R_xn5, junk5, Res("junk5"), ssq5, Res("ssq5"))
            for i in range(9):
                rows = 128 if i < 8 else 64
                r0 = i * 128 if i < 8 else 2048
                b = i % 2
                p.dma("sp", hpb[b][0:rows, :], xall[r0:r0 + rows, :], writes=[R_hpb[b]])
                Rm = R_mix if i < 8 else R_mix_s
                for q in range(4):
                    bk = 4 + q
                    for hd in range(16):
                        p.op("pe", lambda e, bk=bk, hd=hd, i=i, rows=rows, q=q: e.matmul(
                            out=banks[bk][0:rows, :], lhsT=mixT[:, hd, i * 128:i * 128 + rows],
                            rhs=wo[:, hd, q * 512:(q + 1) * 512], start=(hd == 0), stop=(hd == 15)),
                            reads=[Rm[hd], R_wo[hd // 2]], writes=[RB[bk]])
                    p.op("dve", lambda e, bk=bk, b=b, rows=rows, q=q: e.tensor_tensor(
                        out=hpb[b][0:rows, q * 512:(q + 1) * 512], in0=banks[bk][0:rows, :],
                        in1=hpb[b][0:rows, q * 512:(q + 1) * 512], op=ALU.add),
                        reads=[RB[bk], R_hpb[b]], writes=[R_hpb[b]])
                p.dma("sp", hp_scr[i * 128:i * 128 + rows, :], hpb[b][0:rows, :], reads=[R_hpb[b]])
                if os.environ.get("DBGOUT"):
                    if i == 0:
                        _NC_CACHE["dbg_hp"] = nc.dram_tensor("dbg_hp", [NOWN, D], F32, kind="ExternalOutput").ap()
                    p.dma("sp", _NC_CACHE["dbg_hp"][i * 128:i * 128 + rows, :], hpb[b][0:rows, :], reads=[R_hpb[b]])
                norm_transpose(i, hpb[b], R_hpb[b], rows, 16, h1T, R_h1T[i], i * 128, (i % 2) * 2, bufs=bufs)

        CH = 9 * 128 * 2

        def gT_view(c):
            return view_at(hT_off + c * CH, [128, NOWN], BF16)

        R_vb = [Res(f"vb{c}") for c in range(32)]

        def layer1():
            vb = view_at(hT_off, [128, 32, 9, 128], BF16)
            offB = [hT_off + 32 * CH]

            def allocB(shape, dt):
                esz = 4 if dt == F32 else 2
                n = int(np.prod(shape[1:]))
                off = offB[0]
                offB[0] += (n * esz + 63) // 64 * 64
                assert offB[0] <= mark0, (offB[0], mark0)
                return view_at(off, shape, dt)
            vsf = allocB([128, 4096], F32)
            R_vsf = Res("vsf")
            WspT = allocB([128, 16, 128], BF16)
            RSW = allocB([128, 16, 128], F32)
            bspB = allocB([128, 16, 128], F32)
            R_c1 = Res("l1consts")
            wtmp = vsf[:, 0:2048].rearrange("p (g s) -> p g s", g=16)
            wtb = vsf[:, 2048:3072].bitcast(BF16).rearrange("p (g s) -> p g s", g=16)
            mixed = alloc([128, NOWN], F32, "mixed")
            tprod = alloc([128, NOWN], F32, "tprod")
            szb = alloc([128, NOWN], BF16, "szb")
            bias2 = alloc([128, 128], F32, "bias2")
            BDs = alloc([128, 16, 64], BF16, "BDs")
            st1 = alloc([128, 9, 16], F32, "st1")
            st2 = alloc([128, 9, 16], F32, "st2")
            s1 = alloc([128, 9], F32, "s1")
            s2 = alloc([128, 9], F32, "s2")
            rstd1 = alloc([128, 9], F32, "rstd1")
            nmr1 = alloc([128, 9], F32, "nmr1")
            junk6 = alloc([128, 256], BF16, "junk6")
            R_w6 = {k: Res(k) for k in ("mixed", "tprod", "szb", "bias2", "BDs", "st", "junk6")}
            gamT = lngbT[:, 0:32]
            betT = lngbT[:, 32:64]
            p.dma("sp", wtmp, w_sp.rearrange("g t s -> t g s"), writes=[R_vsf])
            p.dma("sp", bspB.rearrange("p g t -> p (g t)"), b_sp.rearrange("g t -> (g t)").partition_broadcast(128),
                  writes=[R_c1])
            p.op("dve", lambda e: e.tensor_copy(out=wtb, in_=wtmp), reads=[R_vsf], writes=[R_vsf])
            for hf in range(2):
                pv = bank_bf(hf)
                for gg in range(8):
                    g = hf * 8 + gg
                    p.op("pe", lambda e, pv=pv, gg=gg, g=g: e.transpose(out=pv[:, gg * 128:(gg + 1) * 128], in_=wtb[:, g, :],
                                                                          identity=ident_b),
                         reads=[R_vsf, R_cst], writes=[RB[hf]])
                p.op("dve", lambda e, pv=pv, hf=hf: e.tensor_tensor(
                    out=WspT[:, hf * 8:(hf + 1) * 8, :], in0=pv.rearrange("p (g t) -> p g t", g=8),
                    in1=MLE_b.unsqueeze(1).to_broadcast([128, 8, 128]), op=ALU.mult),
                    reads=[RB[hf], R_cst], writes=[R_c1])
            for g4 in range(4):
                bk = 2 + g4 % 2
                p.op("pe", lambda e, bk=bk, g4=g4: e.matmul(out=banks[bk][:, :], lhsT=ones_b,
                                                            rhs=WspT[:, g4 * 4:(g4 + 1) * 4, :], start=True, stop=True),
                     reads=[R_c1, R_cst], writes=[RB[bk]])
                p.op("dve", lambda e, bk=bk, g4=g4: e.tensor_copy(
                    out=RSW[:, g4 * 4:(g4 + 1) * 4, :], in_=banks[bk][:, :].rearrange("p (g t) -> p g t", g=4)),
                    reads=[RB[bk]], writes=[R_c1])
            W16t = alloc([128, 16, 64], BF16, "W16t")
            R_w16 = Res("W16t")
            for bb in range(4):
                p.op("dve", lambda e, bb=bb: e.tensor_copy(out=W16t[0:16, :, 16 * bb:16 * bb + 16], in_=WspT[0:16, :, 0:16]),
                     reads=[R_c1], writes=[R_w16])
            for hf in range(2):
                p.op("pe", lambda e, hf=hf: e.matmul(out=banks[2 + hf][0:64, :], lhsT=cst_b[0:16, I_SEL, 0:64],
                                                     rhs=W16t[0:16, hf * 8:(hf + 1) * 8, :], start=True, stop=True),
                     reads=[R_w16, R_cst], writes=[RB[2 + hf]])
                p.op("dve", lambda e, hf=hf: e.tensor_tensor(
                    out=BDs[0:64, hf * 8:(hf + 1) * 8, :], in0=banks[2 + hf][0:64, :].rearrange("p (g t) -> p g t", g=8),
                    in1=cst_b[0:64, I_SAME, 0:64].unsqueeze(1).to_broadcast([64, 8, 64]), op=ALU.mult),
                    reads=[RB[2 + hf], R_cst], writes=[R_w6["BDs"]])
            p.op("dve", lambda e: e.memset(st1, 0.0), writes=[R_w6["st"]])
            p.op("dve", lambda e: e.memset(st2, 0.0), writes=[R_w6["st"]])
            if stage >= 6:
                for sv in range(16):
                    wvs, Rwvs = pre_v[sv] if sv < len(pre_v) else load_slab(w_in1, 16, 4096 + sv * 256, 256)
                    for i in range(9):
                        rows = 128 if i < 8 else 64
                        bk = next_bank(2, 8)
                        for k in range(16):
                            p.op("pe", lambda e, bk=bk, k=k, i=i, rows=rows, wvs=wvs: e.matmul(
                                out=banks[bk][0:rows, 0:256], lhsT=h1T[:, k, i * 128:i * 128 + rows], rhs=wvs[:, k, :],
                                start=(k == 0), stop=(k == 15)), reads=[R_h1T[i], Rwvs[k]], writes=[RB[bk]])
                        p.op("act", lambda e, bk=bk, i=i, rows=rows, sv=sv: e.activation(
                            out=vb[0:rows, 2 * sv:2 * sv + 2, i, :],
                            in_=banks[bk][0:rows, 0:256].rearrange("p (a b) -> p a b", a=2), func=AF.Copy,
                            accum_out=st1[0:rows, i, sv:sv + 1]),
                            reads=[RB[bk]], writes=[R_vb[2 * sv], R_vb[2 * sv + 1], R_w6["st"]])
                        p.op("act", lambda e, bk=bk, i=i, rows=rows, sv=sv: e.activation(
                            out=junk6[0:rows, :], in_=banks[bk][0:rows, 0:256], func=AF.Square,
                            accum_out=st2[0:rows, i, sv:sv + 1]),
                            reads=[RB[bk]], writes=[R_w6["junk6"], R_w6["st"]])
                        if i == 8:
                            p.op("dve", lambda e, bk=bk, sv=sv: e.tensor_copy(out=vsf[0:64, sv * 256:(sv + 1) * 256],
                                                                              in_=banks[bk][0:64, 0:256]),
                                 reads=[RB[bk]], writes=[R_vsf])
                p.op("dve", lambda e: e.reduce_sum(out=s1, in_=st1, axis=AX.X), reads=[R_w6["st"]], writes=[R_w6["st"]])
                p.op("dve", lambda e: e.reduce_sum(out=s2, in_=st2, axis=AX.X), reads=[R_w6["st"]], writes=[R_w6["st"]])
                p.op("dve", lambda e: e.tensor_scalar(out=s1, in0=s1, scalar1=1.0 / 4096, scalar2=None, op0=ALU.mult),
                     reads=[R_w6["st"]], writes=[R_w6["st"]])
                p.op("dve", lambda e: e.tensor_tensor(out=nmr1, in0=s1, in1=s1, op=ALU.mult),
                     reads=[R_w6["st"]], writes=[R_w6["st"]])
                p.op("dve", lambda e: e.scalar_tensor_tensor(out=s2, in0=s2, scalar=1.0 / 4096, in1=nmr1, op0=ALU.mult,
                                                             op1=ALU.subtract),
                     reads=[R_w6["st"]], writes=[R_w6["st"]])
                p.op("act", lambda e: e.activation(out=rstd1, in_=s2, func=AF.Ln, bias=1e-5), reads=[R_w6["st"]],
                     writes=[R_w6["st"]])
                p.op("act", lambda e: e.activation(out=rstd1, in_=rstd1, func=AF.Exp, scale=-0.5), reads=[R_w6["st"]],
                     writes=[R_w6["st"]])
                p.op("dve", lambda e: e.scalar_tensor_tensor(out=nmr1, in0=s1, scalar=-1.0, in1=rstd1, op0=ALU.mult,
                                                             op1=ALU.mult),
                     reads=[R_w6["st"]], writes=[R_w6["st"]])
                for i in range(9):
                    rows = 128 if i < 8 else 64
                    p.op("dve", lambda e, i=i, rows=rows: e.tensor_scalar(
                        out=vb[0:rows, :, i, :], in0=vb[0:rows, :, i, :], scalar1=rstd1[0:rows, i:i + 1],
                        scalar2=nmr1[0:rows, i:i + 1], op0=ALU.mult, op1=ALU.add),
                        reads=[R_w6["st"]] + R_vb, writes=R_vb)
                gb = mixed[0:64, 0:1024]
                for j in range(8):
                    cs = slice(j * 512, (j + 1) * 512)
                    p.dma("sp", gb[:, 0:512], ln_g[cs].partition_broadcast(64), writes=[R_w6["mixed"]])
                    p.dma("sp", gb[:, 512:1024], ln_b[cs].partition_broadcast(64), writes=[R_w6["mixed"]])
                    p.op("dve", lambda e, cs=cs: e.tensor_scalar(out=vsf[0:64, cs], in0=vsf[0:64, cs], scalar1=rstd1[0:64, 8:9],
                                                                 scalar2=nmr1[0:64, 8:9], op0=ALU.mult, op1=ALU.add),
                         reads=[R_vsf, R_w6["st"]], writes=[R_vsf])
                    p.op("dve", lambda e, cs=cs: e.tensor_tensor(out=vsf[0:64, cs], in0=vsf[0:64, cs], in1=gb[:, 0:512],
                                                                 op=ALU.mult), reads=[R_vsf, R_w6["mixed"]], writes=[R_vsf])
                    p.op("dve", lambda e, cs=cs: e.tensor_tensor(out=vsf[0:64, cs], in0=vsf[0:64, cs], in1=gb[:, 512:1024],
                                                                 op=ALU.add), reads=[R_vsf, R_w6["mixed"]], writes=[R_vsf])
                    p.dma("sp", o_sgu[:, cs], vsf[0:64, cs], reads=[R_vsf])
            if stage >= 7:
                n0 = len(slab_raw)
                for ex in (range(2) if "ext" in DBG else []):
                    slab_raw.append(vsf[:, ex * 2048:(ex + 1) * 2048].bitcast(BF16))
                    R_slab.append([Res(f"slabx{ex}_{q}") for q in range(4)])
                    slab_first[n0 + ex] = [R_vsf]
                us = zs = Rus = Rzs = None
                for c in range(32):
                    g = c // 2
                    if c % 2 == 0:
                        us, Rus = load_slab(w_in1, 16, (c // 2) * 256, 256)
                        zs, Rzs = load_slab(w_in1, 16, 8192 + (c // 2) * 256, 256)
                    wc = slice((c % 2) * 128, (c % 2 + 1) * 128)
                    for i in range(8):
                        bk = 4 + i // 4
                        p.op("pe", lambda e, bk=bk, i=i, c=c, g=g: e.matmul(
                            out=banks[bk][:, (i % 4) * 128:(i % 4 + 1) * 128], lhsT=vb[:, c, i, :], rhs=WspT[:, g, :],
                            start=True, stop=True), reads=[R_vb[c], R_c1], writes=[RB[bk]])
                    p.op("pe", lambda e, c=c, g=g: e.matmul(out=banks[7][:, 128:192], lhsT=vb[0:64, c, 8, :],
                                                            rhs=BDs[0:64, g, :], start=True, stop=True),
                         reads=[R_vb[c], R_w6["BDs"]], writes=[RB[7]])
                    for (slab, Rs, bk0, soff) in ((us, Rus, 0, 0), (zs, Rzs, 2, 64)):
                        for (t0, t1, bk, col) in ((0, 512, bk0, 0), (512, 1024, bk0 + 1, 0), (1024, 1088, 6, soff)):
                            n = t1 - t0
                            Rr = [R_h1T[t] for t in range(t0 // 128, (t1 + 127) // 128)]
                            for k in range(16):
                                p.op("pe", lambda e, bk=bk, col=col, n=n, k=k, wc=wc, t0=t0, t1=t1, slab=slab: e.matmul(
                                    out=banks[bk][:, col:col + n], lhsT=slab[:, k, wc], rhs=h1T[:, k, t0:t1],
                                    start=(k == 0), stop=(k == 15)), reads=Rr + [Rs[k]], writes=[RB[bk]])
                    p.op("dve", lambda e, c=c, g=g: e.scalar_tensor_tensor(
                        out=bias2, in0=RSW[:, g, :], scalar=betT[:, c:c + 1], in1=bspB[:, g, :], op0=ALU.mult, op1=ALU.add),
                        reads=[R_c1, R_cst], writes=[R_w6["bias2"]])
                    for hf in range(2):
                        p.op("dve", lambda e, hf=hf, c=c: e.scalar_tensor_tensor(
                            out=mixed[:, hf * 512:(hf + 1) * 512].rearrange("p (i t) -> p i t", i=4),
                            in0=banks[4 + hf][:, :].rearrange("p (i t) -> p i t", i=4), scalar=gamT[:, c:c + 1],
                            in1=bias2.unsqueeze(1).to_broadcast([128, 4, 128]), op0=ALU.mult, op1=ALU.add),
                            reads=[RB[4 + hf], R_w6["bias2"], R_cst], writes=[R_w6["mixed"]])
                    p.op("dve", lambda e, c=c: e.scalar_tensor_tensor(
                        out=mixed[:, 1024:1088].rearrange("p (i t) -> p i t", i=4),
                        in0=banks[7][:, 128:192].rearrange("p (i t) -> p i t", i=4), scalar=gamT[:, c:c + 1],
                        in1=bias2[:, 0:16].unsqueeze(1).to_broadcast([128, 4, 16]), op0=ALU.mult, op1=ALU.add),
                        reads=[RB[7], R_w6["bias2"], R_cst], writes=[R_w6["mixed"]])
                    for (bk, pc, oc) in ((2, slice(0, 512), slice(0, 512)), (3, slice(0, 512), slice(512, 1024)),
                                         (6, slice(64, 128), slice(1024, 1088))):
                        p.op("act", lambda e, bk=bk, pc=pc, oc=oc: e.activation(out=szb[:, oc], in_=banks[bk][:, pc],
                                                                                func=AF.Silu),
                             reads=[RB[bk]], writes=[R_w6["szb"]])
                    for (bk, pc, oc) in ((0, slice(0, 512), slice(0, 512)), (1, slice(0, 512), slice(512, 1024)),
                                         (6, slice(0, 64), slice(1024, 1088))):
                        p.op("dve", lambda e, bk=bk, pc=pc, oc=oc: e.tensor_tensor(out=tprod[:, oc], in0=banks[bk][:, pc],
                                                                                   in1=mixed[:, oc], op=ALU.mult),
                             reads=[RB[bk], R_w6["mixed"]], writes=[R_w6["tprod"]])
                    gv = gT_view(c)
                    p.op("pool", lambda e, gv=gv: e.tensor_tensor(out=gv, in0=tprod, in1=szb, op=ALU.mult),
                         reads=[R_w6["tprod"], R_w6["szb"]], writes=[R_vb[c]])

        def out_proj1():
            del slab_raw[NSLAB:]
            del R_slab[NSLAB:]
            off7 = hT_off + 32 * CH
            hpF = view_at(off7, [128, 9, D], F32)
            off7 += 9 * D * 4
            slabB = view_at(off7, [128, 32, 384], BF16)
            off7 += 32 * 384 * 2
            ssq7 = view_at(off7, [128, 16], F32)
            off7 += 64
            junk7 = view_at(off7, [128, 512], BF16)
            off7 += 1024
            assert off7 <= ARENA, off7
            slabA = slab_raw[0]
            assert NSLAB * SLAB_BYTES >= 32 * 384 * 2
            slabA = view_at(slab_off, [128, 32, 384], BF16)
            R_hpF = [Res(f"hpF{i}") for i in range(9)]
            R_s7 = [Res("s7A"), Res("s7B")]
            slabs7 = [slabA, slabB]
            for i in range(9):
                rows = 128 if i < 8 else 64
                p.dma("sp", hpF[0:rows, i, :], hp_scr[i * 128:i * 128 + rows, :], writes=[R_hpF[i]])
            gfB = view_at(slab_off, [128, D], F32)
            R_q7 = Res("q7")

            def final_norm(i, rows):
                for q in range(4):
                    p.op("act", lambda e, i=i, rows=rows, q=q: e.activation(
                        out=junk7[0:rows, :], in_=hpF[0:rows, i, q * 512:(q + 1) * 512], func=AF.Square,
                        accum_out=ssq7[0:rows, q:q + 1]), reads=[R_hpF[i]], writes=[R_q7])
                p.op("dve", lambda e, rows=rows: e.reduce_sum(out=ssq7[0:rows, 4:5], in_=ssq7[0:rows, 0:4], axis=AX.X),
                     reads=[R_q7], writes=[R_q7])
                p.op("act", lambda e, rows=rows: e.activation(out=ssq7[0:rows, 4:5], in_=ssq7[0:rows, 4:5], func=AF.Ln,
                                                              scale=1.0 / D, bias=1e-6), reads=[R_q7], writes=[R_q7])
                p.op("act", lambda e, rows=rows: e.activation(out=ssq7[0:rows, 4:5], in_=ssq7[0:rows, 4:5], func=AF.Exp,
                                                              scale=-0.5), reads=[R_q7], writes=[R_q7])
                p.op("dve", lambda e, i=i, rows=rows: e.scalar_tensor_tensor(
                    out=hpF[0:rows, i, :], in0=hpF[0:rows, i, :], scalar=ssq7[0:rows, 4:5], in1=gfB[0:rows, :],
                    op0=ALU.mult, op1=ALU.mult), reads=[R_hpF[i], R_q7, R_s7[0]], writes=[R_hpF[i]])
                dst = y_own[i * 128:(i + 1) * 128, :] if i < 8 else y_s
                p.dma("sp", dst, hpF[0:rows, i, :], reads=[R_hpF[i]])

            srcw = w_out1.rearrange("(c p) n -> p c n", p=128)
            for j in range(6):
                if j == 5:
                    p.op("dve", lambda e: e.memset(gfB, 0.0), reads=[R_s7[0]], writes=[R_s7[0]])
                    p.dma("sp", gfB, final_g.partition_broadcast(128), reads=[R_s7[0]], writes=[R_s7[0]])
                c0 = j * 384
                n = min(384, D - c0)
                sl_ = slabs7[j % 2]
                Rs = R_s7[j % 2]
                R_parts = []
                for k0 in range(0, 32, 8):
                    p.dma("pool", sl_[:, k0:k0 + 8, 0:n], srcw[:, k0:k0 + 8, c0:c0 + n], writes=[Rs] + [r for rl_ in R_slab for r in rl_])
                for i in range(9):
                    rows = 128 if i < 8 else 64
                    bk = next_bank(0, 8)
                    for c in range(32):
                        gv = gT_view(c)
                        p.op("pe", lambda e, bk=bk, c=c, i=i, rows=rows, n=n, gv=gv, sl_=sl_: e.matmul(
                            out=banks[bk][0:rows, 0:n], lhsT=gv[:, i * 128:i * 128 + rows], rhs=sl_[:, c, 0:n],
                            start=(c == 0), stop=(c == 31)), reads=[R_vb[c], Rs], writes=[RB[bk]])
                    p.op("dve", lambda e, bk=bk, i=i, rows=rows, n=n, c0=c0: e.tensor_tensor(
                        out=hpF[0:rows, i, c0:c0 + n], in0=banks[bk][0:rows, 0:n], in1=hpF[0:rows, i, c0:c0 + n],
                        op=ALU.add), reads=[RB[bk], R_hpF[i]], writes=[R_hpF[i]])
                    if j == 5:
                        final_norm(i, rows)

        if stage >= 5:
            out_proj0()
            p.barrier(scratch)
            Alloc.top = mark1
            if "nol1" not in DBG:
                layer1()
                p.barrier(scratch)
            if stage >= 8:
                out_proj1()
                p.barrier(scratch)

        if os.environ.get("DBGOUT") and stage < 5:
            dbg = nc.dram_tensor("dbg", [128, 16, NOWN], F32, kind="ExternalOutput").ap()
            dst_ = [alloc([128, NOWN], F32, f"dbgst{i}") for i in range(2)]
            R_d = [Res("d0"), Res("d1")]
            for h in range(16):
                p.op("dve", lambda e, h=h: e.tensor_copy(out=dst_[h % 2], in_=mixT[:, h, :]),
                     reads=[R_mix[h], R_mix_s[h]], writes=[R_d[h % 2]])
                p.dma("sp", dbg[:, h, :], dst_[h % 2], reads=[R_d[h % 2]])

        p.emit(st)
        _NC_CACHE["trace"] = p.trace
    return nc


def _consts(half):
    c = np.zeros((14, 128, 128), np.float32)
    i = np.arange(128)
    c[0] = np.eye(128)
    c[1] = (i[:, None] <= i[None, :])
    c[2] = 1.0
    c[3] = 1.0 if half == 0 else 0.0
    c[4] = 1.0 if half == 1 else 0.0
    c[5] = (i[:, None] > i[None, :])
    c[6] = (i[:, None] <= i[None, :])
    c[7] = (i[:, None] < i[None, :])
    c[8] = 0.0
    same = (i[:, None] // 16 == i[None, :] // 16) & (i[:, None] < 64) & (i[None, :] < 64)
    c[9] = same & (i[:, None] <= i[None, :])
    c[10] = same & (i[:, None] <= i[None, :])
    c[11] = same & (i[:, None] < i[None, :])
    c[12] = same
    c[13] = (i[:, None] < 16) & (i[None, :] < 64) & (i[None, :] % 16 == i[:, None])
    return np.ascontiguousarray(c.transpose(1, 0, 2).reshape(128, 14 * 128))


_NC_CACHE = {}


def kernel(x_prompt, x_sample, cache_fox_k, cache_fox_v, cache_fox_logf, cache_sb_k, cache_sb_v,
           norm0_g, w_in0, b_forget, w_out0, norm1_g, w_in1, sgu_ln_g, sgu_ln_b, w_sp, b_sp, w_out1, final_g):
    f = lambda a: np.ascontiguousarray(np.asarray(a, dtype=np.float32))
    x_prompt, x_sample = f(x_prompt), f(x_sample)
    if "nc" not in _NC_CACHE:
        _NC_CACHE["nc"] = build_program(STAGE)
    nc = _NC_CACHE["nc"]
    gT = np.concatenate([f(g).reshape(16, 128).T for g in (norm0_g, norm1_g, final_g)], axis=1)
    lngb = np.concatenate([f(g).reshape(32, 128).T for g in (sgu_ln_g, sgu_ln_b)], axis=1)
    shared = dict(w_in0=f(w_in0), w_out0=f(w_out0), w_in1=f(w_in1), w_out1=f(w_out1),
                  gT=np.ascontiguousarray(gT), bforget=f(b_forget), lngb=np.ascontiguousarray(lngb),
                  ln_g=f(sgu_ln_g), ln_b=f(sgu_ln_b), final_g=f(final_g), w_sp=f(w_sp), b_sp=f(b_sp))
    in_maps = []
    for c in range(NCORES):
        b, half = c // 2, c % 2
        xb = x_prompt[b].reshape(16, 128, D)
        order = OWN[half] + OTHER[half]
        xs = x_sample[4 * c:4 * c + 4].reshape(64, D)
        xall = np.concatenate([xb[order].reshape(2048, D), xs], axis=0)
        m = dict(shared)
        m["xall"] = np.ascontiguousarray(xall)
        m["cfk"] = f(cache_fox_k[4 * c:4 * c + 4]).reshape(4096, 1024)
        m["cfv"] = f(cache_fox_v[4 * c:4 * c + 4]).reshape(4096, 1024)
        m["csk"] = f(cache_sb_k[4 * c:4 * c + 4]).reshape(4096, 1024)
        m["csv"] = f(cache_sb_v[4 * c:4 * c + 4]).reshape(4096, 1024)
        m["clf"] = f(cache_fox_logf[4 * c:4 * c + 4])
        m["cst"] = _consts(half)
        in_maps.append(m)
    res = run_bass_kernel_spmd(nc, in_maps, core_ids=list(range(NCORES)))
    R = res.results
    B, S = 4, 2048

    def gather_prompt(name, width):
        out = np.zeros((B, 16, 128, width), np.float32)
        for c in range(NCORES):
            b, half = c // 2, c % 2
            out[b, OWN[half]] = R[c][name].reshape(8, 128, width)
        return out.reshape(B, S, width)

    def gather_sample(name, width):
        return np.concatenate([R[c][name].reshape(4, 16, width) for c in range(NCORES)], axis=0)

    y_prompt = gather_prompt("y_own", D)
    y_sample = gather_sample("y_s", D)
    fkp = gather_prompt("o_fk", 1024).reshape(B, S, 8, 128)
    fvp = gather_prompt("o_fv", 1024).reshape(B, S, 8, 128)
    lfp = gather_prompt("o_lf", 8)
    fks = gather_sample("o_fk_s", 1024).reshape(32, 16, 8, 128)
    fvs = gather_sample("o_fv_s", 1024).reshape(32, 16, 8, 128)
    lfs = gather_sample("o_lf_s", 8)
    skp = gather_prompt("o_sk", 1024).reshape(B, S, 8, 128)
    svp = gather_prompt("o_sv", 1024).reshape(B, S, 8, 128)
    sks = gather_sample("o_sk_s", 1024).reshape(32, 16, 8, 128)
    svs = gather_sample("o_sv_s", 1024).reshape(32, 16, 8, 128)
    sgu = gather_sample("o_sgu", 4096)
    return (y_prompt, y_sample, fkp, fvp, lfp, fks, fvs, lfs, skp, svp, sks, svs, sgu)
```

```python
import numpy as np
from contextlib import ExitStack
import concourse.bass as bass
import concourse.mybir as mybir
from concourse.bass_utils import run_bass_kernel_spmd

F32 = mybir.dt.float32
BF16 = mybir.dt.bfloat16
AF = mybir.ActivationFunctionType
ALU = mybir.AluOpType

D = 2048
NCORES = 8
SCALE = 128 ** -0.5
OWN = {0: [0, 3, 4, 7, 8, 11, 12, 15], 1: [1, 2, 5, 6, 9, 10, 13, 14]}
OTHER = {h: [b for b in range(16) if b not in OWN[h]] for h in (0, 1)}
NTOK = 2112
NOWN = 1088
STAGE = 99
import os
PARTS = os.environ.get('PARTS', 'KOVQ')
NGROUPS = int(os.environ.get('NGROUPS', '8'))
DBG = os.environ.get('DBG', '')

COMPUTE = ("pe", "act", "dve", "pool")
ALL_ENG = ("pe", "act", "dve", "pool", "sp")


class Res:
    __slots__ = ("name", "writer", "readers", "lock")

    def __init__(self, name="", lock=None):
        self.name = name
        self.writer = None
        self.readers = []
        self.lock = lock


class Op:
    __slots__ = ("eng", "fn", "deps", "is_dma", "count", "needed", "dsem", "dval", "prewait")

    def __init__(self, eng, fn, is_dma):
        self.eng = eng
        self.fn = fn
        self.deps = []
        self.is_dma = is_dma
        self.count = None
        self.needed = False
        self.dsem = None
        self.dval = None
        self.prewait = None


class Prog:
    def __init__(self, nc):
        self.nc = nc
        self.ops = {e: [] for e in ALL_ENG}
        self.n_dma_sems = {"sp": 24, "pool": 24}
        self.dma_rr = {e: 0 for e in ALL_ENG}
        self.dma_last = {}
        self.dma_cnt = {}
        self.phase = Res("phase")

    def _record(self, op, reads, writes):
        deps = []
        for r in reads:
            if r.writer is not None:
                deps.append(r.writer)
        for w in writes:
            if w.writer is not None:
                deps.append(w.writer)
            last = {}
            for r in w.readers:
                if r.is_dma:
                    deps.append(r)
                else:
                    last[r.eng] = r
            deps.extend(last.values())
        seen = set()
        for d in deps:
            if d is op or id(d) in seen:
                continue
            seen.add(id(d))
            if op.eng == "pe" and d.eng == "pe" and not d.is_dma and not op.is_dma:
                continue
            op.deps.append(d)
            d.needed = True
        for r in reads:
            r.readers.append(op)
        for w in writes:
            w.writer = op
            w.readers = []
        self.ops[op.eng].append(op)
        return op

    def op(self, eng, fn, reads=(), writes=(), glob=False):
        reads = list(reads)
        writes = list(writes)
        for r in reads:
            if r.lock is not None and r not in writes:
                writes.append(r.lock)
        if not glob:
            reads.append(self.phase)
        return self._record(Op(eng, fn, False), reads, writes)

    def dma(self, eng, out, in_, reads=(), writes=(), glob=False):
        def fn(e, out=out, in_=in_):
            return e.dma_start(out=out, in_=in_)
        op = Op(eng, fn, True)
        n = self.n_dma_sems[eng]
        slot = self.dma_rr[eng] % n
        self.dma_rr[eng] += 1
        key = (eng, slot)
        cnt = self.dma_cnt.get(key, 0) + 1
        self.dma_cnt[key] = cnt
        op.dsem = key
        op.dval = 16 * cnt
        op.prewait = self.dma_last.get(key)
        self.dma_last[key] = op
        reads = list(reads)
        if not glob:
            reads.append(self.phase)
        return self._record(op, reads, list(writes))

    def barrier(self, scratch):
        o = Op("pool", lambda e: e.memset(scratch, 0.0), False)
        last = {}
        for r in self.phase.readers:
            if r.is_dma:
                o.deps.append(r)
            else:
                last[r.eng] = r
        if self.phase.writer is not None:
            o.deps.append(self.phase.writer)
        for r in last.values():
            o.deps.append(r)
            r.needed = True
        self.phase.writer = o
        self.phase.readers = []
        self.ops["pool"].append(o)

    def emit(self, stack):
        nc = self.nc
        eng_sem = {e: stack.enter_context(nc.semaphore("es_" + e)) for e in COMPUTE}
        dma_sem = {}
        for e, n in self.n_dma_sems.items():
            for s in range(n):
                dma_sem[(e, s)] = stack.enter_context(nc.semaphore(f"ds_{e}{s}"))
        for e in COMPUTE:
            c = 0
            for o in self.ops[e]:
                if o.needed and not o.is_dma:
                    c += 1
                    o.count = c
        block = stack.enter_context(nc.Block())
        prog = self

        prog.trace = {e: [] for e in ALL_ENG}

        def run(engname, eng):
            known = {}
            tr = prog.trace[engname]

            def wait(key, sem, val):
                if known.get(key, 0) >= val:
                    return
                eng.wait_ge(sem, val)
                tr.append(("w", key, val))
                known[key] = val

            def wait_for(d):
                if d.is_dma:
                    wait(d.dsem, dma_sem[d.dsem], d.dval)
                else:
                    if d.eng == engname and engname == "pe":
                        return
                    wait(d.eng, eng_sem[d.eng], d.count)

            def wait_all(ds):
                need = {}
                for d in ds:
                    if d.is_dma:
                        k, v = d.dsem, d.dval
                    else:
                        if d.eng == engname and engname == "pe":
                            continue
                        k, v = d.eng, d.count
                    if need.get(k, 0) < v:
                        need[k] = v
                for k, v in need.items():
                    wait(k, dma_sem[k] if isinstance(k, tuple) else eng_sem[k], v)

            for o in prog.ops[engname]:
                ds = list(o.deps)
                if o.is_dma and o.prewait is not None:
                    ds.append(o.prewait)
                wait_all(ds)
                ins = o.fn(eng)
                if o.is_dma:
                    ins.then_inc(dma_sem[o.dsem], 16)
                    tr.append(("i", o.dsem, 16))
                elif o.needed:
                    ins.then_inc(eng_sem[engname], 1)
                    tr.append(("i", engname, 1))
                else:
                    tr.append(("n", None, 0))
            for key, last in prog.dma_last.items():
                if key[0] == engname:
                    wait(key, dma_sem[key], last.dval)

        @block.tensor
        def _(e):
            run("pe", e)

        @block.scalar
        def _(e):
            run("act", e)

        @block.vector
        def _(e):
            run("dve", e)

        @block.gpsimd
        def _(e):
            run("pool", e)

        @block.sync
        def _(e):
            run("sp", e)


def build_program(stage=99):
    nc = bass.Bass("TRN2", target_bir_lowering=False)

    def din(name, shape):
        return nc.dram_tensor(name, list(shape), F32, kind="ExternalInput").ap()

    def dout(name, shape):
        return nc.dram_tensor(name, list(shape), F32, kind="ExternalOutput").ap()

    xall = din("xall", [NTOK, D])
    ck = [din("cfk", [4096, 1024]), din("csk", [4096, 1024])]
    cv = [din("cfv", [4096, 1024]), din("csv", [4096, 1024])]
    clf = din("clf", [4, 1024, 8])
    w_in0 = din("w_in0", [D, 8200])
    w_out0 = din("w_out0", [D, D])
    w_in1 = din("w_in1", [D, 12288])
    w_out1 = din("w_out1", [4096, D])
    gT_in = din("gT", [128, 48])
    bfg = din("bforget", [8])
    lngb = din("lngb", [128, 64])
    ln_g = din("ln_g", [4096])
    ln_b = din("ln_b", [4096])
    final_g = din("final_g", [D])
    w_sp = din("w_sp", [16, 128, 128])
    b_sp = din("b_sp", [16, 128])
    cst = din("cst", [128, 14 * 128])

    y_own = dout("y_own", [1024, D])
    y_s = dout("y_s", [64, D])
    okv = {}
    for nm in ("fk", "fv", "sk", "sv"):
        okv[nm] = dout("o_" + nm, [1024, 1024])
        okv[nm + "_s"] = dout("o_" + nm + "_s", [64, 1024])
    o_lf = dout("o_lf", [1024, 8])
    o_lf_s = dout("o_lf_s", [64, 8])
    o_sgu = dout("o_sgu", [64, 4096])
    hp_scr = nc.dram_tensor("hp_scr", [NOWN + 64, D], F32, kind="Internal").ap()

    st = ExitStack()
    with st:
        ARENA = 207 * 1024
        arena = st.enter_context(nc.sbuf_tensor("arena", [128, ARENA // 4], F32))
        banks = [st.enter_context(nc.psum_tensor(f"bank{i}", [128, 512], F32)) for i in range(8)]
        RB = [Res(f"bank{i}", lock=Res(f"banklock{i}")) for i in range(8)]
        p = Prog(nc)

        class Alloc:
            top = 0

        def view_at(off, shape, dt):
            esz = 4 if dt == F32 else 2
            n = int(np.prod(shape[1:]))
            assert off % 4 == 0 and off + n * esz <= ARENA, (off, shape)
            a = arena[:, off // 4: off // 4 + (n * esz + 3) // 4]
            if dt != F32:
                a = a.bitcast(dt)
            a = a[:, 0:n]
            if len(shape) == 3:
                a = a.rearrange("p (a b) -> p a b", a=shape[1])
            elif len(shape) == 4:
                a = a.rearrange("p (a b c) -> p a b c", a=shape[1], b=shape[2])
            return a

        def alloc(shape, dt, name=""):
            esz = 4 if dt == F32 else 2
            n = int(np.prod(shape[1:]))
            nbytes = (n * esz + 63) // 64 * 64
            off = Alloc.top
            Alloc.top += nbytes
            assert Alloc.top <= ARENA, (name, Alloc.top)
            return view_at(off, shape, dt)

        def bank_bf(i):
            return banks[i][:, :].bitcast(BF16)

        cst_f = alloc([128, 14, 128], F32, "cst_f")
        cst_b = alloc([128, 14, 128], BF16, "cst_b")
        (I_ID, I_TRILE, I_ONES, I_EEV, I_EOD, I_SL, I_MLE, I_MLT, I_ZERO, I_BDTRI, I_BDLE, I_BDLT, I_SAME, I_SEL) = range(14)
        R_cst = Res("cst")
        p.dma("sp", cst_f, cst.rearrange("p (a b) -> p a b", a=14), writes=[R_cst], glob=True)
        p.op("dve", lambda e: e.tensor_copy(out=cst_b, in_=cst_f), reads=[R_cst], writes=[R_cst], glob=True)
        ident_b = cst_b[:, I_ID, :]
        ones_b = cst_b[:, I_ONES, :]
        zero_b = cst_b[:, I_ZERO, :]
        SL_b = cst_b[:, I_SL, :]
        MLE_b = cst_b[:, I_MLE, :]
        MLT_b = cst_b[:, I_MLT, :]
        E_b = [cst_b[:, I_EEV, :], cst_b[:, I_EOD, :]]
        E_f = [cst_f[:, I_EEV, :], cst_f[:, I_EOD, :]]
        trile_f = cst_f[:, I_TRILE, :]
        ones_f = cst_f[:, I_ONES, :]
        SL_f = cst_f[:, I_SL, :]

        def own_first(pp, bf=True):
            t = E_b if bf else E_f
            return t[0] if pp % 2 == 0 else t[1]

        def other_first(pp, bf=True):
            t = E_b if bf else E_f
            return t[1] if pp % 2 == 0 else t[0]

        gT = alloc([128, 48], F32, "gT")
        lngbT = alloc([128, 64], F32, "lngbT")
        bfB = alloc([128, 8], F32, "bfB")
        p.dma("sp", gT, gT_in, writes=[R_cst], glob=True)
        p.dma("sp", lngbT, lngb, writes=[R_cst], glob=True)
        p.dma("sp", bfB, bfg.partition_broadcast(128), writes=[R_cst], glob=True)
        scratch = alloc([128, 16], F32, "scratch")

        NSLAB = 3
        SLAB_BYTES = 8192
        slab_off = Alloc.top
        slab_raw = [alloc([128, SLAB_BYTES // 2], BF16, f"slab{i}") for i in range(NSLAB)]
        R_slab = [[Res(f"slab{i}_{q}") for q in range(4)] for i in range(NSLAB)]
        slab_ctr = [0]
        slab_first = {}

        def load_slab(w_ap, row_chunks, col0, ncols):
            i = slab_ctr[0] % len(slab_raw)
            slab_ctr[0] += 1
            first_extra = slab_first.pop(i, [])
            assert row_chunks * ncols * 2 <= SLAB_BYTES
            v = slab_raw[i][:, 0:row_chunks * ncols].rearrange("p (c n) -> p c n", c=row_chunks)
            src = w_ap.rearrange("(c p) n -> p c n", p=128)
            step = max(1, row_chunks // 4)
            rl = []
            for qi, c0 in enumerate(range(0, row_chunks, step)):
                p.dma("pool", v[:, c0:c0 + step, :], src[:, c0:c0 + step, col0:col0 + ncols],
                      writes=[R_slab[i][qi]] + first_extra, glob=True)
                rl += [R_slab[i][qi]] * step
            return v, rl

        kvq_slabs = {}

        def issue_kvq(m, h0, parts="kvq"):
            base = m * 4096
            d_ = kvq_slabs.setdefault((m, h0), {})
            for nm_, off_ in (("k", 1024), ("v", 2048), ("q", 0)):
                if nm_ in parts:
                    d_[nm_] = load_slab(w_in0, 16, base + off_ + h0 * 128, 256)

        wlf, R_wlf = load_slab(w_in0, 16, 8192, 8)
        if stage >= 2:
            issue_kvq(0, 0, "kv")

        hT_off = Alloc.top
        hT = alloc([128, 16, NTOK], BF16, "hT")
        R_hT = [Res(f"hT{i}") for i in range(17)]
        mixT = alloc([128, 16, NOWN], BF16, "mixT")
        R_mix = [Res(f"mix{h}") for h in range(16)]
        R_mix_s = [Res(f"mixs{h}") for h in range(16)]
        KTs = alloc([128, 16, 64], BF16, "KTs")
        QTs = alloc([128, 16, 64], BF16, "QTs")
        GTs = alloc([128, 16, 64], BF16, "GTs")
        Vs = alloc([128, 16, 128], BF16, "Vs")
        R_samp = Res("samp_keep")
        logf = alloc([128, 17, 8], F32, "logf")
        Fneg = alloc([128, 16, 8], F32, "Fneg")
        PB = alloc([128, 8, 8], F32, "PB")
        R_F = Res("F")
        mark0 = Alloc.top

        def tile_rows(i):
            return 128 if i < 16 else 64

        def tile_cols(i):
            return slice(i * 128, i * 128 + tile_rows(i))

        xt = [alloc([128, D], F32, f"xt{i}") for i in range(2)]
        R_xt = [Res("xt0"), Res("xt1")]
        xn = [alloc([128, D], BF16, f"xn{i}") for i in range(2)]
        R_xn = [Res("xn0"), Res("xn1")]
        junk = alloc([128, D], BF16, "junk")
        R_junk = Res("junk")
        ssq = alloc([128, 32], F32, "ssq")
        R_ssq = Res("ssq")

        def norm_transpose(i, src_tile, R_src, rows, g_off, dstT, R_dst, col0, pb, bufs=None):
            b = i % 2
            xn, R_xn, junk, R_junk, ssq, R_ssq = bufs
            p.op("act", lambda e: e.activation(out=junk[0:rows, :], in_=src_tile[0:rows, :], func=AF.Square,
                                               accum_out=ssq[0:rows, i:i + 1]),
                 reads=[R_src], writes=[R_junk, R_ssq])
            p.op("act", lambda e: e.activation(out=ssq[0:rows, i:i + 1], in_=ssq[0:rows, i:i + 1], func=AF.Ln,
                                               scale=1.0 / D, bias=1e-6), reads=[R_ssq], writes=[R_ssq])
            p.op("act", lambda e: e.activation(out=ssq[0:rows, i:i + 1], in_=ssq[0:rows, i:i + 1], func=AF.Exp,
                                               scale=-0.5), reads=[R_ssq], writes=[R_ssq])
            p.op("dve", lambda e: e.tensor_scalar(out=xn[b][0:rows, :], in0=src_tile[0:rows, :],
                                                  scalar1=ssq[0:rows, i:i + 1], scalar2=None, op0=ALU.mult),
                 reads=[R_src, R_ssq], writes=[R_xn[b]])
            for half in range(2):
                bk = pb + half
                pv = bank_bf(bk)
                for cc in range(8):
                    c = half * 8 + cc
                    p.op("pe", lambda e, c=c, cc=cc, pv=pv: e.transpose(
                        out=pv[:, cc * 128:cc * 128 + rows], in_=xn[b][0:rows, c * 128:(c + 1) * 128],
                        identity=ident_b[0:rows, 0:rows]),
                        reads=[R_xn[b], R_cst], writes=[RB[bk]])
                pv3 = pv.rearrange("p (a b) -> p a b", a=8)
                p.op("dve", lambda e, half=half, pv3=pv3: e.tensor_tensor(
                    out=dstT[:, half * 8:(half + 1) * 8, col0:col0 + rows], in0=pv3[:, :, 0:rows],
                    in1=gT[:, g_off + half * 8:g_off + (half + 1) * 8].unsqueeze(2).to_broadcast([128, 8, rows]),
                    op=ALU.mult), reads=[RB[bk], R_cst], writes=[R_dst])

        for i in range(17):
            rows = tile_rows(i)
            b = i % 2
            p.dma("sp", xt[b][0:rows, :], xall[i * 128:i * 128 + rows, :], writes=[R_xt[b]])
            norm_transpose(i, xt[b], R_xt[b], rows, 0, hT, R_hT[i], i * 128, (i % 2) * 2,
                           bufs=(xn, R_xn, junk, R_junk, ssq, R_ssq))

        bkL = 4
        for i in range(17):
            rows = tile_rows(i)
            for c in range(16):
                p.op("pe", lambda e, i=i, c=c, rows=rows: e.matmul(
                    out=banks[bkL][0:rows, i * 8:(i + 1) * 8], lhsT=hT[:, c, i * 128:i * 128 + rows],
                    rhs=wlf[:, c, :], start=(c == 0), stop=(c == 15)),
                    reads=[R_hT[i], R_wlf[c]], writes=[RB[bkL]])
        lg = alloc([128, 17, 8], F32, "lg")
        R_lg = Res("lg")
        for (r0, r1, t0, t1) in ((0, 128, 0, 16), (0, 64, 16, 17)):
            nt = t1 - t0
            pv = banks[bkL][r0:r1, t0 * 8:t1 * 8].rearrange("p (a b) -> p a b", a=nt)
            p.op("dve", lambda e, pv=pv, r0=r0, r1=r1, t0=t0, t1=t1, nt=nt: e.tensor_tensor(
                out=lg[r0:r1, t0:t1, :], in0=pv, in1=bfB[r0:r1, :].unsqueeze(1).to_broadcast([r1 - r0, nt, 8]),
                op=ALU.add), reads=[RB[bkL], R_cst], writes=[R_lg])
            p.op("act", lambda e, r0=r0, r1=r1, t0=t0, t1=t1: e.activation(
                out=lg[r0:r1, t0:t1, :], in_=lg[r0:r1, t0:t1, :], func=AF.Exp, scale=-1.0), reads=[R_lg], writes=[R_lg])
            p.op("act", lambda e, r0=r0, r1=r1, t0=t0, t1=t1: e.activation(
                out=lg[r0:r1, t0:t1, :], in_=lg[r0:r1, t0:t1, :], func=AF.Ln, bias=1.0), reads=[R_lg], writes=[R_lg])
            p.op("dve", lambda e, r0=r0, r1=r1, t0=t0, t1=t1: e.tensor_scalar(
                out=logf[r0:r1, t0:t1, :], in0=lg[r0:r1, t0:t1, :], scalar1=-1.0, scalar2=None, op0=ALU.mult),
                reads=[R_lg], writes=[R_F])
        p.dma("sp", o_lf.rearrange("(i p) h -> p i h", p=128), logf[:, 0:8, :], reads=[R_F])
        p.dma("sp", o_lf_s, logf[0:64, 16, :], reads=[R_F])
        Spre = alloc([128, 9, 8], F32, "Spre")
        psum_pair = alloc([128, 8, 8], F32, "psum_pair")
        R_S = Res("Spre")
        p.op("dve", lambda e: e.tensor_tensor(out=psum_pair, in0=logf[:, 0:8, :], in1=logf[:, 8:16, :], op=ALU.add),
             reads=[R_F], writes=[R_S])
        p.op("dve", lambda e: e.memset(Spre[:, 0, :], 0.0), writes=[R_S])
        for pp in range(8):
            p.op("dve", lambda e, pp=pp: e.tensor_tensor(out=Spre[:, pp + 1, :], in0=Spre[:, pp, :],
                                                         in1=psum_pair[:, pp, :], op=ALU.add),
                 reads=[R_S], writes=[R_S])
        bkF = 5
        for k in range(16):
            pp = k % 8
            is_other = k >= 8
            partner = pp if is_other else 8 + pp
            Et = own_first(pp, bf=False) if is_other else other_first(pp, bf=False)
            o = banks[bkF][:, k * 8:(k + 1) * 8]
            p.op("pe", lambda e, o=o, k=k: e.matmul(out=o, lhsT=trile_f, rhs=logf[:, k, :], start=True, stop=False),
                 reads=[R_F, R_cst], writes=[RB[bkF]])
            p.op("pe", lambda e, o=o, pp=pp: e.matmul(out=o, lhsT=ones_f, rhs=Spre[:, pp, :], start=False, stop=False),
                 reads=[R_S, R_cst], writes=[RB[bkF]])
            p.op("pe", lambda e, o=o, Et=Et, partner=partner: e.matmul(out=o, lhsT=Et, rhs=logf[:, partner, :],
                                                                        start=False, stop=True),
                 reads=[R_F, R_cst], writes=[RB[bkF]])
        for pp in range(8):
            o = banks[bkF][:, 128 + pp * 8:128 + (pp + 1) * 8]
            p.op("pe", lambda e, o=o, pp=pp: e.matmul(out=o, lhsT=ones_f, rhs=Spre[:, pp, :], start=True, stop=True),
                 reads=[R_S, R_cst], writes=[RB[bkF]])
        R_F2 = Res("F2")
        p.op("dve", lambda e: e.tensor_scalar(out=Fneg, in0=banks[bkF][:, 0:128].rearrange("p (a b) -> p a b", a=16),
                                              scalar1=-1.0, scalar2=None, op0=ALU.mult),
             reads=[RB[bkF]], writes=[R_F2])
        p.op("dve", lambda e: e.tensor_copy(out=PB.rearrange("p h s -> p s h"),
                                            in_=banks[bkF][:, 128:192].rearrange("p (s h) -> p s h", s=8)),
             reads=[RB[bkF]], writes=[R_F2])

        p.barrier(scratch)
        Alloc.top = mark0

        KT = alloc([128, 2, 2048], BF16, "KT")
        Vg = alloc([128, 16, 256], BF16, "Vg")
        QT = alloc([128, 2, 1024], BF16, "QT")
        GT = alloc([128, 2, 1024], BF16, "GT")
        R_KT = [Res("KT0"), Res("KT1")]
        R_Vg = Res("Vg")
        R_QT = [Res("QT0"), Res("QT1")]
        R_GT = [Res("GT0"), Res("GT1")]
        stg = [alloc([128, 256], F32, f"stg{i}") for i in range(4)]
        R_stg = [Res(f"stg{i}") for i in range(4)]
        stg_ctr = [0]
        wA = alloc([128, 1024], F32, "wA")
        wB = alloc([128, 1024], F32, "wB")
        wC = alloc([128, 1024], F32, "wC")
        wD = alloc([128, 1024], F32, "wD")
        wE = alloc([128, 1024], F32, "wE")
        hA = alloc([128, 1024], BF16, "hA")
        hB = alloc([128, 1024], BF16, "hB")
        hC = alloc([128, 1024], BF16, "hC")
        hD = alloc([128, 1024], BF16, "hD")
        hE = alloc([128, 1024], BF16, "hE")
        R_w = {k: Res(k) for k in ("wA", "wB", "wC", "wD", "wE", "hA", "hB", "hC", "hD", "hE")}
        ev_ctr = [0]

        def evac(out, in_, reads, writes, func=None, eng="dve"):
            if "noevac" in DBG:
                return
            if "dveonly" in DBG and func is None:
                p.op("dve", lambda e: e.tensor_copy(out=out, in_=in_), reads=reads, writes=writes)
                return
            if "actonly" in DBG:
                f = func if func is not None else AF.Copy
                p.op("act", lambda e: e.activation(out=out, in_=in_, func=f), reads=reads, writes=writes)
                return
            if func is not None or eng == "act":
                f = func if func is not None else AF.Copy
                p.op("act", lambda e: e.activation(out=out, in_=in_, func=f), reads=reads, writes=writes)
            else:
                p.op("dve", lambda e: e.tensor_copy(out=out, in_=in_), reads=reads, writes=writes)
            ev_ctr[0] += 1

        pb_ctr = [0]

        def next_bank(lo=0, hi=8):
            b = lo + pb_ctr[0] % (hi - lo)
            pb_ctr[0] += 1
            return b

        def col_chunks(c0, c1):
            out = []
            c = c0
            while c < c1:
                e_ = min(c1, (c // 512 + 1) * 512)
                out.append((c, e_))
                c = e_
            return out

        out_names = {0: ("fk", "fv"), 1: ("sk", "sv")}

        def project_group(m, h0):
            base = m * 4096
            hg = m * 8 + h0
            kname, vname = out_names[m]
            d_ = kvq_slabs.pop((m, h0))
            (wk, Rwk), (wv, Rwv), (wq_, Rwq_) = d_["k"], d_["v"], d_["q"]
            for hh in (range(2) if "K" in PARTS else []):
                for (c0, c1) in ((0, 512), (512, 1024), (1024, 1536), (1536, 2048), (2048, 2112)):
                    bk = next_bank(0, 4)
                    n = c1 - c0
                    Rr = [R_hT[t] for t in range(c0 // 128, (c1 + 127) // 128)]
                    for c in range(16):
                        p.op("pe", lambda e, bk=bk, hh=hh, c=c, c0=c0, c1=c1, n=n: e.matmul(
                            out=banks[bk][:, 0:n], lhsT=wk[:, c, hh * 128:(hh + 1) * 128], rhs=hT[:, c, c0:c1],
                            start=(c == 0), stop=(c == 15)), reads=Rr + [Rwk[c]], writes=[RB[bk]])
                    if c0 < 2048:
                        evac(KT[:, hh, c0:c1], banks[bk][:, 0:n], [RB[bk]], [R_KT[hh]])
                    else:
                        evac(KTs[:, hg + hh, :], banks[bk][:, 0:64], [RB[bk]], [R_samp])
            for i in ((list(range(8)) + [16]) if "O" in PARTS else []):
                rows = tile_rows(i)
                bk = next_bank(0, 4)
                for c in range(16):
                    p.op("pe", lambda e, bk=bk, c=c, i=i, rows=rows: e.matmul(
                        out=banks[bk][0:rows, 0:256], lhsT=hT[:, c, i * 128:i * 128 + rows], rhs=wk[:, c, :],
                        start=(c == 0), stop=(c == 15)), reads=[R_hT[i], Rwk[c]], writes=[RB[bk]])
                s = stg_ctr[0] % 4
                stg_ctr[0] += 1
                evac(stg[s][0:rows, :], banks[bk][0:rows, 0:256], [RB[bk]], [R_stg[s]], eng="act")
                dst = okv[kname][i * 128:(i + 1) * 128, h0 * 128:h0 * 128 + 256] if i < 8 else \
                    okv[kname + "_s"][:, h0 * 128:h0 * 128 + 256]
                p.dma("sp", dst, stg[s][0:rows, :], reads=[R_stg[s]])
            wg_, Rwg_ = load_slab(w_in0, 16, base + 3072 + h0 * 128, 256)
            for i in (range(17) if "V" in PARTS else []):
                rows = tile_rows(i)
                bk = next_bank(0, 4)
                for c in range(16):
                    p.op("pe", lambda e, bk=bk, c=c, i=i, rows=rows: e.matmul(
                        out=banks[bk][0:rows, 0:256], lhsT=hT[:, c, i * 128:i * 128 + rows], rhs=wv[:, c, :],
                        start=(c == 0), stop=(c == 15)), reads=[R_hT[i], Rwv[c]], writes=[RB[bk]])
                if i < 16:
                    evac(Vg[:, i, :], banks[bk][:, 0:256], [RB[bk]], [R_Vg])
                else:
                    evac(Vs[0:64, hg:hg + 2, :], banks[bk][0:64, 0:256].rearrange("p (a b) -> p a b", a=2),
                         [RB[bk]], [R_samp])
                if i < 8 or i == 16:
                    s = stg_ctr[0] % 4
                    stg_ctr[0] += 1
                    evac(stg[s][0:rows, :], banks[bk][0:rows, 0:256], [RB[bk]], [R_stg[s]], eng="act")
                    dst = okv[vname][i * 128:(i + 1) * 128, h0 * 128:h0 * 128 + 256] if i < 8 else \
                        okv[vname + "_s"][:, h0 * 128:h0 * 128 + 256]
                    p.dma("sp", dst, stg[s][0:rows, :], reads=[R_stg[s]])
            for (ws, Rws, dstT, R_dst, dsts, func) in (
                    (wq_, Rwq_, QT, R_QT, QTs, None),
                    (wg_, Rwg_, GT, R_GT, GTs, AF.Silu)):
                for hh in (range(2) if "Q" in PARTS else []):
                    for (c0, c1) in ((0, 512), (512, 1024), (2048, 2112)):
                        bk = next_bank(0, 4)
                        n = c1 - c0
                        Rr = [R_hT[t] for t in range(c0 // 128, (c1 + 127) // 128)]
                        for c in range(16):
                            p.op("pe", lambda e, bk=bk, hh=hh, c=c, c0=c0, c1=c1, n=n, ws=ws: e.matmul(
                                out=banks[bk][:, 0:n], lhsT=ws[:, c, hh * 128:(hh + 1) * 128], rhs=hT[:, c, c0:c1],
                                start=(c == 0), stop=(c == 15)), reads=Rr + [Rws[c]], writes=[RB[bk]])
                        if c0 < 2048:
                            evac(dstT[:, hh, c0:c1], banks[bk][:, 0:n], [RB[bk]], [R_dst[hh]], func=func)
                        else:
                            evac(dsts[:, hg + hh, :], banks[bk][:, 0:64], [RB[bk]], [R_samp], func=func)

        def fox_head(hh, h):
            for q in range(2):
                p.op("pe", lambda e, q=q: e.matmul(out=banks[4 + q][:, :], lhsT=zero_b, rhs=QT[:, hh, q * 512:(q + 1) * 512],
                                                    start=True, stop=False), reads=[R_QT[hh], R_cst], writes=[RB[4 + q]])
                p.op("pe", lambda e, q=q: e.matmul(out=banks[6 + q][:, :], lhsT=zero_b, rhs=QT[:, hh, q * 512:(q + 1) * 512],
                                                    start=True, stop=False), reads=[R_QT[hh], R_cst], writes=[RB[6 + q]])
            its = [(pp, is_other) for pp in range(8) for is_other in (False, True)]

            def geom(it):
                pp, is_other = its[it]
                kb = 8 + pp if is_other else pp
                c0 = pp * 128
                return pp, is_other, kb, c0, (it % 2) * 2, col_chunks(c0, 1024)

            def qk(it):
                pp, is_other, kb, c0, sb0, chunks = geom(it)
                for (a, b_) in chunks:
                    bk = sb0 + a // 512
                    p.op("pe", lambda e, bk=bk, a=a, b_=b_, kb=kb: e.matmul(
                        out=banks[bk][:, a % 512:a % 512 + (b_ - a)], lhsT=KT[:, hh, kb * 128:(kb + 1) * 128],
                        rhs=QT[:, hh, a:b_], start=True, stop=True),
                        reads=[R_KT[hh], R_QT[hh]], writes=[RB[bk]])

            def elem(it):
                pp, is_other, kb, c0, sb0, chunks = geom(it)
                tmp = (wA, wB)[it % 2]
                Rtmp = (R_w["wA"], R_w["wB"])[it % 2]
                PT = (hA, hB)[it % 2]
                RPT = (R_w["hA"], R_w["hB"])[it % 2]
                for (a, b_) in chunks:
                    bk = sb0 + a // 512
                    ns = (b_ - a) // 128
                    s0 = a // 128
                    p.op("dve", lambda e, bk=bk, a=a, b_=b_, ns=ns, s0=s0, tmp=tmp: e.scalar_tensor_tensor(
                        out=tmp[:, a:b_].rearrange("p (s t) -> p s t", s=ns),
                        in0=banks[bk][:, a % 512:a % 512 + (b_ - a)].rearrange("p (s t) -> p s t", s=ns),
                        scalar=SCALE,
                        in1=PB[:, h, s0:s0 + ns].unsqueeze(2).to_broadcast([128, ns, 128]),
                        op0=ALU.mult, op1=ALU.add), reads=[RB[bk], R_F2], writes=[Rtmp])
                p.op("act", lambda e, c0=c0, kb=kb, tmp=tmp, PT=PT: e.activation(
                    out=PT[:, c0:1024], in_=tmp[:, c0:1024], func=AF.Exp, bias=Fneg[:, kb, h:h + 1], scale=1.0),
                    reads=[Rtmp, R_F2], writes=[RPT])
                mk = other_first(pp) if is_other else MLE_b
                p.op("pool", lambda e, c0=c0, mk=mk, PT=PT: e.tensor_tensor(
                    out=PT[:, c0:c0 + 128], in0=PT[:, c0:c0 + 128], in1=mk, op=ALU.mult),
                    reads=[RPT, R_cst], writes=[RPT])

            def pv(it):
                pp, is_other, kb, c0, sb0, chunks = geom(it)
                PT = (hA, hB)[it % 2]
                RPT = (R_w["hA"], R_w["hB"])[it % 2]
                for (a, b_) in chunks:
                    q = a // 512
                    last = is_other and ((pp == 3 and q == 0) or pp == 7)
                    p.op("pe", lambda e, q=q, a=a, b_=b_, kb=kb, last=last, PT=PT: e.matmul(
                        out=banks[4 + q][:, a % 512:a % 512 + (b_ - a)], lhsT=Vg[:, kb, hh * 128:(hh + 1) * 128],
                        rhs=PT[:, a:b_], start=False, stop=last), reads=[R_Vg, RPT], writes=[RB[4 + q]])
                    p.op("pe", lambda e, q=q, a=a, b_=b_, last=last, PT=PT: e.matmul(
                        out=banks[6 + q][:, a % 512:a % 512 + (b_ - a)], lhsT=ones_b,
                        rhs=PT[:, a:b_], start=False, stop=last), reads=[RPT, R_cst], writes=[RB[6 + q]])

            qk(0)
            for it in range(16):
                if it + 1 < 16:
                    qk(it + 1)
                elem(it)
                pv(it)
            for q in range(2):
                cs = slice(q * 512, (q + 1) * 512)
                p.op("dve", lambda e, q=q, cs=cs: e.reciprocal(out=wC[:, cs], in_=banks[6 + q][:, :]),
                     reads=[RB[6 + q]], writes=[R_w["wC"]])
                p.op("dve", lambda e, q=q, cs=cs: e.tensor_tensor(out=wC[:, cs], in0=banks[4 + q][:, :], in1=wC[:, cs],
                                                                 op=ALU.mult),
                     reads=[RB[4 + q], R_w["wC"]], writes=[R_w["wC"]])
                p.op("pool", lambda e, cs=cs: e.tensor_tensor(out=mixT[:, h, cs], in0=wC[:, cs], in1=GT[:, hh, cs],
                                                              op=ALU.mult),
                     reads=[R_w["wC"], R_GT[hh]], writes=[R_mix[h]])

        def sb_head(hh, h):
            SPS, SPSb = wE, hE
            for q in range(2):
                p.op("pe", lambda e, q=q: e.matmul(out=banks[6 + q][:, :], lhsT=zero_b, rhs=QT[:, hh, q * 512:(q + 1) * 512],
                                                    start=True, stop=False), reads=[R_QT[hh], R_cst], writes=[RB[6 + q]])
            p.op("pool", lambda e: e.memset(SPS, 0.0), writes=[R_w["wE"]])
            p.op("pool", lambda e: e.memset(SPSb, 0.0), writes=[R_w["hE"]])
            for pp in range(7, -1, -1):
                c0 = pp * 128
                chunks = col_chunks(c0, 1024)
                blocks = ((pp, 0, wA, "wA", hA, "hA", MLT_b), (8 + pp, 2, wB, "wB", hB, "hB", other_first(pp)))
                for (kb, zb, sp, spn, spm, spmn, mk) in blocks:
                    for (a, b_) in chunks:
                        bk = zb + a // 512
                        p.op("pe", lambda e, bk=bk, a=a, b_=b_, kb=kb: e.matmul(
                            out=banks[bk][:, a % 512:a % 512 + (b_ - a)], lhsT=KT[:, hh, kb * 128:(kb + 1) * 128],
                            rhs=QT[:, hh, a:b_], start=True, stop=True),
                            reads=[R_KT[hh], R_QT[hh]], writes=[RB[bk]])
                    for (a, b_) in chunks:
                        bk = zb + a // 512
                        p.op("act", lambda e, bk=bk, a=a, b_=b_: e.activation(
                            out=wC[:, a:b_], in_=banks[bk][:, a % 512:a % 512 + (b_ - a)], func=AF.Exp, scale=SCALE),
                            reads=[RB[bk]], writes=[R_w["wC"]])
                    p.op("act", lambda e, c0=c0, sp=sp: e.activation(out=sp[:, c0:1024], in_=wC[:, c0:1024], func=AF.Ln,
                                                                     bias=1.0),
                         reads=[R_w["wC"]], writes=[R_w[spn]])
                    p.op("pool", lambda e, c0=c0, sp=sp, spm=spm, mk=mk: e.tensor_tensor(
                        out=spm[:, c0:c0 + 128], in0=sp[:, c0:c0 + 128], in1=mk, op=ALU.mult),
                        reads=[R_w[spn], R_cst], writes=[R_w[spmn]])
                    if c0 + 128 < 1024:
                        p.op("pool", lambda e, c0=c0, sp=sp, spm=spm: e.tensor_copy(
                            out=spm[:, c0 + 128:1024], in_=sp[:, c0 + 128:1024]),
                            reads=[R_w[spn]], writes=[R_w[spmn]])
                for bi, (kb, zb, sp, spn, spm, spmn, mk) in enumerate(blocks):
                    ospm, ospmn = (hB, "hB") if bi == 0 else (hA, "hA")
                    Et = own_first(pp) if bi == 0 else other_first(pp)
                    for (a, b_) in chunks:
                        bk = 4 + a // 512
                        o = banks[bk][:, a % 512:a % 512 + (b_ - a)]
                        p.op("pe", lambda e, o=o, a=a, b_=b_, spm=spm: e.matmul(out=o, lhsT=SL_b, rhs=spm[:, a:b_],
                                                                                 start=True, stop=False),
                             reads=[R_w[spmn], R_cst], writes=[RB[bk]])
                        p.op("pe", lambda e, o=o, a=a, b_=b_: e.matmul(out=o, lhsT=ones_b, rhs=SPSb[:, a:b_],
                                                                       start=False, stop=False),
                             reads=[R_w["hE"], R_cst], writes=[RB[bk]])
                        p.op("pe", lambda e, o=o, a=a, b_=b_, Et=Et, ospm=ospm: e.matmul(
                            out=o, lhsT=Et, rhs=ospm[:, a:b_], start=False, stop=True),
                            reads=[R_w[ospmn], R_cst], writes=[RB[bk]])
                    u, un = (wC, "wC") if bi == 0 else (wD, "wD")
                    aT, aTn = (hC, "hC") if bi == 0 else (hD, "hD")
                    for (a, b_) in chunks:
                        bz = zb + a // 512
                        bc = 4 + a // 512
                        p.op("dve", lambda e, bz=bz, a=a, b_=b_, sp=sp, u=u: e.scalar_tensor_tensor(
                            out=u[:, a:b_], in0=banks[bz][:, a % 512:a % 512 + (b_ - a)], scalar=SCALE, in1=sp[:, a:b_],
                            op0=ALU.mult, op1=ALU.subtract), reads=[RB[bz], R_w[spn]], writes=[R_w[un]])
                        p.op("dve", lambda e, bc=bc, a=a, b_=b_, u=u: e.tensor_tensor(
                            out=u[:, a:b_], in0=u[:, a:b_], in1=banks[bc][:, a % 512:a % 512 + (b_ - a)],
                            op=ALU.subtract), reads=[RB[bc], R_w[un]], writes=[R_w[un]])
                    p.op("act", lambda e, c0=c0, u=u, aT=aT: e.activation(out=aT[:, c0:1024], in_=u[:, c0:1024],
                                                                         func=AF.Exp),
                         reads=[R_w[un]], writes=[R_w[aTn]])
                    p.op("pool", lambda e, c0=c0, aT=aT, mk=mk: e.tensor_tensor(
                        out=aT[:, c0:c0 + 128], in0=aT[:, c0:c0 + 128], in1=mk, op=ALU.mult),
                        reads=[R_w[aTn], R_cst], writes=[R_w[aTn]])
                    for (a, b_) in chunks:
                        q = a // 512
                        last = (pp == 0 and bi == 1)
                        p.op("pe", lambda e, q=q, a=a, b_=b_, kb=kb, last=last, aT=aT: e.matmul(
                            out=banks[6 + q][:, a % 512:a % 512 + (b_ - a)], lhsT=Vg[:, kb, hh * 128:(hh + 1) * 128],
                            rhs=aT[:, a:b_], start=False, stop=last), reads=[R_Vg, R_w[aTn]], writes=[RB[6 + q]])
                if pp > 0:
                    for (spm, spmn) in ((hA, "hA"), (hB, "hB")):
                        p.op("pool", lambda e, c0=c0, spm=spm: e.tensor_tensor(
                            out=SPS[:, c0:1024], in0=SPS[:, c0:1024], in1=spm[:, c0:1024], op=ALU.add),
                            reads=[R_w[spmn], R_w["wE"]], writes=[R_w["wE"]])
                    p.op("pool", lambda e, c0=c0: e.tensor_copy(out=SPSb[:, c0:1024], in_=SPS[:, c0:1024]),
                         reads=[R_w["wE"]], writes=[R_w["hE"]])
            for q in range(2):
                cs = slice(q * 512, (q + 1) * 512)
                p.op("dve", lambda e, q=q, cs=cs: e.tensor_tensor(out=mixT[:, 8 + h, cs], in0=banks[6 + q][:, :],
                                                                 in1=GT[:, hh, cs], op=ALU.mult),
                     reads=[RB[6 + q], R_GT[hh]], writes=[R_mix[8 + h]])

        R_ch = [{k: Res(f"ch{ci}_{k}") for k in ("wA", "wB", "wC", "wD", "wE", "hA", "hB", "hC", "hD", "hE")}
                for ci in range(2)]

        def sb_chain(ci, hh, h):
            zb, cb, ob = 4 * ci, 4 * ci + 2, 4 * ci + 3
            wo_ = 512 * ci
            Rc = R_ch[ci]

            def W(buf):
                return buf[:, wo_:wo_ + 512]
            spA, spB, uA, uB, SPS = W(wA), W(wB), W(wC), W(wD), W(wE)
            spmA, spmB, aA, aB, SPSb = W(hA), W(hB), W(hC), W(hD), W(hE)
            for base, pmax in ((512, 7), (0, 3)):
                p.op("pe", lambda e, base=base: e.matmul(out=banks[ob][:, :], lhsT=zero_b, rhs=QT[:, hh, base:base + 512],
                                                        start=True, stop=False),
                     reads=[R_QT[hh], R_cst], writes=[RB[ob]])
                p.op("pool", lambda e: e.memset(SPS, 0.0), writes=[Rc["wE"]])
                p.op("pool", lambda e: e.memset(SPSb, 0.0), writes=[Rc["hE"]])
                yield
                for pp in range(pmax, -1, -1):
                    lo = max(pp * 128, base) - base
                    n = 512 - lo
                    diag = pp * 128 >= base
                    g0 = base + lo
                    blocks = ((pp, 0, spA, "wA", spmA, "hA", MLT_b, uA, "wC", aA, "hC"),
                              (8 + pp, 1, spB, "wB", spmB, "hB", other_first(pp), uB, "wD", aB, "hD"))
                    for (kb, zi, sp, spn, spm, spmn, mk, u, un, aT, aTn) in blocks:
                        p.op("pe", lambda e, zi=zi, kb=kb, lo=lo, g0=g0, base=base: e.matmul(
                            out=banks[zb + zi][:, lo:512], lhsT=KT[:, hh, kb * 128:(kb + 1) * 128],
                            rhs=QT[:, hh, g0:base + 512], start=True, stop=True),
                            reads=[R_KT[hh], R_QT[hh]], writes=[RB[zb + zi]])
                    yield
                    for (kb, zi, sp, spn, spm, spmn, mk, u, un, aT, aTn) in blocks:
                        p.op("act", lambda e, zi=zi, lo=lo, u=u: e.activation(out=u[:, lo:512], in_=banks[zb + zi][:, lo:512],
                                                                             func=AF.Exp, scale=SCALE),
                             reads=[RB[zb + zi]], writes=[Rc[un]])
                        p.op("act", lambda e, lo=lo, sp=sp, u=u: e.activation(out=sp[:, lo:512], in_=u[:, lo:512],
                                                                             func=AF.Ln, bias=1.0),
                             reads=[Rc[un]], writes=[Rc[spn]])
                    yield
                    for (kb, zi, sp, spn, spm, spmn, mk, u, un, aT, aTn) in blocks:
                        if diag:
                            p.op("dve", lambda e, lo=lo, sp=sp, spm=spm, mk=mk: e.tensor_tensor(
                                out=spm[:, lo:lo + 128], in0=sp[:, lo:lo + 128], in1=mk, op=ALU.mult),
                                reads=[Rc[spn], R_cst], writes=[Rc[spmn]])
                            if lo + 128 < 512:
                                p.op("dve", lambda e, lo=lo, sp=sp, spm=spm: e.tensor_copy(
                                    out=spm[:, lo + 128:512], in_=sp[:, lo + 128:512]),
                                    reads=[Rc[spn]], writes=[Rc[spmn]])
                        else:
                            p.op("dve", lambda e, lo=lo, sp=sp, spm=spm: e.tensor_copy(out=spm[:, lo:512], in_=sp[:, lo:512]),
                                 reads=[Rc[spn]], writes=[Rc[spmn]])
                        p.op("dve", lambda e, zi=zi, lo=lo, sp=sp, u=u: e.scalar_tensor_tensor(
                            out=u[:, lo:512], in0=banks[zb + zi][:, lo:512], scalar=SCALE, in1=sp[:, lo:512],
                            op0=ALU.mult, op1=ALU.subtract), reads=[RB[zb + zi], Rc[spn]], writes=[Rc[un]])
                    yield
                    for bi, (kb, zi, sp, spn, spm, spmn, mk, u, un, aT, aTn) in enumerate(blocks):
                        ospm, ospmn = (spmB, "hB") if bi == 0 else (spmA, "hA")
                        Et = own_first(pp) if bi == 0 else other_first(pp)
                        o = banks[zb + zi][:, lo:512]
                        p.op("pe", lambda e, o=o, lo=lo, spm=spm: e.matmul(out=o, lhsT=SL_b, rhs=spm[:, lo:512],
                                                                          start=True, stop=False),
                             reads=[Rc[spmn], R_cst], writes=[RB[zb + zi]])
                        p.op("pe", lambda e, o=o, lo=lo: e.matmul(out=o, lhsT=ones_b, rhs=SPSb[:, lo:512],
                                                                  start=False, stop=False),
                             reads=[Rc["hE"], R_cst], writes=[RB[zb + zi]])
                        p.op("pe", lambda e, o=o, lo=lo, Et=Et, ospm=ospm: e.matmul(out=o, lhsT=Et, rhs=ospm[:, lo:512],
                                                                                    start=False, stop=True),
                             reads=[Rc[ospmn], R_cst], writes=[RB[zb + zi]])
                    yield
                    for (kb, zi, sp, spn, spm, spmn, mk, u, un, aT, aTn) in blocks:
                        p.op("dve", lambda e, zi=zi, lo=lo, u=u: e.tensor_tensor(
                            out=u[:, lo:512], in0=u[:, lo:512], in1=banks[zb + zi][:, lo:512], op=ALU.subtract),
                            reads=[RB[zb + zi], Rc[un]], writes=[Rc[un]])
                    yield
                    for (kb, zi, sp, spn, spm, spmn, mk, u, un, aT, aTn) in blocks:
                        p.op("act", lambda e, lo=lo, u=u, aT=aT: e.activation(out=aT[:, lo:512], in_=u[:, lo:512], func=AF.Exp),
                             reads=[Rc[un]], writes=[Rc[aTn]])
                        if diag:
                            p.op("pool", lambda e, lo=lo, aT=aT, mk=mk: e.tensor_tensor(
                                out=aT[:, lo:lo + 128], in0=aT[:, lo:lo + 128], in1=mk, op=ALU.mult),
                                reads=[Rc[aTn], R_cst], writes=[Rc[aTn]])
                    yield
                    for bi, (kb, zi, sp, spn, spm, spmn, mk, u, un, aT, aTn) in enumerate(blocks):
                        last = (pp == 0 and bi == 1)
                        p.op("pe", lambda e, lo=lo, kb=kb, last=last, aT=aT: e.matmul(
                            out=banks[ob][:, lo:512], lhsT=Vg[:, kb, hh * 128:(hh + 1) * 128],
                            rhs=aT[:, lo:512], start=False, stop=last), reads=[R_Vg, Rc[aTn]], writes=[RB[ob]])
                    yield
                    if pp > 0:
                        for (spm, spmn) in ((spmA, "hA"), (spmB, "hB")):
                            p.op("pool", lambda e, lo=lo, spm=spm: e.tensor_tensor(
                                out=SPS[:, lo:512], in0=SPS[:, lo:512], in1=spm[:, lo:512], op=ALU.add),
                                reads=[Rc[spmn], Rc["wE"]], writes=[Rc["wE"]])
                        p.op("pool", lambda e, lo=lo: e.tensor_copy(out=SPSb[:, lo:512], in_=SPS[:, lo:512]),
                             reads=[Rc["wE"]], writes=[Rc["hE"]])
                        yield
                p.op("dve", lambda e, base=base: e.tensor_tensor(out=mixT[:, 8 + h, base:base + 512], in0=banks[ob][:, :],
                                                                 in1=GT[:, hh, base:base + 512], op=ALU.mult),
                     reads=[RB[ob], R_GT[hh]], writes=[R_mix[8 + h]])
                yield

        def fox_chain(ci, hh, h):
            sbk, ob, db = 4 * ci, 4 * ci + 2, 4 * ci + 3
            wo_ = 512 * ci
            Rc = R_ch[ci]

            def W(buf):
                return buf[:, wo_:wo_ + 512]
            tmps = (W(wA), W(wB))
            Rtmps = (Rc["wA"], Rc["wB"])
            PTs = (W(hA), W(hB))
            RPTs = (Rc["hA"], Rc["hB"])
            fin_ = W(wC)
            for base, pmax in ((512, 7), (0, 3)):
                for bk_ in (ob, db):
                    p.op("pe", lambda e, bk_=bk_, base=base: e.matmul(out=banks[bk_][:, :], lhsT=zero_b,
                                                                    rhs=QT[:, hh, base:base + 512], start=True, stop=False),
                         reads=[R_QT[hh], R_cst], writes=[RB[bk_]])
                its = [(pp, io) for pp in range(pmax + 1) for io in (False, True)]
                nit = len(its)

                def geom(it, base=base, its=its):
                    pp, is_other = its[it]
                    kb = 8 + pp if is_other else pp
                    lo = max(pp * 128, base) - base
                    return pp, is_other, kb, lo, pp * 128 >= base

                def qk(it, base=base, geom=geom):
                    pp, is_other, kb, lo, diag = geom(it)
                    bk = sbk + it % 2
                    p.op("pe", lambda e, bk=bk, kb=kb, lo=lo, base=base: e.matmul(
                        out=banks[bk][:, lo:512], lhsT=KT[:, hh, kb * 128:(kb + 1) * 128],
                        rhs=QT[:, hh, base + lo:base + 512], start=True, stop=True),
                        reads=[R_KT[hh], R_QT[hh]], writes=[RB[bk]])

                qk(0)
                qk(1)
                yield
                for j in range(nit // 2):
                    pair = (2 * j, 2 * j + 1)
                    for it in pair:
                        pp, is_other, kb, lo, diag = geom(it)
                        bk = sbk + it % 2
                        tmp, Rtmp = tmps[it % 2], Rtmps[it % 2]
                        ns = (512 - lo) // 128
                        s0 = (base + lo) // 128
                        p.op("dve", lambda e, bk=bk, lo=lo, ns=ns, s0=s0, tmp=tmp: e.scalar_tensor_tensor(
                            out=tmp[:, lo:512].rearrange("p (s t) -> p s t", s=ns),
                            in0=banks[bk][:, lo:512].rearrange("p (s t) -> p s t", s=ns), scalar=SCALE,
                            in1=PB[:, h, s0:s0 + ns].unsqueeze(2).to_broadcast([128, ns, 128]),
                            op0=ALU.mult, op1=ALU.add), reads=[RB[bk], R_F2], writes=[Rtmp])
                    yield
                    if 2 * j + 2 < nit:
                        qk(2 * j + 2)
                        qk(2 * j + 3)
                    for it in pair:
                        pp, is_other, kb, lo, diag = geom(it)
                        tmp, Rtmp, PT, RPT = tmps[it % 2], Rtmps[it % 2], PTs[it % 2], RPTs[it % 2]
                        p.op("act", lambda e, lo=lo, kb=kb, tmp=tmp, PT=PT: e.activation(
                            out=PT[:, lo:512], in_=tmp[:, lo:512], func=AF.Exp, bias=Fneg[:, kb, h:h + 1], scale=1.0),
                            reads=[Rtmp, R_F2], writes=[RPT])
                        if diag:
                            mk = other_first(pp) if is_other else MLE_b
                            p.op("pool", lambda e, lo=lo, mk=mk, PT=PT: e.tensor_tensor(
                                out=PT[:, lo:lo + 128], in0=PT[:, lo:lo + 128], in1=mk, op=ALU.mult),
                                reads=[RPT, R_cst], writes=[RPT])
                    yield
                    for it in pair:
                        pp, is_other, kb, lo, diag = geom(it)
                        PT, RPT = PTs[it % 2], RPTs[it % 2]
                        last = (it == nit - 1)
                        p.op("pe", lambda e, lo=lo, kb=kb, PT=PT: e.matmul(
                            out=banks[ob][:, lo:512], lhsT=Vg[:, kb, hh * 128:(hh + 1) * 128], rhs=PT[:, lo:512],
                            start=False, stop=False), reads=[R_Vg, RPT], writes=[RB[ob]])
                        p.op("pe", lambda e, lo=lo, PT=PT: e.matmul(
                            out=banks[db][:, lo:512], lhsT=ones_b, rhs=PT[:, lo:512], start=False, stop=False),
                            reads=[RPT, R_cst], writes=[RB[db]])
                    yield
                for bk_ in (ob, db):
                    p.op("pe", lambda e, bk_=bk_, base=base: e.matmul(out=banks[bk_][:, :], lhsT=zero_b,
                                                                    rhs=QT[:, hh, base:base + 512], start=False, stop=True),
                         reads=[R_QT[hh], R_cst], writes=[RB[bk_]])
                p.op("dve", lambda e: e.reciprocal(out=fin_, in_=banks[db][:, :]), reads=[RB[db]], writes=[Rc["wC"]])
                p.op("dve", lambda e: e.tensor_tensor(out=fin_, in0=banks[ob][:, :], in1=fin_, op=ALU.mult),
                     reads=[RB[ob], Rc["wC"]], writes=[Rc["wC"]])
                p.op("pool", lambda e, base=base: e.tensor_tensor(out=mixT[:, h, base:base + 512], in0=fin_,
                                                                  in1=GT[:, hh, base:base + 512], op=ALU.mult),
                     reads=[Rc["wC"], R_GT[hh]], writes=[R_mix[h]])
                yield

        def run_interleaved(gens):
            gens = list(gens)
            while gens:
                for g_ in list(gens):
                    try:
                        next(g_)
                    except StopIteration:
                        gens.remove(g_)

        if stage >= 2:
            groups = [(m, h0) for m in range(2) for h0 in range(0, 8, 2)][:NGROUPS]
            issue_kvq(0, 0, "q")
            for gi, (m, h0) in enumerate(groups):
                project_group(m, h0)
                if gi + 1 < len(groups):
                    issue_kvq(*groups[gi + 1])
                if stage >= 3:
                    if m == 0:
                        if "oldfox" in DBG:
                            for hh in range(2):
                                fox_head(hh, h0 + hh)
                        else:
                            run_interleaved([fox_chain(0, 0, h0), fox_chain(1, 1, h0 + 1)])
                    elif "oldsb" in DBG:
                        pass
                    if m == 1 and h0 == 0 and ("oldsb" in DBG) != ("oldfox" in DBG):
                        p.barrier(scratch)
                    if m == 0:
                        pass
                    elif "oldsb" in DBG:
                        for hh in range(2):
                            sb_head(hh, h0 + hh)
                    else:
                        run_interleaved([sb_chain(0, 0, h0), sb_chain(1, 1, h0 + 1)])

        p.barrier(scratch)
        Alloc.top = mark0

        def sample_attention():
            offA = [hT_off]

            def allocA(shape, dt):
                esz = 4 if dt == F32 else 2
                n = int(np.prod(shape[1:]))
                off = offA[0]
                offA[0] += (n * esz + 63) // 64 * 64
                assert offA[0] <= hT_off + 16 * NTOK * 2
                return view_at(off, shape, dt)
            Kc = [allocA([128, 8, 1024], BF16) for _ in range(2)]
            Vc = [allocA([128, 8, 1024], BF16) for _ in range(2)]
            R_Kc = [[Res("Kc0a"), Res("Kc0b")], [Res("Kc1a"), Res("Kc1b")]]
            R_Vc = [[Res("Vc0a"), Res("Vc0b")], [Res("Vc1a"), Res("Vc1b")]]
            KcT = alloc([128, 8, 1024], BF16, "KcT")
            R_KcT = [Res(f"KcT{h}") for h in range(8)]
            clfT = alloc([128, 8, 32], F32, "clfT")
            csuf = alloc([128, 8, 32], F32, "csuf")
            Gsuf = alloc([128, 8, 32], F32, "Gsuf")
            Gnew = alloc([128, 8], F32, "Gnew")
            R_G = Res("G")
            sw1 = alloc([128, 1024], F32, "sw1")
            sw2 = alloc([128, 1024], F32, "sw2")
            sP = [alloc([128, 1024], BF16, f"sP{i}") for i in range(2)]
            R_sP = [Res("sP0"), Res("sP1")]
            R_sw1, R_sw2 = Res("sw1"), Res("sw2")
            Ssuf = alloc([128, 8, 128], F32, "Ssuf")
            Ssufb = alloc([128, 8, 128], BF16, "Ssufb")
            R_Ssuf = Res("Ssuf")
            nw1 = alloc([128, 512], F32, "nw1")
            nw2 = alloc([128, 512], F32, "nw2")
            Pn = alloc([128, 8, 64], BF16, "Pn")
            spmn = alloc([128, 8, 64], BF16, "spmn")
            R_n = {k: Res(k) for k in ("nw1", "nw2", "Pn", "spmn")}
            fin = alloc([128, 512], F32, "fin")
            R_fin = Res("fin")
            BDLE = cst_b[0:64, I_BDLE, 0:64]
            BDLT = cst_b[0:64, I_BDLT, 0:64]
            for b in range(4):
                p.dma("sp", clfT[:, :, b * 8:(b + 1) * 8], clf[b].rearrange("(t p) h -> p t h", p=128), writes=[R_G])
            p.op("dve", lambda e: e.memset(csuf[:, 7, :], 0.0), writes=[R_G])
            for t in range(6, -1, -1):
                p.op("dve", lambda e, t=t: e.tensor_tensor(out=csuf[:, t, :], in0=csuf[:, t + 1, :], in1=clfT[:, t + 1, :],
                                                          op=ALU.add), reads=[R_G], writes=[R_G])
            bG = 7
            for t in range(8):
                o = banks[bG][:, t * 32:(t + 1) * 32]
                p.op("pe", lambda e, o=o, t=t: e.matmul(out=o, lhsT=SL_f, rhs=clfT[:, t, :], start=True, stop=False),
                     reads=[R_G, R_cst], writes=[RB[bG]])
                p.op("pe", lambda e, o=o, t=t: e.matmul(out=o, lhsT=ones_f, rhs=csuf[:, t, :], start=False, stop=True),
                     reads=[R_G, R_cst], writes=[RB[bG]])
            p.op("pe", lambda e: e.matmul(out=banks[bG][0:64, 256:264], lhsT=cst_f[0:64, I_BDTRI, 0:64],
                                          rhs=logf[0:64, 16, :], start=True, stop=True),
                 reads=[R_F, R_cst], writes=[RB[bG]])
            R_G2 = Res("G2")
            p.op("dve", lambda e: e.tensor_copy(out=Gsuf, in_=banks[bG][:, 0:256].rearrange("p (t n) -> p t n", t=8)),
                 reads=[RB[bG]], writes=[R_G2])
            p.op("dve", lambda e: e.tensor_scalar(out=Gnew[0:64, :], in0=banks[bG][0:64, 256:264], scalar1=-1.0,
                                                  scalar2=None, op0=ALU.mult), reads=[RB[bG]], writes=[R_G2])

            if os.environ.get("DBGOUT"):
                dbg4 = nc.dram_tensor("dbg4", [128, 3072], F32, kind="ExternalOutput").ap()
                d4 = alloc([128, 3072], F32, "d4")
                R_d4 = Res("d4")
                p.op("dve", lambda e: e.tensor_copy(out=d4[:, 0:1024], in_=QTs.rearrange("p h q -> p (h q)")), reads=[R_samp], writes=[R_d4])
                p.op("dve", lambda e: e.tensor_copy(out=d4[:, 1024:2048], in_=KTs.rearrange("p h q -> p (h q)")), reads=[R_samp], writes=[R_d4])
                p.op("dve", lambda e: e.tensor_copy(out=d4[:, 2048:3072], in_=GTs.rearrange("p h q -> p (h q)")), reads=[R_samp], writes=[R_d4])
                p.dma("sp", dbg4, d4, reads=[R_d4])
                dbg2 = nc.dram_tensor("dbg2", [128, 264], F32, kind="ExternalOutput").ap()
                p.dma("sp", dbg2[:, 0:256], Gsuf.rearrange("p t n -> p (t n)"), reads=[R_G2])
                p.dma("sp", dbg2[0:64, 256:264], Gnew[0:64, :], reads=[R_G2])
            def issue_cache(j):
                m_, b_ = j // 4, j % 4
                buf_ = j % 2
                for half in range(2):
                    rows = slice(b_ * 1024 + half * 512, b_ * 1024 + (half + 1) * 512)
                    p.dma("pool", Kc[buf_][:, half * 4:(half + 1) * 4, :],
                          ck[m_][rows, :].rearrange("(t p) n -> p t n", p=128), writes=[R_Kc[buf_][half]])
                    p.dma("pool", Vc[buf_][:, half * 4:(half + 1) * 4, :],
                          cv[m_][rows, :].rearrange("(t p) n -> p t n", p=128), writes=[R_Vc[buf_][half]])

            def tr(j):
                buf_ = j % 2
                for h in range(8):
                    bk = h % 2
                    pv = bank_bf(bk)
                    for t in range(8):
                        p.op("pe", lambda e, pv=pv, t=t, h=h, buf_=buf_: e.transpose(
                            out=pv[:, t * 128:(t + 1) * 128], in_=Kc[buf_][:, t, h * 128:(h + 1) * 128],
                            identity=ident_b), reads=[R_Kc[buf_][t // 4], R_cst], writes=[RB[bk]])
                    evac(KcT[:, h, :], pv, [RB[bk]], [R_KcT[h]], eng=("dve" if h % 2 == 0 else "act"))

            issue_cache(0)
            tr(0)
            for m in range(2):
                bO, bD, bN, bC = 4, 5, 6, 7
                hb = m * 8
                for h in range(8):
                    p.op("pe", lambda e, h=h, hb=hb: e.matmul(out=banks[bN][0:64, h * 64:(h + 1) * 64], lhsT=KTs[:, hb + h, :],
                                                        rhs=QTs[:, hb + h, :], start=True, stop=True),
                         reads=[R_samp], writes=[RB[bN]])
                if m == 0:
                    p.op("dve", lambda e: e.scalar_tensor_tensor(
                        out=nw1[0:64, :].rearrange("p (h q) -> p h q", h=8),
                        in0=banks[bN][0:64, :].rearrange("p (h q) -> p h q", h=8), scalar=SCALE,
                        in1=Gnew[0:64, :].unsqueeze(2).to_broadcast([64, 8, 64]), op0=ALU.mult, op1=ALU.add),
                        reads=[RB[bN], R_G2], writes=[R_n["nw1"]])
                    p.op("act", lambda e: e.activation(out=Pn[0:64, :, :], in_=nw1[0:64, :].rearrange("p (h q) -> p h q", h=8),
                                                       func=AF.Exp), reads=[R_n["nw1"]], writes=[R_n["Pn"]])
                    p.op("pool", lambda e: e.tensor_tensor(out=Pn[0:64, :, :], in0=Pn[0:64, :, :],
                                                           in1=BDLE.unsqueeze(1).to_broadcast([64, 8, 64]), op=ALU.mult),
                         reads=[R_n["Pn"], R_cst], writes=[R_n["Pn"]])
                    p.op("pe", lambda e: e.matmul(out=banks[bN][:, :], lhsT=zero_b, rhs=cst_b[:, 0:4, :], start=True,
                                                  stop=False), reads=[R_cst], writes=[RB[bN]])
                    p.op("pe", lambda e: e.matmul(out=banks[bD][:, :], lhsT=ones_b[0:64, :],
                                                  rhs=Pn[0:64, :, :], start=True, stop=True),
                         reads=[R_n["Pn"], R_cst], writes=[RB[bD]])
                else:
                    p.op("act", lambda e: e.activation(out=nw1[0:64, :], in_=banks[bN][0:64, :], func=AF.Exp, scale=SCALE),
                         reads=[RB[bN]], writes=[R_n["nw1"]])
                    p.op("act", lambda e: e.activation(out=nw1[0:64, :], in_=nw1[0:64, :], func=AF.Ln, bias=1.0),
                         reads=[R_n["nw1"]], writes=[R_n["nw1"]])
                    p.op("pool", lambda e: e.tensor_tensor(out=spmn[0:64, :, :],
                                                           in0=nw1[0:64, :].rearrange("p (h q) -> p h q", h=8),
                                                           in1=BDLT.unsqueeze(1).to_broadcast([64, 8, 64]), op=ALU.mult),
                         reads=[R_n["nw1"], R_cst], writes=[R_n["spmn"]])
                    p.op("pe", lambda e: e.matmul(out=banks[bD][0:64, :], lhsT=SL_b[0:64, 0:64], rhs=spmn[0:64, :, :],
                                                  start=True, stop=True), reads=[R_n["spmn"], R_cst], writes=[RB[bD]])
                    p.op("dve", lambda e: e.scalar_tensor_tensor(out=nw2[0:64, :], in0=banks[bN][0:64, :], scalar=SCALE,
                                                                 in1=nw1[0:64, :], op0=ALU.mult, op1=ALU.subtract),
                         reads=[RB[bN], R_n["nw1"]], writes=[R_n["nw2"]])
                    p.op("dve", lambda e: e.tensor_tensor(out=nw2[0:64, :], in0=nw2[0:64, :], in1=banks[bD][0:64, :],
                                                          op=ALU.subtract), reads=[RB[bD], R_n["nw2"]], writes=[R_n["nw2"]])
                    p.op("act", lambda e: e.activation(out=Pn[0:64, :, :], in_=nw2[0:64, :].rearrange("p (h q) -> p h q", h=8),
                                                       func=AF.Exp), reads=[R_n["nw2"]], writes=[R_n["Pn"]])
                    p.op("pool", lambda e: e.tensor_tensor(out=Pn[0:64, :, :], in0=Pn[0:64, :, :],
                                                           in1=BDLT.unsqueeze(1).to_broadcast([64, 8, 64]), op=ALU.mult),
                         reads=[R_n["Pn"], R_cst], writes=[R_n["Pn"]])
                p.op("pe", lambda e: e.matmul(out=banks[bO][:, :], lhsT=zero_b, rhs=cst_b[:, 0:4, :], start=True, stop=False),
                     reads=[R_cst], writes=[RB[bO]])
                for h in range(8):
                    p.op("pe", lambda e, h=h, hb=hb: e.matmul(out=banks[bO][:, h * 64:(h + 1) * 64], lhsT=Vs[0:64, hb + h, :],
                                                        rhs=Pn[0:64, h, :], start=False, stop=False),
                         reads=[R_samp, R_n["Pn"]], writes=[RB[bO]])
                for b in range(4):
                    buf = (m * 4 + b) % 2
                    if m * 4 + b + 1 < 8:
                        issue_cache(m * 4 + b + 1)
                    if "trpipe" not in DBG and m * 4 + b > 0:
                        tr(m * 4 + b)
                    for t in range(8):
                        bk = 2 + t // 4
                        for h in range(8):
                            c0 = (t % 4) * 128 + h * 16
                            p.op("pe", lambda e, bk=bk, c0=c0, t=t, h=h, b=b, hb=hb: e.matmul(
                                out=banks[bk][:, c0:c0 + 16], lhsT=KcT[:, h, t * 128:(t + 1) * 128],
                                rhs=QTs[:, hb + h, b * 16:(b + 1) * 16], start=True, stop=True),
                                reads=[R_KcT[h], R_samp], writes=[RB[bk]])
                    if m * 4 + b + 1 < 8 and "trpipe" in DBG:
                        tr(m * 4 + b + 1)
                    Pb = sP[b % 2]
                    RPb = R_sP[b % 2]
                    Pb4 = Pb.rearrange("p (t h q) -> p t h q", t=8, h=8)
                    if m == 0:
                        for t in range(8):
                            bk = 2 + t // 4
                            cs = slice((t % 4) * 128, (t % 4 + 1) * 128)
                            p.op("dve", lambda e, t=t, b=b, bk=bk, cs=cs: e.scalar_tensor_tensor(
                                out=sw1[:, t * 128:(t + 1) * 128].rearrange("p (h q) -> p h q", h=8),
                                in0=banks[bk][:, cs].rearrange("p (h q) -> p h q", h=8), scalar=SCALE,
                                in1=Gsuf[:, t, b * 8:(b + 1) * 8].unsqueeze(2).to_broadcast([128, 8, 16]),
                                op0=ALU.mult, op1=ALU.add), reads=[RB[bk], R_G2], writes=[R_sw1])
                        p.op("act", lambda e, Pb=Pb: e.activation(out=Pb, in_=sw1, func=AF.Exp), reads=[R_sw1], writes=[RPb])
                        for t in range(8):
                            p.op("pe", lambda e, t=t, b=b, Pb=Pb: e.matmul(
                                out=banks[bN][:, b * 128:(b + 1) * 128],
                                lhsT=ones_b, rhs=Pb[:, t * 128:(t + 1) * 128], start=False,
                                stop=(b == 3 and t == 7)),
                                reads=[RPb, R_cst], writes=[RB[bN]])
                    else:
                        for hf in range(2):
                            p.op("act", lambda e, hf=hf: e.activation(out=sw1[:, hf * 512:(hf + 1) * 512], in_=banks[2 + hf][:, :],
                                                                      func=AF.Exp, scale=SCALE), reads=[RB[2 + hf]], writes=[R_sw1])
                        p.op("act", lambda e: e.activation(out=sw1, in_=sw1, func=AF.Ln, bias=1.0), reads=[R_sw1], writes=[R_sw1])
                        spb = sP[(b + 1) % 2]
                        Rspb = R_sP[(b + 1) % 2]
                        p.op("pool", lambda e, spb=spb: e.tensor_copy(out=spb, in_=sw1), reads=[R_sw1], writes=[Rspb])
                        spb3 = spb.rearrange("p (t n) -> p t n", t=8)
                        p.op("pool", lambda e: e.memset(Ssuf[:, 7, :], 0.0), writes=[R_Ssuf])
                        for t in range(6, -1, -1):
                            p.op("pool", lambda e, t=t, spb3=spb3: e.tensor_tensor(
                                out=Ssuf[:, t, :], in0=Ssuf[:, t + 1, :], in1=spb3[:, t + 1, :], op=ALU.add),
                                reads=[Rspb, R_Ssuf], writes=[R_Ssuf])
                        p.op("pool", lambda e: e.tensor_copy(out=Ssufb, in_=Ssuf), reads=[R_Ssuf], writes=[R_Ssuf])
                        for t in range(8):
                            bk = 5 + t // 4
                            o = banks[bk][:, (t % 4) * 128:(t % 4 + 1) * 128]
                            p.op("pe", lambda e, o=o, t=t, spb3=spb3: e.matmul(out=o, lhsT=SL_b, rhs=spb3[:, t, :],
                                                                               start=True, stop=False),
                                 reads=[Rspb, R_cst], writes=[RB[bk]])
                            p.op("pe", lambda e, o=o, t=t: e.matmul(out=o, lhsT=ones_b, rhs=Ssufb[:, t, :],
                                                                    start=False, stop=False),
                                 reads=[R_Ssuf, R_cst], writes=[RB[bk]])
                            p.op("pe", lambda e, o=o, b=b: e.matmul(out=o.rearrange("p (h q) -> p h q", h=8),
                                                                    lhsT=ones_b[0:64, :],
                                                                    rhs=spmn[0:64, :, b * 16:(b + 1) * 16],
                                                                    start=False, stop=True),
                                 reads=[R_n["spmn"], R_cst], writes=[RB[bk]])
                        for hf in range(2):
                            cs = slice(hf * 512, (hf + 1) * 512)
                            p.op("dve", lambda e, hf=hf, cs=cs: e.scalar_tensor_tensor(
                                out=sw2[:, cs], in0=banks[2 + hf][:, :], scalar=SCALE, in1=sw1[:, cs],
                                op0=ALU.mult, op1=ALU.subtract), reads=[RB[2 + hf], R_sw1], writes=[R_sw2])
                            p.op("dve", lambda e, hf=hf, cs=cs: e.tensor_tensor(
                                out=sw2[:, cs], in0=sw2[:, cs], in1=banks[5 + hf][:, :], op=ALU.subtract),
                                reads=[RB[5 + hf], R_sw2], writes=[R_sw2])
                        p.op("act", lambda e, Pb=Pb: e.activation(out=Pb, in_=sw2, func=AF.Exp), reads=[R_sw2], writes=[RPb])
                    for t in range(8):
                        for h in range(8):
                            last = (b == 3 and t == 7 and h == 7)
                            p.op("pe", lambda e, t=t, h=h, b=b, buf=buf, last=last, Pb4=Pb4: e.matmul(
                                out=banks[bO][:, h * 64 + b * 16:h * 64 + (b + 1) * 16],
                                lhsT=Vc[buf][:, t, h * 128:(h + 1) * 128], rhs=Pb4[:, t, h, :], start=False, stop=last),
                                reads=[R_Vc[buf][t // 4], RPb], writes=[RB[bO]])
                if m == 0:
                    p.op("dve", lambda e: e.tensor_copy(out=fin, in_=banks[bD][:, :]), reads=[RB[bD]], writes=[R_fin])
                    for b in range(4):
                        fv = fin.rearrange("p (h q) -> p h q", h=8)[:, :, b * 16:(b + 1) * 16]
                        p.op("dve", lambda e, b=b, fv=fv: e.tensor_tensor(
                            out=fv, in0=fv, in1=banks[bN][:, b * 128:(b + 1) * 128].rearrange("p (h q) -> p h q", h=8),
                            op=ALU.add), reads=[RB[bN], R_fin], writes=[R_fin])
                    p.op("dve", lambda e: e.reciprocal(out=fin, in_=fin), reads=[R_fin], writes=[R_fin])
                    p.op("dve", lambda e: e.tensor_tensor(out=fin, in0=banks[bO][:, :], in1=fin, op=ALU.mult),
                         reads=[RB[bO], R_fin], writes=[R_fin])
                else:
                    p.op("dve", lambda e: e.tensor_copy(out=fin, in_=banks[bO][:, :]), reads=[RB[bO]], writes=[R_fin])
                if os.environ.get("DBGOUT") and m == 0:
                    dbg3 = nc.dram_tensor("dbg3", [128, 2048], F32, kind="ExternalOutput").ap()
                    p.dma("sp", dbg3[:, 0:512], fin, reads=[R_fin])
                    p.dma("sp", dbg3[:, 512:1536], sw1, reads=[R_sw1])
                    p.dma("sp", dbg3[0:64, 1536:2048], nw1[0:64, :], reads=[R_n["nw1"]])
                p.op("pool", lambda e, hb=hb: e.tensor_tensor(out=mixT[:, hb:hb + 8, 1024:1088],
                                                              in0=fin.rearrange("p (h q) -> p h q", h=8),
                                                              in1=GTs[:, hb:hb + 8, :], op=ALU.mult),
                     reads=[R_fin, R_samp], writes=[R_mix_s[hb + h_] for h_ in range(8)])

        if stage >= 4:
            sample_attention()
            p.barrier(scratch)
            Alloc.top = mark0

        h1T = alloc([128, 16, NOWN], BF16, "h1T")
        R_h1T = [Res(f"h1T{i}") for i in range(9)]
        mark1 = Alloc.top
        AX = mybir.AxisListType

        pre_v = []

        def out_proj0():
            if stage >= 6 and "nopre" not in DBG:
                for sv_ in range(len(slab_raw)):
                    pre_v.append(load_slab(w_in1, 16, 4096 + sv_ * 256, 256))
            wo = view_at(hT_off, [128, 16, D], BF16)
            R_wo = [Res(f"wo{j}") for j in range(8)]
            srcw = w_out0.rearrange("(c p) n -> p c n", p=128)
            for j in range(8):
                p.dma("pool", wo[:, 2 * j:2 * j + 2, :], srcw[:, 2 * j:2 * j + 2, :], writes=[R_wo[j]])
            hpb = [alloc([128, D], F32, f"hpb{i}") for i in range(2)]
            R_hpb = [Res("hpb0"), Res("hpb1")]
            _x5 = alloc([128, D], BF16, "xn5")
            _r5 = Res("xn5")
            xn5 = [_x5, _x5]
            R_xn5 = [_r5, _r5]
            junk5 = alloc([128, D], BF16, "junk5")
            ssq5 = alloc([128, 32], F32, "ssq5")
            bufs = (xn5, R_xn5, junk5, Res("junk5"), ssq5, Res("ssq5"))
            for i in range(9):
                rows = 128 if i < 8 else 64
                r0 = i * 128 if i < 8 else 2048
                b = i % 2
                p.dma("sp", hpb[b][0:rows, :], xall[r0:r0 + rows, :], writes=[R_hpb[b]])
                Rm = R_mix if i < 8 else R_mix_s
                for q in range(4):
                    bk = 4 + q
                    for hd in range(16):
                        p.op("pe", lambda e, bk=bk, hd=hd, i=i, rows=rows, q=q: e.matmul(
                            out=banks[bk][0:rows, :], lhsT=mixT[:, hd, i * 128:i * 128 + rows],
                            rhs=wo[:, hd, q * 512:(q + 1) * 512], start=(hd == 0), stop=(hd == 15)),
                            reads=[Rm[hd], R_wo[hd // 2]], writes=[RB[bk]])
                    p.op("dve", lambda e, bk=bk, b=b, rows=rows, q=q: e.tensor_tensor(
                        out=hpb[b][0:rows, q * 512:(q + 1) * 512], in0=banks[bk][0:rows, :],
                        in1=hpb[b][0:rows, q * 512:(q + 1) * 512], op=ALU.add),
                        reads=[RB[bk], R_hpb[b]], writes=[R_hpb[b]])
                p.dma("sp", hp_scr[i * 128:i * 128 + rows, :], hpb[b][0:rows, :], reads=[R_hpb[b]])
                if os.environ.get("DBGOUT"):
                    if i == 0:
                        _NC_CACHE["dbg_hp"] = nc.dram_tensor("dbg_hp", [NOWN, D], F32, kind="ExternalOutput").ap()
                    p.dma("sp", _NC_CACHE["dbg_hp"][i * 128:i * 128 + rows, :], hpb[b][0:rows, :], reads=[R_hpb[b]])
                norm_transpose(i, hpb[b], R_hpb[b], rows, 16, h1T, R_h1T[i], i * 128, (i % 2) * 2, bufs=bufs)

        CH = 9 * 128 * 2

        def gT_view(c):
            return view_at(hT_off + c * CH, [128, NOWN], BF16)

        R_vb = [Res(f"vb{c}") for c in range(32)]

        def layer1():
            vb = view_at(hT_off, [128, 32, 9, 128], BF16)
            offB = [hT_off + 32 * CH]

            def allocB(shape, dt):
                esz = 4 if dt == F32 else 2
                n = int(np.prod(shape[1:]))
                off = offB[0]
                offB[0] += (n * esz + 63) // 64 * 64
                assert offB[0] <= mark0, (offB[0], mark0)
                return view_at(off, shape, dt)
            vsf = allocB([128, 4096], F32)
            R_vsf = Res("vsf")
            WspT = allocB([128, 16, 128], BF16)
            RSW = allocB([128, 16, 128], F32)
            bspB = allocB([128, 16, 128], F32)
            R_c1 = Res("l1consts")
            wtmp = vsf[:, 0:2048].rearrange("p (g s) -> p g s", g=16)
            wtb = vsf[:, 2048:3072].bitcast(BF16).rearrange("p (g s) -> p g s", g=16)
            mixed = alloc([128, NOWN], F32, "mixed")
            tprod = alloc([128, NOWN], F32, "tprod")
            szb = alloc([128, NOWN], BF16, "szb")
            bias2 = alloc([128, 128], F32, "bias2")
            BDs = alloc([128, 16, 64], BF16, "BDs")
            st1 = alloc([128, 9, 16], F32, "st1")
            st2 = alloc([128, 9, 16], F32, "st2")
            s1 = alloc([128, 9], F32, "s1")
            s2 = alloc([128, 9], F32, "s2")
            rstd1 = alloc([128, 9], F32, "rstd1")
            nmr1 = alloc([128, 9], F32, "nmr1")
            junk6 = alloc([128, 256], BF16, "junk6")
            R_w6 = {k: Res(k) for k in ("mixed", "tprod", "szb", "bias2", "BDs", "st", "junk6")}
            gamT = lngbT[:, 0:32]
            betT = lngbT[:, 32:64]
            p.dma("sp", wtmp, w_sp.rearrange("g t s -> t g s"), writes=[R_vsf])
            p.dma("sp", bspB.rearrange("p g t -> p (g t)"), b_sp.rearrange("g t -> (g t)").partition_broadcast(128),
                  writes=[R_c1])
            p.op("dve", lambda e: e.tensor_copy(out=wtb, in_=wtmp), reads=[R_vsf], writes=[R_vsf])
            for hf in range(2):
                pv = bank_bf(hf)
                for gg in range(8):
                    g = hf * 8 + gg
                    p.op("pe", lambda e, pv=pv, gg=gg, g=g: e.transpose(out=pv[:, gg * 128:(gg + 1) * 128], in_=wtb[:, g, :],
                                                                          identity=ident_b),
                         reads=[R_vsf, R_cst], writes=[RB[hf]])
                p.op("dve", lambda e, pv=pv, hf=hf: e.tensor_tensor(
                    out=WspT[:, hf * 8:(hf + 1) * 8, :], in0=pv.rearrange("p (g t) -> p g t", g=8),
                    in1=MLE_b.unsqueeze(1).to_broadcast([128, 8, 128]), op=ALU.mult),
                    reads=[RB[hf], R_cst], writes=[R_c1])
            for g4 in range(4):
                bk = 2 + g4 % 2
                p.op("pe", lambda e, bk=bk, g4=g4: e.matmul(out=banks[bk][:, :], lhsT=ones_b,
                                                            rhs=WspT[:, g4 * 4:(g4 + 1) * 4, :], start=True, stop=True),
                     reads=[R_c1, R_cst], writes=[RB[bk]])
                p.op("dve", lambda e, bk=bk, g4=g4: e.tensor_copy(
                    out=RSW[:, g4 * 4:(g4 + 1) * 4, :], in_=banks[bk][:, :].rearrange("p (g t) -> p g t", g=4)),
                    reads=[RB[bk]], writes=[R_c1])
            W16t = alloc([128, 16, 64], BF16, "W16t")
            R_w16 = Res("W16t")
            for bb in range(4):
                p.op("dve", lambda e, bb=bb: e.tensor_copy(out=W16t[0:16, :, 16 * bb:16 * bb + 16], in_=WspT[0:16, :, 0:16]),
                     reads=[R_c1], writes=[R_w16])
            for hf in range(2):
                p.op("pe", lambda e, hf=hf: e.matmul(out=banks[2 + hf][0:64, :], lhsT=cst_b[0:16, I_SEL, 0:64],
                                                     rhs=W16t[0:16, hf * 8:(hf + 1) * 8, :], start=True, stop=True),
                     reads=[R_w16, R_cst], writes=[RB[2 + hf]])
                p.op("dve", lambda e, hf=hf: e.tensor_tensor(
                    out=BDs[0:64, hf * 8:(hf + 1) * 8, :], in0=banks[2 + hf][0:64, :].rearrange("p (g t) -> p g t", g=8),
                    in1=cst_b[0:64, I_SAME, 0:64].unsqueeze(1).to_broadcast([64, 8, 64]), op=ALU.mult),
                    reads=[RB[2 + hf], R_cst], writes=[R_w6["BDs"]])
            p.op("dve", lambda e: e.memset(st1, 0.0), writes=[R_w6["st"]])
            p.op("dve", lambda e: e.memset(st2, 0.0), writes=[R_w6["st"]])
            if stage >= 6:
                for sv in range(16):
                    wvs, Rwvs = pre_v[sv] if sv < len(pre_v) else load_slab(w_in1, 16, 4096 + sv * 256, 256)
                    for i in range(9):
                        rows = 128 if i < 8 else 64
                        bk = next_bank(2, 8)
                        for k in range(16):
                            p.op("pe", lambda e, bk=bk, k=k, i=i, rows=rows, wvs=wvs: e.matmul(
                                out=banks[bk][0:rows, 0:256], lhsT=h1T[:, k, i * 128:i * 128 + rows], rhs=wvs[:, k, :],
                                start=(k == 0), stop=(k == 15)), reads=[R_h1T[i], Rwvs[k]], writes=[RB[bk]])
                        p.op("act", lambda e, bk=bk, i=i, rows=rows, sv=sv: e.activation(
                            out=vb[0:rows, 2 * sv:2 * sv + 2, i, :],
                            in_=banks[bk][0:rows, 0:256].rearrange("p (a b) -> p a b", a=2), func=AF.Copy,
                            accum_out=st1[0:rows, i, sv:sv + 1]),
                            reads=[RB[bk]], writes=[R_vb[2 * sv], R_vb[2 * sv + 1], R_w6["st"]])
                        p.op("act", lambda e, bk=bk, i=i, rows=rows, sv=sv: e.activation(
                            out=junk6[0:rows, :], in_=banks[bk][0:rows, 0:256], func=AF.Square,
                            accum_out=st2[0:rows, i, sv:sv + 1]),
                            reads=[RB[bk]], writes=[R_w6["junk6"], R_w6["st"]])
                        if i == 8:
                            p.op("dve", lambda e, bk=bk, sv=sv: e.tensor_copy(out=vsf[0:64, sv * 256:(sv + 1) * 256],
                                                                              in_=banks[bk][0:64, 0:256]),
                                 reads=[RB[bk]], writes=[R_vsf])
                p.op("dve", lambda e: e.reduce_sum(out=s1, in_=st1, axis=AX.X), reads=[R_w6["st"]], writes=[R_w6["st"]])
                p.op("dve", lambda e: e.reduce_sum(out=s2, in_=st2, axis=AX.X), reads=[R_w6["st"]], writes=[R_w6["st"]])
                p.op("dve", lambda e: e.tensor_scalar(out=s1, in0=s1, scalar1=1.0 / 4096, scalar2=None, op0=ALU.mult),
                     reads=[R_w6["st"]], writes=[R_w6["st"]])
                p.op("dve", lambda e: e.tensor_tensor(out=nmr1, in0=s1, in1=s1, op=ALU.mult),
                     reads=[R_w6["st"]], writes=[R_w6["st"]])
                p.op("dve", lambda e: e.scalar_tensor_tensor(out=s2, in0=s2, scalar=1.0 / 4096, in1=nmr1, op0=ALU.mult,
                                                             op1=ALU.subtract),
                     reads=[R_w6["st"]], writes=[R_w6["st"]])
                p.op("act", lambda e: e.activation(out=rstd1, in_=s2, func=AF.Ln, bias=1e-5), reads=[R_w6["st"]],
                     writes=[R_w6["st"]])
                p.op("act", lambda e: e.activation(out=rstd1, in_=rstd1, func=AF.Exp, scale=-0.5), reads=[R_w6["st"]],
                     writes=[R_w6["st"]])
                p.op("dve", lambda e: e.scalar_tensor_tensor(out=nmr1, in0=s1, scalar=-1.0, in1=rstd1, op0=ALU.mult,
                                                             op1=ALU.mult),
                     reads=[R_w6["st"]], writes=[R_w6["st"]])
                for i in range(9):
                    rows = 128 if i < 8 else 64
                    p.op("dve", lambda e, i=i, rows=rows: e.tensor_scalar(
                        out=vb[0:rows, :, i, :], in0=vb[0:rows, :, i, :], scalar1=rstd1[0:rows, i:i + 1],
                        scalar2=nmr1[0:rows, i:i + 1], op0=ALU.mult, op1=ALU.add),
                        reads=[R_w6["st"]] + R_vb, writes=R_vb)
                gb = mixed[0:64, 0:1024]
                for j in range(8):
                    cs = slice(j * 512, (j + 1) * 512)
                    p.dma("sp", gb[:, 0:512], ln_g[cs].partition_broadcast(64), writes=[R_w6["mixed"]])
                    p.dma("sp", gb[:, 512:1024], ln_b[cs].partition_broadcast(64), writes=[R_w6["mixed"]])
                    p.op("dve", lambda e, cs=cs: e.tensor_scalar(out=vsf[0:64, cs], in0=vsf[0:64, cs], scalar1=rstd1[0:64, 8:9],
                                                                 scalar2=nmr1[0:64, 8:9], op0=ALU.mult, op1=ALU.add),
                         reads=[R_vsf, R_w6["st"]], writes=[R_vsf])
                    p.op("dve", lambda e, cs=cs: e.tensor_tensor(out=vsf[0:64, cs], in0=vsf[0:64, cs], in1=gb[:, 0:512],
                                                                 op=ALU.mult), reads=[R_vsf, R_w6["mixed"]], writes=[R_vsf])
                    p.op("dve", lambda e, cs=cs: e.tensor_tensor(out=vsf[0:64, cs], in0=vsf[0:64, cs], in1=gb[:, 512:1024],
                                                                 op=ALU.add), reads=[R_vsf, R_w6["mixed"]], writes=[R_vsf])
                    p.dma("sp", o_sgu[:, cs], vsf[0:64, cs], reads=[R_vsf])
            if stage >= 7:
                n0 = len(slab_raw)
                for ex in (range(2) if "ext" in DBG else []):
                    slab_raw.append(vsf[:, ex * 2048:(ex + 1) * 2048].bitcast(BF16))
                    R_slab.append([Res(f"slabx{ex}_{q}") for q in range(4)])
                    slab_first[n0 + ex] = [R_vsf]
                us = zs = Rus = Rzs = None
                for c in range(32):
                    g = c // 2
                    if c % 2 == 0:
                        us, Rus = load_slab(w_in1, 16, (c // 2) * 256, 256)
                        zs, Rzs = load_slab(w_in1, 16, 8192 + (c // 2) * 256, 256)
                    wc = slice((c % 2) * 128, (c % 2 + 1) * 128)
                    for i in range(8):
                        bk = 4 + i // 4
                        p.op("pe", lambda e, bk=bk, i=i, c=c, g=g: e.matmul(
                            out=banks[bk][:, (i % 4) * 128:(i % 4 + 1) * 128], lhsT=vb[:, c, i, :], rhs=WspT[:, g, :],
                            start=True, stop=True), reads=[R_vb[c], R_c1], writes=[RB[bk]])
                    p.op("pe", lambda e, c=c, g=g: e.matmul(out=banks[7][:, 128:192], lhsT=vb[0:64, c, 8, :],
                                                            rhs=BDs[0:64, g, :], start=True, stop=True),
                         reads=[R_vb[c], R_w6["BDs"]], writes=[RB[7]])
                    for (slab, Rs, bk0, soff) in ((us, Rus, 0, 0), (zs, Rzs, 2, 64)):
                        for (t0, t1, bk, col) in ((0, 512, bk0, 0), (512, 1024, bk0 + 1, 0), (1024, 1088, 6, soff)):
                            n = t1 - t0
                            Rr = [R_h1T[t] for t in range(t0 // 128, (t1 + 127) // 128)]
                            for k in range(16):
                                p.op("pe", lambda e, bk=bk, col=col, n=n, k=k, wc=wc, t0=t0, t1=t1, slab=slab: e.matmul(
                                    out=banks[bk][:, col:col + n], lhsT=slab[:, k, wc], rhs=h1T[:, k, t0:t1],
                                    start=(k == 0), stop=(k == 15)), reads=Rr + [Rs[k]], writes=[RB[bk]])
                    p.op("dve", lambda e, c=c, g=g: e.scalar_tensor_tensor(
                        out=bias2, in0=RSW[:, g, :], scalar=betT[:, c:c + 1], in1=bspB[:, g, :], op0=ALU.mult, op1=ALU.add),
                        reads=[R_c1, R_cst], writes=[R_w6["bias2"]])
                    for hf in range(2):
                        p.op("dve", lambda e, hf=hf, c=c: e.scalar_tensor_tensor(
                            out=mixed[:, hf * 512:(hf + 1) * 512].rearrange("p (i t) -> p i t", i=4),
                            in0=banks[4 + hf][:, :].rearrange("p (i t) -> p i t", i=4), scalar=gamT[:, c:c + 1],
                            in1=bias2.unsqueeze(1).to_broadcast([128, 4, 128]), op0=ALU.mult, op1=ALU.add),
                            reads=[RB[4 + hf], R_w6["bias2"], R_cst], writes=[R_w6["mixed"]])
                    p.op("dve", lambda e, c=c: e.scalar_tensor_tensor(
                        out=mixed[:, 1024:1088].rearrange("p (i t) -> p i t", i=4),
                        in0=banks[7][:, 128:192].rearrange("p (i t) -> p i t", i=4), scalar=gamT[:, c:c + 1],
                        in1=bias2[:, 0:16].unsqueeze(1).to_broadcast([128, 4, 16]), op0=ALU.mult, op1=ALU.add),
                        reads=[RB[7], R_w6["bias2"], R_cst], writes=[R_w6["mixed"]])
                    for (bk, pc, oc) in ((2, slice(0, 512), slice(0, 512)), (3, slice(0, 512), slice(512, 1024)),
                                         (6, slice(64, 128), slice(1024, 1088))):
                        p.op("act", lambda e, bk=bk, pc=pc, oc=oc: e.activation(out=szb[:, oc], in_=banks[bk][:, pc],
                                                                                func=AF.Silu),
                             reads=[RB[bk]], writes=[R_w6["szb"]])
                    for (bk, pc, oc) in ((0, slice(0, 512), slice(0, 512)), (1, slice(0, 512), slice(512, 1024)),
                                         (6, slice(0, 64), slice(1024, 1088))):
                        p.op("dve", lambda e, bk=bk, pc=pc, oc=oc: e.tensor_tensor(out=tprod[:, oc], in0=banks[bk][:, pc],
                                                                                   in1=mixed[:, oc], op=ALU.mult),
                             reads=[RB[bk], R_w6["mixed"]], writes=[R_w6["tprod"]])
                    gv = gT_view(c)
                    p.op("pool", lambda e, gv=gv: e.tensor_tensor(out=gv, in0=tprod, in1=szb, op=ALU.mult),
                         reads=[R_w6["tprod"], R_w6["szb"]], writes=[R_vb[c]])

        def out_proj1():
            del slab_raw[NSLAB:]
            del R_slab[NSLAB:]
            off7 = hT_off + 32 * CH
            hpF = view_at(off7, [128, 9, D], F32)
            off7 += 9 * D * 4
            slabB = view_at(off7, [128, 32, 384], BF16)
            off7 += 32 * 384 * 2
            ssq7 = view_at(off7, [128, 16], F32)
            off7 += 64
            junk7 = view_at(off7, [128, 512], BF16)
            off7 += 1024
            assert off7 <= ARENA, off7
            slabA = slab_raw[0]
            assert NSLAB * SLAB_BYTES >= 32 * 384 * 2
            slabA = view_at(slab_off, [128, 32, 384], BF16)
            R_hpF = [Res(f"hpF{i}") for i in range(9)]
            R_s7 = [Res("s7A"), Res("s7B")]
            slabs7 = [slabA, slabB]
            for i in range(9):
                rows = 128 if i < 8 else 64
                p.dma("sp", hpF[0:rows, i, :], hp_scr[i * 128:i * 128 + rows, :], writes=[R_hpF[i]])
            gfB = view_at(slab_off, [128, D], F32)
            R_q7 = Res("q7")

            def final_norm(i, rows):
                for q in range(4):
                    p.op("act", lambda e, i=i, rows=rows, q=q: e.activation(
                        out=junk7[0:rows, :], in_=hpF[0:rows, i, q * 512:(q + 1) * 512], func=AF.Square,
                        accum_out=ssq7[0:rows, q:q + 1]), reads=[R_hpF[i]], writes=[R_q7])
                p.op("dve", lambda e, rows=rows: e.reduce_sum(out=ssq7[0:rows, 4:5], in_=ssq7[0:rows, 0:4], axis=AX.X),
                     reads=[R_q7], writes=[R_q7])
                p.op("act", lambda e, rows=rows: e.activation(out=ssq7[0:rows, 4:5], in_=ssq7[0:rows, 4:5], func=AF.Ln,
                                                              scale=1.0 / D, bias=1e-6), reads=[R_q7], writes=[R_q7])
                p.op("act", lambda e, rows=rows: e.activation(out=ssq7[0:rows, 4:5], in_=ssq7[0:rows, 4:5], func=AF.Exp,
                                                              scale=-0.5), reads=[R_q7], writes=[R_q7])
                p.op("dve", lambda e, i=i, rows=rows: e.scalar_tensor_tensor(
                    out=hpF[0:rows, i, :], in0=hpF[0:rows, i, :], scalar=ssq7[0:rows, 4:5], in1=gfB[0:rows, :],
                    op0=ALU.mult, op1=ALU.mult), reads=[R_hpF[i], R_q7, R_s7[0]], writes=[R_hpF[i]])
                dst = y_own[i * 128:(i + 1) * 128, :] if i < 8 else y_s
                p.dma("sp", dst, hpF[0:rows, i, :], reads=[R_hpF[i]])

            srcw = w_out1.rearrange("(c p) n -> p c n", p=128)
            for j in range(6):
                if j == 5:
                    p.op("dve", lambda e: e.memset(gfB, 0.0), reads=[R_s7[0]], writes=[R_s7[0]])
                    p.dma("sp", gfB, final_g.partition_broadcast(128), reads=[R_s7[0]], writes=[R_s7[0]])
                c0 = j * 384
                n = min(384, D - c0)
                sl_ = slabs7[j % 2]
                Rs = R_s7[j % 2]
                R_parts = []
                for k0 in range(0, 32, 8):
                    p.dma("pool", sl_[:, k0:k0 + 8, 0:n], srcw[:, k0:k0 + 8, c0:c0 + n], writes=[Rs] + [r for rl_ in R_slab for r in rl_])
                for i in range(9):
                    rows = 128 if i < 8 else 64
                    bk = next_bank(0, 8)
                    for c in range(32):
                        gv = gT_view(c)
                        p.op("pe", lambda e, bk=bk, c=c, i=i, rows=rows, n=n, gv=gv, sl_=sl_: e.matmul(
                            out=banks[bk][0:rows, 0:n], lhsT=gv[:, i * 128:i * 128 + rows], rhs=sl_[:, c, 0:n],
                            start=(c == 0), stop=(c == 31)), reads=[R_vb[c], Rs], writes=[RB[bk]])
                    p.op("dve", lambda e, bk=bk, i=i, rows=rows, n=n, c0=c0: e.tensor_tensor(
                        out=hpF[0:rows, i, c0:c0 + n], in0=banks[bk][0:rows, 0:n], in1=hpF[0:rows, i, c0:c0 + n],
                        op=ALU.add), reads=[RB[bk], R_hpF[i]], writes=[R_hpF[i]])
                    if j == 5:
                        final_norm(i, rows)

        if stage >= 5:
            out_proj0()
            p.barrier(scratch)
            Alloc.top = mark1
            if "nol1" not in DBG:
                layer1()
                p.barrier(scratch)
            if stage >= 8:
                out_proj1()
                p.barrier(scratch)

        if os.environ.get("DBGOUT") and stage < 5:
            dbg = nc.dram_tensor("dbg", [128, 16, NOWN], F32, kind="ExternalOutput").ap()
            dst_ = [alloc([128, NOWN], F32, f"dbgst{i}") for i in range(2)]
            R_d = [Res("d0"), Res("d1")]
            for h in range(16):
                p.op("dve", lambda e, h=h: e.tensor_copy(out=dst_[h % 2], in_=mixT[:, h, :]),
                     reads=[R_mix[h], R_mix_s[h]], writes=[R_d[h % 2]])
                p.dma("sp", dbg[:, h, :], dst_[h % 2], reads=[R_d[h % 2]])

        p.emit(st)
        _NC_CACHE["trace"] = p.trace
    return nc


def _consts(half):
    c = np.zeros((14, 128, 128), np.float32)
    i = np.arange(128)
    c[0] = np.eye(128)
    c[1] = (i[:, None] <= i[None, :])
    c[2] = 1.0
    c[3] = 1.0 if half == 0 else 0.0
    c[4] = 1.0 if half == 1 else 0.0
    c[5] = (i[:, None] > i[None, :])
    c[6] = (i[:, None] <= i[None, :])
    c[7] = (i[:, None] < i[None, :])
    c[8] = 0.0
    same = (i[:, None] // 16 == i[None, :] // 16) & (i[:, None] < 64) & (i[None, :] < 64)
    c[9] = same & (i[:, None] <= i[None, :])
    c[10] = same & (i[:, None] <= i[None, :])
    c[11] = same & (i[:, None] < i[None, :])
    c[12] = same
    c[13] = (i[:, None] < 16) & (i[None, :] < 64) & (i[None, :] % 16 == i[:, None])
    return np.ascontiguousarray(c.transpose(1, 0, 2).reshape(128, 14 * 128))


_NC_CACHE = {}


def kernel(x_prompt, x_sample, cache_fox_k, cache_fox_v, cache_fox_logf, cache_sb_k, cache_sb_v,
           norm0_g, w_in0, b_forget, w_out0, norm1_g, w_in1, sgu_ln_g, sgu_ln_b, w_sp, b_sp, w_out1, final_g):
    f = lambda a: np.ascontiguousarray(np.asarray(a, dtype=np.float32))
    x_prompt, x_sample = f(x_prompt), f(x_sample)
    if "nc" not in _NC_CACHE:
        _NC_CACHE["nc"] = build_program(STAGE)
    nc = _NC_CACHE["nc"]
    gT = np.concatenate([f(g).reshape(16, 128).T for g in (norm0_g, norm1_g, final_g)], axis=1)
    lngb = np.concatenate([f(g).reshape(32, 128).T for g in (sgu_ln_g, sgu_ln_b)], axis=1)
    shared = dict(w_in0=f(w_in0), w_out0=f(w_out0), w_in1=f(w_in1), w_out1=f(w_out1),
                  gT=np.ascontiguousarray(gT), bforget=f(b_forget), lngb=np.ascontiguousarray(lngb),
                  ln_g=f(sgu_ln_g), ln_b=f(sgu_ln_b), final_g=f(final_g), w_sp=f(w_sp), b_sp=f(b_sp))
    in_maps = []
    for c in range(NCORES):
        b, half = c // 2, c % 2
        xb = x_prompt[b].reshape(16, 128, D)
        order = OWN[half] + OTHER[half]
        xs = x_sample[4 * c:4 * c + 4].reshape(64, D)
        xall = np.concatenate([xb[order].reshape(2048, D), xs], axis=0)
        m = dict(shared)
        m["xall"] = np.ascontiguousarray(xall)
        m["cfk"] = f(cache_fox_k[4 * c:4 * c + 4]).reshape(4096, 1024)
        m["cfv"] = f(cache_fox_v[4 * c:4 * c + 4]).reshape(4096, 1024)
        m["csk"] = f(cache_sb_k[4 * c:4 * c + 4]).reshape(4096, 1024)
        m["csv"] = f(cache_sb_v[4 * c:4 * c + 4]).reshape(4096, 1024)
        m["clf"] = f(cache_fox_logf[4 * c:4 * c + 4])
        m["cst"] = _consts(half)
        in_maps.append(m)
    res = run_bass_kernel_spmd(nc, in_maps, core_ids=list(range(NCORES)))
    R = res.results
    B, S = 4, 2048

    def gather_prompt(name, width):
        out = np.zeros((B, 16, 128, width), np.float32)
        for c in range(NCORES):
            b, half = c // 2, c % 2
            out[b, OWN[half]] = R[c][name].reshape(8, 128, width)
        return out.reshape(B, S, width)

    def gather_sample(name, width):
        return np.concatenate([R[c][name].reshape(4, 16, width) for c in range(NCORES)], axis=0)

    y_prompt = gather_prompt("y_own", D)
    y_sample = gather_sample("y_s", D)
    fkp = gather_prompt("o_fk", 1024).reshape(B, S, 8, 128)
    fvp = gather_prompt("o_fv", 1024).reshape(B, S, 8, 128)
    lfp = gather_prompt("o_lf", 8)
    fks = gather_sample("o_fk_s", 1024).reshape(32, 16, 8, 128)
    fvs = gather_sample("o_fv_s", 1024).reshape(32, 16, 8, 128)
    lfs = gather_sample("o_lf_s", 8)
    skp = gather_prompt("o_sk", 1024).reshape(B, S, 8, 128)
    svp = gather_prompt("o_sv", 1024).reshape(B, S, 8, 128)
    sks = gather_sample("o_sk_s", 1024).reshape(32, 16, 8, 128)
    svs = gather_sample("o_sv_s", 1024).reshape(32, 16, 8, 128)
    sgu = gather_sample("o_sgu", 4096)
    return (y_prompt, y_sample, fkp, fvp, lfp, fks, fvs, lfs, skp, svp, sks, svs, sgu)
```

```python
import numpy as np
from contextlib import ExitStack
import concourse.bass as bass
import concourse.mybir as mybir
from concourse.bass_utils import run_bass_kernel_spmd

F32 = mybir.dt.float32
BF16 = mybir.dt.bfloat16
AF = mybir.ActivationFunctionType
ALU = mybir.AluOpType

D = 2048
NCORES = 8
SCALE = 128 ** -0.5
OWN = {0: [0, 3, 4, 7, 8, 11, 12, 15], 1: [1, 2, 5, 6, 9, 10, 13, 14]}
OTHER = {h: [b for b in range(16) if b not in OWN[h]] for h in (0, 1)}
NTOK = 2112
NOWN = 1088
STAGE = 99
import os
PARTS = os.environ.get('PARTS', 'KOVQ')
NGROUPS = int(os.environ.get('NGROUPS', '8'))
DBG = os.environ.get('DBG', '')

COMPUTE = ("pe", "act", "dve", "pool")
ALL_ENG = ("pe", "act", "dve", "pool", "sp")


class Res:
    __slots__ = ("name", "writer", "readers", "lock")

    def __init__(self, name="", lock=None):
        self.name = name
        self.writer = None
        self.readers = []
        self.lock = lock


class Op:
    __slots__ = ("eng", "fn", "deps", "is_dma", "count", "needed", "dsem", "dval", "prewait")

    def __init__(self, eng, fn, is_dma):
        self.eng = eng
        self.fn = fn
        self.deps = []
        self.is_dma = is_dma
        self.count = None
        self.needed = False
        self.dsem = None
        self.dval = None
        self.prewait = None


class Prog:
    def __init__(self, nc):
        self.nc = nc
        self.ops = {e: [] for e in ALL_ENG}
        self.n_dma_sems = {"sp": 24, "pool": 24}
        self.dma_rr = {e: 0 for e in ALL_ENG}
        self.dma_last = {}
        self.dma_cnt = {}
        self.phase = Res("phase")

    def _record(self, op, reads, writes):
        deps = []
        for r in reads:
            if r.writer is not None:
                deps.append(r.writer)
        for w in writes:
            if w.writer is not None:
                deps.append(w.writer)
            last = {}
            for r in w.readers:
                if r.is_dma:
                    deps.append(r)
                else:
                    last[r.eng] = r
            deps.extend(last.values())
        seen = set()
        for d in deps:
            if d is op or id(d) in seen:
                continue
            seen.add(id(d))
            if op.eng == "pe" and d.eng == "pe" and not d.is_dma and not op.is_dma:
                continue
            op.deps.append(d)
            d.needed = True
        for r in reads:
            r.readers.append(op)
        for w in writes:
            w.writer = op
            w.readers = []
        self.ops[op.eng].append(op)
        return op

    def op(self, eng, fn, reads=(), writes=(), glob=False):
        reads = list(reads)
        writes = list(writes)
        for r in reads:
            if r.lock is not None and r not in writes:
                writes.append(r.lock)
        if not glob:
            reads.append(self.phase)
        return self._record(Op(eng, fn, False), reads, writes)

    def dma(self, eng, out, in_, reads=(), writes=(), glob=False):
        def fn(e, out=out, in_=in_):
            return e.dma_start(out=out, in_=in_)
        op = Op(eng, fn, True)
        n = self.n_dma_sems[eng]
        slot = self.dma_rr[eng] % n
        self.dma_rr[eng] += 1
        key = (eng, slot)
        cnt = self.dma_cnt.get(key, 0) + 1
        self.dma_cnt[key] = cnt
        op.dsem = key
        op.dval = 16 * cnt
        op.prewait = self.dma_last.get(key)
        self.dma_last[key] = op
        reads = list(reads)
        if not glob:
            reads.append(self.phase)
        return self._record(op, reads, list(writes))

    def barrier(self, scratch):
        o = Op("pool", lambda e: e.memset(scratch, 0.0), False)
        last = {}
        for r in self.phase.readers:
            if r.is_dma:
                o.deps.append(r)
            else:
                last[r.eng] = r
        if self.phase.writer is not None:
            o.deps.append(self.phase.writer)
        for r in last.values():
            o.deps.append(r)
            r.needed = True
        self.phase.writer = o
        self.phase.readers = []
        self.ops["pool"].append(o)

    def emit(self, stack):
        nc = self.nc
        eng_sem = {e: stack.enter_context(nc.semaphore("es_" + e)) for e in COMPUTE}
        dma_sem = {}
        for e, n in self.n_dma_sems.items():
            for s in range(n):
                dma_sem[(e, s)] = stack.enter_context(nc.semaphore(f"ds_{e}{s}"))
        for e in COMPUTE:
            c = 0
            for o in self.ops[e]:
                if o.needed and not o.is_dma:
                    c += 1
                    o.count = c
        block = stack.enter_context(nc.Block())
        prog = self

        prog.trace = {e: [] for e in ALL_ENG}

        def run(engname, eng):
            known = {}
            tr = prog.trace[engname]

            def wait(key, sem, val):
                if known.get(key, 0) >= val:
                    return
                eng.wait_ge(sem, val)
                tr.append(("w", key, val))
                known[key] = val

            def wait_for(d):
                if d.is_dma:
                    wait(d.dsem, dma_sem[d.dsem], d.dval)
                else:
                    if d.eng == engname and engname == "pe":
                        return
                    wait(d.eng, eng_sem[d.eng], d.count)

            def wait_all(ds):
                need = {}
                for d in ds:
                    if d.is_dma:
                        k, v = d.dsem, d.dval
                    else:
                        if d.eng == engname and engname == "pe":
                            continue
                        k, v = d.eng, d.count
                    if need.get(k, 0) < v:
                        need[k] = v
                for k, v in need.items():
                    wait(k, dma_sem[k] if isinstance(k, tuple) else eng_sem[k], v)

            for o in prog.ops[engname]:
                ds = list(o.deps)
                if o.is_dma and o.prewait is not None:
                    ds.append(o.prewait)
                wait_all(ds)
                ins = o.fn(eng)
                if o.is_dma:
                    ins.then_inc(dma_sem[o.dsem], 16)
                    tr.append(("i", o.dsem, 16))
                elif o.needed:
                    ins.then_inc(eng_sem[engname], 1)
                    tr.append(("i", engname, 1))
                else:
                    tr.append(("n", None, 0))
            for key, last in prog.dma_last.items():
                if key[0] == engname:
                    wait(key, dma_sem[key], last.dval)

        @block.tensor
        def _(e):
            run("pe", e)

        @block.scalar
        def _(e):
            run("act", e)

        @block.vector
        def _(e):
            run("dve", e)

        @block.gpsimd
        def _(e):
            run("pool", e)

        @block.sync
        def _(e):
            run("sp", e)


def build_program(stage=99):
    nc = bass.Bass("TRN2", target_bir_lowering=False)

    def din(name, shape):
        return nc.dram_tensor(name, list(shape), F32, kind="ExternalInput").ap()

    def dout(name, shape):
        return nc.dram_tensor(name, list(shape), F32, kind="ExternalOutput").ap()

    xall = din("xall", [NTOK, D])
    ck = [din("cfk", [4096, 1024]), din("csk", [4096, 1024])]
    cv = [din("cfv", [4096, 1024]), din("csv", [4096, 1024])]
    clf = din("clf", [4, 1024, 8])
    w_in0 = din("w_in0", [D, 8200])
    w_out0 = din("w_out0", [D, D])
    w_in1 = din("w_in1", [D, 12288])
    w_out1 = din("w_out1", [4096, D])
    gT_in = din("gT", [128, 48])
    bfg = din("bforget", [8])
    lngb = din("lngb", [128, 64])
    ln_g = din("ln_g", [4096])
    ln_b = din("ln_b", [4096])
    final_g = din("final_g", [D])
    w_sp = din("w_sp", [16, 128, 128])
    b_sp = din("b_sp", [16, 128])
    cst = din("cst", [128, 14 * 128])

    y_own = dout("y_own", [1024, D])
    y_s = dout("y_s", [64, D])
    okv = {}
    for nm in ("fk", "fv", "sk", "sv"):
        okv[nm] = dout("o_" + nm, [1024, 1024])
        okv[nm + "_s"] = dout("o_" + nm + "_s", [64, 1024])
    o_lf = dout("o_lf", [1024, 8])
    o_lf_s = dout("o_lf_s", [64, 8])
    o_sgu = dout("o_sgu", [64, 4096])
    hp_scr = nc.dram_tensor("hp_scr", [NOWN + 64, D], F32, kind="Internal").ap()

    st = ExitStack()
    with st:
        ARENA = 207 * 1024
        arena = st.enter_context(nc.sbuf_tensor("arena", [128, ARENA // 4], F32))
        banks = [st.enter_context(nc.psum_tensor(f"bank{i}", [128, 512], F32)) for i in range(8)]
        RB = [Res(f"bank{i}", lock=Res(f"banklock{i}")) for i in range(8)]
        p = Prog(nc)

        class Alloc:
            top = 0

        def view_at(off, shape, dt):
            esz = 4 if dt == F32 else 2
            n = int(np.prod(shape[1:]))
            assert off % 4 == 0 and off + n * esz <= ARENA, (off, shape)
            a = arena[:, off // 4: off // 4 + (n * esz + 3) // 4]
            if dt != F32:
                a = a.bitcast(dt)
            a = a[:, 0:n]
            if len(shape) == 3:
                a = a.rearrange("p (a b) -> p a b", a=shape[1])
            elif len(shape) == 4:
                a = a.rearrange("p (a b c) -> p a b c", a=shape[1], b=shape[2])
            return a

        def alloc(shape, dt, name=""):
            esz = 4 if dt == F32 else 2
            n = int(np.prod(shape[1:]))
            nbytes = (n * esz + 63) // 64 * 64
            off = Alloc.top
            Alloc.top += nbytes
            assert Alloc.top <= ARENA, (name, Alloc.top)
            return view_at(off, shape, dt)

        def bank_bf(i):
            return banks[i][:, :].bitcast(BF16)

        cst_f = alloc([128, 14, 128], F32, "cst_f")
        cst_b = alloc([128, 14, 128], BF16, "cst_b")
        (I_ID, I_TRILE, I_ONES, I_EEV, I_EOD, I_SL, I_MLE, I_MLT, I_ZERO, I_BDTRI, I_BDLE, I_BDLT, I_SAME, I_SEL) = range(14)
        R_cst = Res("cst")
        p.dma("sp", cst_f, cst.rearrange("p (a b) -> p a b", a=14), writes=[R_cst], glob=True)
        p.op("dve", lambda e: e.tensor_copy(out=cst_b, in_=cst_f), reads=[R_cst], writes=[R_cst], glob=True)
        ident_b = cst_b[:, I_ID, :]
        ones_b = cst_b[:, I_ONES, :]
        zero_b = cst_b[:, I_ZERO, :]
        SL_b = cst_b[:, I_SL, :]
        MLE_b = cst_b[:, I_MLE, :]
        MLT_b = cst_b[:, I_MLT, :]
        E_b = [cst_b[:, I_EEV, :], cst_b[:, I_EOD, :]]
        E_f = [cst_f[:, I_EEV, :], cst_f[:, I_EOD, :]]
        trile_f = cst_f[:, I_TRILE, :]
        ones_f = cst_f[:, I_ONES, :]
        SL_f = cst_f[:, I_SL, :]

        def own_first(pp, bf=True):
            t = E_b if bf else E_f
            return t[0] if pp % 2 == 0 else t[1]

        def other_first(pp, bf=True):
            t = E_b if bf else E_f
            return t[1] if pp % 2 == 0 else t[0]

        gT = alloc([128, 48], F32, "gT")
        lngbT = alloc([128, 64], F32, "lngbT")
        bfB = alloc([128, 8], F32, "bfB")
        p.dma("sp", gT, gT_in, writes=[R_cst], glob=True)
        p.dma("sp", lngbT, lngb, writes=[R_cst], glob=True)
        p.dma("sp", bfB, bfg.partition_broadcast(128), writes=[R_cst], glob=True)
        scratch = alloc([128, 16], F32, "scratch")

        NSLAB = 3
        SLAB_BYTES = 8192
        slab_off = Alloc.top
        slab_raw = [alloc([128, SLAB_BYTES // 2], BF16, f"slab{i}") for i in range(NSLAB)]
        R_slab = [[Res(f"slab{i}_{q}") for q in range(4)] for i in range(NSLAB)]
        slab_ctr = [0]
        slab_first = {}

        def load_slab(w_ap, row_chunks, col0, ncols):
            i = slab_ctr[0] % len(slab_raw)
            slab_ctr[0] += 1
            first_extra = slab_first.pop(i, [])
            assert row_chunks * ncols * 2 <= SLAB_BYTES
            v = slab_raw[i][:, 0:row_chunks * ncols].rearrange("p (c n) -> p c n", c=row_chunks)
            src = w_ap.rearrange("(c p) n -> p c n", p=128)
            step = max(1, row_chunks // 4)
            rl = []
            for qi, c0 in enumerate(range(0, row_chunks, step)):
                p.dma("pool", v[:, c0:c0 + step, :], src[:, c0:c0 + step, col0:col0 + ncols],
                      writes=[R_slab[i][qi]] + first_extra, glob=True)
                rl += [R_slab[i][qi]] * step
            return v, rl

        kvq_slabs = {}

        def issue_kvq(m, h0, parts="kvq"):
            base = m * 4096
            d_ = kvq_slabs.setdefault((m, h0), {})
            for nm_, off_ in (("k", 1024), ("v", 2048), ("q", 0)):
                if nm_ in parts:
                    d_[nm_] = load_slab(w_in0, 16, base + off_ + h0 * 128, 256)

        wlf, R_wlf = load_slab(w_in0, 16, 8192, 8)
        if stage >= 2:
            issue_kvq(0, 0, "kv")

        hT_off = Alloc.top
        hT = alloc([128, 16, NTOK], BF16, "hT")
        R_hT = [Res(f"hT{i}") for i in range(17)]
        mixT = alloc([128, 16, NOWN], BF16, "mixT")
        R_mix = [Res(f"mix{h}") for h in range(16)]
        R_mix_s = [Res(f"mixs{h}") for h in range(16)]
        KTs = alloc([128, 16, 64], BF16, "KTs")
        QTs = alloc([128, 16, 64], BF16, "QTs")
        GTs = alloc([128, 16, 64], BF16, "GTs")
        Vs = alloc([128, 16, 128], BF16, "Vs")
        R_samp = Res("samp_keep")
        logf = alloc([128, 17, 8], F32, "logf")
        Fneg = alloc([128, 16, 8], F32, "Fneg")
        PB = alloc([128, 8, 8], F32, "PB")
        R_F = Res("F")
        mark0 = Alloc.top

        def tile_rows(i):
            return 128 if i < 16 else 64

        def tile_cols(i):
            return slice(i * 128, i * 128 + tile_rows(i))

        xt = [alloc([128, D], F32, f"xt{i}") for i in range(2)]
        R_xt = [Res("xt0"), Res("xt1")]
        xn = [alloc([128, D], BF16, f"xn{i}") for i in range(2)]
        R_xn = [Res("xn0"), Res("xn1")]
        junk = alloc([128, D], BF16, "junk")
        R_junk = Res("junk")
        ssq = alloc([128, 32], F32, "ssq")
        R_ssq = Res("ssq")

        def norm_transpose(i, src_tile, R_src, rows, g_off, dstT, R_dst, col0, pb, bufs=None):
            b = i % 2
            xn, R_xn, junk, R_junk, ssq, R_ssq = bufs
            p.op("act", lambda e: e.activation(out=junk[0:rows, :], in_=src_tile[0:rows, :], func=AF.Square,
                                               accum_out=ssq[0:rows, i:i + 1]),
                 reads=[R_src], writes=[R_junk, R_ssq])
            p.op("act", lambda e: e.activation(out=ssq[0:rows, i:i + 1], in_=ssq[0:rows, i:i + 1], func=AF.Ln,
                                               scale=1.0 / D, bias=1e-6), reads=[R_ssq], writes=[R_ssq])
            p.op("act", lambda e: e.activation(out=ssq[0:rows, i:i + 1], in_=ssq[0:rows, i:i + 1], func=AF.Exp,
                                               scale=-0.5), reads=[R_ssq], writes=[R_ssq])
            p.op("dve", lambda e: e.tensor_scalar(out=xn[b][0:rows, :], in0=src_tile[0:rows, :],
                                                  scalar1=ssq[0:rows, i:i + 1], scalar2=None, op0=ALU.mult),
                 reads=[R_src, R_ssq], writes=[R_xn[b]])
            for half in range(2):
                bk = pb + half
                pv = bank_bf(bk)
                for cc in range(8):
                    c = half * 8 + cc
                    p.op("pe", lambda e, c=c, cc=cc, pv=pv: e.transpose(
                        out=pv[:, cc * 128:cc * 128 + rows], in_=xn[b][0:rows, c * 128:(c + 1) * 128],
                        identity=ident_b[0:rows, 0:rows]),
                        reads=[R_xn[b], R_cst], writes=[RB[bk]])
                pv3 = pv.rearrange("p (a b) -> p a b", a=8)
                p.op("dve", lambda e, half=half, pv3=pv3: e.tensor_tensor(
                    out=dstT[:, half * 8:(half + 1) * 8, col0:col0 + rows], in0=pv3[:, :, 0:rows],
                    in1=gT[:, g_off + half * 8:g_off + (half + 1) * 8].unsqueeze(2).to_broadcast([128, 8, rows]),
                    op=ALU.mult), reads=[RB[bk], R_cst], writes=[R_dst])

        for i in range(17):
            rows = tile_rows(i)
            b = i % 2
            p.dma("sp", xt[b][0:rows, :], xall[i * 128:i * 128 + rows, :], writes=[R_xt[b]])
            norm_transpose(i, xt[b], R_xt[b], rows, 0, hT, R_hT[i], i * 128, (i % 2) * 2,
                           bufs=(xn, R_xn, junk, R_junk, ssq, R_ssq))

        bkL = 4
        for i in range(17):
            rows = tile_rows(i)
            for c in range(16):
                p.op("pe", lambda e, i=i, c=c, rows=rows: e.matmul(
                    out=banks[bkL][0:rows, i * 8:(i + 1) * 8], lhsT=hT[:, c, i * 128:i * 128 + rows],
                    rhs=wlf[:, c, :], start=(c == 0), stop=(c == 15)),
                    reads=[R_hT[i], R_wlf[c]], writes=[RB[bkL]])
        lg = alloc([128, 17, 8], F32, "lg")
        R_lg = Res("lg")
        for (r0, r1, t0, t1) in ((0, 128, 0, 16), (0, 64, 16, 17)):
            nt = t1 - t0
            pv = banks[bkL][r0:r1, t0 * 8:t1 * 8].rearrange("p (a b) -> p a b", a=nt)
            p.op("dve", lambda e, pv=pv, r0=r0, r1=r1, t0=t0, t1=t1, nt=nt: e.tensor_tensor(
                out=lg[r0:r1, t0:t1, :], in0=pv, in1=bfB[r0:r1, :].unsqueeze(1).to_broadcast([r1 - r0, nt, 8]),
                op=ALU.add), reads=[RB[bkL], R_cst], writes=[R_lg])
            p.op("act", lambda e, r0=r0, r1=r1, t0=t0, t1=t1: e.activation(
                out=lg[r0:r1, t0:t1, :], in_=lg[r0:r1, t0:t1, :], func=AF.Exp, scale=-1.0), reads=[R_lg], writes=[R_lg])
            p.op("act", lambda e, r0=r0, r1=r1, t0=t0, t1=t1: e.activation(
                out=lg[r0:r1, t0:t1, :], in_=lg[r0:r1, t0:t1, :], func=AF.Ln, bias=1.0), reads=[R_lg], writes=[R_lg])
            p.op("dve", lambda e, r0=r0, r1=r1, t0=t0, t1=t1: e.tensor_scalar(
                out=logf[r0:r1, t0:t1, :], in0=lg[r0:r1, t0:t1, :], scalar1=-1.0, scalar2=None, op0=ALU.mult),
                reads=[R_lg], writes=[R_F])
        p.dma("sp", o_lf.rearrange("(i p) h -> p i h", p=128), logf[:, 0:8, :], reads=[R_F])
        p.dma("sp", o_lf_s, logf[0:64, 16, :], reads=[R_F])
        Spre = alloc([128, 9, 8], F32, "Spre")
        psum_pair = alloc([128, 8, 8], F32, "psum_pair")
        R_S = Res("Spre")
        p.op("dve", lambda e: e.tensor_tensor(out=psum_pair, in0=logf[:, 0:8, :], in1=logf[:, 8:16, :], op=ALU.add),
             reads=[R_F], writes=[R_S])
        p.op("dve", lambda e: e.memset(Spre[:, 0, :], 0.0), writes=[R_S])
        for pp in range(8):
            p.op("dve", lambda e, pp=pp: e.tensor_tensor(out=Spre[:, pp + 1, :], in0=Spre[:, pp, :],
                                                         in1=psum_pair[:, pp, :], op=ALU.add),
                 reads=[R_S], writes=[R_S])
        bkF = 5
        for k in range(16):
            pp = k % 8
            is_other = k >= 8
            partner = pp if is_other else 8 + pp
            Et = own_first(pp, bf=False) if is_other else other_first(pp, bf=False)
            o = banks[bkF][:, k * 8:(k + 1) * 8]
            p.op("pe", lambda e, o=o, k=k: e.matmul(out=o, lhsT=trile_f, rhs=logf[:, k, :], start=True, stop=False),
                 reads=[R_F, R_cst], writes=[RB[bkF]])
            p.op("pe", lambda e, o=o, pp=pp: e.matmul(out=o, lhsT=ones_f, rhs=Spre[:, pp, :], start=False, stop=False),
                 reads=[R_S, R_cst], writes=[RB[bkF]])
            p.op("pe", lambda e, o=o, Et=Et, partner=partner: e.matmul(out=o, lhsT=Et, rhs=logf[:, partner, :],
                                                                        start=False, stop=True),
                 reads=[R_F, R_cst], writes=[RB[bkF]])
        for pp in range(8):
            o = banks[bkF][:, 128 + pp * 8:128 + (pp + 1) * 8]
            p.op("pe", lambda e, o=o, pp=pp: e.matmul(out=o, lhsT=ones_f, rhs=Spre[:, pp, :], start=True, stop=True),
                 reads=[R_S, R_cst], writes=[RB[bkF]])
        R_F2 = Res("F2")
        p.op("dve", lambda e: e.tensor_scalar(out=Fneg, in0=banks[bkF][:, 0:128].rearrange("p (a b) -> p a b", a=16),
                                              scalar1=-1.0, scalar2=None, op0=ALU.mult),
             reads=[RB[bkF]], writes=[R_F2])
        p.op("dve", lambda e: e.tensor_copy(out=PB.rearrange("p h s -> p s h"),
                                            in_=banks[bkF][:, 128:192].rearrange("p (s h) -> p s h", s=8)),
             reads=[RB[bkF]], writes=[R_F2])

        p.barrier(scratch)
        Alloc.top = mark0

        KT = alloc([128, 2, 2048], BF16, "KT")
        Vg = alloc([128, 16, 256], BF16, "Vg")
        QT = alloc([128, 2, 1024], BF16, "QT")
        GT = alloc([128, 2, 1024], BF16, "GT")
        R_KT = [Res("KT0"), Res("KT1")]
        R_Vg = Res("Vg")
        R_QT = [Res("QT0"), Res("QT1")]
        R_GT = [Res("GT0"), Res("GT1")]
        stg = [alloc([128, 256], F32, f"stg{i}") for i in range(4)]
        R_stg = [Res(f"stg{i}") for i in range(4)]
        stg_ctr = [0]
        wA = alloc([128, 1024], F32, "wA")
        wB = alloc([128, 1024], F32, "wB")
        wC = alloc([128, 1024], F32, "wC")
        wD = alloc([128, 1024], F32, "wD")
        wE = alloc([128, 1024], F32, "wE")
        hA = alloc([128, 1024], BF16, "hA")
        hB = alloc([128, 1024], BF16, "hB")
        hC = alloc([128, 1024], BF16, "hC")
        hD = alloc([128, 1024], BF16, "hD")
        hE = alloc([128, 1024], BF16, "hE")
        R_w = {k: Res(k) for k in ("wA", "wB", "wC", "wD", "wE", "hA", "hB", "hC", "hD", "hE")}
        ev_ctr = [0]

        def evac(out, in_, reads, writes, func=None, eng="dve"):
            if "noevac" in DBG:
                return
            if "dveonly" in DBG and func is None:
                p.op("dve", lambda e: e.tensor_copy(out=out, in_=in_), reads=reads, writes=writes)
                return
            if "actonly" in DBG:
                f = func if func is not None else AF.Copy
                p.op("act", lambda e: e.activation(out=out, in_=in_, func=f), reads=reads, writes=writes)
                return
            if func is not None or eng == "act":
                f = func if func is not None else AF.Copy
                p.op("act", lambda e: e.activation(out=out, in_=in_, func=f), reads=reads, writes=writes)
            else:
                p.op("dve", lambda e: e.tensor_copy(out=out, in_=in_), reads=reads, writes=writes)
            ev_ctr[0] += 1

        pb_ctr = [0]

        def next_bank(lo=0, hi=8):
            b = lo + pb_ctr[0] % (hi - lo)
            pb_ctr[0] += 1
            return b

        def col_chunks(c0, c1):
            out = []
            c = c0
            while c < c1:
                e_ = min(c1, (c // 512 + 1) * 512)
                out.append((c, e_))
                c = e_
            return out

        out_names = {0: ("fk", "fv"), 1: ("sk", "sv")}

        def project_group(m, h0):
            base = m * 4096
            hg = m * 8 + h0
            kname, vname = out_names[m]
            d_ = kvq_slabs.pop((m, h0))
            (wk, Rwk), (wv, Rwv), (wq_, Rwq_) = d_["k"], d_["v"], d_["q"]
            for hh in (range(2) if "K" in PARTS else []):
                for (c0, c1) in ((0, 512), (512, 1024), (1024, 1536), (1536, 2048), (2048, 2112)):
                    bk = next_bank(0, 4)
                    n = c1 - c0
                    Rr = [R_hT[t] for t in range(c0 // 128, (c1 + 127) // 128)]
                    for c in range(16):
                        p.op("pe", lambda e, bk=bk, hh=hh, c=c, c0=c0, c1=c1, n=n: e.matmul(
                            out=banks[bk][:, 0:n], lhsT=wk[:, c, hh * 128:(hh + 1) * 128], rhs=hT[:, c, c0:c1],
                            start=(c == 0), stop=(c == 15)), reads=Rr + [Rwk[c]], writes=[RB[bk]])
                    if c0 < 2048:
                        evac(KT[:, hh, c0:c1], banks[bk][:, 0:n], [RB[bk]], [R_KT[hh]])
                    else:
                        evac(KTs[:, hg + hh, :], banks[bk][:, 0:64], [RB[bk]], [R_samp])
            for i in ((list(range(8)) + [16]) if "O" in PARTS else []):
                rows = tile_rows(i)
                bk = next_bank(0, 4)
                for c in range(16):
                    p.op("pe", lambda e, bk=bk, c=c, i=i, rows=rows: e.matmul(
                        out=banks[bk][0:rows, 0:256], lhsT=hT[:, c, i * 128:i * 128 + rows], rhs=wk[:, c, :],
                        start=(c == 0), stop=(c == 15)), reads=[R_hT[i], Rwk[c]], writes=[RB[bk]])
                s = stg_ctr[0] % 4
                stg_ctr[0] += 1
                evac(stg[s][0:rows, :], banks[bk][0:rows, 0:256], [RB[bk]], [R_stg[s]], eng="act")
                dst = okv[kname][i * 128:(i + 1) * 128, h0 * 128:h0 * 128 + 256] if i < 8 else \
                    okv[kname + "_s"][:, h0 * 128:h0 * 128 + 256]
                p.dma("sp", dst, stg[s][0:rows, :], reads=[R_stg[s]])
            wg_, Rwg_ = load_slab(w_in0, 16, base + 3072 + h0 * 128, 256)
            for i in (range(17) if "V" in PARTS else []):
                rows = tile_rows(i)
                bk = next_bank(0, 4)
                for c in range(16):
                    p.op("pe", lambda e, bk=bk, c=c, i=i, rows=rows: e.matmul(
                        out=banks[bk][0:rows, 0:256], lhsT=hT[:, c, i * 128:i * 128 + rows], rhs=wv[:, c, :],
                        start=(c == 0), stop=(c == 15)), reads=[R_hT[i], Rwv[c]], writes=[RB[bk]])
                if i < 16:
                    evac(Vg[:, i, :], banks[bk][:, 0:256], [RB[bk]], [R_Vg])
                else:
                    evac(Vs[0:64, hg:hg + 2, :], banks[bk][0:64, 0:256].rearrange("p (a b) -> p a b", a=2),
                         [RB[bk]], [R_samp])
                if i < 8 or i == 16:
                    s = stg_ctr[0] % 4
                    stg_ctr[0] += 1
                    evac(stg[s][0:rows, :], banks[bk][0:rows, 0:256], [RB[bk]], [R_stg[s]], eng="act")
                    dst = okv[vname][i * 128:(i + 1) * 128, h0 * 128:h0 * 128 + 256] if i < 8 else \
                        okv[vname + "_s"][:, h0 * 128:h0 * 128 + 256]
                    p.dma("sp", dst, stg[s][0:rows, :], reads=[R_stg[s]])
            for (ws, Rws, dstT, R_dst, dsts, func) in (
                    (wq_, Rwq_, QT, R_QT, QTs, None),
                    (wg_, Rwg_, GT, R_GT, GTs, AF.Silu)):
                for hh in (range(2) if "Q" in PARTS else []):
                    for (c0, c1) in ((0, 512), (512, 1024), (2048, 2112)):
                        bk = next_bank(0, 4)
                        n = c1 - c0
                        Rr = [R_hT[t] for t in range(c0 // 128, (c1 + 127) // 128)]
                        for c in range(16):
                            p.op("pe", lambda e, bk=bk, hh=hh, c=c, c0=c0, c1=c1, n=n, ws=ws: e.matmul(
                                out=banks[bk][:, 0:n], lhsT=ws[:, c, hh * 128:(hh + 1) * 128], rhs=hT[:, c, c0:c1],
                                start=(c == 0), stop=(c == 15)), reads=Rr + [Rws[c]], writes=[RB[bk]])
                        if c0 < 2048:
                            evac(dstT[:, hh, c0:c1], banks[bk][:, 0:n], [RB[bk]], [R_dst[hh]], func=func)
                        else:
                            evac(dsts[:, hg + hh, :], banks[bk][:, 0:64], [RB[bk]], [R_samp], func=func)

        def fox_head(hh, h):
            for q in range(2):
                p.op("pe", lambda e, q=q: e.matmul(out=banks[4 + q][:, :], lhsT=zero_b, rhs=QT[:, hh, q * 512:(q + 1) * 512],
                                                    start=True, stop=False), reads=[R_QT[hh], R_cst], writes=[RB[4 + q]])
                p.op("pe", lambda e, q=q: e.matmul(out=banks[6 + q][:, :], lhsT=zero_b, rhs=QT[:, hh, q * 512:(q + 1) * 512],
                                                    start=True, stop=False), reads=[R_QT[hh], R_cst], writes=[RB[6 + q]])
            its = [(pp, is_other) for pp in range(8) for is_other in (False, True)]

            def geom(it):
                pp, is_other = its[it]
                kb = 8 + pp if is_other else pp
                c0 = pp * 128
                return pp, is_other, kb, c0, (it % 2) * 2, col_chunks(c0, 1024)

            def qk(it):
                pp, is_other, kb, c0, sb0, chunks = geom(it)
                for (a, b_) in chunks:
                    bk = sb0 + a // 512
                    p.op("pe", lambda e, bk=bk, a=a, b_=b_, kb=kb: e.matmul(
                        out=banks[bk][:, a % 512:a % 512 + (b_ - a)], lhsT=KT[:, hh, kb * 128:(kb + 1) * 128],
                        rhs=QT[:, hh, a:b_], start=True, stop=True),
                        reads=[R_KT[hh], R_QT[hh]], writes=[RB[bk]])

            def elem(it):
                pp, is_other, kb, c0, sb0, chunks = geom(it)
                tmp = (wA, wB)[it % 2]
                Rtmp = (R_w["wA"], R_w["wB"])[it % 2]
                PT = (hA, hB)[it % 2]
                RPT = (R_w["hA"], R_w["hB"])[it % 2]
                for (a, b_) in chunks:
                    bk = sb0 + a // 512
                    ns = (b_ - a) // 128
                    s0 = a // 128
                    p.op("dve", lambda e, bk=bk, a=a, b_=b_, ns=ns, s0=s0, tmp=tmp: e.scalar_tensor_tensor(
                        out=tmp[:, a:b_].rearrange("p (s t) -> p s t", s=ns),
                        in0=banks[bk][:, a % 512:a % 512 + (b_ - a)].rearrange("p (s t) -> p s t", s=ns),
                        scalar=SCALE,
                        in1=PB[:, h, s0:s0 + ns].unsqueeze(2).to_broadcast([128, ns, 128]),
                        op0=ALU.mult, op1=ALU.add), reads=[RB[bk], R_F2], writes=[Rtmp])
                p.op("act", lambda e, c0=c0, kb=kb, tmp=tmp, PT=PT: e.activation(
                    out=PT[:, c0:1024], in_=tmp[:, c0:1024], func=AF.Exp, bias=Fneg[:, kb, h:h + 1], scale=1.0),
                    reads=[Rtmp, R_F2], writes=[RPT])
                mk = other_first(pp) if is_other else MLE_b
                p.op("pool", lambda e, c0=c0, mk=mk, PT=PT: e.tensor_tensor(
                    out=PT[:, c0:c0 + 128], in0=PT[:, c0:c0 + 128], in1=mk, op=ALU.mult),
                    reads=[RPT, R_cst], writes=[RPT])

            def pv(it):
                pp, is_other, kb, c0, sb0, chunks = geom(it)
                PT = (hA, hB)[it % 2]
                RPT = (R_w["hA"], R_w["hB"])[it % 2]
                for (a, b_) in chunks:
                    q = a // 512
                    last = is_other and ((pp == 3 and q == 0) or pp == 7)
                    p.op("pe", lambda e, q=q, a=a, b_=b_, kb=kb, last=last, PT=PT: e.matmul(
                        out=banks[4 + q][:, a % 512:a % 512 + (b_ - a)], lhsT=Vg[:, kb, hh * 128:(hh + 1) * 128],
                        rhs=PT[:, a:b_], start=False, stop=last), reads=[R_Vg, RPT], writes=[RB[4 + q]])
                    p.op("pe", lambda e, q=q, a=a, b_=b_, last=last, PT=PT: e.matmul(
                        out=banks[6 + q][:, a % 512:a % 512 + (b_ - a)], lhsT=ones_b,
                        rhs=PT[:, a:b_], start=False, stop=last), reads=[RPT, R_cst], writes=[RB[6 + q]])

            qk(0)
            for it in range(16):
                if it + 1 < 16:
                    qk(it + 1)
                elem(it)
                pv(it)
            for q in range(2):
                cs = slice(q * 512, (q + 1) * 512)
                p.op("dve", lambda e, q=q, cs=cs: e.reciprocal(out=wC[:, cs], in_=banks[6 + q][:, :]),
                     reads=[RB[6 + q]], writes=[R_w["wC"]])
                p.op("dve", lambda e, q=q, cs=cs: e.tensor_tensor(out=wC[:, cs], in0=banks[4 + q][:, :], in1=wC[:, cs],
                                                                 op=ALU.mult),
                     reads=[RB[4 + q], R_w["wC"]], writes=[R_w["wC"]])
                p.op("pool", lambda e, cs=cs: e.tensor_tensor(out=mixT[:, h, cs], in0=wC[:, cs], in1=GT[:, hh, cs],
                                                              op=ALU.mult),
                     reads=[R_w["wC"], R_GT[hh]], writes=[R_mix[h]])

        def sb_head(hh, h):
            SPS, SPSb = wE, hE
            for q in range(2):
                p.op("pe", lambda e, q=q: e.matmul(out=banks[6 + q][:, :], lhsT=zero_b, rhs=QT[:, hh, q * 512:(q + 1) * 512],
                                                    start=True, stop=False), reads=[R_QT[hh], R_cst], writes=[RB[6 + q]])
            p.op("pool", lambda e: e.memset(SPS, 0.0), writes=[R_w["wE"]])
            p.op("pool", lambda e: e.memset(SPSb, 0.0), writes=[R_w["hE"]])
            for pp in range(7, -1, -1):
                c0 = pp * 128
                chunks = col_chunks(c0, 1024)
                blocks = ((pp, 0, wA, "wA", hA, "hA", MLT_b), (8 + pp, 2, wB, "wB", hB, "hB", other_first(pp)))
                for (kb, zb, sp, spn, spm, spmn, mk) in blocks:
                    for (a, b_) in chunks:
                        bk = zb + a // 512
                        p.op("pe", lambda e, bk=bk, a=a, b_=b_, kb=kb: e.matmul(
                            out=banks[bk][:, a % 512:a % 512 + (b_ - a)], lhsT=KT[:, hh, kb * 128:(kb + 1) * 128],
                            rhs=QT[:, hh, a:b_], start=True, stop=True),
                            reads=[R_KT[hh], R_QT[hh]], writes=[RB[bk]])
                    for (a, b_) in chunks:
                        bk = zb + a // 512
                        p.op("act", lambda e, bk=bk, a=a, b_=b_: e.activation(
                            out=wC[:, a:b_], in_=banks[bk][:, a % 512:a % 512 + (b_ - a)], func=AF.Exp, scale=SCALE),
                            reads=[RB[bk]], writes=[R_w["wC"]])
                    p.op("act", lambda e, c0=c0, sp=sp: e.activation(out=sp[:, c0:1024], in_=wC[:, c0:1024], func=AF.Ln,
                                                                     bias=1.0),
                         reads=[R_w["wC"]], writes=[R_w[spn]])
                    p.op("pool", lambda e, c0=c0, sp=sp, spm=spm, mk=mk: e.tensor_tensor(
                        out=spm[:, c0:c0 + 128], in0=sp[:, c0:c0 + 128], in1=mk, op=ALU.mult),
                        reads=[R_w[spn], R_cst], writes=[R_w[spmn]])
                    if c0 + 128 < 1024:
                        p.op("pool", lambda e, c0=c0, sp=sp, spm=spm: e.tensor_copy(
                            out=spm[:, c0 + 128:1024], in_=sp[:, c0 + 128:1024]),
                            reads=[R_w[spn]], writes=[R_w[spmn]])
                for bi, (kb, zb, sp, spn, spm, spmn, mk) in enumerate(blocks):
                    ospm, ospmn = (hB, "hB") if bi == 0 else (hA, "hA")
                    Et = own_first(pp) if bi == 0 else other_first(pp)
                    for (a, b_) in chunks:
                        bk = 4 + a // 512
                        o = banks[bk][:, a % 512:a % 512 + (b_ - a)]
                        p.op("pe", lambda e, o=o, a=a, b_=b_, spm=spm: e.matmul(out=o, lhsT=SL_b, rhs=spm[:, a:b_],
                                                                                 start=True, stop=False),
                             reads=[R_w[spmn], R_cst], writes=[RB[bk]])
                        p.op("pe", lambda e, o=o, a=a, b_=b_: e.matmul(out=o, lhsT=ones_b, rhs=SPSb[:, a:b_],
                                                                       start=False, stop=False),
                             reads=[R_w["hE"], R_cst], writes=[RB[bk]])
                        p.op("pe", lambda e, o=o, a=a, b_=b_, Et=Et, ospm=ospm: e.matmul(
                            out=o, lhsT=Et, rhs=ospm[:, a:b_], start=False, stop=True),
                            reads=[R_w[ospmn], R_cst], writes=[RB[bk]])
                    u, un = (wC, "wC") if bi == 0 else (wD, "wD")
                    aT, aTn = (hC, "hC") if bi == 0 else (hD, "hD")
                    for (a, b_) in chunks:
                        bz = zb + a // 512
                        bc = 4 + a // 512
                        p.op("dve", lambda e, bz=bz, a=a, b_=b_, sp=sp, u=u: e.scalar_tensor_tensor(
                            out=u[:, a:b_], in0=banks[bz][:, a % 512:a % 512 + (b_ - a)], scalar=SCALE, in1=sp[:, a:b_],
                            op0=ALU.mult, op1=ALU.subtract), reads=[RB[bz], R_w[spn]], writes=[R_w[un]])
                        p.op("dve", lambda e, bc=bc, a=a, b_=b_, u=u: e.tensor_tensor(
                            out=u[:, a:b_], in0=u[:, a:b_], in1=banks[bc][:, a % 512:a % 512 + (b_ - a)],
                            op=ALU.subtract), reads=[RB[bc], R_w[un]], writes=[R_w[un]])
                    p.op("act", lambda e, c0=c0, u=u, aT=aT: e.activation(out=aT[:, c0:1024], in_=u[:, c0:1024],
                                                                         func=AF.Exp),
                         reads=[R_w[un]], writes=[R_w[aTn]])
                    p.op("pool", lambda e, c0=c0, aT=aT, mk=mk: e.tensor_tensor(
                        out=aT[:, c0:c0 + 128], in0=aT[:, c0:c0 + 128], in1=mk, op=ALU.mult),
                        reads=[R_w[aTn], R_cst], writes=[R_w[aTn]])
                    for (a, b_) in chunks:
                        q = a // 512
                        last = (pp == 0 and bi == 1)
                        p.op("pe", lambda e, q=q, a=a, b_=b_, kb=kb, last=last, aT=aT: e.matmul(
                            out=banks[6 + q][:, a % 512:a % 512 + (b_ - a)], lhsT=Vg[:, kb, hh * 128:(hh + 1) * 128],
                            rhs=aT[:, a:b_], start=False, stop=last), reads=[R_Vg, R_w[aTn]], writes=[RB[6 + q]])
                if pp > 0:
                    for (spm, spmn) in ((hA, "hA"), (hB, "hB")):
                        p.op("pool", lambda e, c0=c0, spm=spm: e.tensor_tensor(
                            out=SPS[:, c0:1024], in0=SPS[:, c0:1024], in1=spm[:, c0:1024], op=ALU.add),
                            reads=[R_w[spmn], R_w["wE"]], writes=[R_w["wE"]])
                    p.op("pool", lambda e, c0=c0: e.tensor_copy(out=SPSb[:, c0:1024], in_=SPS[:, c0:1024]),
                         reads=[R_w["wE"]], writes=[R_w["hE"]])
            for q in range(2):
                cs = slice(q * 512, (q + 1) * 512)
                p.op("dve", lambda e, q=q, cs=cs: e.tensor_tensor(out=mixT[:, 8 + h, cs], in0=banks[6 + q][:, :],
                                                                 in1=GT[:, hh, cs], op=ALU.mult),
                     reads=[RB[6 + q], R_GT[hh]], writes=[R_mix[8 + h]])

        R_ch = [{k: Res(f"ch{ci}_{k}") for k in ("wA", "wB", "wC", "wD", "wE", "hA", "hB", "hC", "hD", "hE")}
                for ci in range(2)]

        def sb_chain(ci, hh, h):
            zb, cb, ob = 4 * ci, 4 * ci + 2, 4 * ci + 3
            wo_ = 512 * ci
            Rc = R_ch[ci]

            def W(buf):
                return buf[:, wo_:wo_ + 512]
            spA, spB, uA, uB, SPS = W(wA), W(wB), W(wC), W(wD), W(wE)

            def qk_pair(pp_, base_):
                lo_ = max(pp_ * 128, base_) - base_
                for (kb_, zi_) in ((pp_, 0), (8 + pp_, 1)):
                    p.op("pe", lambda e, zi_=zi_, kb_=kb_, lo_=lo_, base_=base_: e.matmul(
                        out=banks[zb + zi_][:, lo_:512], lhsT=KT[:, hh, kb_ * 128:(kb_ + 1) * 128],
                        rhs=QT[:, hh, base_ + lo_:base_ + 512], start=True, stop=True),
                        reads=[R_KT[hh], R_QT[hh]], writes=[RB[zb + zi_]])
            spmA, spmB, aA, aB, SPSb = W(hA), W(hB), W(hC), W(hD), W(hE)
            for base, pmax in ((512, 7), (0, 3)):
                p.op("pe", lambda e, base=base: e.matmul(out=banks[ob][:, :], lhsT=zero_b, rhs=QT[:, hh, base:base + 512],
                                                        start=True, stop=False),
                     reads=[R_QT[hh], R_cst], writes=[RB[ob]])
                p.op("pool", lambda e: e.memset(SPS, 0.0), writes=[Rc["wE"]])
                p.op("pool", lambda e: e.memset(SPSb, 0.0), writes=[Rc["hE"]])
                yield
                for pp in range(pmax, -1, -1):
                    lo = max(pp * 128, base) - base
                    n = 512 - lo
                    diag = pp * 128 >= base
                    g0 = base + lo
                    blocks = ((pp, 0, spA, "wA", spmA, "hA", MLT_b, uA, "wC", aA, "hC"),
                              (8 + pp, 1, spB, "wB", spmB, "hB", other_first(pp), uB, "wD", aB, "hD"))
                    if pp == pmax:
                        qk_pair(pp, base)
                        yield
                    for (kb, zi, sp, spn, spm, spmn, mk, u, un, aT, aTn) in blocks:
                        p.op("act", lambda e, zi=zi, lo=lo, u=u: e.activation(out=u[:, lo:512], in_=banks[zb + zi][:, lo:512],
                                                                             func=AF.Exp, scale=SCALE),
                             reads=[RB[zb + zi]], writes=[Rc[un]])
                        p.op("act", lambda e, lo=lo, sp=sp, u=u: e.activation(out=sp[:, lo:512], in_=u[:, lo:512],
                                                                             func=AF.Ln, bias=1.0),
                             reads=[Rc[un]], writes=[Rc[spn]])
                    yield
                    for (kb, zi, sp, spn, spm, spmn, mk, u, un, aT, aTn) in blocks:
                        if diag:
                            p.op("dve", lambda e, lo=lo, sp=sp, spm=spm, mk=mk: e.tensor_tensor(
                                out=spm[:, lo:lo + 128], in0=sp[:, lo:lo + 128], in1=mk, op=ALU.mult),
                                reads=[Rc[spn], R_cst], writes=[Rc[spmn]])
                            if lo + 128 < 512:
                                p.op("dve", lambda e, lo=lo, sp=sp, spm=spm: e.tensor_copy(
                                    out=spm[:, lo + 128:512], in_=sp[:, lo + 128:512]),
                                    reads=[Rc[spn]], writes=[Rc[spmn]])
                        else:
                            p.op("dve", lambda e, lo=lo, sp=sp, spm=spm: e.tensor_copy(out=spm[:, lo:512], in_=sp[:, lo:512]),
                                 reads=[Rc[spn]], writes=[Rc[spmn]])
                        p.op("dve", lambda e, zi=zi, lo=lo, sp=sp, u=u: e.scalar_tensor_tensor(
                            out=u[:, lo:512], in0=banks[zb + zi][:, lo:512], scalar=SCALE, in1=sp[:, lo:512],
                            op0=ALU.mult, op1=ALU.subtract), reads=[RB[zb + zi], Rc[spn]], writes=[Rc[un]])
                    yield
                    for bi, (kb, zi, sp, spn, spm, spmn, mk, u, un, aT, aTn) in enumerate(blocks):
                        ospm, ospmn = (spmB, "hB") if bi == 0 else (spmA, "hA")
                        Et = own_first(pp) if bi == 0 else other_first(pp)
                        o = banks[zb + zi][:, lo:512]
                        p.op("pe", lambda e, o=o, lo=lo, spm=spm: e.matmul(out=o, lhsT=SL_b, rhs=spm[:, lo:512],
                                                                          start=True, stop=False),
                             reads=[Rc[spmn], R_cst], writes=[RB[zb + zi]])
                        p.op("pe", lambda e, o=o, lo=lo: e.matmul(out=o, lhsT=ones_b, rhs=SPSb[:, lo:512],
                                                                  start=False, stop=False),
                             reads=[Rc["hE"], R_cst], writes=[RB[zb + zi]])
                        p.op("pe", lambda e, o=o, lo=lo, Et=Et, ospm=ospm: e.matmul(out=o, lhsT=Et, rhs=ospm[:, lo:512],
                                                                                    start=False, stop=True),
                             reads=[Rc[ospmn], R_cst], writes=[RB[zb + zi]])
                    yield
                    for (kb, zi, sp, spn, spm, spmn, mk, u, un, aT, aTn) in blocks:
                        p.op("dve", lambda e, zi=zi, lo=lo, u=u: e.tensor_tensor(
                            out=u[:, lo:512], in0=u[:, lo:512], in1=banks[zb + zi][:, lo:512], op=ALU.subtract),
                            reads=[RB[zb + zi], Rc[un]], writes=[Rc[un]])
                    yield
                    if pp > 0:
                        qk_pair(pp - 1, base)
                    for (kb, zi, sp, spn, spm, spmn, mk, u, un, aT, aTn) in blocks:
                        p.op("act", lambda e, lo=lo, u=u, aT=aT: e.activation(out=aT[:, lo:512], in_=u[:, lo:512], func=AF.Exp),
                             reads=[Rc[un]], writes=[Rc[aTn]])
                        if diag:
                            p.op("pool", lambda e, lo=lo, aT=aT, mk=mk: e.tensor_tensor(
                                out=aT[:, lo:lo + 128], in0=aT[:, lo:lo + 128], in1=mk, op=ALU.mult),
                                reads=[Rc[aTn], R_cst], writes=[Rc[aTn]])
                    yield
                    for bi, (kb, zi, sp, spn, spm, spmn, mk, u, un, aT, aTn) in enumerate(blocks):
                        last = (pp == 0 and bi == 1)
                        p.op("pe", lambda e, lo=lo, kb=kb, last=last, aT=aT: e.matmul(
                            out=banks[ob][:, lo:512], lhsT=Vg[:, kb, hh * 128:(hh + 1) * 128],
                            rhs=aT[:, lo:512], start=False, stop=last), reads=[R_Vg, Rc[aTn]], writes=[RB[ob]])
                    yield
                    if pp > 0:
                        for (spm, spmn) in ((spmA, "hA"), (spmB, "hB")):
                            p.op("pool", lambda e, lo=lo, spm=spm: e.tensor_tensor(
                                out=SPS[:, lo:512], in0=SPS[:, lo:512], in1=spm[:, lo:512], op=ALU.add),
                                reads=[Rc[spmn], Rc["wE"]], writes=[Rc["wE"]])
                        p.op("pool", lambda e, lo=lo: e.tensor_copy(out=SPSb[:, lo:512], in_=SPS[:, lo:512]),
                             reads=[Rc["wE"]], writes=[Rc["hE"]])
                        yield
                p.op("dve", lambda e, base=base: e.tensor_tensor(out=mixT[:, 8 + h, base:base + 512], in0=banks[ob][:, :],
                                                                 in1=GT[:, hh, base:base + 512], op=ALU.mult),
                     reads=[RB[ob], R_GT[hh]], writes=[R_mix[8 + h]])
                yield

        def fox_chain(ci, hh, h):
            sbk, ob, db = 4 * ci, 4 * ci + 2, 4 * ci + 3
            wo_ = 512 * ci
            Rc = R_ch[ci]

            def W(buf):
                return buf[:, wo_:wo_ + 512]
            tmps = (W(wA), W(wB))
            Rtmps = (Rc["wA"], Rc["wB"])
            PTs = (W(hA), W(hB))
            RPTs = (Rc["hA"], Rc["hB"])
            fin_ = W(wC)
            for base, pmax in ((512, 7), (0, 3)):
                for bk_ in (ob, db):
                    p.op("pe", lambda e, bk_=bk_, base=base: e.matmul(out=banks[bk_][:, :], lhsT=zero_b,
                                                                    rhs=QT[:, hh, base:base + 512], start=True, stop=False),
                         reads=[R_QT[hh], R_cst], writes=[RB[bk_]])
                its = [(pp, io) for pp in range(pmax + 1) for io in (False, True)]
                nit = len(its)

                def geom(it, base=base, its=its):
                    pp, is_other = its[it]
                    kb = 8 + pp if is_other else pp
                    lo = max(pp * 128, base) - base
                    return pp, is_other, kb, lo, pp * 128 >= base

                def qk(it, base=base, geom=geom):
                    pp, is_other, kb, lo, diag = geom(it)
                    bk = sbk + it % 2
                    p.op("pe", lambda e, bk=bk, kb=kb, lo=lo, base=base: e.matmul(
                        out=banks[bk][:, lo:512], lhsT=KT[:, hh, kb * 128:(kb + 1) * 128],
                        rhs=QT[:, hh, base + lo:base + 512], start=True, stop=True),
                        reads=[R_KT[hh], R_QT[hh]], writes=[RB[bk]])

                qk(0)
                qk(1)
                yield
                for j in range(nit // 2):
                    pair = (2 * j, 2 * j + 1)
                    for it in pair:
                        pp, is_other, kb, lo, diag = geom(it)
                        bk = sbk + it % 2
                        tmp, Rtmp = tmps[it % 2], Rtmps[it % 2]
                        ns = (512 - lo) // 128
                        s0 = (base + lo) // 128
                        p.op("dve", lambda e, bk=bk, lo=lo, ns=ns, s0=s0, tmp=tmp: e.scalar_tensor_tensor(
                            out=tmp[:, lo:512].rearrange("p (s t) -> p s t", s=ns),
                            in0=banks[bk][:, lo:512].rearrange("p (s t) -> p s t", s=ns), scalar=SCALE,
                            in1=PB[:, h, s0:s0 + ns].unsqueeze(2).to_broadcast([128, ns, 128]),
                            op0=ALU.mult, op1=ALU.add), reads=[RB[bk], R_F2], writes=[Rtmp])
                    yield
                    if 2 * j + 2 < nit:
                        qk(2 * j + 2)
                        qk(2 * j + 3)
                    for it in pair:
                        pp, is_other, kb, lo, diag = geom(it)
                        tmp, Rtmp, PT, RPT = tmps[it % 2], Rtmps[it % 2], PTs[it % 2], RPTs[it % 2]
                        p.op("act", lambda e, lo=lo, kb=kb, tmp=tmp, PT=PT: e.activation(
                            out=PT[:, lo:512], in_=tmp[:, lo:512], func=AF.Exp, bias=Fneg[:, kb, h:h + 1], scale=1.0),
                            reads=[Rtmp, R_F2], writes=[RPT])
                        if diag:
                            mk = other_first(pp) if is_other else MLE_b
                            p.op("pool", lambda e, lo=lo, mk=mk, PT=PT: e.tensor_tensor(
                                out=PT[:, lo:lo + 128], in0=PT[:, lo:lo + 128], in1=mk, op=ALU.mult),
                                reads=[RPT, R_cst], writes=[RPT])
                    yield
                    for it in pair:
                        pp, is_other, kb, lo, diag = geom(it)
                        PT, RPT = PTs[it % 2], RPTs[it % 2]
                        last = (it == nit - 1)
                        p.op("pe", lambda e, lo=lo, kb=kb, PT=PT: e.matmul(
                            out=banks[ob][:, lo:512], lhsT=Vg[:, kb, hh * 128:(hh + 1) * 128], rhs=PT[:, lo:512],
                            start=False, stop=False), reads=[R_Vg, RPT], writes=[RB[ob]])
                        p.op("pe", lambda e, lo=lo, PT=PT: e.matmul(
                            out=banks[db][:, lo:512], lhsT=ones_b, rhs=PT[:, lo:512], start=False, stop=False),
                            reads=[RPT, R_cst], writes=[RB[db]])
                    yield
                for bk_ in (ob, db):
                    p.op("pe", lambda e, bk_=bk_, base=base: e.matmul(out=banks[bk_][:, :], lhsT=zero_b,
                                                                    rhs=QT[:, hh, base:base + 512], start=False, stop=True),
                         reads=[R_QT[hh], R_cst], writes=[RB[bk_]])
                p.op("dve", lambda e: e.reciprocal(out=fin_, in_=banks[db][:, :]), reads=[RB[db]], writes=[Rc["wC"]])
                p.op("dve", lambda e: e.tensor_tensor(out=fin_, in0=banks[ob][:, :], in1=fin_, op=ALU.mult),
                     reads=[RB[ob], Rc["wC"]], writes=[Rc["wC"]])
                p.op("pool", lambda e, base=base: e.tensor_tensor(out=mixT[:, h, base:base + 512], in0=fin_,
                                                                  in1=GT[:, hh, base:base + 512], op=ALU.mult),
                     reads=[Rc["wC"], R_GT[hh]], writes=[R_mix[h]])
                yield

        def run_interleaved(gens):
            gens = list(gens)
            while gens:
                for g_ in list(gens):
                    try:
                        next(g_)
                    except StopIteration:
                        gens.remove(g_)

        if stage >= 2:
            groups = [(m, h0) for m in range(2) for h0 in range(0, 8, 2)][:NGROUPS]
            issue_kvq(0, 0, "q")
            for gi, (m, h0) in enumerate(groups):
                project_group(m, h0)
                if gi + 1 < len(groups):
                    issue_kvq(*groups[gi + 1])
                if stage >= 3:
                    if m == 0:
                        if "oldfox" in DBG:
                            for hh in range(2):
                                fox_head(hh, h0 + hh)
                        else:
                            run_interleaved([fox_chain(0, 0, h0), fox_chain(1, 1, h0 + 1)])
                    elif "oldsb" in DBG:
                        pass
                    if m == 1 and h0 == 0 and ("oldsb" in DBG) != ("oldfox" in DBG):
                        p.barrier(scratch)
                    if m == 0:
                        pass
                    elif "oldsb" in DBG:
                        for hh in range(2):
                            sb_head(hh, h0 + hh)
                    else:
                        run_interleaved([sb_chain(0, 0, h0), sb_chain(1, 1, h0 + 1)])

        p.barrier(scratch)
        Alloc.top = mark0

        def sample_attention():
            offA = [hT_off]

            def allocA(shape, dt):
                esz = 4 if dt == F32 else 2
                n = int(np.prod(shape[1:]))
                off = offA[0]
                offA[0] += (n * esz + 63) // 64 * 64
                assert offA[0] <= hT_off + 16 * NTOK * 2
                return view_at(off, shape, dt)
            Kc = [allocA([128, 8, 1024], BF16) for _ in range(2)]
            Vc = [allocA([128, 8, 1024], BF16) for _ in range(2)]
            R_Kc = [[Res("Kc0a"), Res("Kc0b")], [Res("Kc1a"), Res("Kc1b")]]
            R_Vc = [[Res("Vc0a"), Res("Vc0b")], [Res("Vc1a"), Res("Vc1b")]]
            KcT = alloc([128, 8, 1024], BF16, "KcT")
            R_KcT = [Res(f"KcT{h}") for h in range(8)]
            clfT = alloc([128, 8, 32], F32, "clfT")
            csuf = alloc([128, 8, 32], F32, "csuf")
            Gsuf = alloc([128, 8, 32], F32, "Gsuf")
            Gnew = alloc([128, 8], F32, "Gnew")
            R_G = Res("G")
            sw1 = alloc([128, 1024], F32, "sw1")
            sw2 = alloc([128, 1024], F32, "sw2")
            sP = [alloc([128, 1024], BF16, f"sP{i}") for i in range(2)]
            R_sP = [Res("sP0"), Res("sP1")]
            R_sw1, R_sw2 = Res("sw1"), Res("sw2")
            Ssuf = alloc([128, 8, 128], F32, "Ssuf")
            Ssufb = alloc([128, 8, 128], BF16, "Ssufb")
            R_Ssuf = Res("Ssuf")
            nw1 = alloc([128, 512], F32, "nw1")
            nw2 = alloc([128, 512], F32, "nw2")
            Pn = alloc([128, 8, 64], BF16, "Pn")
            spmn = alloc([128, 8, 64], BF16, "spmn")
            R_n = {k: Res(k) for k in ("nw1", "nw2", "Pn", "spmn")}
            fin = alloc([128, 512], F32, "fin")
            R_fin = Res("fin")
            BDLE = cst_b[0:64, I_BDLE, 0:64]
            BDLT = cst_b[0:64, I_BDLT, 0:64]
            for b in range(4):
                p.dma("sp", clfT[:, :, b * 8:(b + 1) * 8], clf[b].rearrange("(t p) h -> p t h", p=128), writes=[R_G])
            p.op("dve", lambda e: e.memset(csuf[:, 7, :], 0.0), writes=[R_G])
            for t in range(6, -1, -1):
                p.op("dve", lambda e, t=t: e.tensor_tensor(out=csuf[:, t, :], in0=csuf[:, t + 1, :], in1=clfT[:, t + 1, :],
                                                          op=ALU.add), reads=[R_G], writes=[R_G])
            bG = 7
            for t in range(8):
                o = banks[bG][:, t * 32:(t + 1) * 32]
                p.op("pe", lambda e, o=o, t=t: e.matmul(out=o, lhsT=SL_f, rhs=clfT[:, t, :], start=True, stop=False),
                     reads=[R_G, R_cst], writes=[RB[bG]])
                p.op("pe", lambda e, o=o, t=t: e.matmul(out=o, lhsT=ones_f, rhs=csuf[:, t, :], start=False, stop=True),
                     reads=[R_G, R_cst], writes=[RB[bG]])
            p.op("pe", lambda e: e.matmul(out=banks[bG][0:64, 256:264], lhsT=cst_f[0:64, I_BDTRI, 0:64],
                                          rhs=logf[0:64, 16, :], start=True, stop=True),
                 reads=[R_F, R_cst], writes=[RB[bG]])
            R_G2 = Res("G2")
            p.op("dve", lambda e: e.tensor_copy(out=Gsuf, in_=banks[bG][:, 0:256].rearrange("p (t n) -> p t n", t=8)),
                 reads=[RB[bG]], writes=[R_G2])
            p.op("dve", lambda e: e.tensor_scalar(out=Gnew[0:64, :], in0=banks[bG][0:64, 256:264], scalar1=-1.0,
                                                  scalar2=None, op0=ALU.mult), reads=[RB[bG]], writes=[R_G2])

            if os.environ.get("DBGOUT"):
                dbg4 = nc.dram_tensor("dbg4", [128, 3072], F32, kind="ExternalOutput").ap()
                d4 = alloc([128, 3072], F32, "d4")
                R_d4 = Res("d4")
                p.op("dve", lambda e: e.tensor_copy(out=d4[:, 0:1024], in_=QTs.rearrange("p h q -> p (h q)")), reads=[R_samp], writes=[R_d4])
                p.op("dve", lambda e: e.tensor_copy(out=d4[:, 1024:2048], in_=KTs.rearrange("p h q -> p (h q)")), reads=[R_samp], writes=[R_d4])
                p.op("dve", lambda e: e.tensor_copy(out=d4[:, 2048:3072], in_=GTs.rearrange("p h q -> p (h q)")), reads=[R_samp], writes=[R_d4])
                p.dma("sp", dbg4, d4, reads=[R_d4])
                dbg2 = nc.dram_tensor("dbg2", [128, 264], F32, kind="ExternalOutput").ap()
                p.dma("sp", dbg2[:, 0:256], Gsuf.rearrange("p t n -> p (t n)"), reads=[R_G2])
                p.dma("sp", dbg2[0:64, 256:264], Gnew[0:64, :], reads=[R_G2])
            def issue_cache(j):
                m_, b_ = j // 4, j % 4
                buf_ = j % 2
                for half in range(2):
                    rows = slice(b_ * 1024 + half * 512, b_ * 1024 + (half + 1) * 512)
                    p.dma("pool", Kc[buf_][:, half * 4:(half + 1) * 4, :],
                          ck[m_][rows, :].rearrange("(t p) n -> p t n", p=128), writes=[R_Kc[buf_][half]])
                    p.dma("pool", Vc[buf_][:, half * 4:(half + 1) * 4, :],
                          cv[m_][rows, :].rearrange("(t p) n -> p t n", p=128), writes=[R_Vc[buf_][half]])

            def tr(j):
                buf_ = j % 2
                for h in range(8):
                    bk = h % 2
                    pv = bank_bf(bk)
                    for t in range(8):
                        p.op("pe", lambda e, pv=pv, t=t, h=h, buf_=buf_: e.transpose(
                            out=pv[:, t * 128:(t + 1) * 128], in_=Kc[buf_][:, t, h * 128:(h + 1) * 128],
                            identity=ident_b), reads=[R_Kc[buf_][t // 4], R_cst], writes=[RB[bk]])
                    evac(KcT[:, h, :], pv, [RB[bk]], [R_KcT[h]], eng=("dve" if h % 2 == 0 else "act"))

            issue_cache(0)
            tr(0)
            for m in range(2):
                bO, bD, bN, bC = 4, 5, 6, 7
                hb = m * 8
                for h in range(8):
                    p.op("pe", lambda e, h=h, hb=hb: e.matmul(out=banks[bN][0:64, h * 64:(h + 1) * 64], lhsT=KTs[:, hb + h, :],
                                                        rhs=QTs[:, hb + h, :], start=True, stop=True),
                         reads=[R_samp], writes=[RB[bN]])
                if m == 0:
                    p.op("dve", lambda e: e.scalar_tensor_tensor(
                        out=nw1[0:64, :].rearrange("p (h q) -> p h q", h=8),
                        in0=banks[bN][0:64, :].rearrange("p (h q) -> p h q", h=8), scalar=SCALE,
                        in1=Gnew[0:64, :].unsqueeze(2).to_broadcast([64, 8, 64]), op0=ALU.mult, op1=ALU.add),
                        reads=[RB[bN], R_G2], writes=[R_n["nw1"]])
                    p.op("act", lambda e: e.activation(out=Pn[0:64, :, :], in_=nw1[0:64, :].rearrange("p (h q) -> p h q", h=8),
                                                       func=AF.Exp), reads=[R_n["nw1"]], writes=[R_n["Pn"]])
                    p.op("pool", lambda e: e.tensor_tensor(out=Pn[0:64, :, :], in0=Pn[0:64, :, :],
                                                           in1=BDLE.unsqueeze(1).to_broadcast([64, 8, 64]), op=ALU.mult),
                         reads=[R_n["Pn"], R_cst], writes=[R_n["Pn"]])
                    p.op("pe", lambda e: e.matmul(out=banks[bN][:, :], lhsT=zero_b, rhs=cst_b[:, 0:4, :], start=True,
                                                  stop=False), reads=[R_cst], writes=[RB[bN]])
                    p.op("pe", lambda e: e.matmul(out=banks[bD][:, :], lhsT=ones_b[0:64, :],
                                                  rhs=Pn[0:64, :, :], start=True, stop=True),
                         reads=[R_n["Pn"], R_cst], writes=[RB[bD]])
                else:
                    p.op("act", lambda e: e.activation(out=nw1[0:64, :], in_=banks[bN][0:64, :], func=AF.Exp, scale=SCALE),
                         reads=[RB[bN]], writes=[R_n["nw1"]])
                    p.op("act", lambda e: e.activation(out=nw1[0:64, :], in_=nw1[0:64, :], func=AF.Ln, bias=1.0),
                         reads=[R_n["nw1"]], writes=[R_n["nw1"]])
                    p.op("pool", lambda e: e.tensor_tensor(out=spmn[0:64, :, :],
                                                           in0=nw1[0:64, :].rearrange("p (h q) -> p h q", h=8),
                                                           in1=BDLT.unsqueeze(1).to_broadcast([64, 8, 64]), op=ALU.mult),
                         reads=[R_n["nw1"], R_cst], writes=[R_n["spmn"]])
                    p.op("pe", lambda e: e.matmul(out=banks[bD][0:64, :], lhsT=SL_b[0:64, 0:64], rhs=spmn[0:64, :, :],
                                                  start=True, stop=True), reads=[R_n["spmn"], R_cst], writes=[RB[bD]])
                    p.op("dve", lambda e: e.scalar_tensor_tensor(out=nw2[0:64, :], in0=banks[bN][0:64, :], scalar=SCALE,
                                                                 in1=nw1[0:64, :], op0=ALU.mult, op1=ALU.subtract),
                         reads=[RB[bN], R_n["nw1"]], writes=[R_n["nw2"]])
                    p.op("dve", lambda e: e.tensor_tensor(out=nw2[0:64, :], in0=nw2[0:64, :], in1=banks[bD][0:64, :],
                                                          op=ALU.subtract), reads=[RB[bD], R_n["nw2"]], writes=[R_n["nw2"]])
                    p.op("act", lambda e: e.activation(out=Pn[0:64, :, :], in_=nw2[0:64, :].rearrange("p (h q) -> p h q", h=8),
                                                       func=AF.Exp), reads=[R_n["nw2"]], writes=[R_n["Pn"]])
                    p.op("pool", lambda e: e.tensor_tensor(out=Pn[0:64, :, :], in0=Pn[0:64, :, :],
                                                           in1=BDLT.unsqueeze(1).to_broadcast([64, 8, 64]), op=ALU.mult),
                         reads=[R_n["Pn"], R_cst], writes=[R_n["Pn"]])
                p.op("pe", lambda e: e.matmul(out=banks[bO][:, :], lhsT=zero_b, rhs=cst_b[:, 0:4, :], start=True, stop=False),
                     reads=[R_cst], writes=[RB[bO]])
                for h in range(8):
                    p.op("pe", lambda e, h=h, hb=hb: e.matmul(out=banks[bO][:, h * 64:(h + 1) * 64], lhsT=Vs[0:64, hb + h, :],
                                                        rhs=Pn[0:64, h, :], start=False, stop=False),
                         reads=[R_samp, R_n["Pn"]], writes=[RB[bO]])
                for b in range(4):
                    buf = (m * 4 + b) % 2
                    if m * 4 + b + 1 < 8:
                        issue_cache(m * 4 + b + 1)
                    if "trpipe" not in DBG and m * 4 + b > 0:
                        tr(m * 4 + b)
                    for t in range(8):
                        bk = 2 + t // 4
                        for h in range(8):
                            c0 = (t % 4) * 128 + h * 16
                            p.op("pe", lambda e, bk=bk, c0=c0, t=t, h=h, b=b, hb=hb: e.matmul(
                                out=banks[bk][:, c0:c0 + 16], lhsT=KcT[:, h, t * 128:(t + 1) * 128],
                                rhs=QTs[:, hb + h, b * 16:(b + 1) * 16], start=True, stop=True),
                                reads=[R_KcT[h], R_samp], writes=[RB[bk]])
                    if m * 4 + b + 1 < 8 and "trpipe" in DBG:
                        tr(m * 4 + b + 1)
                    Pb = sP[b % 2]
                    RPb = R_sP[b % 2]
                    Pb4 = Pb.rearrange("p (t h q) -> p t h q", t=8, h=8)
                    if m == 0:
                        for t in range(8):
                            bk = 2 + t // 4
                            cs = slice((t % 4) * 128, (t % 4 + 1) * 128)
                            p.op("dve", lambda e, t=t, b=b, bk=bk, cs=cs: e.scalar_tensor_tensor(
                                out=sw1[:, t * 128:(t + 1) * 128].rearrange("p (h q) -> p h q", h=8),
                                in0=banks[bk][:, cs].rearrange("p (h q) -> p h q", h=8), scalar=SCALE,
                                in1=Gsuf[:, t, b * 8:(b + 1) * 8].unsqueeze(2).to_broadcast([128, 8, 16]),
                                op0=ALU.mult, op1=ALU.add), reads=[RB[bk], R_G2], writes=[R_sw1])
                        p.op("act", lambda e, Pb=Pb: e.activation(out=Pb, in_=sw1, func=AF.Exp), reads=[R_sw1], writes=[RPb])
                        for t in range(8):
                            p.op("pe", lambda e, t=t, b=b, Pb=Pb: e.matmul(
                                out=banks[bN][:, b * 128:(b + 1) * 128],
                                lhsT=ones_b, rhs=Pb[:, t * 128:(t + 1) * 128], start=False,
                                stop=(b == 3 and t == 7)),
                                reads=[RPb, R_cst], writes=[RB[bN]])
                    else:
                        for hf in range(2):
                            p.op("act", lambda e, hf=hf: e.activation(out=sw1[:, hf * 512:(hf + 1) * 512], in_=banks[2 + hf][:, :],
                                                                      func=AF.Exp, scale=SCALE), reads=[RB[2 + hf]], writes=[R_sw1])
                        p.op("act", lambda e: e.activation(out=sw1, in_=sw1, func=AF.Ln, bias=1.0), reads=[R_sw1], writes=[R_sw1])
                        spb = sP[(b + 1) % 2]
                        Rspb = R_sP[(b + 1) % 2]
                        p.op("pool", lambda e, spb=spb: e.tensor_copy(out=spb, in_=sw1), reads=[R_sw1], writes=[Rspb])
                        spb3 = spb.rearrange("p (t n) -> p t n", t=8)
                        p.op("pool", lambda e: e.memset(Ssuf[:, 7, :], 0.0), writes=[R_Ssuf])
                        for t in range(6, -1, -1):
                            p.op("pool", lambda e, t=t, spb3=spb3: e.tensor_tensor(
                                out=Ssuf[:, t, :], in0=Ssuf[:, t + 1, :], in1=spb3[:, t + 1, :], op=ALU.add),
                                reads=[Rspb, R_Ssuf], writes=[R_Ssuf])
                        p.op("pool", lambda e: e.tensor_copy(out=Ssufb, in_=Ssuf), reads=[R_Ssuf], writes=[R_Ssuf])
                        for t in range(8):
                            bk = 5 + t // 4
                            o = banks[bk][:, (t % 4) * 128:(t % 4 + 1) * 128]
                            p.op("pe", lambda e, o=o, t=t, spb3=spb3: e.matmul(out=o, lhsT=SL_b, rhs=spb3[:, t, :],
                                                                               start=True, stop=False),
                                 reads=[Rspb, R_cst], writes=[RB[bk]])
                            p.op("pe", lambda e, o=o, t=t: e.matmul(out=o, lhsT=ones_b, rhs=Ssufb[:, t, :],
                                                                    start=False, stop=False),
                                 reads=[R_Ssuf, R_cst], writes=[RB[bk]])
                            p.op("pe", lambda e, o=o, b=b: e.matmul(out=o.rearrange("p (h q) -> p h q", h=8),
                                                                    lhsT=ones_b[0:64, :],
                                                                    rhs=spmn[0:64, :, b * 16:(b + 1) * 16],
                                                                    start=False, stop=True),
                                 reads=[R_n["spmn"], R_cst], writes=[RB[bk]])
                        for hf in range(2):
                            cs = slice(hf * 512, (hf + 1) * 512)
                            p.op("dve", lambda e, hf=hf, cs=cs: e.scalar_tensor_tensor(
                                out=sw2[:, cs], in0=banks[2 + hf][:, :], scalar=SCALE, in1=sw1[:, cs],
                                op0=ALU.mult, op1=ALU.subtract), reads=[RB[2 + hf], R_sw1], writes=[R_sw2])
                            p.op("dve", lambda e, hf=hf, cs=cs: e.tensor_tensor(
                                out=sw2[:, cs], in0=sw2[:, cs], in1=banks[5 + hf][:, :], op=ALU.subtract),
                                reads=[RB[5 + hf], R_sw2], writes=[R_sw2])
                        p.op("act", lambda e, Pb=Pb: e.activation(out=Pb, in_=sw2, func=AF.Exp), reads=[R_sw2], writes=[RPb])
                    for t in range(8):
                        for h in range(8):
                            last = (b == 3 and t == 7 and h == 7)
                            p.op("pe", lambda e, t=t, h=h, b=b, buf=buf, last=last, Pb4=Pb4: e.matmul(
                                out=banks[bO][:, h * 64 + b * 16:h * 64 + (b + 1) * 16],
                                lhsT=Vc[buf][:, t, h * 128:(h + 1) * 128], rhs=Pb4[:, t, h, :], start=False, stop=last),
                                reads=[R_Vc[buf][t // 4], RPb], writes=[RB[bO]])
                if m == 0:
                    p.op("dve", lambda e: e.tensor_copy(out=fin, in_=banks[bD][:, :]), reads=[RB[bD]], writes=[R_fin])
                    for b in range(4):
                        fv = fin.rearrange("p (h q) -> p h q", h=8)[:, :, b * 16:(b + 1) * 16]
                        p.op("dve", lambda e, b=b, fv=fv: e.tensor_tensor(
                            out=fv, in0=fv, in1=banks[bN][:, b * 128:(b + 1) * 128].rearrange("p (h q) -> p h q", h=8),
                            op=ALU.add), reads=[RB[bN], R_fin], writes=[R_fin])
                    p.op("dve", lambda e: e.reciprocal(out=fin, in_=fin), reads=[R_fin], writes=[R_fin])
                    p.op("dve", lambda e: e.tensor_tensor(out=fin, in0=banks[bO][:, :], in1=fin, op=ALU.mult),
                         reads=[RB[bO], R_fin], writes=[R_fin])
                else:
                    p.op("dve", lambda e: e.tensor_copy(out=fin, in_=banks[bO][:, :]), reads=[RB[bO]], writes=[R_fin])
                if os.environ.get("DBGOUT") and m == 0:
                    dbg3 = nc.dram_tensor("dbg3", [128, 2048], F32, kind="ExternalOutput").ap()
                    p.dma("sp", dbg3[:, 0:512], fin, reads=[R_fin])
                    p.dma("sp", dbg3[:, 512:1536], sw1, reads=[R_sw1])
                    p.dma("sp", dbg3[0:64, 1536:2048], nw1[0:64, :], reads=[R_n["nw1"]])
                p.op("pool", lambda e, hb=hb: e.tensor_tensor(out=mixT[:, hb:hb + 8, 1024:1088],
                                                              in0=fin.rearrange("p (h q) -> p h q", h=8),
                                                              in1=GTs[:, hb:hb + 8, :], op=ALU.mult),
                     reads=[R_fin, R_samp], writes=[R_mix_s[hb + h_] for h_ in range(8)])

        if stage >= 4:
            sample_attention()
            p.barrier(scratch)
            Alloc.top = mark0

        h1T = alloc([128, 16, NOWN], BF16, "h1T")
        R_h1T = [Res(f"h1T{i}") for i in range(9)]
        mark1 = Alloc.top
        AX = mybir.AxisListType

        pre_v = []

        def out_proj0():
            if stage >= 6 and "nopre" not in DBG:
                for sv_ in range(len(slab_raw)):
                    pre_v.append(load_slab(w_in1, 16, 4096 + sv_ * 256, 256))
            wo = view_at(hT_off, [128, 16, D], BF16)
            R_wo = [Res(f"wo{j}") for j in range(8)]
            srcw = w_out0.rearrange("(c p) n -> p c n", p=128)
            for j in range(8):
                p.dma("pool", wo[:, 2 * j:2 * j + 2, :], srcw[:, 2 * j:2 * j + 2, :], writes=[R_wo[j]])
            hpb = [alloc([128, D], F32, f"hpb{i}") for i in range(2)]
            R_hpb = [Res("hpb0"), Res("hpb1")]
            _x5 = alloc([128, D], BF16, "xn5")
            _r5 = Res("xn5")
            xn5 = [_x5, _x5]
            R_xn5 = [_r5, _r5]
            junk5 = alloc([128, D], BF16, "junk5")
            ssq5 = alloc([128, 32], F32, "ssq5")
            bufs = (xn5, R_xn5, junk5, Res("junk5"), ssq5, Res("ssq5"))
            for i in range(9):
                rows = 128 if i < 8 else 64
                r0 = i * 128 if i < 8 else 2048
                b = i % 2
                p.dma("sp", hpb[b][0:rows, :], xall[r0:r0 + rows, :], writes=[R_hpb[b]])
                Rm = R_mix if i < 8 else R_mix_s
                for q in range(4):
                    bk = 4 + q
                    for hd in range(16):
                        p.op("pe", lambda e, bk=bk, hd=hd, i=i, rows=rows, q=q: e.matmul(
                            out=banks[bk][0:rows, :], lhsT=mixT[:, hd, i * 128:i * 128 + rows],
                            rhs=wo[:, hd, q * 512:(q + 1) * 512], start=(hd == 0), stop=(hd == 15)),
                            reads=[Rm[hd], R_wo[hd // 2]], writes=[RB[bk]])
                    p.op("dve", lambda e, bk=bk, b=b, rows=rows, q=q: e.tensor_tensor(
                        out=hpb[b][0:rows, q * 512:(q + 1) * 512], in0=banks[bk][0:rows, :],
                        in1=hpb[b][0:rows, q * 512:(q + 1) * 512], op=ALU.add),
                        reads=[RB[bk], R_hpb[b]], writes=[R_hpb[b]])
                p.dma("sp", hp_scr[i * 128:i * 128 + rows, :], hpb[b][0:rows, :], reads=[R_hpb[b]])
                if os.environ.get("DBGOUT"):
                    if i == 0:
                        _NC_CACHE["dbg_hp"] = nc.dram_tensor("dbg_hp", [NOWN, D], F32, kind="ExternalOutput").ap()
                    p.dma("sp", _NC_CACHE["dbg_hp"][i * 128:i * 128 + rows, :], hpb[b][0:rows, :], reads=[R_hpb[b]])
                norm_transpose(i, hpb[b], R_hpb[b], rows, 16, h1T, R_h1T[i], i * 128, (i % 2) * 2, bufs=bufs)

        CH = 9 * 128 * 2

        def gT_view(c):
            return view_at(hT_off + c * CH, [128, NOWN], BF16)

        R_vb = [Res(f"vb{c}") for c in range(32)]

        def layer1():
            vb = view_at(hT_off, [128, 32, 9, 128], BF16)
            offB = [hT_off + 32 * CH]

            def allocB(shape, dt):
                esz = 4 if dt == F32 else 2
                n = int(np.prod(shape[1:]))
                off = offB[0]
                offB[0] += (n * esz + 63) // 64 * 64
                assert offB[0] <= mark0, (offB[0], mark0)
                return view_at(off, shape, dt)
            vsf = allocB([128, 4096], F32)
            R_vsf = Res("vsf")
            WspT = allocB([128, 16, 128], BF16)
            RSW = allocB([128, 16, 128], F32)
            bspB = allocB([128, 16, 128], F32)
            R_c1 = Res("l1consts")
            wtmp = vsf[:, 0:2048].rearrange("p (g s) -> p g s", g=16)
            wtb = vsf[:, 2048:3072].bitcast(BF16).rearrange("p (g s) -> p g s", g=16)
            mixed = alloc([128, NOWN], F32, "mixed")
            tprod = alloc([128, NOWN], F32, "tprod")
            szb = alloc([128, NOWN], BF16, "szb")
            bias2 = alloc([128, 128], F32, "bias2")
            BDs = alloc([128, 16, 64], BF16, "BDs")
            st1 = alloc([128, 9, 16], F32, "st1")
            st2 = alloc([128, 9, 16], F32, "st2")
            s1 = alloc([128, 9], F32, "s1")
            s2 = alloc([128, 9], F32, "s2")
            rstd1 = alloc([128, 9], F32, "rstd1")
            nmr1 = alloc([128, 9], F32, "nmr1")
            junk6 = alloc([128, 256], BF16, "junk6")
            R_w6 = {k: Res(k) for k in ("mixed", "tprod", "szb", "bias2", "BDs", "st", "junk6")}
            gamT = lngbT[:, 0:32]
            betT = lngbT[:, 32:64]
            p.dma("sp", wtmp, w_sp.rearrange("g t s -> t g s"), writes=[R_vsf])
            p.dma("sp", bspB.rearrange("p g t -> p (g t)"), b_sp.rearrange("g t -> (g t)").partition_broadcast(128),
                  writes=[R_c1])
            p.op("dve", lambda e: e.tensor_copy(out=wtb, in_=wtmp), reads=[R_vsf], writes=[R_vsf])
            for hf in range(2):
                pv = bank_bf(hf)
                for gg in range(8):
                    g = hf * 8 + gg
                    p.op("pe", lambda e, pv=pv, gg=gg, g=g: e.transpose(out=pv[:, gg * 128:(gg + 1) * 128], in_=wtb[:, g, :],
                                                                          identity=ident_b),
                         reads=[R_vsf, R_cst], writes=[RB[hf]])
                p.op("dve", lambda e, pv=pv, hf=hf: e.tensor_tensor(
                    out=WspT[:, hf * 8:(hf + 1) * 8, :], in0=pv.rearrange("p (g t) -> p g t", g=8),
                    in1=MLE_b.unsqueeze(1).to_broadcast([128, 8, 128]), op=ALU.mult),
                    reads=[RB[hf], R_cst], writes=[R_c1])
            for g4 in range(4):
                bk = 2 + g4 % 2
                p.op("pe", lambda e, bk=bk, g4=g4: e.matmul(out=banks[bk][:, :], lhsT=ones_b,
                                                            rhs=WspT[:, g4 * 4:(g4 + 1) * 4, :], start=True, stop=True),
                     reads=[R_c1, R_cst], writes=[RB[bk]])
                p.op("dve", lambda e, bk=bk, g4=g4: e.tensor_copy(
                    out=RSW[:, g4 * 4:(g4 + 1) * 4, :], in_=banks[bk][:, :].rearrange("p (g t) -> p g t", g=4)),
                    reads=[RB[bk]], writes=[R_c1])
            W16t = alloc([128, 16, 64], BF16, "W16t")
            R_w16 = Res("W16t")
            for bb in range(4):
                p.op("dve", lambda e, bb=bb: e.tensor_copy(out=W16t[0:16, :, 16 * bb:16 * bb + 16], in_=WspT[0:16, :, 0:16]),
                     reads=[R_c1], writes=[R_w16])
            for hf in range(2):
                p.op("pe", lambda e, hf=hf: e.matmul(out=banks[2 + hf][0:64, :], lhsT=cst_b[0:16, I_SEL, 0:64],
                                                     rhs=W16t[0:16, hf * 8:(hf + 1) * 8, :], start=True, stop=True),
                     reads=[R_w16, R_cst], writes=[RB[2 + hf]])
                p.op("dve", lambda e, hf=hf: e.tensor_tensor(
                    out=BDs[0:64, hf * 8:(hf + 1) * 8, :], in0=banks[2 + hf][0:64, :].rearrange("p (g t) -> p g t", g=8),
                    in1=cst_b[0:64, I_SAME, 0:64].unsqueeze(1).to_broadcast([64, 8, 64]), op=ALU.mult),
                    reads=[RB[2 + hf], R_cst], writes=[R_w6["BDs"]])
            p.op("dve", lambda e: e.memset(st1, 0.0), writes=[R_w6["st"]])
            p.op("dve", lambda e: e.memset(st2, 0.0), writes=[R_w6["st"]])
            if stage >= 6:
                for sv in range(16):
                    wvs, Rwvs = pre_v[sv] if sv < len(pre_v) else load_slab(w_in1, 16, 4096 + sv * 256, 256)
                    for i in range(9):
                        rows = 128 if i < 8 else 64
                        bk = next_bank(2, 8)
                        for k in range(16):
                            p.op("pe", lambda e, bk=bk, k=k, i=i, rows=rows, wvs=wvs: e.matmul(
                                out=banks[bk][0:rows, 0:256], lhsT=h1T[:, k, i * 128:i * 128 + rows], rhs=wvs[:, k, :],
                                start=(k == 0), stop=(k == 15)), reads=[R_h1T[i], Rwvs[k]], writes=[RB[bk]])
                        p.op("act", lambda e, bk=bk, i=i, rows=rows, sv=sv: e.activation(
                            out=vb[0:rows, 2 * sv:2 * sv + 2, i, :],
                            in_=banks[bk][0:rows, 0:256].rearrange("p (a b) -> p a b", a=2), func=AF.Copy,
                            accum_out=st1[0:rows, i, sv:sv + 1]),
                            reads=[RB[bk]], writes=[R_vb[2 * sv], R_vb[2 * sv + 1], R_w6["st"]])
                        p.op("act", lambda e, bk=bk, i=i, rows=rows, sv=sv: e.activation(
                            out=junk6[0:rows, :], in_=banks[bk][0:rows, 0:256], func=AF.Square,
                            accum_out=st2[0:rows, i, sv:sv + 1]),
                            reads=[RB[bk]], writes=[R_w6["junk6"], R_w6["st"]])
                        if i == 8:
                            p.op("dve", lambda e, bk=bk, sv=sv: e.tensor_copy(out=vsf[0:64, sv * 256:(sv + 1) * 256],
                                                                              in_=banks[bk][0:64, 0:256]),
                                 reads=[RB[bk]], writes=[R_vsf])
                p.op("dve", lambda e: e.reduce_sum(out=s1, in_=st1, axis=AX.X), reads=[R_w6["st"]], writes=[R_w6["st"]])
                p.op("dve", lambda e: e.reduce_sum(out=s2, in_=st2, axis=AX.X), reads=[R_w6["st"]], writes=[R_w6["st"]])
                p.op("dve", lambda e: e.tensor_scalar(out=s1, in0=s1, scalar1=1.0 / 4096, scalar2=None, op0=ALU.mult),
                     reads=[R_w6["st"]], writes=[R_w6["st"]])
                p.op("dve", lambda e: e.tensor_tensor(out=nmr1, in0=s1, in1=s1, op=ALU.mult),
                     reads=[R_w6["st"]], writes=[R_w6["st"]])
                p.op("dve", lambda e: e.scalar_tensor_tensor(out=s2, in0=s2, scalar=1.0 / 4096, in1=nmr1, op0=ALU.mult,
                                                             op1=ALU.subtract),
                     reads=[R_w6["st"]], writes=[R_w6["st"]])
                p.op("act", lambda e: e.activation(out=rstd1, in_=s2, func=AF.Ln, bias=1e-5), reads=[R_w6["st"]],
                     writes=[R_w6["st"]])
                p.op("act", lambda e: e.activation(out=rstd1, in_=rstd1, func=AF.Exp, scale=-0.5), reads=[R_w6["st"]],
                     writes=[R_w6["st"]])
                p.op("dve", lambda e: e.scalar_tensor_tensor(out=nmr1, in0=s1, scalar=-1.0, in1=rstd1, op0=ALU.mult,
                                                             op1=ALU.mult),
                     reads=[R_w6["st"]], writes=[R_w6["st"]])
                for i in range(9):
                    rows = 128 if i < 8 else 64
                    p.op("dve", lambda e, i=i, rows=rows: e.tensor_scalar(
                        out=vb[0:rows, :, i, :], in0=vb[0:rows, :, i, :], scalar1=rstd1[0:rows, i:i + 1],
                        scalar2=nmr1[0:rows, i:i + 1], op0=ALU.mult, op1=ALU.add),
                        reads=[R_w6["st"]] + R_vb, writes=R_vb)
                gb = mixed[0:64, 0:1024]
                for j in range(8):
                    cs = slice(j * 512, (j + 1) * 512)
                    p.dma("sp", gb[:, 0:512], ln_g[cs].partition_broadcast(64), writes=[R_w6["mixed"]])
                    p.dma("sp", gb[:, 512:1024], ln_b[cs].partition_broadcast(64), writes=[R_w6["mixed"]])
                    p.op("dve", lambda e, cs=cs: e.tensor_scalar(out=vsf[0:64, cs], in0=vsf[0:64, cs], scalar1=rstd1[0:64, 8:9],
                                                                 scalar2=nmr1[0:64, 8:9], op0=ALU.mult, op1=ALU.add),
                         reads=[R_vsf, R_w6["st"]], writes=[R_vsf])
                    p.op("dve", lambda e, cs=cs: e.tensor_tensor(out=vsf[0:64, cs], in0=vsf[0:64, cs], in1=gb[:, 0:512],
                                                                 op=ALU.mult), reads=[R_vsf, R_w6["mixed"]], writes=[R_vsf])
                    p.op("dve", lambda e, cs=cs: e.tensor_tensor(out=vsf[0:64, cs], in0=vsf[0:64, cs], in1=gb[:, 512:1024],
                                                                 op=ALU.add), reads=[R_vsf, R_w6["mixed"]], writes=[R_vsf])
                    p.dma("sp", o_sgu[:, cs], vsf[0:64, cs], reads=[R_vsf])
            if stage >= 7:
                n0 = len(slab_raw)
                for ex in (range(2) if "ext" in DBG else []):
                    slab_raw.append(vsf[:, ex * 2048:(ex + 1) * 2048].bitcast(BF16))
                    R_slab.append([Res(f"slabx{ex}_{q}") for q in range(4)])
                    slab_first[n0 + ex] = [R_vsf]
                us = zs = Rus = Rzs = None
                for c in range(32):
                    g = c // 2
                    if c % 2 == 0:
                        us, Rus = load_slab(w_in1, 16, (c // 2) * 256, 256)
                        zs, Rzs = load_slab(w_in1, 16, 8192 + (c // 2) * 256, 256)
                    wc = slice((c % 2) * 128, (c % 2 + 1) * 128)
                    for i in range(8):
                        bk = 4 + i // 4
                        p.op("pe", lambda e, bk=bk, i=i, c=c, g=g: e.matmul(
                            out=banks[bk][:, (i % 4) * 128:(i % 4 + 1) * 128], lhsT=vb[:, c, i, :], rhs=WspT[:, g, :],
                            start=True, stop=True), reads=[R_vb[c], R_c1], writes=[RB[bk]])
                    p.op("pe", lambda e, c=c, g=g: e.matmul(out=banks[7][:, 128:192], lhsT=vb[0:64, c, 8, :],
                                                            rhs=BDs[0:64, g, :], start=True, stop=True),
                         reads=[R_vb[c], R_w6["BDs"]], writes=[RB[7]])
                    for (slab, Rs, bk0, soff) in ((us, Rus, 0, 0), (zs, Rzs, 2, 64)):
                        for (t0, t1, bk, col) in ((0, 512, bk0, 0), (512, 1024, bk0 + 1, 0), (1024, 1088, 6, soff)):
                            n = t1 - t0
                            Rr = [R_h1T[t] for t in range(t0 // 128, (t1 + 127) // 128)]
                            for k in range(16):
                                p.op("pe", lambda e, bk=bk, col=col, n=n, k=k, wc=wc, t0=t0, t1=t1, slab=slab: e.matmul(
                                    out=banks[bk][:, col:col + n], lhsT=slab[:, k, wc], rhs=h1T[:, k, t0:t1],
                                    start=(k == 0), stop=(k == 15)), reads=Rr + [Rs[k]], writes=[RB[bk]])
                    p.op("dve", lambda e, c=c, g=g: e.scalar_tensor_tensor(
                        out=bias2, in0=RSW[:, g, :], scalar=betT[:, c:c + 1], in1=bspB[:, g, :], op0=ALU.mult, op1=ALU.add),
                        reads=[R_c1, R_cst], writes=[R_w6["bias2"]])
                    for hf in range(2):
                        p.op("dve", lambda e, hf=hf, c=c: e.scalar_tensor_tensor(
                            out=mixed[:, hf * 512:(hf + 1) * 512].rearrange("p (i t) -> p i t", i=4),
                            in0=banks[4 + hf][:, :].rearrange("p (i t) -> p i t", i=4), scalar=gamT[:, c:c + 1],
                            in1=bias2.unsqueeze(1).to_broadcast([128, 4, 128]), op0=ALU.mult, op1=ALU.add),
                            reads=[RB[4 + hf], R_w6["bias2"], R_cst], writes=[R_w6["mixed"]])
                    p.op("dve", lambda e, c=c: e.scalar_tensor_tensor(
                        out=mixed[:, 1024:1088].rearrange("p (i t) -> p i t", i=4),
                        in0=banks[7][:, 128:192].rearrange("p (i t) -> p i t", i=4), scalar=gamT[:, c:c + 1],
                        in1=bias2[:, 0:16].unsqueeze(1).to_broadcast([128, 4, 16]), op0=ALU.mult, op1=ALU.add),
                        reads=[RB[7], R_w6["bias2"], R_cst], writes=[R_w6["mixed"]])
                    for (bk, pc, oc) in ((2, slice(0, 512), slice(0, 512)), (3, slice(0, 512), slice(512, 1024)),
                                         (6, slice(64, 128), slice(1024, 1088))):
                        p.op("act", lambda e, bk=bk, pc=pc, oc=oc: e.activation(out=szb[:, oc], in_=banks[bk][:, pc],
                                                                                func=AF.Silu),
                             reads=[RB[bk]], writes=[R_w6["szb"]])
                    for (bk, pc, oc) in ((0, slice(0, 512), slice(0, 512)), (1, slice(0, 512), slice(512, 1024)),
                                         (6, slice(0, 64), slice(1024, 1088))):
                        p.op("dve", lambda e, bk=bk, pc=pc, oc=oc: e.tensor_tensor(out=tprod[:, oc], in0=banks[bk][:, pc],
                                                                                   in1=mixed[:, oc], op=ALU.mult),
                             reads=[RB[bk], R_w6["mixed"]], writes=[R_w6["tprod"]])
                    gv = gT_view(c)
                    p.op("pool", lambda e, gv=gv: e.tensor_tensor(out=gv, in0=tprod, in1=szb, op=ALU.mult),
                         reads=[R_w6["tprod"], R_w6["szb"]], writes=[R_vb[c]])

        def out_proj1():
            del slab_raw[NSLAB:]
            del R_slab[NSLAB:]
            off7 = hT_off + 32 * CH
            hpF = view_at(off7, [128, 9, D], F32)
            off7 += 9 * D * 4
            slabB = view_at(off7, [128, 32, 384], BF16)
            off7 += 32 * 384 * 2
            ssq7 = view_at(off7, [128, 16], F32)
            off7 += 64
            junk7 = view_at(off7, [128, 512], BF16)
            off7 += 1024
            assert off7 <= ARENA, off7
            slabA = slab_raw[0]
            assert NSLAB * SLAB_BYTES >= 32 * 384 * 2
            slabA = view_at(slab_off, [128, 32, 384], BF16)
            R_hpF = [Res(f"hpF{i}") for i in range(9)]
            R_s7 = [Res("s7A"), Res("s7B")]
            slabs7 = [slabA, slabB]
            for i in range(9):
                rows = 128 if i < 8 else 64
                p.dma("sp", hpF[0:rows, i, :], hp_scr[i * 128:i * 128 + rows, :], writes=[R_hpF[i]])
            gfB = view_at(slab_off, [128, D], F32)
            R_q7 = Res("q7")

            def final_norm(i, rows):
                for q in range(4):
                    p.op("act", lambda e, i=i, rows=rows, q=q: e.activation(
                        out=junk7[0:rows, :], in_=hpF[0:rows, i, q * 512:(q + 1) * 512], func=AF.Square,
                        accum_out=ssq7[0:rows, q:q + 1]), reads=[R_hpF[i]], writes=[R_q7])
                p.op("dve", lambda e, rows=rows: e.reduce_sum(out=ssq7[0:rows, 4:5], in_=ssq7[0:rows, 0:4], axis=AX.X),
                     reads=[R_q7], writes=[R_q7])
                p.op("act", lambda e, rows=rows: e.activation(out=ssq7[0:rows, 4:5], in_=ssq7[0:rows, 4:5], func=AF.Ln,
                                                              scale=1.0 / D, bias=1e-6), reads=[R_q7], writes=[R_q7])
                p.op("act", lambda e, rows=rows: e.activation(out=ssq7[0:rows, 4:5], in_=ssq7[0:rows, 4:5], func=AF.Exp,
                                                              scale=-0.5), reads=[R_q7], writes=[R_q7])
                p.op("dve", lambda e, i=i, rows=rows: e.scalar_tensor_tensor(
                    out=hpF[0:rows, i, :], in0=hpF[0:rows, i, :], scalar=ssq7[0:rows, 4:5], in1=gfB[0:rows, :],
                    op0=ALU.mult, op1=ALU.mult), reads=[R_hpF[i], R_q7, R_s7[0]], writes=[R_hpF[i]])
                dst = y_own[i * 128:(i + 1) * 128, :] if i < 8 else y_s
                p.dma("sp", dst, hpF[0:rows, i, :], reads=[R_hpF[i]])

            srcw = w_out1.rearrange("(c p) n -> p c n", p=128)
            for j in range(6):
                if j == 5:
                    p.op("dve", lambda e: e.memset(gfB, 0.0), reads=[R_s7[0]], writes=[R_s7[0]])
                    p.dma("sp", gfB, final_g.partition_broadcast(128), reads=[R_s7[0]], writes=[R_s7[0]])
                c0 = j * 384
                n = min(384, D - c0)
                sl_ = slabs7[j % 2]
                Rs = R_s7[j % 2]
                R_parts = []
                for k0 in range(0, 32, 8):
                    p.dma("pool", sl_[:, k0:k0 + 8, 0:n], srcw[:, k0:k0 + 8, c0:c0 + n], writes=[Rs] + [r for rl_ in R_slab for r in rl_])
                for i in range(9):
                    rows = 128 if i < 8 else 64
                    bk = next_bank(0, 8)
                    for c in range(32):
                        gv = gT_view(c)
                        p.op("pe", lambda e, bk=bk, c=c, i=i, rows=rows, n=n, gv=gv, sl_=sl_: e.matmul(
                            out=banks[bk][0:rows, 0:n], lhsT=gv[:, i * 128:i * 128 + rows], rhs=sl_[:, c, 0:n],
                            start=(c == 0), stop=(c == 31)), reads=[R_vb[c], Rs], writes=[RB[bk]])
                    p.op("dve", lambda e, bk=bk, i=i, rows=rows, n=n, c0=c0: e.tensor_tensor(
                        out=hpF[0:rows, i, c0:c0 + n], in0=banks[bk][0:rows, 0:n], in1=hpF[0:rows, i, c0:c0 + n],
                        op=ALU.add), reads=[RB[bk], R_hpF[i]], writes=[R_hpF[i]])
                    if j == 5:
                        final_norm(i, rows)

        if stage >= 5:
            out_proj0()
            p.barrier(scratch)
            Alloc.top = mark1
            if "nol1" not in DBG:
                layer1()
                p.barrier(scratch)
            if stage >= 8:
                out_proj1()
                p.barrier(scratch)

        if os.environ.get("DBGOUT") and stage < 5:
            dbg = nc.dram_tensor("dbg", [128, 16, NOWN], F32, kind="ExternalOutput").ap()
            dst_ = [alloc([128, NOWN], F32, f"dbgst{i}") for i in range(2)]
            R_d = [Res("d0"), Res("d1")]
            for h in range(16):
                p.op("dve", lambda e, h=h: e.tensor_copy(out=dst_[h % 2], in_=mixT[:, h, :]),
                     reads=[R_mix[h], R_mix_s[h]], writes=[R_d[h % 2]])
                p.dma("sp", dbg[:, h, :], dst_[h % 2], reads=[R_d[h % 2]])

        p.emit(st)
        _NC_CACHE["trace"] = p.trace
    return nc


def _consts(half):
    c = np.zeros((14, 128, 128), np.float32)
    i = np.arange(128)
    c[0] = np.eye(128)
    c[1] = (i[:, None] <= i[None, :])
    c[2] = 1.0
    c[3] = 1.0 if half == 0 else 0.0
    c[4] = 1.0 if half == 1 else 0.0
    c[5] = (i[:, None] > i[None, :])
    c[6] = (i[:, None] <= i[None, :])
    c[7] = (i[:, None] < i[None, :])
    c[8] = 0.0
    same = (i[:, None] // 16 == i[None, :] // 16) & (i[:, None] < 64) & (i[None, :] < 64)
    c[9] = same & (i[:, None] <= i[None, :])
    c[10] = same & (i[:, None] <= i[None, :])
    c[11] = same & (i[:, None] < i[None, :])
    c[12] = same
    c[13] = (i[:, None] < 16) & (i[None, :] < 64) & (i[None, :] % 16 == i[:, None])
    return np.ascontiguousarray(c.transpose(1, 0, 2).reshape(128, 14 * 128))


_NC_CACHE = {}


def kernel(x_prompt, x_sample, cache_fox_k, cache_fox_v, cache_fox_logf, cache_sb_k, cache_sb_v,
           norm0_g, w_in0, b_forget, w_out0, norm1_g, w_in1, sgu_ln_g, sgu_ln_b, w_sp, b_sp, w_out1, final_g):
    f = lambda a: np.ascontiguousarray(np.asarray(a, dtype=np.float32))
    x_prompt, x_sample = f(x_prompt), f(x_sample)
    if "nc" not in _NC_CACHE:
        _NC_CACHE["nc"] = build_program(STAGE)
    nc = _NC_CACHE["nc"]
    gT = np.concatenate([f(g).reshape(16, 128).T for g in (norm0_g, norm1_g, final_g)], axis=1)
    lngb = np.concatenate([f(g).reshape(32, 128).T for g in (sgu_ln_g, sgu_ln_b)], axis=1)
    shared = dict(w_in0=f(w_in0), w_out0=f(w_out0), w_in1=f(w_in1), w_out1=f(w_out1),
                  gT=np.ascontiguousarray(gT), bforget=f(b_forget), lngb=np.ascontiguousarray(lngb),
                  ln_g=f(sgu_ln_g), ln_b=f(sgu_ln_b), final_g=f(final_g), w_sp=f(w_sp), b_sp=f(b_sp))
    in_maps = []
    for c in range(NCORES):
        b, half = c // 2, c % 2
        xb = x_prompt[b].reshape(16, 128, D)
        order = OWN[half] + OTHER[half]
        xs = x_sample[4 * c:4 * c + 4].reshape(64, D)
        xall = np.concatenate([xb[order].reshape(2048, D), xs], axis=0)
        m = dict(shared)
        m["xall"] = np.ascontiguousarray(xall)
        m["cfk"] = f(cache_fox_k[4 * c:4 * c + 4]).reshape(4096, 1024)
        m["cfv"] = f(cache_fox_v[4 * c:4 * c + 4]).reshape(4096, 1024)
        m["csk"] = f(cache_sb_k[4 * c:4 * c + 4]).reshape(4096, 1024)
        m["csv"] = f(cache_sb_v[4 * c:4 * c + 4]).reshape(4096, 1024)
        m["clf"] = f(cache_fox_logf[4 * c:4 * c + 4])
        m["cst"] = _consts(half)
        in_maps.append(m)
    res = run_bass_kernel_spmd(nc, in_maps, core_ids=list(range(NCORES)))
    R = res.results
    B, S = 4, 2048

    def gather_prompt(name, width):
        out = np.zeros((B, 16, 128, width), np.float32)
        for c in range(NCORES):
            b, half = c // 2, c % 2
            out[b, OWN[half]] = R[c][name].reshape(8, 128, width)
        return out.reshape(B, S, width)

    def gather_sample(name, width):
        return np.concatenate([R[c][name].reshape(4, 16, width) for c in range(NCORES)], axis=0)

    y_prompt = gather_prompt("y_own", D)
    y_sample = gather_sample("y_s", D)
    fkp = gather_prompt("o_fk", 1024).reshape(B, S, 8, 128)
    fvp = gather_prompt("o_fv", 1024).reshape(B, S, 8, 128)
    lfp = gather_prompt("o_lf", 8)
    fks = gather_sample("o_fk_s", 1024).reshape(32, 16, 8, 128)
    fvs = gather_sample("o_fv_s", 1024).reshape(32, 16, 8, 128)
    lfs = gather_sample("o_lf_s", 8)
    skp = gather_prompt("o_sk", 1024).reshape(B, S, 8, 128)
    svp = gather_prompt("o_sv", 1024).reshape(B, S, 8, 128)
    sks = gather_sample("o_sk_s", 1024).reshape(32, 16, 8, 128)
    svs = gather_sample("o_sv_s", 1024).reshape(32, 16, 8, 128)
    sgu = gather_sample("o_sgu", 4096)
    return (y_prompt, y_sample, fkp, fvp, lfp, fks, fvs, lfs, skp, svp, sks, svs, sgu)
```
